# Optimizing a Trainium2 kernel written in Bass

```python
import math
import jax, jax.numpy as jnp
from jax import lax
import numpy as np

D_MODEL = 1024
BATCH = 8
SEQ = 4096
DEPTH = 2

N_A_LAYERS = DEPTH // 2
N_B_LAYERS = DEPTH - N_A_LAYERS

MLA_HEADS = 8
MLA_NOPE = 128
MLA_ROPE = 64
MLA_V = 128
MLA_Q_LORA = 384
MLA_KV_LORA = 256

DIFF_HEADS = 8
DIFF_HEAD_DIM = 64
DIFF_QK_WIDTH = DIFF_HEADS * 2 * DIFF_HEAD_DIM
DIFF_V_WIDTH = DIFF_HEADS * 2 * DIFF_HEAD_DIM

D_FF = -(-8 * D_MODEL // (3 * 256)) * 256

ROPE_THETA = 10000.0
Q_BLOCK = 128
DEEPNORM_ALPHA = (2.0 * DEPTH) ** 0.25
DEEPNORM_BETA = (8.0 * DEPTH) ** -0.25
LN_EPS = 1e-5
RMS_EPS = 1e-6

kernel_name = "yoco_mla_diffattn_deepnorm"

F32 = jnp.float32


def _layernorm(x, g, b):
    xf = x.astype(F32)
    mu = jnp.mean(xf, axis=-1, keepdims=True)
    var = jnp.mean(jnp.square(xf - mu), axis=-1, keepdims=True)
    y = (xf - mu) * lax.rsqrt(var + LN_EPS)
    return (y * g.astype(F32) + b.astype(F32)).astype(x.dtype)


def _rmsnorm(x, g):
    xf = x.astype(F32)
    y = xf * lax.rsqrt(jnp.mean(jnp.square(xf), axis=-1, keepdims=True) + RMS_EPS)
    return (y * g.astype(F32)).astype(x.dtype)


def _rope_tables(seq_len, dim):
    inv = 1.0 / (ROPE_THETA ** (jnp.arange(0, dim, 2, dtype=F32) / dim))
    ang = jnp.arange(seq_len, dtype=F32)[:, None] * inv[None, :]
    ang = jnp.concatenate([ang, ang], axis=-1)
    return jnp.cos(ang), jnp.sin(ang)


def _rope(x, cos, sin):
    shape = (1, x.shape[1]) + (1,) * (x.ndim - 3) + (x.shape[-1],)
    c = cos.reshape(shape)
    s = sin.reshape(shape)
    xf = x.astype(F32)
    half = x.shape[-1] // 2
    rot = jnp.concatenate([-xf[..., half:], xf[..., :half]], axis=-1)
    return (xf * c + rot * s).astype(x.dtype)


def _causal_mask(i, seq_len):
    qpos = i * Q_BLOCK + jnp.arange(Q_BLOCK)
    kpos = jnp.arange(seq_len)
    return kpos[None, :] <= qpos[:, None]


def _sweep_query_blocks(block_fn, seq_len):
    out = lax.map(block_fn, jnp.arange(seq_len // Q_BLOCK))
    out = jnp.moveaxis(out, 0, 1)
    return out.reshape((out.shape[0], seq_len) + out.shape[3:])


def _mla(h, w_dq, q_norm, w_uq, w_dkv, kv_norm, w_ukv, w_o, cos, sin):
    B, S, _ = h.shape
    cq = _rmsnorm(h @ w_dq, q_norm)
    q = (cq @ w_uq).reshape(B, S, MLA_HEADS, MLA_NOPE + MLA_ROPE)
    q_nope = q[..., :MLA_NOPE]
    q_rope = _rope(q[..., MLA_NOPE:], cos, sin)
    ckv = h @ w_dkv
    c = _rmsnorm(ckv[..., :MLA_KV_LORA], kv_norm)
    k_rope = _rope(ckv[..., MLA_KV_LORA:], cos, sin)
    kv = (c @ w_ukv).reshape(B, S, MLA_HEADS, MLA_NOPE + MLA_V)
    k_nope = kv[..., :MLA_NOPE]
    v = kv[..., MLA_NOPE:]
    scale = (MLA_NOPE + MLA_ROPE) ** -0.5

    def block(i):
        start = i * Q_BLOCK
        qn = lax.dynamic_slice_in_dim(q_nope, start, Q_BLOCK, axis=1)
        qr = lax.dynamic_slice_in_dim(q_rope, start, Q_BLOCK, axis=1)
        s = (jnp.einsum('bqhd,bkhd->bhqk', qn, k_nope).astype(F32)
             + jnp.einsum('bqhr,bkr->bhqk', qr, k_rope).astype(F32)) * scale
        s = jnp.where(_causal_mask(i, S)[None, None], s, -jnp.inf)
        p = jax.nn.softmax(s, axis=-1).astype(v.dtype)
        return jnp.einsum('bhqk,bkhd->bqhd', p, v)

    o = _sweep_query_blocks(block, S)
    return o.reshape(B, S, MLA_HEADS * MLA_V) @ w_o


def _shared_kv(h, w_kv, cos, sin):
    B, S, _ = h.shape
    kv = h @ w_kv
    k = _rope(kv[..., :DIFF_QK_WIDTH].reshape(B, S, DIFF_HEADS, 2, DIFF_HEAD_DIM), cos, sin)
    v = kv[..., DIFF_QK_WIDTH:].reshape(B, S, DIFF_HEADS, 2 * DIFF_HEAD_DIM)
    return k, v


def _diff_attn(h, k, v, w_q, lq1, lk1, lq2, lk2, subln, w_o, lambda_init, cos, sin):
    B, S, _ = h.shape
    q = _rope((h @ w_q).reshape(B, S, DIFF_HEADS, 2, DIFF_HEAD_DIM), cos, sin)
    lam = (jnp.exp(jnp.sum(lq1.astype(F32) * lk1.astype(F32)))
           - jnp.exp(jnp.sum(lq2.astype(F32) * lk2.astype(F32))) + lambda_init)
    scale = DIFF_HEAD_DIM ** -0.5

    def block(i):
        qb = lax.dynamic_slice_in_dim(q, i * Q_BLOCK, Q_BLOCK, axis=1)
        s = jnp.einsum('bqhcd,bkhcd->bhcqk', qb, k).astype(F32) * scale
        s = jnp.where(_causal_mask(i, S)[None, None, None], s, -jnp.inf)
        p = jax.nn.softmax(s, axis=-1)
        w = (p[:, :, 0] - lam * p[:, :, 1]).astype(v.dtype)
        return jnp.einsum('bhqk,bkhe->bqhe', w, v)

    o = _sweep_query_blocks(block, S)
    o = _rmsnorm(o, subln) * (1.0 - lambda_init)
    return o.reshape(B, S, DIFF_V_WIDTH) @ w_o


def _swiglu(h, w_gate_up, w_down):
    gu = h @ w_gate_up
    g, u = gu[..., :D_FF], gu[..., D_FF:]
    return (jax.nn.silu(g) * u) @ w_down


def _w(k, shape, fan_in, scale=1.0):
    return jax.random.normal(k, shape, F32) * (scale * fan_in ** -0.5)


def setup_inputs(seed: int = 0) -> dict:
    key = jax.random.key(seed)
    ks = iter(jax.random.split(key, 32))
    nA, nB, L = N_A_LAYERS, N_B_LAYERS, DEPTH
    x = jax.random.normal(next(ks), (BATCH, SEQ, D_MODEL), F32)

    mla_w_dq = _w(next(ks), (nA, D_MODEL, MLA_Q_LORA), D_MODEL)
    mla_q_norm = 1.0 + 0.02 * jax.random.normal(next(ks), (nA, MLA_Q_LORA), F32)
    mla_w_uq = _w(next(ks), (nA, MLA_Q_LORA, MLA_HEADS * (MLA_NOPE + MLA_ROPE)), MLA_Q_LORA)
    mla_w_dkv = _w(next(ks), (nA, D_MODEL, MLA_KV_LORA + MLA_ROPE), D_MODEL)
    mla_kv_norm = 1.0 + 0.02 * jax.random.normal(next(ks), (nA, MLA_KV_LORA), F32)
    uk = _w(next(ks), (nA, MLA_KV_LORA, MLA_HEADS, MLA_NOPE), MLA_KV_LORA)
    uv = _w(next(ks), (nA, MLA_KV_LORA, MLA_HEADS, MLA_V), MLA_KV_LORA, DEEPNORM_BETA)
    mla_w_ukv = jnp.concatenate([uk, uv], axis=-1).reshape(nA, MLA_KV_LORA, MLA_HEADS * (MLA_NOPE + MLA_V))
    mla_w_o = _w(next(ks), (nA, MLA_HEADS * MLA_V, D_MODEL), MLA_HEADS * MLA_V, DEEPNORM_BETA)

    wk = _w(next(ks), (D_MODEL, DIFF_QK_WIDTH), D_MODEL)
    wv = _w(next(ks), (D_MODEL, DIFF_V_WIDTH), D_MODEL, DEEPNORM_BETA)
    kv_w = jnp.concatenate([wk, wv], axis=-1)

    diff_w_q = _w(next(ks), (nB, D_MODEL, DIFF_QK_WIDTH), D_MODEL)
    diff_lq1 = 0.1 * jax.random.normal(next(ks), (nB, DIFF_HEAD_DIM), F32)
    diff_lk1 = 0.1 * jax.random.normal(next(ks), (nB, DIFF_HEAD_DIM), F32)
    diff_lq2 = 0.1 * jax.random.normal(next(ks), (nB, DIFF_HEAD_DIM), F32)
    diff_lk2 = 0.1 * jax.random.normal(next(ks), (nB, DIFF_HEAD_DIM), F32)
    diff_subln = 1.0 + 0.02 * jax.random.normal(next(ks), (nB, 2 * DIFF_HEAD_DIM), F32)
    diff_w_o = _w(next(ks), (nB, DIFF_V_WIDTH, D_MODEL), DIFF_V_WIDTH, DEEPNORM_BETA)

    ln1_g = 1.0 + 0.02 * jax.random.normal(next(ks), (L, D_MODEL), F32)
    ln1_b = 0.02 * jax.random.normal(next(ks), (L, D_MODEL), F32)
    ln2_g = 1.0 + 0.02 * jax.random.normal(next(ks), (L, D_MODEL), F32)
    ln2_b = 0.02 * jax.random.normal(next(ks), (L, D_MODEL), F32)
    ffn_w_gate_up = _w(next(ks), (L, D_MODEL, 2 * D_FF), D_MODEL)
    ffn_w_down = _w(next(ks), (L, D_FF, D_MODEL), D_FF, DEEPNORM_BETA)

    return {"x": x,
            "mla_w_dq": mla_w_dq, "mla_q_norm": mla_q_norm, "mla_w_uq": mla_w_uq,
            "mla_w_dkv": mla_w_dkv, "mla_kv_norm": mla_kv_norm, "mla_w_ukv": mla_w_ukv,
            "mla_w_o": mla_w_o,
            "kv_w": kv_w,
            "diff_w_q": diff_w_q, "diff_lq1": diff_lq1, "diff_lk1": diff_lk1,
            "diff_lq2": diff_lq2, "diff_lk2": diff_lk2, "diff_subln": diff_subln,
            "diff_w_o": diff_w_o,
            "ln1_g": ln1_g, "ln1_b": ln1_b, "ln2_g": ln2_g, "ln2_b": ln2_b,
            "ffn_w_gate_up": ffn_w_gate_up, "ffn_w_down": ffn_w_down}


def reference(x, mla_w_dq, mla_q_norm, mla_w_uq, mla_w_dkv, mla_kv_norm, mla_w_ukv, mla_w_o,
              kv_w, diff_w_q, diff_lq1, diff_lk1, diff_lq2, diff_lk2, diff_subln, diff_w_o,
              ln1_g, ln1_b, ln2_g, ln2_b, ffn_w_gate_up, ffn_w_down):
    S = x.shape[1]
    cos_m, sin_m = _rope_tables(S, MLA_ROPE)
    cos_d, sin_d = _rope_tables(S, DIFF_HEAD_DIM)
    h = x
    k_sh = v_sh = None
    for l in range(DEPTH):
        if l < N_A_LAYERS:
            a = _mla(h, mla_w_dq[l], mla_q_norm[l], mla_w_uq[l], mla_w_dkv[l], mla_kv_norm[l],
                     mla_w_ukv[l], mla_w_o[l], cos_m, sin_m)
        else:
            if l == N_A_LAYERS:
                k_sh, v_sh = _shared_kv(h, kv_w, cos_d, sin_d)
            j = l - N_A_LAYERS
            lambda_init = 0.8 - 0.6 * math.exp(-0.3 * l)
            a = _diff_attn(h, k_sh, v_sh, diff_w_q[j], diff_lq1[j], diff_lk1[j], diff_lq2[j],
                           diff_lk2[j], diff_subln[j], diff_w_o[j], lambda_init, cos_d, sin_d)
        h = _layernorm(DEEPNORM_ALPHA * h + a, ln1_g[l], ln1_b[l])
        h = _layernorm(DEEPNORM_ALPHA * h + _swiglu(h, ffn_w_gate_up[l], ffn_w_down[l]), ln2_g[l], ln2_b[l])
    return h
```

```python
import math
from contextlib import ExitStack

import numpy as np
import ml_dtypes
import concourse.bass as bass
import concourse.mybir as mybir
from concourse.bass_utils import run_bass_kernel_spmd

F32 = mybir.dt.float32
BF16 = mybir.dt.bfloat16
AF = mybir.ActivationFunctionType
ALU = mybir.AluOpType

S = 4096
D = 1024
NT = 32
NG = 8
GW = 512
H = 8
DFF = 2816
NF = 22
ALPHA = 4.0 ** 0.25
LN_EPS = 1e-5
RMS_EPS = 1e-6
SC0 = 192.0 ** -0.5
SC1 = 64.0 ** -0.5
LAMBDA_INIT = 0.8 - 0.6 * math.exp(-0.3 * 1)
NEG = -30000.0

ENGS = ("pe", "act", "dve", "pool", "sp")


class _Op:
    __slots__ = ("eng", "fn", "deps", "signal", "slot", "sigval", "idx", "phase")


class Prog:
    def __init__(self, nc, same_engine_sync=True):
        self.nc = nc
        self.ops = []
        self.last_w = {}
        self.readers = {}
        self.same_engine_sync = same_engine_sync
        self.fence = set()
        self.last_on_eng = {}
        self.dma_since = []
        self.phase = 0

    def barrier(self):
        self.phase += 1
        self.fence = set(self.last_on_eng.values()) | set(self.dma_since)
        self.dma_since = []
        self.last_w = {}
        self.readers = {}

    def op(self, eng, fn, reads=(), writes=(), slot=None):
        o = _Op()
        o.eng, o.fn, o.slot, o.signal, o.sigval = eng, fn, slot, slot is not None, None
        o.idx = len(self.ops)
        o.phase = self.phase
        deps = set(self.fence)
        for r in reads:
            w = self.last_w.get(r)
            if w is not None:
                deps.add(w)
        for r in writes:
            w = self.last_w.get(r)
            if w is not None:
                deps.add(w)
            for rd in self.readers.get(r, ()):
                deps.add(rd)
        if eng == "pool" and slot is not None:
            if getattr(self, "last_pool_dma", None) is not None:
                deps.add(self.last_pool_dma)
            self.last_pool_dma = o.idx
        o.deps = deps
        for r in reads:
            lst = self.readers.setdefault(r, [])
            if slot is None:
                lst[:] = [i for i in lst if not (self.ops[i].slot is None and self.ops[i].eng == eng)]
            lst.append(o.idx)
        for r in writes:
            self.last_w[r] = o.idx
            self.readers[r] = []
        self.ops.append(o)
        if slot is None:
            self.last_on_eng[eng] = o.idx
        else:
            self.dma_since.append(o.idx)
        return o.idx

    def _skip(self, p, o):
        if p.slot is None and o.slot is None and p.eng == o.eng:
            return p.eng == "pe" or not self.same_engine_sync
        return False

    def emit(self):
        nc = self.nc
        ops = self.ops
        for o in ops:
            for d in o.deps:
                p = ops[d]
                if p.slot is None and not self._skip(p, o):
                    p.signal = True
        cnt = {e: 0 for e in ENGS}
        slotcnt = {}
        physmap = {}
        nphys = {}
        for o in ops:
            if o.slot is not None:
                key = (o.phase, o.slot)
                if key not in physmap:
                    physmap[key] = nphys.get(o.phase, 0)
                    nphys[o.phase] = physmap[key] + 1
                ph = physmap[key]
                slotcnt[ph] = slotcnt.get(ph, 0) + 16
                o.sigval = (("dma", ph), slotcnt[ph])
            elif o.signal:
                cnt[o.eng] += 1
                o.sigval = (o.eng, cnt[o.eng])
        keys = [e for e in ENGS if cnt[e] > 0] + [("dma", s) for s in slotcnt]
        self.n_sems = len(keys)
        with ExitStack() as st:
            sems = {}
            for i, k in enumerate(keys):
                sems[k] = st.enter_context(nc.semaphore("sm%d" % i))
            block = st.enter_context(nc.Block())
            per_eng = {e: [o for o in ops if o.eng == e] for e in ENGS}

            def run(engname, eng):
                waited = {}
                for o in per_eng[engname]:
                    need = {}
                    for d in o.deps:
                        p = ops[d]
                        if p.sigval is None or self._skip(p, o):
                            continue
                        k, v = p.sigval
                        if need.get(k, 0) < v:
                            need[k] = v
                    for k, v in need.items():
                        if waited.get(k, 0) < v:
                            eng.wait_ge(sems[k], v)
                            waited[k] = v
                    ins = o.fn(eng)
                    if o.sigval is not None:
                        ins.then_inc(sems[o.sigval[0]], 16 if o.slot is not None else 1)
                if engname == "sp":
                    for s_, v in slotcnt.items():
                        if waited.get(("dma", s_), 0) < v:
                            eng.wait_ge(sems[("dma", s_)], v)

            @block.sync
            def _(e):
                run("sp", e)

            @block.tensor
            def _(e):
                run("pe", e)

            @block.scalar
            def _(e):
                run("act", e)

            @block.vector
            def _(e):
                run("dve", e)

            @block.gpsimd
            def _(e):
                run("pool", e)


def I(method, *args, **kwargs):
    return lambda e: getattr(e, method)(*args, **kwargs)


class Ring:
    def __init__(self, tiles, name):
        self.tiles, self.name, self.i = tiles, name, -1

    def next(self):
        self.i += 1
        k = self.i % len(self.tiles)
        return self.tiles[k], (self.name, k)


def build_program(phases=(1, 2, 3, 4, 5, 6, 7, 8), debug=()):
    nc = bass.Bass("TRN2", target_bir_lowering=False)

    def din(name, shape, dt=F32):
        return nc.dram_tensor(name, list(shape), dt, kind="ExternalInput").ap()

    def dscr(name, shape, dt):
        kind = "ExternalOutput" if name in debug else "Internal"
        return nc.dram_tensor(name, list(shape), dt, kind=kind).ap()

    x = din("x", [S, D])
    w_dq = din("mla_w_dq", [D, 384])
    q_norm = din("mla_q_norm", [384])
    w_uq = din("mla_w_uq", [384, 1536])
    w_dkv = din("mla_w_dkv", [D, 320])
    kv_norm = din("mla_kv_norm", [256])
    w_ukv = din("mla_w_ukv", [256, 2048])
    w_o0 = din("mla_w_o", [D, D])
    kv_w = din("kv_w", [D, 2048])
    w_q1 = din("diff_w_q", [D, D])
    lq1 = din("diff_lq1", [1, 64])
    lk1 = din("diff_lk1", [1, 64])
    lq2 = din("diff_lq2", [1, 64])
    lk2 = din("diff_lk2", [1, 64])
    subln = din("diff_subln", [128])
    w_o1 = din("diff_w_o", [D, D])
    ln1_g = din("ln1_g", [2, D])
    ln1_b = din("ln1_b", [2, D])
    ln2_g = din("ln2_g", [2, D])
    ln2_b = din("ln2_b", [2, D])
    w_gu = din("ffn_w_gate_up", [2, D, 2 * DFF])
    w_dn = din("ffn_w_down", [2, DFF, D])
    ident_d = din("c_ident", [128, 128], BF16)
    mask_d = din("c_mask", [128, 4, GW], BF16)
    cosm_d = din("c_cosm", [64, S])
    sinm_d = din("c_sinm", [64, S])
    cosd_d = din("c_cosd", [128, S])
    sind_d = din("c_sind", [128, S])
    out = nc.dram_tensor("out", [S, D], F32, kind="ExternalOutput").ap()

    QN0 = dscr("QN0", [H, 128, S], BF16)
    QR0 = dscr("QR0", [H, 64, S], BF16)
    KN0 = dscr("KN0", [H, 128, S], BF16)
    KR0 = dscr("KR0", [64, S], BF16)
    V0 = dscr("V0", [S, D], BF16)
    OT0 = dscr("OT0", [H, 128, S], BF16)
    H1 = dscr("H1", [S, D], F32)
    H1T = dscr("H1T", [8, 128, S], BF16)
    H2 = dscr("H2", [S, D], F32)
    H2T = dscr("H2T", [8, 128, S], BF16)
    QT1 = dscr("QT1", [H, 128, S], BF16)
    KT1 = dscr("KT1", [H, 128, S], BF16)
    V1 = dscr("V1", [S, D], BF16)
    OT1 = dscr("OT1", [H, 128, S], BF16)
    H3 = dscr("H3", [S, D], F32)
    H3T = dscr("H3T", [8, 128, S], BF16)

    P = Prog(nc)
    uid = [0]

    def scope():
        st = ExitStack()

        def sb(shape, dt, name=None):
            uid[0] += 1
            return st.enter_context(nc.sbuf_tensor("%s_%d" % (name or "t", uid[0]), list(shape), dt))

        def ps(shape, dt, name=None):
            uid[0] += 1
            return st.enter_context(nc.psum_tensor("%s_%d" % (name or "p", uid[0]), list(shape), dt))

        return st, sb, ps

    def sbring(sb, n, shape, dt, name):
        return Ring([sb(shape, dt, name) for _ in range(n)], name + str(uid[0]))

    evac_rr = [0]

    def evac(out_ap, in_ap, reads, writes, scale=None, eng=None):
        if eng is None:
            evac_rr[0] += 1
            eng = "act" if evac_rr[0] % 2 else "dve"
        if eng == "act":
            if scale is None:
                P.op("act", I("activation", out=out_ap, in_=in_ap, func=AF.Copy), reads=reads, writes=writes)
            else:
                P.op("act", I("activation", out=out_ap, in_=in_ap, func=AF.Copy, scale=scale), reads=reads, writes=writes)
        else:
            if scale is None:
                P.op("dve", I("tensor_copy", out=out_ap, in_=in_ap), reads=reads, writes=writes)
            else:
                P.op("dve", I("tensor_scalar", out=out_ap, in0=in_ap, scalar1=scale, scalar2=None, op0=ALU.mult),
                     reads=reads, writes=writes)

    def load_w_bf16(dst_tile, src_ap, nchunk, res_prefix, split=1):
        n = src_ap.shape[-1]
        v = src_ap.rearrange("(c p) n -> p c n", p=128)
        step = n // split
        for s_ in range(split):
            lo, hi = s_ * step, (s_ + 1) * step
            P.op("pool", I("dma_start", out=dst_tile[:, :, lo:hi], in_=v[:, :, lo:hi]),
                 writes=[(res_prefix, s_)], slot=(res_prefix, s_))

    def ln_and_transpose(sb_tiles, z, zres, Gt, Bt, hnew_ring, dst_f32_rows, HTd, t, ident, pt_ring, mh):
        stt, mv, ve, rstd, hb, hTt = sb_tiles
        stt_t, stt_r = stt.next()
        mv_t, mv_r = mv.next()
        ve_t, ve_r = ve.next()
        rs_t, rs_r = rstd.next()
        hn_t, hn_r = hnew_ring.next()
        zr = list(zres)
        P.op("dve", I("bn_stats", out=stt_t[:, 0, :], in_=z[:, 0:512]), reads=zr, writes=[(stt_r, 0)])
        P.op("dve", I("bn_stats", out=stt_t[:, 1, :], in_=z[:, 512:1024]), reads=zr, writes=[(stt_r, 1)])
        P.op("dve", I("bn_aggr", out=mv_t[:], in_=stt_t[:].rearrange("p a b -> p (a b)")),
             reads=[(stt_r, 0), (stt_r, 1)], writes=[mv_r])
        P.op("dve", I("tensor_scalar", out=ve_t[:], in0=mv_t[:, 1:2], scalar1=LN_EPS, scalar2=None, op0=ALU.add),
             reads=[mv_r], writes=[ve_r])
        P.op("pool", I("tensor_tensor", out=rs_t[:], in0=ve_t[:], in1=mh[:], op=ALU.pow), reads=[ve_r, "mh"], writes=[rs_r])
        P.op("dve", I("tensor_scalar", out=z[:], in0=z[:], scalar1=mv_t[:, 0:1], scalar2=rs_t[:],
                                              op0=ALU.subtract, op1=ALU.mult), reads=zr + [mv_r, rs_r], writes=zr)
        P.op("pool", I("tensor_tensor", out=z[:], in0=z[:], in1=Gt[:], op=ALU.mult), reads=zr + ["lnG"], writes=zr)
        P.op("pool", I("tensor_tensor", out=hn_t[:], in0=z[:], in1=Bt[:], op=ALU.add), reads=zr + ["lnB"], writes=[hn_r])
        P.op("sp", I("dma_start", out=dst_f32_rows, in_=hn_t[:]), reads=[hn_r], slot=hn_r)
        if HTd is None:
            return
        hb_t, hb_r = hb.next()
        P.op("act", I("activation", out=hb_t[:], in_=hn_t[:], func=AF.Copy), reads=[hn_r], writes=[hb_r])
        pt_t, pt_r = pt_ring.next()
        for c in range(8):
            P.op("pe", I("transpose", out=pt_t[:, c, :], in_=hb_t[:, c * 128:(c + 1) * 128], identity=ident[:]),
                 reads=[hb_r, "ident"], writes=[pt_r])
        hT_t, hT_r = hTt.next()
        P.op("dve", I("tensor_copy", out=hT_t[:], in_=pt_t[:]), reads=[pt_r], writes=[hT_r])
        P.op("sp", I("dma_start", out=HTd.rearrange("c p t -> p c t")[:, :, t * 128:(t + 1) * 128], in_=hT_t[:]),
             reads=[hT_r], slot=hT_r)

    def ln_tiles(sb):
        return (sbring(sb, 2, [128, 2, 6], F32, "stt"), sbring(sb, 2, [128, 2], F32, "mv"), sbring(sb, 2, [128, 1], F32, "ve"),
                sbring(sb, 2, [128, 1], F32, "rstd"), sbring(sb, 2, [128, D], BF16, "hb"), sbring(sb, 2, [128, 8, 128], BF16, "hTt"))

    def load_consts(sb, need_ident=True, need_mask=False, need_ones=False, need_mh=False):
        r = {}
        if need_ident:
            r["ident"] = sb([128, 128], BF16, "ident")
            P.op("sp", I("dma_start", out=r["ident"][:], in_=ident_d[:, :]), writes=["ident"], slot="ident")
        if need_mask:
            r["mask"] = sb([128, 4, GW], BF16, "mask")
            P.op("sp", I("dma_start", out=r["mask"][:], in_=mask_d[:, :, :]), writes=["mask"], slot="mask")
        if need_ones:
            r["ones"] = sb([128, 128], BF16, "ones")
            P.op("pool", I("memset", r["ones"][:], 1.0), writes=["ones"])
        if need_mh:
            r["mh"] = sb([128, 1], F32, "mh")
            P.op("pool", I("memset", r["mh"][:], -0.5), writes=["mh"])
        return r

    def rms_fm(banks, bank_res, nchunk, ndim, gain_t, out_tile, out_res, sq_ring, ss_bank, ss_res, ln_ring, rstd_ring, ones):
        sqs = []
        for m in range(nchunk):
            sq_t, sq_r = sq_ring.next()
            P.op("act", I("activation", out=sq_t[:], in_=banks[m][:], func=AF.Square),
                 reads=[bank_res[m]], writes=[sq_r])
            sqs.append((sq_t, sq_r))
        for m in range(nchunk):
            P.op("pe", I("matmul", ss_bank[:], lhsT=ones[:], rhs=sqs[m][0][:], start=(m == 0), stop=(m == nchunk - 1)),
                 reads=["ones", sqs[m][1]], writes=[ss_res])
        ln_t, ln_r = ln_ring.next()
        rs_t, rs_r = rstd_ring.next()
        P.op("act", I("activation", out=ln_t[:], in_=ss_bank[:], func=AF.Ln, scale=1.0 / ndim, bias=RMS_EPS),
             reads=[ss_res], writes=[ln_r])
        P.op("act", I("activation", out=rs_t[:], in_=ln_t[:], func=AF.Exp, scale=-0.5), reads=[ln_r], writes=[rs_r])
        for m in range(nchunk):
            P.op("dve", I("scalar_tensor_tensor", out=out_tile[:, m, :], in0=banks[m][:], scalar=gain_t[:, m:m + 1],
                                                              in1=rs_t[:], op0=ALU.mult, op1=ALU.mult),
                 reads=[bank_res[m], rs_r, "gains"], writes=[(out_res, m)])

    def phase1():
        st, sb, ps = scope()
        with st:
            C = load_consts(sb, need_ident=True, need_ones=True)
            ident, ones = C["ident"], C["ones"]
            wdq = sb([128, 8, 384], BF16, "wdq")
            wdkv = sb([128, 8, 384], BF16, "wdkv")
            wuq = sb([128, 3, 1536], BF16, "wuq")
            wuqr = sb([128, 3, 8, 64], BF16, "wuqr")
            wukv = sb([128, 2, 2048], BF16, "wukv")
            gq = sb([128, 3], F32, "gq")
            gkv = sb([128, 2], F32, "gkv")
            load_w_bf16(wdq, w_dq, 8, "wdq")
            P.op("pool", I("dma_start", out=wdkv[:, :, 0:320], in_=w_dkv.rearrange("(c p) n -> p c n", p=128)),
                 writes=["wdkv"], slot="wdkv")
            load_w_bf16(wuq, w_uq, 3, "wuq")
            load_w_bf16(wukv, w_ukv, 2, "wukv")
            for m in range(3):
                P.op("sp", I("dma_start", out=gq[:, m:m + 1], in_=q_norm.rearrange("(c p o) -> c p o", p=128, o=1)[m]),
                     writes=["gains"], slot=("gq", m))
            for m in range(2):
                P.op("sp", I("dma_start", out=gkv[:, m:m + 1], in_=kv_norm.rearrange("(c p o) -> c p o", p=128, o=1)[m]),
                     writes=["gains"], slot=("gkv", m))
            P.op("pool", I("tensor_scalar", out=wdkv[:, :, 320:352], in0=wdkv[:, :, 288:320], scalar1=-1.0, scalar2=None, op0=ALU.mult),
                 reads=["wdkv"], writes=["wdkvr"])
            P.op("pool", I("tensor_copy", out=wdkv[:, :, 352:384], in_=wdkv[:, :, 256:288]), reads=["wdkv"], writes=["wdkvr2"])
            wuq4 = wuq[:].rearrange("p c (h d) -> p c h d", d=192)
            for c in range(3):
                P.op("pool", I("tensor_scalar", out=wuqr[:, c, :, 0:32], in0=wuq4[:, c, :, 160:192], scalar1=-1.0, scalar2=None,
                                                             op0=ALU.mult), reads=[("wuq", 0)], writes=[("wuqr", c, 0)])
                P.op("pool", I("tensor_copy", out=wuqr[:, c, :, 32:64], in_=wuq4[:, c, :, 128:160]),
                     reads=[("wuq", 0)], writes=[("wuqr", c, 1)])
            wuqr_res = [("wuqr", c, k) for c in range(3) for k in range(2)]
            wdkv_res = ["wdkv", "wdkvr", "wdkvr2"]

            xs_ring = sbring(sb, 2, [128, D], F32, "xs")
            xb_ring = sbring(sb, 2, [128, D], BF16, "xb")
            xT_ring = sbring(sb, 2, [128, 8, GW], BF16, "xT")
            pt_ring = Ring([ps([128, 8, 128], BF16, "pt")], "pt1")
            banks = [ps([128, GW], F32, "bk") for _ in range(7)]
            bres = [("bank1", i) for i in range(7)]
            sq_ring = sbring(sb, 3, [128, GW], BF16, "sq")
            ln_ring = sbring(sb, 1, [128, GW], F32, "lnss")
            rstd_ring = sbring(sb, 2, [128, GW], F32, "rstdfm")
            cqn_ring = sbring(sb, 2, [128, 3, GW], BF16, "cqn")
            cn_ring = sbring(sb, 2, [128, 2, GW], BF16, "cn")
            cos_ring = sbring(sb, 2, [64, GW], F32, "cosm")
            sin_ring = sbring(sb, 2, [64, GW], F32, "sinm")
            t1_ring = sbring(sb, 2, [64, GW], F32, "t1")
            t2_ring = sbring(sb, 2, [64, GW], F32, "t2")
            qnst_ring = sbring(sb, 2, [128, H, GW], BF16, "qnst")
            qrst_ring = sbring(sb, 2, [64, H, GW], BF16, "qrst")
            knst_ring = sbring(sb, 2, [128, H, GW], BF16, "knst")
            vst_ring = sbring(sb, 2, [128, 4, D], BF16, "vst")
            krst_ring = sbring(sb, 2, [64, GW], BF16, "krst")
            bi = [0]

            def nb():
                bi[0] += 1
                k = bi[0] % 7
                return banks[k], bres[k]

            for g in range(NG):
                gs = slice(g * GW, (g + 1) * GW)
                cos_t, cos_r = cos_ring.next()
                sin_t, sin_r = sin_ring.next()
                P.op("sp", I("dma_start", out=cos_t[:], in_=cosm_d[:, gs]), writes=[cos_r], slot=cos_r)
                P.op("sp", I("dma_start", out=sin_t[:], in_=sinm_d[:, gs]), writes=[sin_r], slot=sin_r)
                xT_t, xT_r = xT_ring.next()
                for i in range(4):
                    t = g * 4 + i
                    xs_t, xs_r = xs_ring.next()
                    xb_t, xb_r = xb_ring.next()
                    P.op("sp", I("dma_start", out=xs_t[:], in_=x[t * 128:(t + 1) * 128, :]), writes=[xs_r], slot=xs_r)
                    P.op("pool", I("tensor_copy", out=xb_t[:], in_=xs_t[:]), reads=[xs_r], writes=[xb_r])
                    pt_t, pt_r = pt_ring.next()
                    for c in range(8):
                        P.op("pe", I("transpose", out=pt_t[:, c, :], in_=xb_t[:, c * 128:(c + 1) * 128],
                                                                                  identity=ident[:]), reads=[xb_r, "ident"], writes=[pt_r])
                    evac(xT_t[:, :, i * 128:(i + 1) * 128], pt_t[:], [pt_r], [(xT_r, i)])
                xT_all = [(xT_r, i) for i in range(4)]
                cqb = [nb() for _ in range(3)]
                for m in range(3):
                    for c in range(8):
                        P.op("pe", I("matmul", cqb[m][0][:], lhsT=wdq[:, c, m * 128:(m + 1) * 128], rhs=xT_t[:, c, :],
                                                                             start=(c == 0), stop=(c == 7)),
                             reads=xT_all + [("wdq", 0)], writes=[cqb[m][1]])
                ssb, ssr = nb()
                cqn_t, cqn_r = cqn_ring.next()
                rms_fm([b[0] for b in cqb], [b[1] for b in cqb], 3, 384, gq, cqn_t, cqn_r, sq_ring, ssb, ssr, ln_ring, rstd_ring, ones)
                cqn_all = [(cqn_r, m) for m in range(3)]
                cb = [nb() for _ in range(2)]
                for m in range(2):
                    for c in range(8):
                        P.op("pe", I("matmul", cb[m][0][:], lhsT=wdkv[:, c, m * 128:(m + 1) * 128], rhs=xT_t[:, c, :],
                                                                            start=(c == 0), stop=(c == 7)),
                             reads=xT_all + wdkv_res, writes=[cb[m][1]])
                ssb, ssr = nb()
                cn_t, cn_r = cn_ring.next()
                rms_fm([b[0] for b in cb], [b[1] for b in cb], 2, 256, gkv, cn_t, cn_r, sq_ring, ssb, ssr, ln_ring, rstd_ring, ones)
                cn_all = [(cn_r, m) for m in range(2)]
                ab, ar = nb()
                bb, br = nb()
                for c in range(8):
                    P.op("pe", I("matmul", ab[0:64, :], lhsT=wdkv[:, c, 256:320], rhs=xT_t[:, c, :], start=(c == 0), stop=(c == 7)),
                         reads=xT_all + wdkv_res, writes=[ar])
                for c in range(8):
                    P.op("pe", I("matmul", bb[0:64, :], lhsT=wdkv[:, c, 320:384], rhs=xT_t[:, c, :], start=(c == 0), stop=(c == 7)),
                         reads=xT_all + wdkv_res, writes=[br])
                t1, t1r = t1_ring.next()
                t2, t2r = t2_ring.next()
                kr_t, kr_r = krst_ring.next()
                P.op("dve", I("tensor_tensor", out=t1[:], in0=ab[0:64, :], in1=cos_t[:], op=ALU.mult),
                     reads=[ar, cos_r], writes=[t1r])
                P.op("dve", I("tensor_tensor", out=t2[:], in0=bb[0:64, :], in1=sin_t[:], op=ALU.mult),
                     reads=[br, sin_r], writes=[t2r])
                P.op("pool", I("tensor_tensor", out=kr_t[:], in0=t1[:], in1=t2[:], op=ALU.add),
                     reads=[t1r, t2r], writes=[kr_r])
                P.op("sp", I("dma_start", out=KR0[:, gs], in_=kr_t[:]), reads=[kr_r], writes=[("KR0", g)], slot=kr_r)
                qn_t, qn_r = qnst_ring.next()
                qr_t, qr_r = qrst_ring.next()
                kn_t, kn_r = knst_ring.next()
                for h in range(H):
                    bk, bkr = nb()
                    for c in range(3):
                        P.op("pe", I("matmul", bk[:], lhsT=wuq[:, c, h * 192:h * 192 + 128], rhs=cqn_t[:, c, :],
                                                                       start=(c == 0), stop=(c == 2)), reads=cqn_all + [("wuq", 0)], writes=[bkr])
                    evac(qn_t[:, h, :], bk[:], [bkr], [(qn_r, h)], scale=SC0)
                    ab, ar = nb()
                    bb, br = nb()
                    for c in range(3):
                        P.op("pe", I("matmul", ab[0:64, :], lhsT=wuq[:, c, h * 192 + 128:h * 192 + 192], rhs=cqn_t[:, c, :],
                                                                       start=(c == 0), stop=(c == 2)), reads=cqn_all + [("wuq", 0)], writes=[ar])
                    for c in range(3):
                        P.op("pe", I("matmul", bb[0:64, :], lhsT=wuqr[:, c, h, :], rhs=cqn_t[:, c, :],
                                                                       start=(c == 0), stop=(c == 2)), reads=cqn_all + wuqr_res, writes=[br])
                    t1, t1r = t1_ring.next()
                    t2, t2r = t2_ring.next()
                    P.op("dve", I("scalar_tensor_tensor", out=t1[:], in0=ab[0:64, :], scalar=SC0, in1=cos_t[:],
                                                                                          op0=ALU.mult, op1=ALU.mult), reads=[ar, cos_r], writes=[t1r])
                    P.op("dve", I("scalar_tensor_tensor", out=t2[:], in0=bb[0:64, :], scalar=SC0, in1=sin_t[:],
                                                                                          op0=ALU.mult, op1=ALU.mult), reads=[br, sin_r], writes=[t2r])
                    P.op("pool", I("tensor_tensor", out=qr_t[:, h, :], in0=t1[:], in1=t2[:], op=ALU.add),
                         reads=[t1r, t2r], writes=[(qr_r, h)])
                    bk, bkr = nb()
                    for c in range(2):
                        P.op("pe", I("matmul", bk[:], lhsT=wukv[:, c, h * 256:h * 256 + 128], rhs=cn_t[:, c, :],
                                                                       start=(c == 0), stop=(c == 1)), reads=cn_all + [("wukv", 0)], writes=[bkr])
                    evac(kn_t[:, h, :], bk[:], [bkr], [(kn_r, h)])
                P.op("sp", I("dma_start", out=QN0.rearrange("h p t -> p h t")[:, :, gs], in_=qn_t[:]),
                     reads=[(qn_r, h) for h in range(H)], writes=[("QN0", g)], slot=qn_r)
                P.op("sp", I("dma_start", out=QR0.rearrange("h p t -> p h t")[:, :, gs], in_=qr_t[:]),
                     reads=[(qr_r, h) for h in range(H)], writes=[("QR0", g)], slot=qr_r)
                P.op("sp", I("dma_start", out=KN0.rearrange("h p t -> p h t")[:, :, gs], in_=kn_t[:]),
                     reads=[(kn_r, h) for h in range(H)], writes=[("KN0", g)], slot=kn_r)
                v_t, v_r = vst_ring.next()
                wv4 = wukv[:].rearrange("p c (h d) -> p c h d", d=256)
                for i in range(4):
                    for hh in range(2):
                        bk, bkr = nb()
                        for c in range(2):
                            P.op("pe", I("matmul", bk[:].rearrange("p (h d) -> p h d", d=128),
                                                                               lhsT=cn_t[:, c, i * 128:(i + 1) * 128],
                                                                               rhs=wv4[:, c, hh * 4:(hh + 1) * 4, 128:256], start=(c == 0), stop=(c == 1)),
                                 reads=cn_all + [("wukv", 0)], writes=[bkr])
                        evac(v_t[:, i, hh * 512:(hh + 1) * 512], bk[:], [bkr], [(v_r, i, hh)])
                P.op("sp", I("dma_start", out=V0[g * GW:(g + 1) * GW, :].rearrange("(i p) n -> p i n", p=128), in_=v_t[:]),
                     reads=[(v_r, i, hh) for i in range(4) for hh in range(2)], writes=[("V0", g)], slot=v_r)
            P.barrier()

    def attention(layer):
        st, sb, ps = scope()
        with st:
            C = load_consts(sb, need_ident=True, need_mask=True, need_ones=True)
            ident, mask, ones = C["ident"], C["mask"], C["ones"]
            nmap = 1 if layer == 0 else 2
            if layer == 0:
                qn_ring = sbring(sb, 2, [128, S], BF16, "qn")
                qr_ring = sbring(sb, 2, [64, S], BF16, "qr")
                kn_ring = sbring(sb, 2, [128, S], BF16, "kn")
                kr = sb([64, S], BF16, "kr")
                P.op("sp", I("dma_start", out=kr[:], in_=KR0[:, :]), writes=["kr"], slot="kr")
                Vd, OTd = V0, OT0
            else:
                qn_ring = sbring(sb, 2, [128, S], BF16, "q1")
                kn_ring = sbring(sb, 2, [128, S], BF16, "k1")
                Vd, OTd = V1, OT1
                lt = [sb([128, 64], F32, "lt") for _ in range(4)]
                for k_, src in enumerate((lq1, lk1, lq2, lk2)):
                    P.op("sp", I("dma_start", out=lt[k_][:], in_=src[0:1, :].partition_broadcast(128)),
                         writes=[("lt", k_)], slot=("lt", k_))
                pr = [sb([128, 64], F32, "pr") for _ in range(2)]
                sm = [sb([128, 1], F32, "lsm") for _ in range(2)]
                ex = [sb([128, 1], F32, "lex") for _ in range(2)]
                neglam = sb([128, 1], F32, "neglam")
                gsub = sb([128, 1], F32, "gsub")
                junk = sb([128, 64], F32, "junk")
                for k_ in range(2):
                    P.op("dve", I("tensor_tensor", out=pr[k_][:], in0=lt[2 * k_][:], in1=lt[2 * k_ + 1][:], op=ALU.mult),
                         reads=[("lt", 2 * k_), ("lt", 2 * k_ + 1)], writes=[("pr", k_)])
                    P.op("act", I("activation", out=junk[:], in_=pr[k_][:], func=AF.Copy, accum_out=sm[k_][:]),
                         reads=[("pr", k_)], writes=[("lsm", k_), "junk"])
                    P.op("act", I("activation", out=ex[k_][:], in_=sm[k_][:], func=AF.Exp), reads=[("lsm", k_)], writes=[("lex", k_)])
                P.op("dve", I("tensor_tensor", out=neglam[:], in0=ex[1][:], in1=ex[0][:], op=ALU.subtract),
                     reads=[("lex", 0), ("lex", 1)], writes=["neglam0"])
                P.op("dve", I("tensor_scalar", out=neglam[:], in0=neglam[:], scalar1=-LAMBDA_INIT, scalar2=None, op0=ALU.add),
                     reads=["neglam0"], writes=["neglam"])
                P.op("sp", I("dma_start", out=gsub[:], in_=subln.rearrange("(p o) -> p o", o=1)), writes=["gsub0"], slot="gsub")
                P.op("pool", I("tensor_scalar", out=gsub[:], in0=gsub[:], scalar1=1.0 - LAMBDA_INIT, scalar2=None, op0=ALU.mult),
                     reads=["gsub0"], writes=["gsub"])
            v_ring = sbring(sb, 2, [128, NT, 128], BF16, "v")
            NS = 4 if layer == 0 else 2
            pT_rings = [sbring(sb, 4, [128, GW], BF16, "pT%d" % m) for m in range(nmap)]
            S_rings = [Ring([ps([128, GW], F32, "S") for _ in range(NS)], "S%d_%d" % (layer, m)) for m in range(nmap)]
            NO = 2 if layer == 0 else 1
            O_rings = [Ring([ps([128, GW], F32, "O") for _ in range(NO)], "O%d_%d" % (layer, m)) for m in range(nmap)]
            M_rings = [Ring([ps([128, GW], F32, "M") for _ in range(NO)], "M%d_%d" % (layer, m)) for m in range(nmap)]
            rs_ring = sbring(sb, 2 * nmap, [128, GW], F32, "rs")
            ost_ring = sbring(sb, 2, [128, GW], BF16, "ost")
            if layer == 1:
                tt_ring = sbring(sb, 4, [128, GW], F32, "tt")
                o_ring = sbring(sb, 2, [128, GW], F32, "o")
                sq_ring = sbring(sb, 2, [128, GW], BF16, "sq5")
                ln_ring = sbring(sb, 2, [128, GW], F32, "ln5")
            LOOK = 2 if layer == 0 else 1

            for h in range(H):
                qn_t, qn_r = qn_ring.next()
                kn_t, kn_r = kn_ring.next()
                v_t, v_r = v_ring.next()
                if layer == 0:
                    qr_t, qr_r = qr_ring.next()
                    P.op("sp", I("dma_start", out=qn_t[:], in_=QN0[h]), reads=[("QN0", g) for g in range(NG)], writes=[qn_r], slot=qn_r)
                    P.op("sp", I("dma_start", out=qr_t[:], in_=QR0[h]), reads=[("QR0", g) for g in range(NG)], writes=[qr_r], slot=qr_r)
                    P.op("sp", I("dma_start", out=kn_t[:], in_=KN0[h]), reads=[("KN0", g) for g in range(NG)], writes=[kn_r], slot=kn_r)
                else:
                    P.op("sp", I("dma_start", out=qn_t[:], in_=QT1[h]), writes=[qn_r], slot=qn_r)
                    P.op("sp", I("dma_start", out=kn_t[:], in_=KT1[h]), writes=[kn_r], slot=kn_r)
                P.op("sp", I("dma_start", out=v_t[:], in_=Vd.rearrange("(t p) (h d) -> h p t d", p=128, d=128)[h]),
                     writes=[v_r], slot=v_r)
                pairs = [(g, kt) for g in range(NG) for kt in range(4 * g + 4)]
                state = {}

                def emit_qk(n):
                    g, kt = pairs[n]
                    gs = slice(g * GW, (g + 1) * GW)
                    ks = slice(kt * 128, (kt + 1) * 128)
                    j = kt - 4 * g
                    Sb = []
                    for m in range(nmap):
                        S_t, S_r = S_rings[m].next()
                        if layer == 0:
                            P.op("pe", I("matmul", S_t[:], lhsT=kn_t[:, ks], rhs=qn_t[:, gs], start=True, stop=False),
                                 reads=[kn_r, qn_r], writes=[S_r])
                            P.op("pe", I("matmul", S_t[:], lhsT=kr[:, ks], rhs=qr_t[:, gs], start=False, stop=(j < 0)),
                                 reads=["kr", qr_r], writes=[S_r])
                        else:
                            lo, hi = m * 64, (m + 1) * 64
                            P.op("pe", I("matmul", S_t[:], lhsT=kn_t[lo:hi, ks], rhs=qn_t[lo:hi, gs], start=True, stop=(j < 0)),
                                 reads=[kn_r, qn_r], writes=[S_r])
                        if j >= 0:
                            P.op("pe", I("matmul", S_t[:], lhsT=ident[:], rhs=mask[:, j, :], start=False, stop=True),
                                 reads=["ident", "mask"], writes=[S_r])
                        Sb.append((S_t, S_r))
                    state[n] = Sb

                def emit_rest(n):
                    g, kt = pairs[n]
                    last = 4 * g + 3
                    Sb = state.pop(n)
                    if kt == 0:
                        state["O"] = [O_rings[m].next() for m in range(nmap)]
                        state["M"] = [M_rings[m].next() for m in range(nmap)]
                    for m in range(nmap):
                        S_t, S_r = Sb[m]
                        pT_t, pT_r = pT_rings[m].next()
                        P.op("act", I("activation", out=pT_t[:], in_=S_t[:], func=AF.Exp), reads=[S_r], writes=[pT_r])
                        O_t, O_r = state["O"][m]
                        M_t, M_r = state["M"][m]
                        P.op("pe", I("matmul", O_t[:], lhsT=v_t[:, kt, :], rhs=pT_t[:], start=(kt == 0), stop=(kt == last)),
                             reads=[v_r, pT_r], writes=[O_r])
                        P.op("pe", I("matmul", M_t[:], lhsT=ones[:], rhs=pT_t[:], start=(kt == 0), stop=(kt == last)),
                             reads=["ones", pT_r], writes=[M_r])
                    if kt == last:
                        finish(g)

                def finish(g):
                    gs = slice(g * GW, (g + 1) * GW)
                    ost_t, ost_r = ost_ring.next()
                    if layer == 0:
                        O_t, O_r = state["O"][0]
                        M_t, M_r = state["M"][0]
                        rs_t, rs_r = rs_ring.next()
                        P.op("dve", I("reciprocal", out=rs_t[:], in_=M_t[:]), reads=[M_r], writes=[rs_r])
                        P.op("dve", I("tensor_tensor", out=ost_t[:], in0=O_t[:], in1=rs_t[:], op=ALU.mult), reads=[O_r, rs_r], writes=[ost_r])
                    else:
                        tts = []
                        for m in range(2):
                            O_t, O_r = state["O"][m]
                            M_t, M_r = state["M"][m]
                            rs_t, rs_r = rs_ring.next()
                            tt_t, tt_r = tt_ring.next()
                            P.op("dve", I("reciprocal", out=rs_t[:], in_=M_t[:]), reads=[M_r], writes=[rs_r])
                            P.op("dve", I("tensor_tensor", out=tt_t[:], in0=O_t[:], in1=rs_t[:], op=ALU.mult),
                                 reads=[O_r, rs_r], writes=[tt_r])
                            tts.append((tt_t, tt_r))
                        o_t, o_r = o_ring.next()
                        P.op("dve", I("scalar_tensor_tensor", out=o_t[:], in0=tts[1][0][:], scalar=neglam[:], in1=tts[0][0][:],
                                                                     op0=ALU.mult, op1=ALU.add), reads=[tts[0][1], tts[1][1], "neglam"], writes=[o_r])
                        sq_t, sq_r = sq_ring.next()
                        P.op("pool", I("tensor_tensor", out=sq_t[:], in0=o_t[:], in1=o_t[:], op=ALU.mult), reads=[o_r], writes=[sq_r])
                        M_t, M_r = state["M"][0]
                        P.op("pe", I("matmul", M_t[:], lhsT=ones[:], rhs=sq_t[:], start=True, stop=True), reads=["ones", sq_r], writes=[M_r])
                        ln_t, ln_r = ln_ring.next()
                        rs_t, rs_r = rs_ring.next()
                        P.op("act", I("activation", out=ln_t[:], in_=M_t[:], func=AF.Ln, scale=1.0 / 128, bias=RMS_EPS), reads=[M_r], writes=[ln_r])
                        P.op("act", I("activation", out=rs_t[:], in_=ln_t[:], func=AF.Exp, scale=-0.5), reads=[ln_r], writes=[rs_r])
                        P.op("dve", I("scalar_tensor_tensor", out=ost_t[:], in0=o_t[:], scalar=gsub[:], in1=rs_t[:], op0=ALU.mult, op1=ALU.mult),
                             reads=[o_r, rs_r, "gsub"], writes=[ost_r])
                    P.op("sp", I("dma_start", out=OTd[h][:, gs], in_=ost_t[:]), reads=[ost_r], writes=[("OT", h, g)], slot=ost_r)

                N = len(pairs)
                for n in range(min(LOOK, N)):
                    emit_qk(n)
                for n in range(N):
                    if n + LOOK < N:
                        emit_qk(n + LOOK)
                    emit_rest(n)
            P.barrier()

    def outproj_ln(layer):
        st, sb, ps = scope()
        OTd = OT0 if layer == 0 else OT1
        wo_d = w_o0 if layer == 0 else w_o1
        res_d = x if layer == 0 else H2
        Hd, HTd = (H1, H1T) if layer == 0 else (H3, H3T)
        with st:
            C = load_consts(sb, need_ident=True, need_mh=True)
            ident, mh = C["ident"], C["mh"]
            wo = sb([128, 8, D], BF16, "wo")
            load_w_bf16(wo, wo_d, 8, "wo")
            Gt = sb([128, D], F32, "lnG")
            Bt = sb([128, D], F32, "lnB")
            P.op("sp", I("dma_start", out=Gt[:], in_=ln1_g[layer:layer + 1, :].partition_broadcast(128)), writes=["lnG"], slot="lnG")
            P.op("sp", I("dma_start", out=Bt[:], in_=ln1_b[layer:layer + 1, :].partition_broadcast(128)), writes=["lnB"], slot="lnB")
            ot_ring = sbring(sb, 2, [128, H, GW], BF16, "otg")
            res_ring = sbring(sb, 3, [128, D], F32, "res")
            hn_ring = sbring(sb, 2, [128, D], F32, "hn")
            lnt = ln_tiles(sb)
            pt_ring = Ring([ps([128, 8, 128], BF16, "pt") for _ in range(2)], "pt3")
            a_ring = Ring([ps([128, GW], F32, "a") for _ in range(6)], "a3")
            for g in range(NG):
                gs = slice(g * GW, (g + 1) * GW)
                ot_t, ot_r = ot_ring.next()
                P.op("sp", I("dma_start", out=ot_t[:], in_=OTd.rearrange("h p t -> p h t")[:, :, gs]), writes=[ot_r], slot=ot_r)
                for i in range(4):
                    t = g * 4 + i
                    res_t, res_r = res_ring.next()
                    P.op("sp", I("dma_start", out=res_t[:], in_=res_d[t * 128:(t + 1) * 128, :]), writes=[(res_r, 0), (res_r, 1)], slot=res_r)
                    z_t, z_r = res_t, res_r
                    for hh in range(2):
                        a_t, a_r = a_ring.next()
                        for h in range(H):
                            P.op("pe", I("matmul", a_t[:], lhsT=ot_t[:, h, i * 128:(i + 1) * 128],
                                                                                           rhs=wo[:, h, hh * 512:(hh + 1) * 512], start=(h == 0), stop=(h == H - 1)),
                                 reads=[ot_r, ("wo", 0)], writes=[a_r])
                        P.op("dve", I("scalar_tensor_tensor",
                            out=z_t[:, hh * 512:(hh + 1) * 512], in0=res_t[:, hh * 512:(hh + 1) * 512], scalar=ALPHA, in1=a_t[:], op0=ALU.mult, op1=ALU.add),
                            reads=[(res_r, hh), a_r], writes=[(z_r, hh)])
                    ln_and_transpose(lnt, z_t, [(z_r, 0), (z_r, 1)], Gt, Bt, hn_ring, Hd[t * 128:(t + 1) * 128, :], HTd, t, ident, pt_ring, mh)
            P.barrier()

    def ffn_ln(layer):
        st, sb, ps = scope()
        HTin = H1T if layer == 0 else H3T
        Hin = H1 if layer == 0 else H3
        Hd, HTd = (H2, H2T) if layer == 0 else (out, None)
        with st:
            C = load_consts(sb, need_ident=True, need_mh=True)
            ident, mh = C["ident"], C["mh"]
            wgu = sb([128, 8, 2 * DFF], BF16, "wgu")
            wdn = sb([128, NF, D], BF16, "wdn")
            NSPL = 11
            v = w_gu[layer].rearrange("(c p) n -> p c n", p=128)
            for s_ in range(NSPL):
                for half in range(2):
                    lo = half * DFF + s_ * 256
                    P.op("pool", I("dma_start", out=wgu[:, :, lo:lo + 256], in_=v[:, :, lo:lo + 256]),
                         writes=[("wgu", half, s_)], slot=("wgu", half, s_))
            vd = w_dn[layer].rearrange("(f p) n -> p f n", p=128)
            for s_ in range(2):
                P.op("pool", I("dma_start", out=wdn[:, s_ * 11:(s_ + 1) * 11, :], in_=vd[:, s_ * 11:(s_ + 1) * 11, :]),
                     writes=[("wdn", s_)], slot=("wdn", s_))
            Gt = sb([128, D], F32, "lnG")
            Bt = sb([128, D], F32, "lnB")
            P.op("sp", I("dma_start", out=Gt[:], in_=ln2_g[layer:layer + 1, :].partition_broadcast(128)), writes=["lnG"], slot="lnG")
            P.op("sp", I("dma_start", out=Bt[:], in_=ln2_b[layer:layer + 1, :].partition_broadcast(128)), writes=["lnB"], slot="lnB")
            hin_ring = sbring(sb, 1, [128, 8, GW], BF16, "hin")
            actT = sb([128, NF, GW], BF16, "actT")
            sg_ring = sbring(sb, 2, [128, GW], F32, "sg")
            res_ring = sbring(sb, 2, [128, D], F32, "res")
            hn_ring = sbring(sb, 2, [128, D], F32, "hn")
            lnt = ln_tiles(sb)
            pt_ring = Ring([ps([128, 8, 128], BF16, "pt")], "pt4")
            g_ring = Ring([ps([128, GW], F32, "gb") for _ in range(2)], "gbk")
            u_ring = Ring([ps([128, GW], F32, "ub") for _ in range(2)], "ubk")
            d_ring = Ring([ps([128, GW], F32, "db") for _ in range(3)], "dbk")
            for g in range(NG):
                gs = slice(g * GW, (g + 1) * GW)
                hin_t, hin_r = hin_ring.next()
                P.op("sp", I("dma_start", out=hin_t[:], in_=HTin.rearrange("c p t -> p c t")[:, :, gs]), writes=[hin_r], slot=hin_r)
                for f in range(NF):
                    gb, gr = g_ring.next()
                    ub, ur = u_ring.next()
                    wres = [("wgu", 0, f // 2), ("wgu", 1, f // 2)]
                    for c in range(8):
                        P.op("pe", I("matmul", gb[:], lhsT=wgu[:, c, f * 128:(f + 1) * 128], rhs=hin_t[:, c, :],
                                                                                start=(c == 0), stop=(c == 7)), reads=[hin_r, wres[0]], writes=[gr])
                    for c in range(8):
                        P.op("pe", I("matmul", ub[:], lhsT=wgu[:, c, DFF + f * 128:DFF + (f + 1) * 128], rhs=hin_t[:, c, :],
                                                                                start=(c == 0), stop=(c == 7)), reads=[hin_r, wres[1]], writes=[ur])
                    sg_t, sg_r = sg_ring.next()
                    P.op("act", I("activation", out=sg_t[:], in_=gb[:], func=AF.Silu), reads=[gr], writes=[sg_r])
                    P.op("dve", I("tensor_tensor", out=actT[:, f, :], in0=ub[:], in1=sg_t[:], op=ALU.mult),
                         reads=[ur, sg_r], writes=[("actT", f)])
                for i in range(4):
                    t = g * 4 + i
                    res_t, res_r = res_ring.next()
                    P.op("sp", I("dma_start", out=res_t[:], in_=Hin[t * 128:(t + 1) * 128, :]), writes=[(res_r, 0), (res_r, 1)], slot=res_r)
                    z_t, z_r = res_t, res_r
                    for hh in range(2):
                        db, dr = d_ring.next()
                        for f in range(NF):
                            P.op("pe", I("matmul", db[:], lhsT=actT[:, f, i * 128:(i + 1) * 128],
                                                                              rhs=wdn[:, f, hh * 512:(hh + 1) * 512], start=(f == 0), stop=(f == NF - 1)),
                                 reads=[("actT", f), ("wdn", f // 11)], writes=[dr])
                        P.op("dve", I("scalar_tensor_tensor",
                            out=z_t[:, hh * 512:(hh + 1) * 512], in0=res_t[:, hh * 512:(hh + 1) * 512], scalar=ALPHA, in1=db[:], op0=ALU.mult, op1=ALU.add),
                            reads=[(res_r, hh), dr], writes=[(z_r, hh)])
                    ln_and_transpose(lnt, z_t, [(z_r, 0), (z_r, 1)], Gt, Bt, hn_ring, Hd[t * 128:(t + 1) * 128, :], HTd, t, ident, pt_ring, mh)
            P.barrier()

    def phase_proj1():
        st, sb, ps = scope()
        with st:
            wk = sb([128, 8, D], BF16, "wk")
            wkr = sb([128, 8, D], BF16, "wkr")
            wq = sb([128, 8, D], BF16, "wq")
            wqr = sb([128, 8, D], BF16, "wqr")
            wv = sb([128, 8, D], BF16, "wv")
            kvv = kv_w.rearrange("(c p) n -> p c n", p=128)
            P.op("pool", I("dma_start", out=wk[:], in_=kvv[:, :, 0:D]), writes=["wk"], slot="wk")
            P.op("pool", I("dma_start", out=wq[:], in_=w_q1.rearrange("(c p) n -> p c n", p=128)), writes=["wq"], slot="wq")
            P.op("pool", I("dma_start", out=wv[:], in_=kvv[:, :, D:2 * D]), writes=["wv"], slot="wv")
            for (src, dst, nm) in ((wk, wkr, "wk"), (wq, wqr, "wq")):
                s4 = src[:].rearrange("p c (b d) -> p c b d", d=64)
                d4 = dst[:].rearrange("p c (b d) -> p c b d", d=64)
                for c in range(8):
                    P.op("pool", I("tensor_scalar", out=d4[:, c, :, 0:32], in0=s4[:, c, :, 32:64], scalar1=-1.0, scalar2=None, op0=ALU.mult),
                         reads=[nm], writes=[(nm + "r", c, 0)])
                    P.op("pool", I("tensor_copy", out=d4[:, c, :, 32:64], in_=s4[:, c, :, 0:32]),
                         reads=[nm], writes=[(nm + "r", c, 1)])
            rres = {nm: [(nm + "r", c, k) for c in range(8) for k in range(2)] for nm in ("wk", "wq")}
            hin_ring = sbring(sb, 2, [128, 8, GW], BF16, "hin")
            cos_ring = sbring(sb, 2, [128, GW], F32, "cosd")
            sin_ring = sbring(sb, 2, [128, GW], F32, "sind")
            t1_ring = sbring(sb, 3, [128, GW], F32, "t1")
            t2_ring = sbring(sb, 3, [128, GW], F32, "t2")
            kst_ring = sbring(sb, 2, [128, H, GW], BF16, "kst")
            qst_ring = sbring(sb, 2, [128, H, GW], BF16, "qst")
            vst_ring = sbring(sb, 2, [128, 4, D], BF16, "vst")
            banks = Ring([ps([128, GW], F32, "bk") for _ in range(8)], "bank5")
            for g in range(NG):
                gs = slice(g * GW, (g + 1) * GW)
                hin_t, hin_r = hin_ring.next()
                cos_t, cos_r = cos_ring.next()
                sin_t, sin_r = sin_ring.next()
                P.op("sp", I("dma_start", out=hin_t[:], in_=H2T.rearrange("c p t -> p c t")[:, :, gs]), writes=[hin_r], slot=hin_r)
                P.op("sp", I("dma_start", out=cos_t[:], in_=cosd_d[:, gs]), writes=[cos_r], slot=cos_r)
                P.op("sp", I("dma_start", out=sin_t[:], in_=sind_d[:, gs]), writes=[sin_r], slot=sin_r)
                k_t, k_r = kst_ring.next()
                q_t, q_r = qst_ring.next()
                for (w_, wr_, nm, dst_t, dst_r, sc) in ((wk, wkr, "wk", k_t, k_r, 1.0), (wq, wqr, "wq", q_t, q_r, SC1)):
                    for h in range(H):
                        ab, ar = banks.next()
                        bb, br = banks.next()
                        for c in range(8):
                            P.op("pe", I("matmul", ab[:], lhsT=w_[:, c, h * 128:(h + 1) * 128], rhs=hin_t[:, c, :],
                                                                                           start=(c == 0), stop=(c == 7)), reads=[hin_r, nm], writes=[ar])
                        for c in range(8):
                            P.op("pe", I("matmul", bb[:], lhsT=wr_[:, c, h * 128:(h + 1) * 128], rhs=hin_t[:, c, :],
                                                                                             start=(c == 0), stop=(c == 7)), reads=[hin_r] + rres[nm], writes=[br])
                        t1, t1r = t1_ring.next()
                        t2, t2r = t2_ring.next()
                        P.op("dve", I("scalar_tensor_tensor", out=t1[:], in0=ab[:], scalar=sc, in1=cos_t[:],
                                                                                                   op0=ALU.mult, op1=ALU.mult), reads=[ar, cos_r], writes=[t1r])
                        P.op("dve", I("scalar_tensor_tensor", out=t2[:], in0=bb[:], scalar=sc, in1=sin_t[:],
                                                                                                   op0=ALU.mult, op1=ALU.mult), reads=[br, sin_r], writes=[t2r])
                        P.op("pool", I("tensor_tensor", out=dst_t[:, h, :], in0=t1[:], in1=t2[:], op=ALU.add),
                             reads=[t1r, t2r], writes=[(dst_r, h)])
                P.op("sp", I("dma_start", out=KT1.rearrange("h p t -> p h t")[:, :, gs], in_=k_t[:]),
                     reads=[(k_r, h) for h in range(H)], slot=k_r)
                P.op("sp", I("dma_start", out=QT1.rearrange("h p t -> p h t")[:, :, gs], in_=q_t[:]),
                     reads=[(q_r, h) for h in range(H)], slot=q_r)
                v_t, v_r = vst_ring.next()
                for i in range(4):
                    for hh in range(2):
                        bk, bkr = banks.next()
                        for c in range(8):
                            P.op("pe", I("matmul", bk[:], lhsT=hin_t[:, c, i * 128:(i + 1) * 128],
                                                                                           rhs=wv[:, c, hh * 512:(hh + 1) * 512], start=(c == 0), stop=(c == 7)),
                                 reads=[hin_r, "wv"], writes=[bkr])
                        evac(v_t[:, i, hh * 512:(hh + 1) * 512], bk[:], [bkr], [(v_r, i, hh)], eng="act")
                P.op("sp", I("dma_start", out=V1[g * GW:(g + 1) * GW, :].rearrange("(i p) n -> p i n", p=128), in_=v_t[:]),
                     reads=[(v_r, i, hh) for i in range(4) for hh in range(2)], slot=v_r)
            P.barrier()

    if 1 in phases:
        phase1()
    if 2 in phases:
        attention(0)
    if 3 in phases:
        outproj_ln(0)
    if 4 in phases:
        ffn_ln(0)
    if 5 in phases:
        phase_proj1()
    if 6 in phases:
        attention(1)
    if 7 in phases:
        outproj_ln(1)
    if 8 in phases:
        ffn_ln(1)
    P.emit()
    return nc, P


def _rope_tables(dim):
    inv = (1.0 / (10000.0 ** (np.arange(0, dim, 2, dtype=np.float32) / np.float32(dim)))).astype(np.float32)
    ang = np.arange(S, dtype=np.float32)[:, None] * inv[None, :]
    ang = np.concatenate([ang, ang], axis=-1).astype(np.float32)
    return np.ascontiguousarray(np.cos(ang).T.astype(np.float32)), np.ascontiguousarray(np.sin(ang).T.astype(np.float32))


def _consts():
    ident = np.eye(128, dtype=np.float32).astype(ml_dtypes.bfloat16)
    ki = np.arange(128)[:, None, None]
    j = np.arange(4)[None, :, None]
    qi = np.arange(GW)[None, None, :]
    mask = np.where(qi >= 128 * j + ki, 0.0, NEG).astype(np.float32).astype(ml_dtypes.bfloat16)
    cm, sm = _rope_tables(64)
    cd = np.ascontiguousarray(np.concatenate([cm, cm], axis=0))
    sd = np.ascontiguousarray(np.concatenate([sm, sm], axis=0))
    return {"c_ident": ident, "c_mask": np.ascontiguousarray(mask), "c_cosm": cm, "c_sinm": sm, "c_cosd": cd, "c_sind": sd}


_SQUEEZE = ("mla_w_dq", "mla_q_norm", "mla_w_uq", "mla_w_dkv", "mla_kv_norm", "mla_w_ukv", "mla_w_o",
            "diff_w_q", "diff_subln", "diff_w_o")


def make_in_maps(inputs, n_cores=8):
    common = dict(_consts())
    for k, v in inputs.items():
        if k == "x":
            continue
        a = np.ascontiguousarray(np.asarray(v, dtype=np.float32))
        if k in _SQUEEZE:
            a = np.ascontiguousarray(a[0])
        common[k] = a
    xs = np.asarray(inputs["x"], dtype=np.float32)
    maps = []
    for c in range(n_cores):
        m = dict(common)
        m["x"] = np.ascontiguousarray(xs[c])
        maps.append(m)
    return maps


_CACHE = {}


def kernel(**inputs):
    if "nc" not in _CACHE:
        _CACHE["nc"] = build_program()[0]
    nc = _CACHE["nc"]
    in_maps = make_in_maps(inputs, 8)
    res = run_bass_kernel_spmd(nc, in_maps, core_ids=list(range(8)))
    return np.stack([np.asarray(r["out"], dtype=np.float32) for r in res.results], axis=0)
```

```python
import math
from contextlib import ExitStack

import numpy as np
import ml_dtypes
import concourse.bass as bass
import concourse.mybir as mybir
from concourse.bass_utils import run_bass_kernel_spmd

F32 = mybir.dt.float32
BF16 = mybir.dt.bfloat16
AF = mybir.ActivationFunctionType
ALU = mybir.AluOpType

S = 4096
D = 1024
NT = 32
NG = 8
GW = 512
H = 8
DFF = 2816
NF = 22
ALPHA = 4.0 ** 0.25
LN_EPS = 1e-5
RMS_EPS = 1e-6
SC0 = 192.0 ** -0.5
SC1 = 64.0 ** -0.5
LAMBDA_INIT = 0.8 - 0.6 * math.exp(-0.3 * 1)
NEG = -30000.0

ENGS = ("pe", "act", "dve", "pool", "sp")


class _Op:
    __slots__ = ("eng", "fn", "deps", "signal", "slot", "sigval", "idx", "phase")


class Prog:
    def __init__(self, nc, same_engine_sync=True):
        self.nc = nc
        self.ops = []
        self.last_w = {}
        self.readers = {}
        self.same_engine_sync = same_engine_sync
        self.fence = set()
        self.last_on_eng = {}
        self.dma_since = []
        self.phase = 0

    def barrier(self):
        self.phase += 1
        self.fence = set(self.last_on_eng.values()) | set(self.dma_since)
        self.dma_since = []
        self.last_w = {}
        self.readers = {}

    def op(self, eng, fn, reads=(), writes=(), slot=None):
        o = _Op()
        o.eng, o.fn, o.slot, o.signal, o.sigval = eng, fn, slot, slot is not None, None
        o.idx = len(self.ops)
        o.phase = self.phase
        deps = set(self.fence)
        for r in reads:
            w = self.last_w.get(r)
            if w is not None:
                deps.add(w)
        for r in writes:
            w = self.last_w.get(r)
            if w is not None:
                deps.add(w)
            for rd in self.readers.get(r, ()):
                deps.add(rd)
        if eng == "pool" and slot is not None:
            if getattr(self, "last_pool_dma", None) is not None:
                deps.add(self.last_pool_dma)
            self.last_pool_dma = o.idx
        o.deps = deps
        for r in reads:
            lst = self.readers.setdefault(r, [])
            if slot is None:
                lst[:] = [i for i in lst if not (self.ops[i].slot is None and self.ops[i].eng == eng)]
            lst.append(o.idx)
        for r in writes:
            self.last_w[r] = o.idx
            self.readers[r] = []
        self.ops.append(o)
        if slot is None:
            self.last_on_eng[eng] = o.idx
        else:
            self.dma_since.append(o.idx)
        return o.idx

    def _skip(self, p, o):
        if p.slot is None and o.slot is None and p.eng == o.eng:
            return p.eng == "pe" or not self.same_engine_sync
        return False

    def emit(self):
        nc = self.nc
        ops = self.ops
        for o in ops:
            for d in o.deps:
                p = ops[d]
                if p.slot is None and not self._skip(p, o):
                    p.signal = True
        cnt = {e: 0 for e in ENGS}
        slotcnt = {}
        physmap = {}
        nphys = {}
        for o in ops:
            if o.slot is not None:
                key = (o.eng, o.phase, o.slot)
                if key not in physmap:
                    physmap[key] = (o.eng, nphys.get((o.eng, o.phase), 0))
                    nphys[(o.eng, o.phase)] = physmap[key][1] + 1
                ph = physmap[key]
                slotcnt[ph] = slotcnt.get(ph, 0) + 16
                o.sigval = (("dma", ph), slotcnt[ph])
            elif o.signal:
                cnt[o.eng] += 1
                o.sigval = (o.eng, cnt[o.eng])
        keys = [e for e in ENGS if cnt[e] > 0] + [("dma", s) for s in slotcnt]
        self.n_sems = len(keys)
        with ExitStack() as st:
            sems = {}
            for i, k in enumerate(keys):
                sems[k] = st.enter_context(nc.semaphore("sm%d" % i))
            block = st.enter_context(nc.Block())
            per_eng = {e: [o for o in ops if o.eng == e] for e in ENGS}

            def run(engname, eng):
                waited = {}
                for o in per_eng[engname]:
                    need = {}
                    for d in o.deps:
                        p = ops[d]
                        if p.sigval is None or self._skip(p, o):
                            continue
                        k, v = p.sigval
                        if need.get(k, 0) < v:
                            need[k] = v
                    for k, v in need.items():
                        if waited.get(k, 0) < v:
                            eng.wait_ge(sems[k], v)
                            waited[k] = v
                    ins = o.fn(eng)
                    if o.sigval is not None:
                        ins.then_inc(sems[o.sigval[0]], 16 if o.slot is not None else 1)
                if engname == "sp":
                    for s_, v in slotcnt.items():
                        if waited.get(("dma", s_), 0) < v:
                            eng.wait_ge(sems[("dma", s_)], v)

            @block.sync
            def _(e):
                run("sp", e)

            @block.tensor
            def _(e):
                run("pe", e)

            @block.scalar
            def _(e):
                run("act", e)

            @block.vector
            def _(e):
                run("dve", e)

            @block.gpsimd
            def _(e):
                run("pool", e)


def I(method, *args, **kwargs):
    return lambda e: getattr(e, method)(*args, **kwargs)


class Ring:
    def __init__(self, tiles, name):
        self.tiles, self.name, self.i = tiles, name, -1

    def next(self):
        self.i += 1
        k = self.i % len(self.tiles)
        return self.tiles[k], (self.name, k)


def build_program(phases=(1, 2, 3, 4, 5, 6, 7, 8), debug=()):
    nc = bass.Bass("TRN2", target_bir_lowering=False)

    def din(name, shape, dt=F32):
        return nc.dram_tensor(name, list(shape), dt, kind="ExternalInput").ap()

    def dscr(name, shape, dt):
        kind = "ExternalOutput" if name in debug else "Internal"
        return nc.dram_tensor(name, list(shape), dt, kind=kind).ap()

    x = din("x", [S, D])
    w_dq = din("mla_w_dq", [D, 384])
    q_norm = din("mla_q_norm", [384])
    w_uq = din("mla_w_uq", [384, 1536])
    w_dkv = din("mla_w_dkv", [D, 320])
    kv_norm = din("mla_kv_norm", [256])
    w_ukv = din("mla_w_ukv", [256, 2048])
    w_o0 = din("mla_w_o", [D, D])
    kv_w = din("kv_w", [D, 2048])
    w_q1 = din("diff_w_q", [D, D])
    lq1 = din("diff_lq1", [1, 64])
    lk1 = din("diff_lk1", [1, 64])
    lq2 = din("diff_lq2", [1, 64])
    lk2 = din("diff_lk2", [1, 64])
    subln = din("diff_subln", [128])
    w_o1 = din("diff_w_o", [D, D])
    ln1_g = din("ln1_g", [2, D])
    ln1_b = din("ln1_b", [2, D])
    ln2_g = din("ln2_g", [2, D])
    ln2_b = din("ln2_b", [2, D])
    w_gu = din("ffn_w_gate_up", [2, D, 2 * DFF])
    w_dn = din("ffn_w_down", [2, DFF, D])
    ident_d = din("c_ident", [128, 128], BF16)
    mask_d = din("c_mask", [128, 4, GW], BF16)
    cosm_d = din("c_cosm", [64, S])
    sinm_d = din("c_sinm", [64, S])
    cosd_d = din("c_cosd", [128, S])
    sind_d = din("c_sind", [128, S])
    out = nc.dram_tensor("out", [S, D], F32, kind="ExternalOutput").ap()

    QN0 = dscr("QN0", [H, 128, S], BF16)
    QR0 = dscr("QR0", [H, 64, S], BF16)
    KN0 = dscr("KN0", [H, 128, S], BF16)
    KR0 = dscr("KR0", [64, S], BF16)
    V0 = dscr("V0", [S, D], BF16)
    OT0 = dscr("OT0", [H, 128, S], BF16)
    H1 = dscr("H1", [S, D], F32)
    H1T = dscr("H1T", [8, 128, S], BF16)
    H2 = dscr("H2", [S, D], F32)
    H2T = dscr("H2T", [8, 128, S], BF16)
    QT1 = dscr("QT1", [H, 128, S], BF16)
    KT1 = dscr("KT1", [H, 128, S], BF16)
    V1 = dscr("V1", [S, D], BF16)
    OT1 = dscr("OT1", [H, 128, S], BF16)
    H3 = dscr("H3", [S, D], F32)
    H3T = dscr("H3T", [8, 128, S], BF16)

    P = Prog(nc)
    uid = [0]

    def scope():
        st = ExitStack()

        def sb(shape, dt, name=None):
            uid[0] += 1
            return st.enter_context(nc.sbuf_tensor("%s_%d" % (name or "t", uid[0]), list(shape), dt))

        def ps(shape, dt, name=None):
            uid[0] += 1
            return st.enter_context(nc.psum_tensor("%s_%d" % (name or "p", uid[0]), list(shape), dt))

        return st, sb, ps

    def sbring(sb, n, shape, dt, name):
        return Ring([sb(shape, dt, name) for _ in range(n)], name + str(uid[0]))

    evac_rr = [0]

    def evac(out_ap, in_ap, reads, writes, scale=None, eng=None):
        if eng is None:
            evac_rr[0] += 1
            eng = "act" if evac_rr[0] % 2 else "dve"
        if eng == "act":
            if scale is None:
                P.op("act", I("activation", out=out_ap, in_=in_ap, func=AF.Copy), reads=reads, writes=writes)
            else:
                P.op("act", I("activation", out=out_ap, in_=in_ap, func=AF.Copy, scale=scale), reads=reads, writes=writes)
        else:
            if scale is None:
                P.op("dve", I("tensor_copy", out=out_ap, in_=in_ap), reads=reads, writes=writes)
            else:
                P.op("dve", I("tensor_scalar", out=out_ap, in0=in_ap, scalar1=scale, scalar2=None, op0=ALU.mult),
                     reads=reads, writes=writes)

    def load_w_bf16(dst_tile, src_ap, nchunk, res_prefix, split=1):
        n = src_ap.shape[-1]
        v = src_ap.rearrange("(c p) n -> p c n", p=128)
        step = n // split
        for s_ in range(split):
            lo, hi = s_ * step, (s_ + 1) * step
            P.op("pool", I("dma_start", out=dst_tile[:, :, lo:hi], in_=v[:, :, lo:hi]),
                 writes=[(res_prefix, s_)], slot=(res_prefix, s_))

    class LNPipe:
        def __init__(self, sb, ps, Gt, Bt, ident, mh, HTd, pt_ring):
            self.stt = sbring(sb, 2, [128, 2, 6], F32, "stt")
            self.mv = sbring(sb, 2, [128, 2], F32, "mv")
            self.ve = sbring(sb, 2, [128, 1], F32, "ve")
            self.rstd = sbring(sb, 3, [128, 1], F32, "rstd")
            self.nmr = sbring(sb, 3, [128, 1], F32, "nmr")
            self.hn = sbring(sb, 2, [128, D], F32, "hn")
            self.Gt, self.Bt, self.ident, self.mh, self.HTd, self.pt_ring = Gt, Bt, ident, mh, HTd, pt_ring
            if HTd is not None:
                self.hb = sbring(sb, 2, [128, D], BF16, "hb")
                self.hTt = sbring(sb, 2, [128, 8, 128], BF16, "hTt")
            self.st = {}
            self.n = 0

        def push(self, z, zres, dst_rows, t):
            k = self.n
            self.n += 1
            zr = list(zres)
            stt_t, stt_r = self.stt.next()
            mv_t, mv_r = self.mv.next()
            ve_t, ve_r = self.ve.next()
            rs_t, rs_r = self.rstd.next()
            nm_t, nm_r = self.nmr.next()
            self.st[k] = dict(z=z, zr=zr, dst=dst_rows, t=t)
            P.op("dve", I("bn_stats", out=stt_t[:, 0, :], in_=z[:, 0:512]), reads=zr, writes=[(stt_r, 0)])
            P.op("dve", I("bn_stats", out=stt_t[:, 1, :], in_=z[:, 512:1024]), reads=zr, writes=[(stt_r, 1)])
            P.op("dve", I("bn_aggr", out=mv_t[:], in_=stt_t[:].rearrange("p a b -> p (a b)")),
                 reads=[(stt_r, 0), (stt_r, 1)], writes=[mv_r])
            P.op("dve", I("tensor_scalar", out=ve_t[:], in0=mv_t[:, 1:2], scalar1=LN_EPS, scalar2=None, op0=ALU.add),
                 reads=[mv_r], writes=[ve_r])
            P.op("pool", I("tensor_tensor", out=rs_t[:], in0=ve_t[:], in1=self.mh[:], op=ALU.pow), reads=[ve_r, "mh"], writes=[rs_r])
            self._stage2(k - 1)
            self._stage3(k - 2)
            P.op("dve", I("tensor_scalar", out=nm_t[:], in0=mv_t[:, 0:1], scalar1=-1.0, scalar2=rs_t[:], op0=ALU.mult, op1=ALU.mult),
                 reads=[mv_r, rs_r], writes=[nm_r])
            P.op("act", I("activation", out=z[:], in_=z[:], func=AF.Identity, scale=rs_t[:], bias=nm_t[:]),
                 reads=zr + [rs_r, nm_r], writes=zr)

        def _stage2(self, k):
            if k < 0 or k not in self.st:
                return
            d = self.st[k]
            z, zr = d["z"], d["zr"]
            hn_t, hn_r = self.hn.next()
            P.op("dve", I("tensor_tensor", out=z[:], in0=z[:], in1=self.Gt[:], op=ALU.mult), reads=zr + ["lnG"], writes=zr)
            P.op("pool", I("tensor_tensor", out=hn_t[:], in0=z[:], in1=self.Bt[:], op=ALU.add), reads=zr + ["lnB"], writes=[hn_r])
            P.op("sp", I("dma_start", out=d["dst"], in_=hn_t[:]), reads=[hn_r], slot=hn_r)
            if self.HTd is not None:
                hb_t, hb_r = self.hb.next()
                P.op("act", I("activation", out=hb_t[:], in_=hn_t[:], func=AF.Copy), reads=[hn_r], writes=[hb_r])
                d["hb"] = (hb_t, hb_r)

        def _stage3(self, k):
            if k < 0 or k not in self.st:
                return
            d = self.st.pop(k)
            if self.HTd is None:
                return
            hb_t, hb_r = d["hb"]
            t = d["t"]
            pt_t, pt_r = self.pt_ring.next()
            for c in range(8):
                P.op("pe", I("transpose", out=pt_t[:, c, :], in_=hb_t[:, c * 128:(c + 1) * 128], identity=self.ident[:]),
                     reads=[hb_r, "ident"], writes=[pt_r])
            hT_t, hT_r = self.hTt.next()
            P.op("dve", I("tensor_copy", out=hT_t[:], in_=pt_t[:]), reads=[pt_r], writes=[hT_r])
            P.op("sp", I("dma_start", out=self.HTd.rearrange("c p t -> p c t")[:, :, t * 128:(t + 1) * 128], in_=hT_t[:]),
                 reads=[hT_r], slot=hT_r)

        def flush(self):
            self._stage2(self.n - 1)
            self._stage3(self.n - 2)
            self._stage3(self.n - 1)

    def load_consts(sb, need_ident=True, need_mask=False, need_ones=False, need_mh=False):
        r = {}
        if need_ident:
            r["ident"] = sb([128, 128], BF16, "ident")
            P.op("sp", I("dma_start", out=r["ident"][:], in_=ident_d[:, :]), writes=["ident"], slot="ident")
        if need_mask:
            r["mask"] = sb([128, 4, GW], BF16, "mask")
            P.op("sp", I("dma_start", out=r["mask"][:], in_=mask_d[:, :, :]), writes=["mask"], slot="mask")
        if need_ones:
            r["ones"] = sb([128, 128], BF16, "ones")
            P.op("pool", I("memset", r["ones"][:], 1.0), writes=["ones"])
        if need_mh:
            r["mh"] = sb([128, 1], F32, "mh")
            P.op("pool", I("memset", r["mh"][:], -0.5), writes=["mh"])
        return r

    def rms_fm(banks, bank_res, nchunk, ndim, gain_t, out_tile, out_res, sq_ring, ss_bank, ss_res, ln_ring, rstd_ring, ones):
        sqs = []
        for m in range(nchunk):
            sq_t, sq_r = sq_ring.next()
            P.op("act", I("activation", out=sq_t[:], in_=banks[m][:], func=AF.Square),
                 reads=[bank_res[m]], writes=[sq_r])
            sqs.append((sq_t, sq_r))
        for m in range(nchunk):
            P.op("pe", I("matmul", ss_bank[:], lhsT=ones[:], rhs=sqs[m][0][:], start=(m == 0), stop=(m == nchunk - 1)),
                 reads=["ones", sqs[m][1]], writes=[ss_res])
        ln_t, ln_r = ln_ring.next()
        rs_t, rs_r = rstd_ring.next()
        P.op("act", I("activation", out=ln_t[:], in_=ss_bank[:], func=AF.Ln, scale=1.0 / ndim, bias=RMS_EPS),
             reads=[ss_res], writes=[ln_r])
        P.op("act", I("activation", out=rs_t[:], in_=ln_t[:], func=AF.Exp, scale=-0.5), reads=[ln_r], writes=[rs_r])
        for m in range(nchunk):
            P.op("dve", I("scalar_tensor_tensor", out=out_tile[:, m, :], in0=banks[m][:], scalar=gain_t[:, m:m + 1],
                                                              in1=rs_t[:], op0=ALU.mult, op1=ALU.mult),
                 reads=[bank_res[m], rs_r, "gains"], writes=[(out_res, m)])

    def phase1():
        st, sb, ps = scope()
        with st:
            C = load_consts(sb, need_ident=True, need_ones=True)
            ident, ones = C["ident"], C["ones"]
            wdq = sb([128, 8, 384], BF16, "wdq")
            wdkv = sb([128, 8, 384], BF16, "wdkv")
            wuq = sb([128, 3, 1536], BF16, "wuq")
            wuqr = sb([128, 3, 8, 64], BF16, "wuqr")
            wukv = sb([128, 2, 2048], BF16, "wukv")
            gq = sb([128, 3], F32, "gq")
            gkv = sb([128, 2], F32, "gkv")
            load_w_bf16(wdq, w_dq, 8, "wdq")
            P.op("pool", I("dma_start", out=wdkv[:, :, 0:320], in_=w_dkv.rearrange("(c p) n -> p c n", p=128)),
                 writes=["wdkv"], slot="wdkv")
            load_w_bf16(wuq, w_uq, 3, "wuq")
            load_w_bf16(wukv, w_ukv, 2, "wukv")
            for m in range(3):
                P.op("sp", I("dma_start", out=gq[:, m:m + 1], in_=q_norm.rearrange("(c p o) -> c p o", p=128, o=1)[m]),
                     writes=["gains"], slot=("gq", m))
            for m in range(2):
                P.op("sp", I("dma_start", out=gkv[:, m:m + 1], in_=kv_norm.rearrange("(c p o) -> c p o", p=128, o=1)[m]),
                     writes=["gains"], slot=("gkv", m))
            P.op("pool", I("tensor_scalar", out=wdkv[:, :, 320:352], in0=wdkv[:, :, 288:320], scalar1=-1.0, scalar2=None, op0=ALU.mult),
                 reads=["wdkv"], writes=["wdkvr"])
            P.op("pool", I("tensor_copy", out=wdkv[:, :, 352:384], in_=wdkv[:, :, 256:288]), reads=["wdkv"], writes=["wdkvr2"])
            wuq4 = wuq[:].rearrange("p c (h d) -> p c h d", d=192)
            for c in range(3):
                P.op("pool", I("tensor_scalar", out=wuqr[:, c, :, 0:32], in0=wuq4[:, c, :, 160:192], scalar1=-1.0, scalar2=None,
                                                             op0=ALU.mult), reads=[("wuq", 0)], writes=[("wuqr", c, 0)])
                P.op("pool", I("tensor_copy", out=wuqr[:, c, :, 32:64], in_=wuq4[:, c, :, 128:160]),
                     reads=[("wuq", 0)], writes=[("wuqr", c, 1)])
            wuqr_res = [("wuqr", c, k) for c in range(3) for k in range(2)]
            wdkv_res = ["wdkv", "wdkvr", "wdkvr2"]

            xs_ring = sbring(sb, 2, [128, D], F32, "xs")
            xb_ring = sbring(sb, 2, [128, D], BF16, "xb")
            xT_ring = sbring(sb, 2, [128, 8, GW], BF16, "xT")
            pt_ring = Ring([ps([128, 8, 128], BF16, "pt")], "pt1")
            banks = [ps([128, GW], F32, "bk") for _ in range(7)]
            bres = [("bank1", i) for i in range(7)]
            sq_ring = sbring(sb, 3, [128, GW], BF16, "sq")
            ln_ring = sbring(sb, 1, [128, GW], F32, "lnss")
            rstd_ring = sbring(sb, 2, [128, GW], F32, "rstdfm")
            cqn_ring = sbring(sb, 2, [128, 3, GW], BF16, "cqn")
            cn_ring = sbring(sb, 2, [128, 2, GW], BF16, "cn")
            cos_ring = sbring(sb, 2, [64, GW], F32, "cosm")
            sin_ring = sbring(sb, 2, [64, GW], F32, "sinm")
            t1_ring = sbring(sb, 2, [64, GW], F32, "t1")
            t2_ring = sbring(sb, 2, [64, GW], F32, "t2")
            qnst_ring = sbring(sb, 2, [128, H, GW], BF16, "qnst")
            qrst_ring = sbring(sb, 2, [64, H, GW], BF16, "qrst")
            knst_ring = sbring(sb, 2, [128, H, GW], BF16, "knst")
            vst_ring = sbring(sb, 2, [128, 4, D], BF16, "vst")
            krst_ring = sbring(sb, 2, [64, GW], BF16, "krst")
            bi = [0]

            def nb():
                bi[0] += 1
                k = bi[0] % 7
                return banks[k], bres[k]

            for g in range(NG):
                gs = slice(g * GW, (g + 1) * GW)
                cos_t, cos_r = cos_ring.next()
                sin_t, sin_r = sin_ring.next()
                P.op("sp", I("dma_start", out=cos_t[:], in_=cosm_d[:, gs]), writes=[cos_r], slot=cos_r)
                P.op("sp", I("dma_start", out=sin_t[:], in_=sinm_d[:, gs]), writes=[sin_r], slot=sin_r)
                xT_t, xT_r = xT_ring.next()
                for i in range(4):
                    t = g * 4 + i
                    xs_t, xs_r = xs_ring.next()
                    xb_t, xb_r = xb_ring.next()
                    P.op("sp", I("dma_start", out=xs_t[:], in_=x[t * 128:(t + 1) * 128, :]), writes=[xs_r], slot=xs_r)
                    P.op("pool", I("tensor_copy", out=xb_t[:], in_=xs_t[:]), reads=[xs_r], writes=[xb_r])
                    pt_t, pt_r = pt_ring.next()
                    for c in range(8):
                        P.op("pe", I("transpose", out=pt_t[:, c, :], in_=xb_t[:, c * 128:(c + 1) * 128],
                                                                                  identity=ident[:]), reads=[xb_r, "ident"], writes=[pt_r])
                    evac(xT_t[:, :, i * 128:(i + 1) * 128], pt_t[:], [pt_r], [(xT_r, i)])
                xT_all = [(xT_r, i) for i in range(4)]
                cqb = [nb() for _ in range(3)]
                for m in range(3):
                    for c in range(8):
                        P.op("pe", I("matmul", cqb[m][0][:], lhsT=wdq[:, c, m * 128:(m + 1) * 128], rhs=xT_t[:, c, :],
                                                                             start=(c == 0), stop=(c == 7)),
                             reads=xT_all + [("wdq", 0)], writes=[cqb[m][1]])
                ssb, ssr = nb()
                cqn_t, cqn_r = cqn_ring.next()
                rms_fm([b[0] for b in cqb], [b[1] for b in cqb], 3, 384, gq, cqn_t, cqn_r, sq_ring, ssb, ssr, ln_ring, rstd_ring, ones)
                cqn_all = [(cqn_r, m) for m in range(3)]
                cb = [nb() for _ in range(2)]
                for m in range(2):
                    for c in range(8):
                        P.op("pe", I("matmul", cb[m][0][:], lhsT=wdkv[:, c, m * 128:(m + 1) * 128], rhs=xT_t[:, c, :],
                                                                            start=(c == 0), stop=(c == 7)),
                             reads=xT_all + wdkv_res, writes=[cb[m][1]])
                ssb, ssr = nb()
                cn_t, cn_r = cn_ring.next()
                rms_fm([b[0] for b in cb], [b[1] for b in cb], 2, 256, gkv, cn_t, cn_r, sq_ring, ssb, ssr, ln_ring, rstd_ring, ones)
                cn_all = [(cn_r, m) for m in range(2)]
                ab, ar = nb()
                bb, br = nb()
                for c in range(8):
                    P.op("pe", I("matmul", ab[0:64, :], lhsT=wdkv[:, c, 256:320], rhs=xT_t[:, c, :], start=(c == 0), stop=(c == 7)),
                         reads=xT_all + wdkv_res, writes=[ar])
                for c in range(8):
                    P.op("pe", I("matmul", bb[0:64, :], lhsT=wdkv[:, c, 320:384], rhs=xT_t[:, c, :], start=(c == 0), stop=(c == 7)),
                         reads=xT_all + wdkv_res, writes=[br])
                t1, t1r = t1_ring.next()
                t2, t2r = t2_ring.next()
                kr_t, kr_r = krst_ring.next()
                P.op("dve", I("tensor_tensor", out=t1[:], in0=ab[0:64, :], in1=cos_t[:], op=ALU.mult),
                     reads=[ar, cos_r], writes=[t1r])
                P.op("dve", I("tensor_tensor", out=t2[:], in0=bb[0:64, :], in1=sin_t[:], op=ALU.mult),
                     reads=[br, sin_r], writes=[t2r])
                P.op("pool", I("tensor_tensor", out=kr_t[:], in0=t1[:], in1=t2[:], op=ALU.add),
                     reads=[t1r, t2r], writes=[kr_r])
                P.op("sp", I("dma_start", out=KR0[:, gs], in_=kr_t[:]), reads=[kr_r], writes=[("KR0", g)], slot=kr_r)
                qn_t, qn_r = qnst_ring.next()
                qr_t, qr_r = qrst_ring.next()
                kn_t, kn_r = knst_ring.next()
                for h in range(H):
                    bk, bkr = nb()
                    for c in range(3):
                        P.op("pe", I("matmul", bk[:], lhsT=wuq[:, c, h * 192:h * 192 + 128], rhs=cqn_t[:, c, :],
                                                                       start=(c == 0), stop=(c == 2)), reads=cqn_all + [("wuq", 0)], writes=[bkr])
                    evac(qn_t[:, h, :], bk[:], [bkr], [(qn_r, h)], scale=SC0)
                    ab, ar = nb()
                    bb, br = nb()
                    for c in range(3):
                        P.op("pe", I("matmul", ab[0:64, :], lhsT=wuq[:, c, h * 192 + 128:h * 192 + 192], rhs=cqn_t[:, c, :],
                                                                       start=(c == 0), stop=(c == 2)), reads=cqn_all + [("wuq", 0)], writes=[ar])
                    for c in range(3):
                        P.op("pe", I("matmul", bb[0:64, :], lhsT=wuqr[:, c, h, :], rhs=cqn_t[:, c, :],
                                                                       start=(c == 0), stop=(c == 2)), reads=cqn_all + wuqr_res, writes=[br])
                    t1, t1r = t1_ring.next()
                    t2, t2r = t2_ring.next()
                    P.op("dve", I("scalar_tensor_tensor", out=t1[:], in0=ab[0:64, :], scalar=SC0, in1=cos_t[:],
                                                                                          op0=ALU.mult, op1=ALU.mult), reads=[ar, cos_r], writes=[t1r])
                    P.op("dve", I("scalar_tensor_tensor", out=t2[:], in0=bb[0:64, :], scalar=SC0, in1=sin_t[:],
                                                                                          op0=ALU.mult, op1=ALU.mult), reads=[br, sin_r], writes=[t2r])
                    P.op("pool", I("tensor_tensor", out=qr_t[:, h, :], in0=t1[:], in1=t2[:], op=ALU.add),
                         reads=[t1r, t2r], writes=[(qr_r, h)])
                    bk, bkr = nb()
                    for c in range(2):
                        P.op("pe", I("matmul", bk[:], lhsT=wukv[:, c, h * 256:h * 256 + 128], rhs=cn_t[:, c, :],
                                                                       start=(c == 0), stop=(c == 1)), reads=cn_all + [("wukv", 0)], writes=[bkr])
                    evac(kn_t[:, h, :], bk[:], [bkr], [(kn_r, h)])
                P.op("sp", I("dma_start", out=QN0.rearrange("h p t -> p h t")[:, :, gs], in_=qn_t[:]),
                     reads=[(qn_r, h) for h in range(H)], writes=[("QN0", g)], slot=qn_r)
                P.op("sp", I("dma_start", out=QR0.rearrange("h p t -> p h t")[:, :, gs], in_=qr_t[:]),
                     reads=[(qr_r, h) for h in range(H)], writes=[("QR0", g)], slot=qr_r)
                P.op("sp", I("dma_start", out=KN0.rearrange("h p t -> p h t")[:, :, gs], in_=kn_t[:]),
                     reads=[(kn_r, h) for h in range(H)], writes=[("KN0", g)], slot=kn_r)
                v_t, v_r = vst_ring.next()
                wv4 = wukv[:].rearrange("p c (h d) -> p c h d", d=256)
                for i in range(4):
                    for hh in range(2):
                        bk, bkr = nb()
                        for c in range(2):
                            P.op("pe", I("matmul", bk[:].rearrange("p (h d) -> p h d", d=128),
                                                                               lhsT=cn_t[:, c, i * 128:(i + 1) * 128],
                                                                               rhs=wv4[:, c, hh * 4:(hh + 1) * 4, 128:256], start=(c == 0), stop=(c == 1)),
                                 reads=cn_all + [("wukv", 0)], writes=[bkr])
                        evac(v_t[:, i, hh * 512:(hh + 1) * 512], bk[:], [bkr], [(v_r, i, hh)])
                P.op("sp", I("dma_start", out=V0[g * GW:(g + 1) * GW, :].rearrange("(i p) n -> p i n", p=128), in_=v_t[:]),
                     reads=[(v_r, i, hh) for i in range(4) for hh in range(2)], writes=[("V0", g)], slot=v_r)
            P.barrier()

    def attention(layer):
        st, sb, ps = scope()
        with st:
            nmap = 1 if layer == 0 else 2
            if layer == 0:
                S_ring = Ring([ps([128, 1, GW], F32, "S") for _ in range(3)], "S0")
                NSET = 2
            else:
                S_ring = Ring([ps([128, 2, GW], F32, "S") for _ in range(2)], "S1")
                NSET = 1
            nacc = 4 * nmap
            nbank = (nacc + 2) // 3
            O_sets = [[ps([128, GW], F32, "O") for _ in range(nbank)] for _ in range(NSET)]
            pt = ps([128, 8, 128], BF16, "ptA")
            AST = 130

            def acc_ap(set_i, a, lo=0, hi=129):
                return O_sets[set_i][a // 3][:, (a % 3) * AST + lo:(a % 3) * AST + hi]

            C = load_consts(sb, need_ident=True, need_mask=True)
            ident, mask = C["ident"], C["mask"]
            if layer == 0:
                qn_ring = sbring(sb, 2, [128, S], BF16, "qn")
                qr_ring = sbring(sb, 2, [64, S], BF16, "qr")
                kn_ring = sbring(sb, 2, [128, S], BF16, "kn")
                kr = sb([64, S], BF16, "kr")
                P.op("sp", I("dma_start", out=kr[:], in_=KR0[:, :]), writes=["kr"], slot="kr")
                Vd, OTd = V0, OT0
            else:
                qn_ring = sbring(sb, 2, [128, S], BF16, "q1")
                kn_ring = sbring(sb, 2, [128, S], BF16, "k1")
                Vd, OTd = V1, OT1
                mh = sb([128, 1], F32, "mh")
                P.op("pool", I("memset", mh[:], -0.5), writes=["mh"])
                lt = [sb([128, 64], F32, "lt") for _ in range(4)]
                for k_, src in enumerate((lq1, lk1, lq2, lk2)):
                    P.op("sp", I("dma_start", out=lt[k_][:], in_=src[0:1, :].partition_broadcast(128)),
                         writes=[("lt", k_)], slot=("lt", k_))
                pr = [sb([128, 64], F32, "pr") for _ in range(2)]
                sm = [sb([128, 1], F32, "lsm") for _ in range(2)]
                ex = [sb([128, 1], F32, "lex") for _ in range(2)]
                neglam = sb([128, 1], F32, "neglam")
                gvec = sb([128, 128], F32, "gvec")
                junk = sb([128, 64], F32, "junk")
                for k_ in range(2):
                    P.op("dve", I("tensor_tensor", out=pr[k_][:], in0=lt[2 * k_][:], in1=lt[2 * k_ + 1][:], op=ALU.mult),
                         reads=[("lt", 2 * k_), ("lt", 2 * k_ + 1)], writes=[("pr", k_)])
                    P.op("act", I("activation", out=junk[:], in_=pr[k_][:], func=AF.Copy, accum_out=sm[k_][:]),
                         reads=[("pr", k_)], writes=[("lsm", k_), "junk"])
                    P.op("act", I("activation", out=ex[k_][:], in_=sm[k_][:], func=AF.Exp), reads=[("lsm", k_)], writes=[("lex", k_)])
                P.op("dve", I("tensor_tensor", out=neglam[:], in0=ex[1][:], in1=ex[0][:], op=ALU.subtract),
                     reads=[("lex", 0), ("lex", 1)], writes=["neglam0"])
                P.op("dve", I("tensor_scalar", out=neglam[:], in0=neglam[:], scalar1=-LAMBDA_INIT, scalar2=None, op0=ALU.add),
                     reads=["neglam0"], writes=["neglam"])
                P.op("sp", I("dma_start", out=gvec[:], in_=subln.rearrange("(o n) -> o n", o=1).partition_broadcast(128)), writes=["gvec0"], slot="gvec")
                P.op("pool", I("tensor_scalar", out=gvec[:], in0=gvec[:], scalar1=1.0 - LAMBDA_INIT, scalar2=None, op0=ALU.mult),
                     reads=["gvec0"], writes=["gvec"])
                o1_ring = sbring(sb, 3, [128, 128], F32, "o1")
                o_ring = sbring(sb, 3, [128, 128], F32, "o")
                junk2 = sb([128, 128], F32, "junk2")
                ss_ring = sbring(sb, 3, [128, 1], F32, "ss")
                rstd_ring = sbring(sb, 3, [128, 1], F32, "rstdA")
                c2_ring = sbring(sb, 3, [128, 1], F32, "c2")
            v_ring = sbring(sb, 2, [128, NT, AST], BF16, "v")
            for k_ in range(2):
                P.op("pool", I("memset", v_ring.tiles[k_][:, :, 128:130], 1.0), writes=[(("vones", k_))])
            pT_ring = sbring(sb, 4, [128, nmap, GW], BF16, "pT")
            rinv_ring = sbring(sb, 4 * nmap, [128, 1], F32, "rinv")
            ob_ring = sbring(sb, 3, [128, 128], BF16, "ob")
            ost_ring = sbring(sb, 2, [128, GW], BF16, "ost")
            LOOK = 2 if layer == 0 else 1
            heads = {}

            def load_head(h):
                if h >= H:
                    return
                d = {}
                d["qn"] = qn_ring.next()
                d["kn"] = kn_ring.next()
                d["v"] = v_ring.next()
                if layer == 0:
                    d["qr"] = qr_ring.next()
                    P.op("sp", I("dma_start", out=d["qn"][0][:], in_=QN0[h]), writes=[d["qn"][1]], slot=d["qn"][1])
                    P.op("sp", I("dma_start", out=d["qr"][0][:], in_=QR0[h]), writes=[d["qr"][1]], slot=d["qr"][1])
                    P.op("sp", I("dma_start", out=d["kn"][0][:], in_=KN0[h]), writes=[d["kn"][1]], slot=d["kn"][1])
                else:
                    P.op("sp", I("dma_start", out=d["qn"][0][:], in_=QT1[h]), writes=[d["qn"][1]], slot=d["qn"][1])
                    P.op("sp", I("dma_start", out=d["kn"][0][:], in_=KT1[h]), writes=[d["kn"][1]], slot=d["kn"][1])
                vk = v_ring.i % 2
                P.op("sp", I("dma_start", out=d["v"][0][:, :, 0:128], in_=Vd.rearrange("(t p) (h d) -> h p t d", p=128, d=128)[h]),
                     reads=[("vones", vk)], writes=[d["v"][1]], slot=d["v"][1])
                heads[h] = d

            set_ctr = [0]
            load_head(0)
            for h in range(H):
                load_head(h + 1)
                hd = heads.pop(h)
                qn_t, qn_r = hd["qn"]
                kn_t, kn_r = hd["kn"]
                v_t, v_r = hd["v"]
                if layer == 0:
                    qr_t, qr_r = hd["qr"]
                pairs = [(g, kt) for g in range(NG) for kt in range(4 * g + 4)]
                state = {}
                gstate = {}
                pending = []

                def emit_qk(n):
                    g, kt = pairs[n]
                    j = max(kt - 4 * g, 0)
                    q0 = g * GW + 128 * j
                    q1 = (g + 1) * GW
                    ks = slice(kt * 128, (kt + 1) * 128)
                    diag = kt - 4 * g >= 0
                    S_t, S_r = S_ring.next()
                    for m in range(nmap):
                        so = S_t[:, m, 128 * j:GW]
                        if layer == 0:
                            P.op("pe", I("matmul", so, lhsT=kn_t[:, ks], rhs=qn_t[:, q0:q1], start=True, stop=False),
                                 reads=[kn_r, qn_r], writes=[(S_r, m)])
                            P.op("pe", I("matmul", so, lhsT=kr[:, ks], rhs=qr_t[:, q0:q1], start=False, stop=not diag),
                                 reads=["kr", qr_r], writes=[(S_r, m)])
                        else:
                            lo, hi = m * 64, (m + 1) * 64
                            P.op("pe", I("matmul", so, lhsT=kn_t[lo:hi, ks], rhs=qn_t[lo:hi, q0:q1], start=True, stop=not diag),
                                 reads=[kn_r, qn_r], writes=[(S_r, m)])
                        if diag:
                            P.op("pe", I("matmul", S_t[:, m, 128 * j:128 * j + 128], lhsT=ident[:], rhs=mask[:, 0, 0:128], start=False, stop=True),
                                 reads=["ident", "mask"], writes=[(S_r, m)])
                    state[n] = (S_t, S_r)

                def emit_rest(n):
                    g, kt = pairs[n]
                    j = max(kt - 4 * g, 0)
                    S_t, S_r = state.pop(n)
                    if kt == 0:
                        set_ctr[0] += 1
                        gstate[g] = dict(set=set_ctr[0] % NSET, started=set())
                    gs_ = gstate[g]
                    si = gs_["set"]
                    pT_t, pT_r = pT_ring.next()
                    P.op("act", I("activation", out=pT_t[:, :, 128 * j:GW], in_=S_t[:, :, 128 * j:GW], func=AF.Exp),
                         reads=[(S_r, m) for m in range(nmap)], writes=[pT_r])
                    if kt == 0 and pending:
                        P.op("pe", I("transpose", out=pt[:, 7, :], in_=ident[:], identity=ident[:]), reads=["ident"], writes=[("ptA", 7), "pe_tick"])
                        flush_pending()
                    for m in range(nmap):
                        for i in range(j, 4):
                            a = m * 4 + i
                            bank = a // 3
                            first = bank not in gs_["started"]
                            gs_["started"].add(bank)
                            P.op("pe", I("matmul", acc_ap(si, a), lhsT=pT_t[:, m, i * 128:(i + 1) * 128], rhs=v_t[:, kt, 0:129],
                                         start=first, stop=(kt == 4 * g + i), skip_group_check=True),
                                 reads=[pT_r, v_r], writes=[("acc", si, a), "pe_tick"])
                    flush_pending()
                    if kt >= 4 * g:
                        pending.append((g, kt - 4 * g, si))

                def flush_pending():
                    while pending:
                        finish(*pending.pop(0))

                def finish(g, i, si):
                    if i == 0:
                        gstate[g]["ost"] = ost_ring.next()
                    ost_t, ost_r = gstate[g]["ost"]
                    ob_t, ob_r = ob_ring.next()
                    if layer == 0:
                        ri_t, ri_r = rinv_ring.next()
                        P.op("dve", I("reciprocal", out=ri_t[:], in_=acc_ap(si, i, 128, 129)), reads=[("acc", si, i), "pe_tick"], writes=[ri_r])
                        P.op("dve", I("tensor_scalar", out=ob_t[:], in0=acc_ap(si, i, 0, 128), scalar1=ri_t[:], scalar2=None, op0=ALU.mult),
                             reads=[("acc", si, i), ri_r], writes=[ob_r])
                    else:
                        ri1, ri1r = rinv_ring.next()
                        ri2, ri2r = rinv_ring.next()
                        c2_t, c2_r = c2_ring.next()
                        o1_t, o1_r = o1_ring.next()
                        o_t, o_r = o_ring.next()
                        ss_t, ss_r = ss_ring.next()
                        rs_t, rs_r = rstd_ring.next()
                        P.op("dve", I("reciprocal", out=ri1[:], in_=acc_ap(si, i, 128, 129)), reads=[("acc", si, i), "pe_tick"], writes=[ri1r])
                        P.op("dve", I("reciprocal", out=ri2[:], in_=acc_ap(si, 4 + i, 128, 129)), reads=[("acc", si, 4 + i)], writes=[ri2r])
                        P.op("dve", I("tensor_tensor", out=c2_t[:], in0=ri2[:], in1=neglam[:], op=ALU.mult), reads=[ri2r, "neglam"], writes=[c2_r])
                        P.op("dve", I("tensor_scalar", out=o1_t[:], in0=acc_ap(si, i, 0, 128), scalar1=ri1[:], scalar2=None, op0=ALU.mult),
                             reads=[("acc", si, i), ri1r], writes=[o1_r])
                        P.op("dve", I("scalar_tensor_tensor", out=o_t[:], in0=acc_ap(si, 4 + i, 0, 128), scalar=c2_t[:], in1=o1_t[:],
                                      op0=ALU.mult, op1=ALU.add), reads=[("acc", si, 4 + i), c2_r, o1_r], writes=[o_r])
                        P.op("dve", I("tensor_tensor", out=junk2[:], in0=o_t[:], in1=o_t[:], op=ALU.mult), reads=[o_r], writes=["junk2"])
                        P.op("dve", I("tensor_reduce", out=ss_t[:], in_=junk2[:], axis=mybir.AxisListType.X, op=ALU.add), reads=["junk2"], writes=[ss_r])
                        P.op("dve", I("tensor_scalar", out=ss_t[:], in0=ss_t[:], scalar1=1.0 / 128, scalar2=RMS_EPS, op0=ALU.mult, op1=ALU.add),
                             reads=[ss_r], writes=[ss_r])
                        P.op("act", I("activation", out=ss_t[:], in_=ss_t[:], func=AF.Ln), reads=[ss_r], writes=[ss_r])
                        P.op("act", I("activation", out=rs_t[:], in_=ss_t[:], func=AF.Exp, scale=-0.5), reads=[ss_r], writes=[rs_r])
                        P.op("dve", I("scalar_tensor_tensor", out=ob_t[:], in0=o_t[:], scalar=rs_t[:], in1=gvec[:], op0=ALU.mult, op1=ALU.mult),
                             reads=[o_r, rs_r, "gvec"], writes=[ob_r])
                    P.op("pe", I("transpose", out=pt[:, i, :], in_=ob_t[:], identity=ident[:]), reads=[ob_r, "ident"], writes=[("ptA", i)])
                    evac(ost_t[:, i * 128:(i + 1) * 128], pt[:, i, :], [("ptA", i)], [(ost_r, i)], eng="dve")
                    if i == 3:
                        P.op("sp", I("dma_start", out=OTd[h][:, g * GW:(g + 1) * GW], in_=ost_t[:]), reads=[(ost_r, k_) for k_ in range(4)], slot=ost_r)

                N = len(pairs)
                for n in range(min(LOOK, N)):
                    emit_qk(n)
                for n in range(N):
                    if n + LOOK < N:
                        emit_qk(n + LOOK)
                    emit_rest(n)
                P.op("pe", I("transpose", out=pt[:, 7, :], in_=ident[:], identity=ident[:]), reads=["ident"], writes=[("ptA", 7), "pe_tick"])
                flush_pending()
            P.barrier()

    def attention_old(layer):
        st, sb, ps = scope()
        with st:
            C = load_consts(sb, need_ident=True, need_mask=True, need_ones=True)
            ident, mask, ones = C["ident"], C["mask"], C["ones"]
            nmap = 1 if layer == 0 else 2
            if layer == 0:
                qn_ring = sbring(sb, 2, [128, S], BF16, "qn")
                qr_ring = sbring(sb, 2, [64, S], BF16, "qr")
                kn_ring = sbring(sb, 2, [128, S], BF16, "kn")
                kr = sb([64, S], BF16, "kr")
                P.op("sp", I("dma_start", out=kr[:], in_=KR0[:, :]), writes=["kr"], slot="kr")
                Vd, OTd = V0, OT0
            else:
                qn_ring = sbring(sb, 2, [128, S], BF16, "q1")
                kn_ring = sbring(sb, 2, [128, S], BF16, "k1")
                Vd, OTd = V1, OT1
                lt = [sb([128, 64], F32, "lt") for _ in range(4)]
                for k_, src in enumerate((lq1, lk1, lq2, lk2)):
                    P.op("sp", I("dma_start", out=lt[k_][:], in_=src[0:1, :].partition_broadcast(128)),
                         writes=[("lt", k_)], slot=("lt", k_))
                pr = [sb([128, 64], F32, "pr") for _ in range(2)]
                sm = [sb([128, 1], F32, "lsm") for _ in range(2)]
                ex = [sb([128, 1], F32, "lex") for _ in range(2)]
                neglam = sb([128, 1], F32, "neglam")
                gsub = sb([128, 1], F32, "gsub")
                junk = sb([128, 64], F32, "junk")
                for k_ in range(2):
                    P.op("dve", I("tensor_tensor", out=pr[k_][:], in0=lt[2 * k_][:], in1=lt[2 * k_ + 1][:], op=ALU.mult),
                         reads=[("lt", 2 * k_), ("lt", 2 * k_ + 1)], writes=[("pr", k_)])
                    P.op("act", I("activation", out=junk[:], in_=pr[k_][:], func=AF.Copy, accum_out=sm[k_][:]),
                         reads=[("pr", k_)], writes=[("lsm", k_), "junk"])
                    P.op("act", I("activation", out=ex[k_][:], in_=sm[k_][:], func=AF.Exp), reads=[("lsm", k_)], writes=[("lex", k_)])
                P.op("dve", I("tensor_tensor", out=neglam[:], in0=ex[1][:], in1=ex[0][:], op=ALU.subtract),
                     reads=[("lex", 0), ("lex", 1)], writes=["neglam0"])
                P.op("dve", I("tensor_scalar", out=neglam[:], in0=neglam[:], scalar1=-LAMBDA_INIT, scalar2=None, op0=ALU.add),
                     reads=["neglam0"], writes=["neglam"])
                P.op("sp", I("dma_start", out=gsub[:], in_=subln.rearrange("(p o) -> p o", o=1)), writes=["gsub0"], slot="gsub")
                P.op("pool", I("tensor_scalar", out=gsub[:], in0=gsub[:], scalar1=1.0 - LAMBDA_INIT, scalar2=None, op0=ALU.mult),
                     reads=["gsub0"], writes=["gsub"])
            v_ring = sbring(sb, 2, [128, NT, 128], BF16, "v")
            NS = 4 if layer == 0 else 2
            pT_rings = [sbring(sb, 4, [128, GW], BF16, "pT%d" % m) for m in range(nmap)]
            S_rings = [Ring([ps([128, GW], F32, "S") for _ in range(NS)], "S%d_%d" % (layer, m)) for m in range(nmap)]
            NO = 2 if layer == 0 else 1
            O_rings = [Ring([ps([128, GW], F32, "O") for _ in range(NO)], "O%d_%d" % (layer, m)) for m in range(nmap)]
            M_rings = [Ring([ps([128, GW], F32, "M") for _ in range(NO)], "M%d_%d" % (layer, m)) for m in range(nmap)]
            rs_ring = sbring(sb, 2 * nmap, [128, GW], F32, "rs")
            ost_ring = sbring(sb, 2, [128, GW], BF16, "ost")
            if layer == 1:
                tt_ring = sbring(sb, 4, [128, GW], F32, "tt")
                o_ring = sbring(sb, 2, [128, GW], F32, "o")
                sq_ring = sbring(sb, 2, [128, GW], BF16, "sq5")
                ln_ring = sbring(sb, 2, [128, GW], F32, "ln5")
            LOOK = 2 if layer == 0 else 1

            for h in range(H):
                qn_t, qn_r = qn_ring.next()
                kn_t, kn_r = kn_ring.next()
                v_t, v_r = v_ring.next()
                if layer == 0:
                    qr_t, qr_r = qr_ring.next()
                    P.op("sp", I("dma_start", out=qn_t[:], in_=QN0[h]), reads=[("QN0", g) for g in range(NG)], writes=[qn_r], slot=qn_r)
                    P.op("sp", I("dma_start", out=qr_t[:], in_=QR0[h]), reads=[("QR0", g) for g in range(NG)], writes=[qr_r], slot=qr_r)
                    P.op("sp", I("dma_start", out=kn_t[:], in_=KN0[h]), reads=[("KN0", g) for g in range(NG)], writes=[kn_r], slot=kn_r)
                else:
                    P.op("sp", I("dma_start", out=qn_t[:], in_=QT1[h]), writes=[qn_r], slot=qn_r)
                    P.op("sp", I("dma_start", out=kn_t[:], in_=KT1[h]), writes=[kn_r], slot=kn_r)
                P.op("sp", I("dma_start", out=v_t[:], in_=Vd.rearrange("(t p) (h d) -> h p t d", p=128, d=128)[h]),
                     writes=[v_r], slot=v_r)
                pairs = [(g, kt) for g in range(NG) for kt in range(4 * g + 4)]
                state = {}

                def emit_qk(n):
                    g, kt = pairs[n]
                    gs = slice(g * GW, (g + 1) * GW)
                    ks = slice(kt * 128, (kt + 1) * 128)
                    j = kt - 4 * g
                    Sb = []
                    for m in range(nmap):
                        S_t, S_r = S_rings[m].next()
                        if layer == 0:
                            P.op("pe", I("matmul", S_t[:], lhsT=kn_t[:, ks], rhs=qn_t[:, gs], start=True, stop=False),
                                 reads=[kn_r, qn_r], writes=[S_r])
                            P.op("pe", I("matmul", S_t[:], lhsT=kr[:, ks], rhs=qr_t[:, gs], start=False, stop=(j < 0)),
                                 reads=["kr", qr_r], writes=[S_r])
                        else:
                            lo, hi = m * 64, (m + 1) * 64
                            P.op("pe", I("matmul", S_t[:], lhsT=kn_t[lo:hi, ks], rhs=qn_t[lo:hi, gs], start=True, stop=(j < 0)),
                                 reads=[kn_r, qn_r], writes=[S_r])
                        if j >= 0:
                            P.op("pe", I("matmul", S_t[:], lhsT=ident[:], rhs=mask[:, j, :], start=False, stop=True),
                                 reads=["ident", "mask"], writes=[S_r])
                        Sb.append((S_t, S_r))
                    state[n] = Sb

                def emit_rest(n):
                    g, kt = pairs[n]
                    last = 4 * g + 3
                    Sb = state.pop(n)
                    if kt == 0:
                        state["O"] = [O_rings[m].next() for m in range(nmap)]
                        state["M"] = [M_rings[m].next() for m in range(nmap)]
                    for m in range(nmap):
                        S_t, S_r = Sb[m]
                        pT_t, pT_r = pT_rings[m].next()
                        P.op("act", I("activation", out=pT_t[:], in_=S_t[:], func=AF.Exp), reads=[S_r], writes=[pT_r])
                        O_t, O_r = state["O"][m]
                        M_t, M_r = state["M"][m]
                        P.op("pe", I("matmul", O_t[:], lhsT=v_t[:, kt, :], rhs=pT_t[:], start=(kt == 0), stop=(kt == last)),
                             reads=[v_r, pT_r], writes=[O_r])
                        P.op("pe", I("matmul", M_t[:], lhsT=ones[:], rhs=pT_t[:], start=(kt == 0), stop=(kt == last)),
                             reads=["ones", pT_r], writes=[M_r])
                    if kt == last:
                        finish(g)

                def finish(g):
                    gs = slice(g * GW, (g + 1) * GW)
                    ost_t, ost_r = ost_ring.next()
                    if layer == 0:
                        O_t, O_r = state["O"][0]
                        M_t, M_r = state["M"][0]
                        rs_t, rs_r = rs_ring.next()
                        P.op("dve", I("reciprocal", out=rs_t[:], in_=M_t[:]), reads=[M_r], writes=[rs_r])
                        P.op("dve", I("tensor_tensor", out=ost_t[:], in0=O_t[:], in1=rs_t[:], op=ALU.mult), reads=[O_r, rs_r], writes=[ost_r])
                    else:
                        tts = []
                        for m in range(2):
                            O_t, O_r = state["O"][m]
                            M_t, M_r = state["M"][m]
                            rs_t, rs_r = rs_ring.next()
                            tt_t, tt_r = tt_ring.next()
                            P.op("dve", I("reciprocal", out=rs_t[:], in_=M_t[:]), reads=[M_r], writes=[rs_r])
                            P.op("dve", I("tensor_tensor", out=tt_t[:], in0=O_t[:], in1=rs_t[:], op=ALU.mult),
                                 reads=[O_r, rs_r], writes=[tt_r])
                            tts.append((tt_t, tt_r))
                        o_t, o_r = o_ring.next()
                        P.op("dve", I("scalar_tensor_tensor", out=o_t[:], in0=tts[1][0][:], scalar=neglam[:], in1=tts[0][0][:],
                                                                     op0=ALU.mult, op1=ALU.add), reads=[tts[0][1], tts[1][1], "neglam"], writes=[o_r])
                        sq_t, sq_r = sq_ring.next()
                        P.op("pool", I("tensor_tensor", out=sq_t[:], in0=o_t[:], in1=o_t[:], op=ALU.mult), reads=[o_r], writes=[sq_r])
                        M_t, M_r = state["M"][0]
                        P.op("pe", I("matmul", M_t[:], lhsT=ones[:], rhs=sq_t[:], start=True, stop=True), reads=["ones", sq_r], writes=[M_r])
                        ln_t, ln_r = ln_ring.next()
                        rs_t, rs_r = rs_ring.next()
                        P.op("act", I("activation", out=ln_t[:], in_=M_t[:], func=AF.Ln, scale=1.0 / 128, bias=RMS_EPS), reads=[M_r], writes=[ln_r])
                        P.op("act", I("activation", out=rs_t[:], in_=ln_t[:], func=AF.Exp, scale=-0.5), reads=[ln_r], writes=[rs_r])
                        P.op("dve", I("scalar_tensor_tensor", out=ost_t[:], in0=o_t[:], scalar=gsub[:], in1=rs_t[:], op0=ALU.mult, op1=ALU.mult),
                             reads=[o_r, rs_r, "gsub"], writes=[ost_r])
                    P.op("sp", I("dma_start", out=OTd[h][:, gs], in_=ost_t[:]), reads=[ost_r], writes=[("OT", h, g)], slot=ost_r)

                N = len(pairs)
                for n in range(min(LOOK, N)):
                    emit_qk(n)
                for n in range(N):
                    if n + LOOK < N:
                        emit_qk(n + LOOK)
                    emit_rest(n)
            P.barrier()

    def outproj_ln(layer):
        st, sb, ps = scope()
        OTd = OT0 if layer == 0 else OT1
        wo_d = w_o0 if layer == 0 else w_o1
        res_d = x if layer == 0 else H2
        Hd, HTd = (H1, H1T) if layer == 0 else (H3, H3T)
        with st:
            C = load_consts(sb, need_ident=True, need_mh=True)
            ident, mh = C["ident"], C["mh"]
            wo = sb([128, 8, D], BF16, "wo")
            load_w_bf16(wo, wo_d, 8, "wo")
            Gt = sb([128, D], F32, "lnG")
            Bt = sb([128, D], F32, "lnB")
            P.op("sp", I("dma_start", out=Gt[:], in_=ln1_g[layer:layer + 1, :].partition_broadcast(128)), writes=["lnG"], slot="lnG")
            P.op("sp", I("dma_start", out=Bt[:], in_=ln1_b[layer:layer + 1, :].partition_broadcast(128)), writes=["lnB"], slot="lnB")
            ot_ring = sbring(sb, 2, [128, H, GW], BF16, "otg")
            res_ring = sbring(sb, 4, [128, D], F32, "res")
            pt_ring = Ring([ps([128, 8, 128], BF16, "pt") for _ in range(2)], "pt3")
            a_ring = Ring([ps([128, GW], F32, "a") for _ in range(6)], "a3")
            lnp = LNPipe(sb, ps, Gt, Bt, ident, mh, HTd, pt_ring)
            res_q = {}
            ot_q = {}

            def load_res(t):
                if t < NT:
                    res_t, res_r = res_ring.next()
                    P.op("sp", I("dma_start", out=res_t[:], in_=res_d[t * 128:(t + 1) * 128, :]), writes=[(res_r, 0), (res_r, 1)], slot=res_r)
                    res_q[t] = (res_t, res_r)

            def load_ot(g):
                if g < NG:
                    ot_t, ot_r = ot_ring.next()
                    P.op("sp", I("dma_start", out=ot_t[:], in_=OTd.rearrange("h p t -> p h t")[:, :, g * GW:(g + 1) * GW]), writes=[ot_r], slot=ot_r)
                    ot_q[g] = (ot_t, ot_r)

            load_ot(0)
            load_res(0)
            load_res(1)
            for g in range(NG):
                ot_t, ot_r = ot_q.pop(g)
                load_ot(g + 1)
                for i in range(4):
                    t = g * 4 + i
                    load_res(t + 2)
                    res_t, res_r = res_q.pop(t)
                    z_t, z_r = res_t, res_r
                    for hh in range(2):
                        a_t, a_r = a_ring.next()
                        for h in range(H):
                            P.op("pe", I("matmul", a_t[:], lhsT=ot_t[:, h, i * 128:(i + 1) * 128],
                                                                                           rhs=wo[:, h, hh * 512:(hh + 1) * 512], start=(h == 0), stop=(h == H - 1)),
                                 reads=[ot_r, ("wo", 0)], writes=[a_r])
                        P.op("dve", I("scalar_tensor_tensor",
                            out=z_t[:, hh * 512:(hh + 1) * 512], in0=res_t[:, hh * 512:(hh + 1) * 512], scalar=ALPHA, in1=a_t[:], op0=ALU.mult, op1=ALU.add),
                            reads=[(res_r, hh), a_r], writes=[(z_r, hh)])
                    lnp.push(z_t, [(z_r, 0), (z_r, 1)], Hd[t * 128:(t + 1) * 128, :], t)
            lnp.flush()
            P.barrier()

    def ffn_ln(layer):
        st, sb, ps = scope()
        HTin = H1T if layer == 0 else H3T
        Hin = H1 if layer == 0 else H3
        Hd, HTd = (H2, H2T) if layer == 0 else (out, None)
        with st:
            C = load_consts(sb, need_ident=True, need_mh=True)
            ident, mh = C["ident"], C["mh"]
            wgu = sb([128, 8, 2 * DFF], BF16, "wgu")
            wdn = sb([128, NF, D], BF16, "wdn")
            NSPL = 11
            v = w_gu[layer].rearrange("(c p) n -> p c n", p=128)
            for s_ in range(NSPL):
                for half in range(2):
                    lo = half * DFF + s_ * 256
                    P.op("pool", I("dma_start", out=wgu[:, :, lo:lo + 256], in_=v[:, :, lo:lo + 256]),
                         writes=[("wgu", half, s_)], slot=("wgu", half, s_))
            vd = w_dn[layer].rearrange("(f p) n -> p f n", p=128)
            for s_ in range(2):
                P.op("pool", I("dma_start", out=wdn[:, s_ * 11:(s_ + 1) * 11, :], in_=vd[:, s_ * 11:(s_ + 1) * 11, :]),
                     writes=[("wdn", s_)], slot=("wdn", s_))
            Gt = sb([128, D], F32, "lnG")
            Bt = sb([128, D], F32, "lnB")
            P.op("sp", I("dma_start", out=Gt[:], in_=ln2_g[layer:layer + 1, :].partition_broadcast(128)), writes=["lnG"], slot="lnG")
            P.op("sp", I("dma_start", out=Bt[:], in_=ln2_b[layer:layer + 1, :].partition_broadcast(128)), writes=["lnB"], slot="lnB")
            hin_ring = sbring(sb, 1, [128, 8, GW], BF16, "hin")
            actT = sb([128, NF, GW], BF16, "actT")
            sg_ring = sbring(sb, 2, [128, GW], F32, "sg")
            res_ring = sbring(sb, 3, [128, D], F32, "res")
            pt_ring = Ring([ps([128, 8, 128], BF16, "pt")], "pt4")
            lnp = LNPipe(sb, ps, Gt, Bt, ident, mh, HTd, pt_ring)
            g_ring = Ring([ps([128, GW], F32, "gb") for _ in range(2)], "gbk")
            u_ring = Ring([ps([128, GW], F32, "ub") for _ in range(2)], "ubk")
            d_ring = Ring([ps([128, GW], F32, "db") for _ in range(3)], "dbk")
            res_q = {}

            def load_res(t):
                res_t, res_r = res_ring.next()
                P.op("sp", I("dma_start", out=res_t[:], in_=Hin[t * 128:(t + 1) * 128, :]), writes=[(res_r, 0), (res_r, 1)], slot=res_r)
                res_q[t] = (res_t, res_r)

            for g in range(NG):
                gs = slice(g * GW, (g + 1) * GW)
                hin_t, hin_r = hin_ring.next()
                P.op("sp", I("dma_start", out=hin_t[:], in_=HTin.rearrange("c p t -> p c t")[:, :, gs]), writes=[hin_r], slot=hin_r)
                for f in range(NF):
                    gb, gr = g_ring.next()
                    ub, ur = u_ring.next()
                    wres = [("wgu", 0, f // 2), ("wgu", 1, f // 2)]
                    for c in range(8):
                        P.op("pe", I("matmul", gb[:], lhsT=wgu[:, c, f * 128:(f + 1) * 128], rhs=hin_t[:, c, :],
                                                                                start=(c == 0), stop=(c == 7)), reads=[hin_r, wres[0]], writes=[gr])
                    for c in range(8):
                        P.op("pe", I("matmul", ub[:], lhsT=wgu[:, c, DFF + f * 128:DFF + (f + 1) * 128], rhs=hin_t[:, c, :],
                                                                                start=(c == 0), stop=(c == 7)), reads=[hin_r, wres[1]], writes=[ur])
                    sg_t, sg_r = sg_ring.next()
                    P.op("act", I("activation", out=sg_t[:], in_=gb[:], func=AF.Silu), reads=[gr], writes=[sg_r])
                    P.op("dve", I("tensor_tensor", out=actT[:, f, :], in0=ub[:], in1=sg_t[:], op=ALU.mult),
                         reads=[ur, sg_r], writes=[("actT", f)])
                for i in range(4):
                    t = g * 4 + i
                    if i == 0:
                        load_res(t)
                    if i < 3:
                        load_res(t + 1)
                    res_t, res_r = res_q.pop(t)
                    z_t, z_r = res_t, res_r
                    for hh in range(2):
                        db, dr = d_ring.next()
                        for f in range(NF):
                            P.op("pe", I("matmul", db[:], lhsT=actT[:, f, i * 128:(i + 1) * 128],
                                                                              rhs=wdn[:, f, hh * 512:(hh + 1) * 512], start=(f == 0), stop=(f == NF - 1)),
                                 reads=[("actT", f), ("wdn", f // 11)], writes=[dr])
                        P.op("dve", I("scalar_tensor_tensor",
                            out=z_t[:, hh * 512:(hh + 1) * 512], in0=res_t[:, hh * 512:(hh + 1) * 512], scalar=ALPHA, in1=db[:], op0=ALU.mult, op1=ALU.add),
                            reads=[(res_r, hh), dr], writes=[(z_r, hh)])
                    lnp.push(z_t, [(z_r, 0), (z_r, 1)], Hd[t * 128:(t + 1) * 128, :], t)
            lnp.flush()
            P.barrier()

    def phase_proj1():
        st, sb, ps = scope()
        with st:
            wk = sb([128, 8, D], BF16, "wk")
            wkr = sb([128, 8, D], BF16, "wkr")
            wq = sb([128, 8, D], BF16, "wq")
            wqr = sb([128, 8, D], BF16, "wqr")
            wv = sb([128, 8, D], BF16, "wv")
            kvv = kv_w.rearrange("(c p) n -> p c n", p=128)
            P.op("pool", I("dma_start", out=wk[:], in_=kvv[:, :, 0:D]), writes=["wk"], slot="wk")
            P.op("pool", I("dma_start", out=wq[:], in_=w_q1.rearrange("(c p) n -> p c n", p=128)), writes=["wq"], slot="wq")
            P.op("pool", I("dma_start", out=wv[:], in_=kvv[:, :, D:2 * D]), writes=["wv"], slot="wv")
            for (src, dst, nm) in ((wk, wkr, "wk"), (wq, wqr, "wq")):
                s4 = src[:].rearrange("p c (b d) -> p c b d", d=64)
                d4 = dst[:].rearrange("p c (b d) -> p c b d", d=64)
                for c in range(8):
                    P.op("pool", I("tensor_scalar", out=d4[:, c, :, 0:32], in0=s4[:, c, :, 32:64], scalar1=-1.0, scalar2=None, op0=ALU.mult),
                         reads=[nm], writes=[(nm + "r", c, 0)])
                    P.op("pool", I("tensor_copy", out=d4[:, c, :, 32:64], in_=s4[:, c, :, 0:32]),
                         reads=[nm], writes=[(nm + "r", c, 1)])
            rres = {nm: [(nm + "r", c, k) for c in range(8) for k in range(2)] for nm in ("wk", "wq")}
            hin_ring = sbring(sb, 2, [128, 8, GW], BF16, "hin")
            cos_ring = sbring(sb, 2, [128, GW], F32, "cosd")
            sin_ring = sbring(sb, 2, [128, GW], F32, "sind")
            t1_ring = sbring(sb, 3, [128, GW], F32, "t1")
            t2_ring = sbring(sb, 3, [128, GW], F32, "t2")
            kst_ring = sbring(sb, 2, [128, H, GW], BF16, "kst")
            qst_ring = sbring(sb, 2, [128, H, GW], BF16, "qst")
            vst_ring = sbring(sb, 2, [128, 4, D], BF16, "vst")
            banks = Ring([ps([128, GW], F32, "bk") for _ in range(8)], "bank5")
            for g in range(NG):
                gs = slice(g * GW, (g + 1) * GW)
                hin_t, hin_r = hin_ring.next()
                cos_t, cos_r = cos_ring.next()
                sin_t, sin_r = sin_ring.next()
                P.op("sp", I("dma_start", out=hin_t[:], in_=H2T.rearrange("c p t -> p c t")[:, :, gs]), writes=[hin_r], slot=hin_r)
                P.op("sp", I("dma_start", out=cos_t[:], in_=cosd_d[:, gs]), writes=[cos_r], slot=cos_r)
                P.op("sp", I("dma_start", out=sin_t[:], in_=sind_d[:, gs]), writes=[sin_r], slot=sin_r)
                k_t, k_r = kst_ring.next()
                q_t, q_r = qst_ring.next()
                for (w_, wr_, nm, dst_t, dst_r, sc) in ((wk, wkr, "wk", k_t, k_r, 1.0), (wq, wqr, "wq", q_t, q_r, SC1)):
                    for h in range(H):
                        ab, ar = banks.next()
                        bb, br = banks.next()
                        for c in range(8):
                            P.op("pe", I("matmul", ab[:], lhsT=w_[:, c, h * 128:(h + 1) * 128], rhs=hin_t[:, c, :],
                                                                                           start=(c == 0), stop=(c == 7)), reads=[hin_r, nm], writes=[ar])
                        for c in range(8):
                            P.op("pe", I("matmul", bb[:], lhsT=wr_[:, c, h * 128:(h + 1) * 128], rhs=hin_t[:, c, :],
                                                                                             start=(c == 0), stop=(c == 7)), reads=[hin_r] + rres[nm], writes=[br])
                        t1, t1r = t1_ring.next()
                        t2, t2r = t2_ring.next()
                        P.op("dve", I("scalar_tensor_tensor", out=t1[:], in0=ab[:], scalar=sc, in1=cos_t[:],
                                                                                                   op0=ALU.mult, op1=ALU.mult), reads=[ar, cos_r], writes=[t1r])
                        P.op("dve", I("scalar_tensor_tensor", out=t2[:], in0=bb[:], scalar=sc, in1=sin_t[:],
                                                                                                   op0=ALU.mult, op1=ALU.mult), reads=[br, sin_r], writes=[t2r])
                        P.op("pool", I("tensor_tensor", out=dst_t[:, h, :], in0=t1[:], in1=t2[:], op=ALU.add),
                             reads=[t1r, t2r], writes=[(dst_r, h)])
                P.op("sp", I("dma_start", out=KT1.rearrange("h p t -> p h t")[:, :, gs], in_=k_t[:]),
                     reads=[(k_r, h) for h in range(H)], slot=k_r)
                P.op("sp", I("dma_start", out=QT1.rearrange("h p t -> p h t")[:, :, gs], in_=q_t[:]),
                     reads=[(q_r, h) for h in range(H)], slot=q_r)
                v_t, v_r = vst_ring.next()
                for i in range(4):
                    for hh in range(2):
                        bk, bkr = banks.next()
                        for c in range(8):
                            P.op("pe", I("matmul", bk[:], lhsT=hin_t[:, c, i * 128:(i + 1) * 128],
                                                                                           rhs=wv[:, c, hh * 512:(hh + 1) * 512], start=(c == 0), stop=(c == 7)),
                                 reads=[hin_r, "wv"], writes=[bkr])
                        evac(v_t[:, i, hh * 512:(hh + 1) * 512], bk[:], [bkr], [(v_r, i, hh)], eng="act")
                P.op("sp", I("dma_start", out=V1[g * GW:(g + 1) * GW, :].rearrange("(i p) n -> p i n", p=128), in_=v_t[:]),
                     reads=[(v_r, i, hh) for i in range(4) for hh in range(2)], slot=v_r)
            P.barrier()

    if 1 in phases:
        phase1()
    if 2 in phases:
        attention(0)
    if 3 in phases:
        outproj_ln(0)
    if 4 in phases:
        ffn_ln(0)
    if 5 in phases:
        phase_proj1()
    if 6 in phases:
        attention(1)
    if 7 in phases:
        outproj_ln(1)
    if 8 in phases:
        ffn_ln(1)
    P.emit()
    return nc, P


def _rope_tables(dim):
    inv = (1.0 / (10000.0 ** (np.arange(0, dim, 2, dtype=np.float32) / np.float32(dim)))).astype(np.float32)
    ang = np.arange(S, dtype=np.float32)[:, None] * inv[None, :]
    ang = np.concatenate([ang, ang], axis=-1).astype(np.float32)
    return np.ascontiguousarray(np.cos(ang).T.astype(np.float32)), np.ascontiguousarray(np.sin(ang).T.astype(np.float32))


def _consts():
    ident = np.eye(128, dtype=np.float32).astype(ml_dtypes.bfloat16)
    ki = np.arange(128)[:, None, None]
    j = np.arange(4)[None, :, None]
    qi = np.arange(GW)[None, None, :]
    mask = np.where(qi >= 128 * j + ki, 0.0, NEG).astype(np.float32).astype(ml_dtypes.bfloat16)
    cm, sm = _rope_tables(64)
    cd = np.ascontiguousarray(np.concatenate([cm, cm], axis=0))
    sd = np.ascontiguousarray(np.concatenate([sm, sm], axis=0))
    return {"c_ident": ident, "c_mask": np.ascontiguousarray(mask), "c_cosm": cm, "c_sinm": sm, "c_cosd": cd, "c_sind": sd}


_SQUEEZE = ("mla_w_dq", "mla_q_norm", "mla_w_uq", "mla_w_dkv", "mla_kv_norm", "mla_w_ukv", "mla_w_o",
            "diff_w_q", "diff_subln", "diff_w_o")


def make_in_maps(inputs, n_cores=8):
    common = dict(_consts())
    for k, v in inputs.items():
        if k == "x":
            continue
        a = np.ascontiguousarray(np.asarray(v, dtype=np.float32))
        if k in _SQUEEZE:
            a = np.ascontiguousarray(a[0])
        common[k] = a
    xs = np.asarray(inputs["x"], dtype=np.float32)
    maps = []
    for c in range(n_cores):
        m = dict(common)
        m["x"] = np.ascontiguousarray(xs[c])
        maps.append(m)
    return maps


_CACHE = {}


def kernel(**inputs):
    if "nc" not in _CACHE:
        _CACHE["nc"] = build_program()[0]
    nc = _CACHE["nc"]
    in_maps = make_in_maps(inputs, 8)
    res = run_bass_kernel_spmd(nc, in_maps, core_ids=list(range(8)))
    return np.stack([np.asarray(r["out"], dtype=np.float32) for r in res.results], axis=0)
```

```python
import math
from contextlib import ExitStack

import numpy as np
import ml_dtypes
import concourse.bass as bass
import concourse.mybir as mybir
from concourse.bass_utils import run_bass_kernel_spmd

F32 = mybir.dt.float32
BF16 = mybir.dt.bfloat16
AF = mybir.ActivationFunctionType
ALU = mybir.AluOpType

S = 4096
D = 1024
NT = 32
NG = 8
GW = 512
H = 8
DFF = 2816
NF = 22
ALPHA = 4.0 ** 0.25
LN_EPS = 1e-5
RMS_EPS = 1e-6
SC0 = 192.0 ** -0.5
SC1 = 64.0 ** -0.5
LAMBDA_INIT = 0.8 - 0.6 * math.exp(-0.3 * 1)
NEG = -30000.0

ENGS = ("pe", "act", "dve", "pool", "sp")


class _Op:
    __slots__ = ("eng", "fn", "deps", "signal", "slot", "sigval", "idx", "phase")


class Prog:
    def __init__(self, nc, same_engine_sync=True):
        self.nc = nc
        self.ops = []
        self.last_w = {}
        self.readers = {}
        self.same_engine_sync = same_engine_sync
        self.fence = set()
        self.last_on_eng = {}
        self.dma_since = []
        self.phase = 0

    def barrier(self):
        self.phase += 1
        self.fence = set(self.last_on_eng.values()) | set(self.dma_since)
        self.dma_since = []
        self.last_w = {}
        self.readers = {}

    def op(self, eng, fn, reads=(), writes=(), slot=None):
        o = _Op()
        o.eng, o.fn, o.slot, o.signal, o.sigval = eng, fn, slot, slot is not None, None
        o.idx = len(self.ops)
        o.phase = self.phase
        deps = set(self.fence)
        for r in reads:
            w = self.last_w.get(r)
            if w is not None:
                deps.add(w)
        for r in writes:
            w = self.last_w.get(r)
            if w is not None:
                deps.add(w)
            for rd in self.readers.get(r, ()):
                deps.add(rd)
        if eng == "pool" and slot is not None:
            if getattr(self, "last_pool_dma", None) is not None:
                deps.add(self.last_pool_dma)
            self.last_pool_dma = o.idx
        o.deps = deps
        for r in reads:
            lst = self.readers.setdefault(r, [])
            if slot is None:
                lst[:] = [i for i in lst if not (self.ops[i].slot is None and self.ops[i].eng == eng)]
            lst.append(o.idx)
        for r in writes:
            self.last_w[r] = o.idx
            self.readers[r] = []
        self.ops.append(o)
        if slot is None:
            self.last_on_eng[eng] = o.idx
        else:
            self.dma_since.append(o.idx)
        return o.idx

    def _skip(self, p, o):
        if p.slot is None and o.slot is None and p.eng == o.eng:
            return p.eng == "pe" or not self.same_engine_sync
        return False

    def emit(self):
        nc = self.nc
        ops = self.ops
        for o in ops:
            for d in o.deps:
                p = ops[d]
                if p.slot is None and not self._skip(p, o):
                    p.signal = True
        cnt = {e: 0 for e in ENGS}
        slotcnt = {}
        physmap = {}
        nphys = {}
        for o in ops:
            if o.slot is not None:
                key = (o.eng, o.phase, o.slot)
                if key not in physmap:
                    physmap[key] = (o.eng, nphys.get((o.eng, o.phase), 0))
                    nphys[(o.eng, o.phase)] = physmap[key][1] + 1
                ph = physmap[key]
                slotcnt[ph] = slotcnt.get(ph, 0) + 16
                o.sigval = (("dma", ph), slotcnt[ph])
            elif o.signal:
                cnt[o.eng] += 1
                o.sigval = (o.eng, cnt[o.eng])
        keys = [e for e in ENGS if cnt[e] > 0] + [("dma", s) for s in slotcnt]
        self.n_sems = len(keys)
        with ExitStack() as st:
            sems = {}
            for i, k in enumerate(keys):
                sems[k] = st.enter_context(nc.semaphore("sm%d" % i))
            block = st.enter_context(nc.Block())
            per_eng = {e: [o for o in ops if o.eng == e] for e in ENGS}

            def run(engname, eng):
                waited = {}
                for o in per_eng[engname]:
                    need = {}
                    for d in o.deps:
                        p = ops[d]
                        if p.sigval is None or self._skip(p, o):
                            continue
                        k, v = p.sigval
                        if need.get(k, 0) < v:
                            need[k] = v
                    for k, v in need.items():
                        if waited.get(k, 0) < v:
                            eng.wait_ge(sems[k], v)
                            waited[k] = v
                    ins = o.fn(eng)
                    if o.sigval is not None:
                        ins.then_inc(sems[o.sigval[0]], 16 if o.slot is not None else 1)
                if engname == "sp":
                    for s_, v in slotcnt.items():
                        if waited.get(("dma", s_), 0) < v:
                            eng.wait_ge(sems[("dma", s_)], v)

            @block.sync
            def _(e):
                run("sp", e)

            @block.tensor
            def _(e):
                run("pe", e)

            @block.scalar
            def _(e):
                run("act", e)

            @block.vector
            def _(e):
                run("dve", e)

            @block.gpsimd
            def _(e):
                run("pool", e)


def I(method, *args, **kwargs):
    return lambda e: getattr(e, method)(*args, **kwargs)


class Ring:
    def __init__(self, tiles, name):
        self.tiles, self.name, self.i = tiles, name, -1

    def next(self):
        self.i += 1
        k = self.i % len(self.tiles)
        return self.tiles[k], (self.name, k)


def build_program(phases=(1, 2, 3, 4, 5, 6, 7, 8), debug=()):
    nc = bass.Bass("TRN2", target_bir_lowering=False)

    def din(name, shape, dt=F32):
        return nc.dram_tensor(name, list(shape), dt, kind="ExternalInput").ap()

    def dscr(name, shape, dt):
        kind = "ExternalOutput" if name in debug else "Internal"
        return nc.dram_tensor(name, list(shape), dt, kind=kind).ap()

    x = din("x", [S, D])
    w_dq = din("mla_w_dq", [D, 384])
    q_norm = din("mla_q_norm", [384])
    w_uq = din("mla_w_uq", [384, 1536])
    w_dkv = din("mla_w_dkv", [D, 320])
    kv_norm = din("mla_kv_norm", [256])
    w_ukv = din("mla_w_ukv", [256, 2048])
    w_o0 = din("mla_w_o", [D, D])
    kv_w = din("kv_w", [D, 2048])
    w_q1 = din("diff_w_q", [D, D])
    lq1 = din("diff_lq1", [1, 64])
    lk1 = din("diff_lk1", [1, 64])
    lq2 = din("diff_lq2", [1, 64])
    lk2 = din("diff_lk2", [1, 64])
    subln = din("diff_subln", [128])
    w_o1 = din("diff_w_o", [D, D])
    ln1_g = din("ln1_g", [2, D])
    ln1_b = din("ln1_b", [2, D])
    ln2_g = din("ln2_g", [2, D])
    ln2_b = din("ln2_b", [2, D])
    w_gu = din("ffn_w_gate_up", [2, D, 2 * DFF])
    w_dn = din("ffn_w_down", [2, DFF, D])
    ident_d = din("c_ident", [128, 128], BF16)
    mask_d = din("c_mask", [128, 4, GW], BF16)
    cosm_d = din("c_cosm", [64, S])
    sinm_d = din("c_sinm", [64, S])
    cosd_d = din("c_cosd", [128, S])
    sind_d = din("c_sind", [128, S])
    out = nc.dram_tensor("out", [S, D], F32, kind="ExternalOutput").ap()

    QN0 = dscr("QN0", [H, 128, S], BF16)
    QR0 = dscr("QR0", [H, 64, S], BF16)
    KN0 = dscr("KN0", [H, 128, S], BF16)
    KR0 = dscr("KR0", [64, S], BF16)
    V0 = dscr("V0", [S, D], BF16)
    OT0 = dscr("OT0", [H, 128, S], BF16)
    H1 = dscr("H1", [S, D], F32)
    H1T = dscr("H1T", [8, 128, S], BF16)
    H2 = dscr("H2", [S, D], F32)
    H2T = dscr("H2T", [8, 128, S], BF16)
    QT1 = dscr("QT1", [H, 128, S], BF16)
    KT1 = dscr("KT1", [H, 128, S], BF16)
    V1 = dscr("V1", [S, D], BF16)
    OT1 = dscr("OT1", [H, 128, S], BF16)
    H3 = dscr("H3", [S, D], F32)
    H3T = dscr("H3T", [8, 128, S], BF16)

    P = Prog(nc)
    uid = [0]

    def scope():
        st = ExitStack()

        def sb(shape, dt, name=None):
            uid[0] += 1
            return st.enter_context(nc.sbuf_tensor("%s_%d" % (name or "t", uid[0]), list(shape), dt))

        def ps(shape, dt, name=None):
            uid[0] += 1
            return st.enter_context(nc.psum_tensor("%s_%d" % (name or "p", uid[0]), list(shape), dt))

        return st, sb, ps

    def sbring(sb, n, shape, dt, name):
        return Ring([sb(shape, dt, name) for _ in range(n)], name + str(uid[0]))

    evac_rr = [0]

    def evac(out_ap, in_ap, reads, writes, scale=None, eng=None):
        if eng is None:
            evac_rr[0] += 1
            eng = "act" if evac_rr[0] % 2 else "dve"
        if eng == "act":
            if scale is None:
                P.op("act", I("activation", out=out_ap, in_=in_ap, func=AF.Copy), reads=reads, writes=writes)
            else:
                P.op("act", I("activation", out=out_ap, in_=in_ap, func=AF.Copy, scale=scale), reads=reads, writes=writes)
        else:
            if scale is None:
                P.op("dve", I("tensor_copy", out=out_ap, in_=in_ap), reads=reads, writes=writes)
            else:
                P.op("dve", I("tensor_scalar", out=out_ap, in0=in_ap, scalar1=scale, scalar2=None, op0=ALU.mult),
                     reads=reads, writes=writes)

    def load_w_bf16(dst_tile, src_ap, nchunk, res_prefix, split=1):
        n = src_ap.shape[-1]
        v = src_ap.rearrange("(c p) n -> p c n", p=128)
        step = n // split
        for s_ in range(split):
            lo, hi = s_ * step, (s_ + 1) * step
            P.op("pool", I("dma_start", out=dst_tile[:, :, lo:hi], in_=v[:, :, lo:hi]),
                 writes=[(res_prefix, s_)], slot=(res_prefix, s_))

    class LNPipe:
        def __init__(self, sb, ps, Gt, Bt, ident, mh, HTd, pt_ring):
            self.stt = sbring(sb, 2, [128, 2, 6], F32, "stt")
            self.mv = sbring(sb, 2, [128, 2], F32, "mv")
            self.ve = sbring(sb, 2, [128, 1], F32, "ve")
            self.rstd = sbring(sb, 3, [128, 1], F32, "rstd")
            self.nmr = sbring(sb, 3, [128, 1], F32, "nmr")
            self.hn = sbring(sb, 2, [128, D], F32, "hn")
            self.Gt, self.Bt, self.ident, self.mh, self.HTd, self.pt_ring = Gt, Bt, ident, mh, HTd, pt_ring
            if HTd is not None:
                self.hb = sbring(sb, 2, [128, D], BF16, "hb")
                self.hTt = sbring(sb, 2, [128, 8, 128], BF16, "hTt")
            self.st = {}
            self.n = 0

        def push(self, z, zres, dst_rows, t):
            k = self.n
            self.n += 1
            zr = list(zres)
            stt_t, stt_r = self.stt.next()
            mv_t, mv_r = self.mv.next()
            ve_t, ve_r = self.ve.next()
            rs_t, rs_r = self.rstd.next()
            nm_t, nm_r = self.nmr.next()
            self.st[k] = dict(z=z, zr=zr, dst=dst_rows, t=t)
            P.op("dve", I("bn_stats", out=stt_t[:, 0, :], in_=z[:, 0:512]), reads=zr, writes=[(stt_r, 0)])
            P.op("dve", I("bn_stats", out=stt_t[:, 1, :], in_=z[:, 512:1024]), reads=zr, writes=[(stt_r, 1)])
            P.op("dve", I("bn_aggr", out=mv_t[:], in_=stt_t[:].rearrange("p a b -> p (a b)")),
                 reads=[(stt_r, 0), (stt_r, 1)], writes=[mv_r])
            P.op("dve", I("tensor_scalar", out=ve_t[:], in0=mv_t[:, 1:2], scalar1=LN_EPS, scalar2=None, op0=ALU.add),
                 reads=[mv_r], writes=[ve_r])
            P.op("pool", I("tensor_tensor", out=rs_t[:], in0=ve_t[:], in1=self.mh[:], op=ALU.pow), reads=[ve_r, "mh"], writes=[rs_r])
            self._stage2(k - 1)
            self._stage3(k - 2)
            P.op("dve", I("tensor_scalar", out=nm_t[:], in0=mv_t[:, 0:1], scalar1=-1.0, scalar2=rs_t[:], op0=ALU.mult, op1=ALU.mult),
                 reads=[mv_r, rs_r], writes=[nm_r])
            P.op("act", I("activation", out=z[:], in_=z[:], func=AF.Identity, scale=rs_t[:], bias=nm_t[:]),
                 reads=zr + [rs_r, nm_r], writes=zr)

        def _stage2(self, k):
            if k < 0 or k not in self.st:
                return
            d = self.st[k]
            z, zr = d["z"], d["zr"]
            hn_t, hn_r = self.hn.next()
            P.op("dve", I("tensor_tensor", out=z[:], in0=z[:], in1=self.Gt[:], op=ALU.mult), reads=zr + ["lnG"], writes=zr)
            P.op("pool", I("tensor_tensor", out=hn_t[:], in0=z[:], in1=self.Bt[:], op=ALU.add), reads=zr + ["lnB"], writes=[hn_r])
            P.op("sp", I("dma_start", out=d["dst"], in_=hn_t[:]), reads=[hn_r], slot=hn_r)
            if self.HTd is not None:
                hb_t, hb_r = self.hb.next()
                P.op("act", I("activation", out=hb_t[:], in_=hn_t[:], func=AF.Copy), reads=[hn_r], writes=[hb_r])
                d["hb"] = (hb_t, hb_r)

        def _stage3(self, k):
            if k < 0 or k not in self.st:
                return
            d = self.st.pop(k)
            if self.HTd is None:
                return
            hb_t, hb_r = d["hb"]
            t = d["t"]
            pt_t, pt_r = self.pt_ring.next()
            for c in range(8):
                P.op("pe", I("transpose", out=pt_t[:, c, :], in_=hb_t[:, c * 128:(c + 1) * 128], identity=self.ident[:]),
                     reads=[hb_r, "ident"], writes=[pt_r])
            hT_t, hT_r = self.hTt.next()
            P.op("dve", I("tensor_copy", out=hT_t[:], in_=pt_t[:]), reads=[pt_r], writes=[hT_r])
            P.op("sp", I("dma_start", out=self.HTd.rearrange("c p t -> p c t")[:, :, t * 128:(t + 1) * 128], in_=hT_t[:]),
                 reads=[hT_r], slot=hT_r)

        def flush(self):
            self._stage2(self.n - 1)
            self._stage3(self.n - 2)
            self._stage3(self.n - 1)

    def load_consts(sb, need_ident=True, need_mask=False, need_ones=False, need_mh=False):
        r = {}
        if need_ident:
            r["ident"] = sb([128, 128], BF16, "ident")
            P.op("sp", I("dma_start", out=r["ident"][:], in_=ident_d[:, :]), writes=["ident"], slot="ident")
        if need_mask:
            r["mask"] = sb([128, 4, GW], BF16, "mask")
            P.op("sp", I("dma_start", out=r["mask"][:], in_=mask_d[:, :, :]), writes=["mask"], slot="mask")
        if need_ones:
            r["ones"] = sb([128, 128], BF16, "ones")
            P.op("pool", I("memset", r["ones"][:], 1.0), writes=["ones"])
        if need_mh:
            r["mh"] = sb([128, 1], F32, "mh")
            P.op("pool", I("memset", r["mh"][:], -0.5), writes=["mh"])
        return r

    def rms_fm(banks, bank_res, nchunk, ndim, gain_t, out_tile, out_res, sq_ring, ss_bank, ss_res, ln_ring, rstd_ring, ones):
        sqs = []
        for m in range(nchunk):
            sq_t, sq_r = sq_ring.next()
            P.op("act", I("activation", out=sq_t[:], in_=banks[m][:], func=AF.Square),
                 reads=[bank_res[m]], writes=[sq_r])
            sqs.append((sq_t, sq_r))
        for m in range(nchunk):
            P.op("pe", I("matmul", ss_bank[:], lhsT=ones[:], rhs=sqs[m][0][:], start=(m == 0), stop=(m == nchunk - 1)),
                 reads=["ones", sqs[m][1]], writes=[ss_res])
        ln_t, ln_r = ln_ring.next()
        rs_t, rs_r = rstd_ring.next()
        P.op("act", I("activation", out=ln_t[:], in_=ss_bank[:], func=AF.Ln, scale=1.0 / ndim, bias=RMS_EPS),
             reads=[ss_res], writes=[ln_r])
        P.op("act", I("activation", out=rs_t[:], in_=ln_t[:], func=AF.Exp, scale=-0.5), reads=[ln_r], writes=[rs_r])
        for m in range(nchunk):
            P.op("dve", I("scalar_tensor_tensor", out=out_tile[:, m, :], in0=banks[m][:], scalar=gain_t[:, m:m + 1],
                                                              in1=rs_t[:], op0=ALU.mult, op1=ALU.mult),
                 reads=[bank_res[m], rs_r, "gains"], writes=[(out_res, m)])

    def phase1():
        st, sb, ps = scope()
        with st:
            C = load_consts(sb, need_ident=True, need_ones=True)
            ident, ones = C["ident"], C["ones"]
            wdq = sb([128, 8, 384], BF16, "wdq")
            wdkv = sb([128, 8, 384], BF16, "wdkv")
            wuq = sb([128, 3, 1536], BF16, "wuq")
            wuqr = sb([128, 3, 8, 64], BF16, "wuqr")
            wukv = sb([128, 2, 2048], BF16, "wukv")
            gq = sb([128, 3], F32, "gq")
            gkv = sb([128, 2], F32, "gkv")
            load_w_bf16(wdq, w_dq, 8, "wdq")
            P.op("pool", I("dma_start", out=wdkv[:, :, 0:320], in_=w_dkv.rearrange("(c p) n -> p c n", p=128)),
                 writes=["wdkv"], slot="wdkv")
            load_w_bf16(wuq, w_uq, 3, "wuq")
            load_w_bf16(wukv, w_ukv, 2, "wukv")
            for m in range(3):
                P.op("sp", I("dma_start", out=gq[:, m:m + 1], in_=q_norm.rearrange("(c p o) -> c p o", p=128, o=1)[m]),
                     writes=["gains"], slot=("gq", m))
            for m in range(2):
                P.op("sp", I("dma_start", out=gkv[:, m:m + 1], in_=kv_norm.rearrange("(c p o) -> c p o", p=128, o=1)[m]),
                     writes=["gains"], slot=("gkv", m))
            P.op("pool", I("tensor_scalar", out=wdkv[:, :, 320:352], in0=wdkv[:, :, 288:320], scalar1=-1.0, scalar2=None, op0=ALU.mult),
                 reads=["wdkv"], writes=["wdkvr"])
            P.op("pool", I("tensor_copy", out=wdkv[:, :, 352:384], in_=wdkv[:, :, 256:288]), reads=["wdkv"], writes=["wdkvr2"])
            wuq4 = wuq[:].rearrange("p c (h d) -> p c h d", d=192)
            for c in range(3):
                P.op("pool", I("tensor_scalar", out=wuqr[:, c, :, 0:32], in0=wuq4[:, c, :, 160:192], scalar1=-1.0, scalar2=None,
                                                             op0=ALU.mult), reads=[("wuq", 0)], writes=[("wuqr", c, 0)])
                P.op("pool", I("tensor_copy", out=wuqr[:, c, :, 32:64], in_=wuq4[:, c, :, 128:160]),
                     reads=[("wuq", 0)], writes=[("wuqr", c, 1)])
            wuqr_res = [("wuqr", c, k) for c in range(3) for k in range(2)]
            wdkv_res = ["wdkv", "wdkvr", "wdkvr2"]

            xs_ring = sbring(sb, 2, [128, D], F32, "xs")
            xb_ring = sbring(sb, 2, [128, D], BF16, "xb")
            xT_ring = sbring(sb, 2, [128, 8, GW], BF16, "xT")
            pt_ring = Ring([ps([128, 8, 128], BF16, "pt")], "pt1")
            banks = [ps([128, GW], F32, "bk") for _ in range(7)]
            bres = [("bank1", i) for i in range(7)]
            sq_ring = sbring(sb, 3, [128, GW], BF16, "sq")
            ln_ring = sbring(sb, 1, [128, GW], F32, "lnss")
            rstd_ring = sbring(sb, 2, [128, GW], F32, "rstdfm")
            cqn_ring = sbring(sb, 2, [128, 3, GW], BF16, "cqn")
            cn_ring = sbring(sb, 2, [128, 2, GW], BF16, "cn")
            cos_ring = sbring(sb, 2, [64, GW], F32, "cosm")
            sin_ring = sbring(sb, 2, [64, GW], F32, "sinm")
            t1_ring = sbring(sb, 2, [64, GW], F32, "t1")
            t2_ring = sbring(sb, 2, [64, GW], F32, "t2")
            qnst_ring = sbring(sb, 2, [128, H, GW], BF16, "qnst")
            qrst_ring = sbring(sb, 2, [64, H, GW], BF16, "qrst")
            knst_ring = sbring(sb, 2, [128, H, GW], BF16, "knst")
            vst_ring = sbring(sb, 2, [128, 4, D], BF16, "vst")
            krst_ring = sbring(sb, 2, [64, GW], BF16, "krst")
            bi = [0]

            def nb():
                bi[0] += 1
                k = bi[0] % 7
                return banks[k], bres[k]

            for g in range(NG):
                gs = slice(g * GW, (g + 1) * GW)
                cos_t, cos_r = cos_ring.next()
                sin_t, sin_r = sin_ring.next()
                P.op("sp", I("dma_start", out=cos_t[:], in_=cosm_d[:, gs]), writes=[cos_r], slot=cos_r)
                P.op("sp", I("dma_start", out=sin_t[:], in_=sinm_d[:, gs]), writes=[sin_r], slot=sin_r)
                xT_t, xT_r = xT_ring.next()
                for i in range(4):
                    t = g * 4 + i
                    xs_t, xs_r = xs_ring.next()
                    xb_t, xb_r = xb_ring.next()
                    P.op("sp", I("dma_start", out=xs_t[:], in_=x[t * 128:(t + 1) * 128, :]), writes=[xs_r], slot=xs_r)
                    P.op("pool", I("tensor_copy", out=xb_t[:], in_=xs_t[:]), reads=[xs_r], writes=[xb_r])
                    pt_t, pt_r = pt_ring.next()
                    for c in range(8):
                        P.op("pe", I("transpose", out=pt_t[:, c, :], in_=xb_t[:, c * 128:(c + 1) * 128],
                                                                                  identity=ident[:]), reads=[xb_r, "ident"], writes=[pt_r])
                    evac(xT_t[:, :, i * 128:(i + 1) * 128], pt_t[:], [pt_r], [(xT_r, i)])
                xT_all = [(xT_r, i) for i in range(4)]
                cqb = [nb() for _ in range(3)]
                for m in range(3):
                    for c in range(8):
                        P.op("pe", I("matmul", cqb[m][0][:], lhsT=wdq[:, c, m * 128:(m + 1) * 128], rhs=xT_t[:, c, :],
                                                                             start=(c == 0), stop=(c == 7)),
                             reads=xT_all + [("wdq", 0)], writes=[cqb[m][1]])
                ssb, ssr = nb()
                cqn_t, cqn_r = cqn_ring.next()
                rms_fm([b[0] for b in cqb], [b[1] for b in cqb], 3, 384, gq, cqn_t, cqn_r, sq_ring, ssb, ssr, ln_ring, rstd_ring, ones)
                cqn_all = [(cqn_r, m) for m in range(3)]
                cb = [nb() for _ in range(2)]
                for m in range(2):
                    for c in range(8):
                        P.op("pe", I("matmul", cb[m][0][:], lhsT=wdkv[:, c, m * 128:(m + 1) * 128], rhs=xT_t[:, c, :],
                                                                            start=(c == 0), stop=(c == 7)),
                             reads=xT_all + wdkv_res, writes=[cb[m][1]])
                ssb, ssr = nb()
                cn_t, cn_r = cn_ring.next()
                rms_fm([b[0] for b in cb], [b[1] for b in cb], 2, 256, gkv, cn_t, cn_r, sq_ring, ssb, ssr, ln_ring, rstd_ring, ones)
                cn_all = [(cn_r, m) for m in range(2)]
                ab, ar = nb()
                bb, br = nb()
                for c in range(8):
                    P.op("pe", I("matmul", ab[0:64, :], lhsT=wdkv[:, c, 256:320], rhs=xT_t[:, c, :], start=(c == 0), stop=(c == 7)),
                         reads=xT_all + wdkv_res, writes=[ar])
                for c in range(8):
                    P.op("pe", I("matmul", bb[0:64, :], lhsT=wdkv[:, c, 320:384], rhs=xT_t[:, c, :], start=(c == 0), stop=(c == 7)),
                         reads=xT_all + wdkv_res, writes=[br])
                t1, t1r = t1_ring.next()
                t2, t2r = t2_ring.next()
                kr_t, kr_r = krst_ring.next()
                P.op("dve", I("tensor_tensor", out=t1[:], in0=ab[0:64, :], in1=cos_t[:], op=ALU.mult),
                     reads=[ar, cos_r], writes=[t1r])
                P.op("dve", I("tensor_tensor", out=t2[:], in0=bb[0:64, :], in1=sin_t[:], op=ALU.mult),
                     reads=[br, sin_r], writes=[t2r])
                P.op("pool", I("tensor_tensor", out=kr_t[:], in0=t1[:], in1=t2[:], op=ALU.add),
                     reads=[t1r, t2r], writes=[kr_r])
                P.op("sp", I("dma_start", out=KR0[:, gs], in_=kr_t[:]), reads=[kr_r], writes=[("KR0", g)], slot=kr_r)
                qn_t, qn_r = qnst_ring.next()
                qr_t, qr_r = qrst_ring.next()
                kn_t, kn_r = knst_ring.next()
                for h in range(H):
                    bk, bkr = nb()
                    for c in range(3):
                        P.op("pe", I("matmul", bk[:], lhsT=wuq[:, c, h * 192:h * 192 + 128], rhs=cqn_t[:, c, :],
                                                                       start=(c == 0), stop=(c == 2)), reads=cqn_all + [("wuq", 0)], writes=[bkr])
                    evac(qn_t[:, h, :], bk[:], [bkr], [(qn_r, h)], scale=SC0)
                    ab, ar = nb()
                    bb, br = nb()
                    for c in range(3):
                        P.op("pe", I("matmul", ab[0:64, :], lhsT=wuq[:, c, h * 192 + 128:h * 192 + 192], rhs=cqn_t[:, c, :],
                                                                       start=(c == 0), stop=(c == 2)), reads=cqn_all + [("wuq", 0)], writes=[ar])
                    for c in range(3):
                        P.op("pe", I("matmul", bb[0:64, :], lhsT=wuqr[:, c, h, :], rhs=cqn_t[:, c, :],
                                                                       start=(c == 0), stop=(c == 2)), reads=cqn_all + wuqr_res, writes=[br])
                    t1, t1r = t1_ring.next()
                    t2, t2r = t2_ring.next()
                    P.op("dve", I("scalar_tensor_tensor", out=t1[:], in0=ab[0:64, :], scalar=SC0, in1=cos_t[:],
                                                                                          op0=ALU.mult, op1=ALU.mult), reads=[ar, cos_r], writes=[t1r])
                    P.op("dve", I("scalar_tensor_tensor", out=t2[:], in0=bb[0:64, :], scalar=SC0, in1=sin_t[:],
                                                                                          op0=ALU.mult, op1=ALU.mult), reads=[br, sin_r], writes=[t2r])
                    P.op("pool", I("tensor_tensor", out=qr_t[:, h, :], in0=t1[:], in1=t2[:], op=ALU.add),
                         reads=[t1r, t2r], writes=[(qr_r, h)])
                    bk, bkr = nb()
                    for c in range(2):
                        P.op("pe", I("matmul", bk[:], lhsT=wukv[:, c, h * 256:h * 256 + 128], rhs=cn_t[:, c, :],
                                                                       start=(c == 0), stop=(c == 1)), reads=cn_all + [("wukv", 0)], writes=[bkr])
                    evac(kn_t[:, h, :], bk[:], [bkr], [(kn_r, h)])
                P.op("sp", I("dma_start", out=QN0.rearrange("h p t -> p h t")[:, :, gs], in_=qn_t[:]),
                     reads=[(qn_r, h) for h in range(H)], writes=[("QN0", g)], slot=qn_r)
                P.op("sp", I("dma_start", out=QR0.rearrange("h p t -> p h t")[:, :, gs], in_=qr_t[:]),
                     reads=[(qr_r, h) for h in range(H)], writes=[("QR0", g)], slot=qr_r)
                P.op("sp", I("dma_start", out=KN0.rearrange("h p t -> p h t")[:, :, gs], in_=kn_t[:]),
                     reads=[(kn_r, h) for h in range(H)], writes=[("KN0", g)], slot=kn_r)
                v_t, v_r = vst_ring.next()
                wv4 = wukv[:].rearrange("p c (h d) -> p c h d", d=256)
                for i in range(4):
                    for hh in range(2):
                        bk, bkr = nb()
                        for c in range(2):
                            P.op("pe", I("matmul", bk[:].rearrange("p (h d) -> p h d", d=128),
                                                                               lhsT=cn_t[:, c, i * 128:(i + 1) * 128],
                                                                               rhs=wv4[:, c, hh * 4:(hh + 1) * 4, 128:256], start=(c == 0), stop=(c == 1)),
                                 reads=cn_all + [("wukv", 0)], writes=[bkr])
                        evac(v_t[:, i, hh * 512:(hh + 1) * 512], bk[:], [bkr], [(v_r, i, hh)])
                P.op("sp", I("dma_start", out=V0[g * GW:(g + 1) * GW, :].rearrange("(i p) n -> p i n", p=128), in_=v_t[:]),
                     reads=[(v_r, i, hh) for i in range(4) for hh in range(2)], writes=[("V0", g)], slot=v_r)
            P.barrier()

    def attention(layer):
        st, sb, ps = scope()
        with st:
            nmap = 1 if layer == 0 else 2
            if layer == 0:
                S_ring = Ring([ps([128, 1, GW], F32, "S") for _ in range(3)], "S0")
                NSET = 2
            else:
                S_ring = Ring([ps([128, 2, GW], F32, "S") for _ in range(2)], "S1")
                NSET = 1
            nacc = 4 * nmap
            nbank = (nacc + 2) // 3
            O_sets = [[ps([128, GW], F32, "O") for _ in range(nbank)] for _ in range(NSET)]
            pt = ps([128, 8, 128], BF16, "ptA")
            AST = 130

            def acc_ap(set_i, a, lo=0, hi=129):
                return O_sets[set_i][a // 3][:, (a % 3) * AST + lo:(a % 3) * AST + hi]

            C = load_consts(sb, need_ident=True, need_mask=True)
            ident, mask = C["ident"], C["mask"]
            if layer == 0:
                qn_ring = sbring(sb, 2, [128, S], BF16, "qn")
                qr_ring = sbring(sb, 2, [64, S], BF16, "qr")
                kn_ring = sbring(sb, 2, [128, S], BF16, "kn")
                kr = sb([64, S], BF16, "kr")
                P.op("sp", I("dma_start", out=kr[:], in_=KR0[:, :]), writes=["kr"], slot="kr")
                Vd, OTd = V0, OT0
            else:
                qn_ring = sbring(sb, 2, [128, S], BF16, "q1")
                kn_ring = sbring(sb, 2, [128, S], BF16, "k1")
                Vd, OTd = V1, OT1
                mh = sb([128, 1], F32, "mh")
                P.op("pool", I("memset", mh[:], -0.5), writes=["mh"])
                lt = [sb([128, 64], F32, "lt") for _ in range(4)]
                for k_, src in enumerate((lq1, lk1, lq2, lk2)):
                    P.op("sp", I("dma_start", out=lt[k_][:], in_=src[0:1, :].partition_broadcast(128)),
                         writes=[("lt", k_)], slot=("lt", k_))
                pr = [sb([128, 64], F32, "pr") for _ in range(2)]
                sm = [sb([128, 1], F32, "lsm") for _ in range(2)]
                ex = [sb([128, 1], F32, "lex") for _ in range(2)]
                neglam = sb([128, 1], F32, "neglam")
                gvec = sb([128, 128], F32, "gvec")
                junk = sb([128, 64], F32, "junk")
                for k_ in range(2):
                    P.op("dve", I("tensor_tensor", out=pr[k_][:], in0=lt[2 * k_][:], in1=lt[2 * k_ + 1][:], op=ALU.mult),
                         reads=[("lt", 2 * k_), ("lt", 2 * k_ + 1)], writes=[("pr", k_)])
                    P.op("act", I("activation", out=junk[:], in_=pr[k_][:], func=AF.Copy, accum_out=sm[k_][:]),
                         reads=[("pr", k_)], writes=[("lsm", k_), "junk"])
                    P.op("act", I("activation", out=ex[k_][:], in_=sm[k_][:], func=AF.Exp), reads=[("lsm", k_)], writes=[("lex", k_)])
                P.op("dve", I("tensor_tensor", out=neglam[:], in0=ex[1][:], in1=ex[0][:], op=ALU.subtract),
                     reads=[("lex", 0), ("lex", 1)], writes=["neglam0"])
                P.op("dve", I("tensor_scalar", out=neglam[:], in0=neglam[:], scalar1=-LAMBDA_INIT, scalar2=None, op0=ALU.add),
                     reads=["neglam0"], writes=["neglam"])
                P.op("sp", I("dma_start", out=gvec[:], in_=subln.rearrange("(o n) -> o n", o=1).partition_broadcast(128)), writes=["gvec0"], slot="gvec")
                P.op("pool", I("tensor_scalar", out=gvec[:], in0=gvec[:], scalar1=1.0 - LAMBDA_INIT, scalar2=None, op0=ALU.mult),
                     reads=["gvec0"], writes=["gvec"])
                o1_ring = sbring(sb, 3, [128, 128], F32, "o1")
                o_ring = sbring(sb, 3, [128, 128], F32, "o")
                junk2 = sb([128, 128], F32, "junk2")
                ss_ring = sbring(sb, 3, [128, 1], F32, "ss")
                rstd_ring = sbring(sb, 3, [128, 1], F32, "rstdA")
                c2_ring = sbring(sb, 3, [128, 1], F32, "c2")
            v_ring = sbring(sb, 2, [128, NT, AST], BF16, "v")
            for k_ in range(2):
                P.op("pool", I("memset", v_ring.tiles[k_][:, :, 128:130], 1.0), writes=[(("vones", k_))])
            pT_ring = sbring(sb, 4, [128, nmap, GW], BF16, "pT")
            rinv_ring = sbring(sb, 4 * nmap, [128, 1], F32, "rinv")
            ob_ring = sbring(sb, 3, [128, 128], BF16, "ob")
            ost_ring = sbring(sb, 2, [128, GW], BF16, "ost")
            LOOK = 2 if layer == 0 else 1
            heads = {}

            def load_head(h):
                if h >= H:
                    return
                d = {}
                d["qn"] = qn_ring.next()
                d["kn"] = kn_ring.next()
                d["v"] = v_ring.next()
                if layer == 0:
                    d["qr"] = qr_ring.next()
                    P.op("sp", I("dma_start", out=d["qn"][0][:], in_=QN0[h]), writes=[d["qn"][1]], slot=d["qn"][1])
                    P.op("sp", I("dma_start", out=d["qr"][0][:], in_=QR0[h]), writes=[d["qr"][1]], slot=d["qr"][1])
                    P.op("sp", I("dma_start", out=d["kn"][0][:], in_=KN0[h]), writes=[d["kn"][1]], slot=d["kn"][1])
                else:
                    P.op("sp", I("dma_start", out=d["qn"][0][:], in_=QT1[h]), writes=[d["qn"][1]], slot=d["qn"][1])
                    P.op("sp", I("dma_start", out=d["kn"][0][:], in_=KT1[h]), writes=[d["kn"][1]], slot=d["kn"][1])
                vk = v_ring.i % 2
                P.op("sp", I("dma_start", out=d["v"][0][:, :, 0:128], in_=Vd.rearrange("(t p) (h d) -> h p t d", p=128, d=128)[h]),
                     reads=[("vones", vk)], writes=[d["v"][1]], slot=d["v"][1])
                heads[h] = d

            set_ctr = [0]
            load_head(0)
            for h in range(H):
                load_head(h + 1)
                hd = heads.pop(h)
                qn_t, qn_r = hd["qn"]
                kn_t, kn_r = hd["kn"]
                v_t, v_r = hd["v"]
                if layer == 0:
                    qr_t, qr_r = hd["qr"]
                pairs = [(g, kt) for g in range(NG) for kt in range(4 * g + 4)]
                state = {}
                gstate = {}
                pending = []

                def emit_qk(n):
                    g, kt = pairs[n]
                    j = max(kt - 4 * g, 0)
                    q0 = g * GW + 128 * j
                    q1 = (g + 1) * GW
                    ks = slice(kt * 128, (kt + 1) * 128)
                    diag = kt - 4 * g >= 0
                    S_t, S_r = S_ring.next()
                    for m in range(nmap):
                        so = S_t[:, m, 128 * j:GW]
                        if layer == 0:
                            P.op("pe", I("matmul", so, lhsT=kn_t[:, ks], rhs=qn_t[:, q0:q1], start=True, stop=False),
                                 reads=[kn_r, qn_r], writes=[(S_r, m)])
                            P.op("pe", I("matmul", so, lhsT=kr[:, ks], rhs=qr_t[:, q0:q1], start=False, stop=not diag),
                                 reads=["kr", qr_r], writes=[(S_r, m)])
                        else:
                            lo, hi = m * 64, (m + 1) * 64
                            P.op("pe", I("matmul", so, lhsT=kn_t[lo:hi, ks], rhs=qn_t[lo:hi, q0:q1], start=True, stop=not diag),
                                 reads=[kn_r, qn_r], writes=[(S_r, m)])
                        if diag:
                            P.op("pe", I("matmul", S_t[:, m, 128 * j:128 * j + 128], lhsT=ident[:], rhs=mask[:, 0, 0:128], start=False, stop=True),
                                 reads=["ident", "mask"], writes=[(S_r, m)])
                    state[n] = (S_t, S_r)

                def emit_rest(n):
                    g, kt = pairs[n]
                    j = max(kt - 4 * g, 0)
                    S_t, S_r = state.pop(n)
                    if kt == 0:
                        set_ctr[0] += 1
                        gstate[g] = dict(set=set_ctr[0] % NSET, started=set())
                    gs_ = gstate[g]
                    si = gs_["set"]
                    pT_t, pT_r = pT_ring.next()
                    P.op("act", I("activation", out=pT_t[:, :, 128 * j:GW], in_=S_t[:, :, 128 * j:GW], func=AF.Exp),
                         reads=[(S_r, m) for m in range(nmap)], writes=[pT_r])
                    if kt == 0 and pending:
                        P.op("pe", I("transpose", out=pt[:, 7, :], in_=ident[:], identity=ident[:]), reads=["ident"], writes=[("ptA", 7), "pe_tick"])
                        flush_pending()
                    for m in range(nmap):
                        for i in range(j, 4):
                            a = m * 4 + i
                            bank = a // 3
                            first = bank not in gs_["started"]
                            gs_["started"].add(bank)
                            P.op("pe", I("matmul", acc_ap(si, a), lhsT=pT_t[:, m, i * 128:(i + 1) * 128], rhs=v_t[:, kt, 0:129],
                                         start=first, stop=(kt == 4 * g + i), skip_group_check=True),
                                 reads=[pT_r, v_r], writes=[("acc", si, a), "pe_tick"])
                    flush_pending()
                    if kt >= 4 * g:
                        pending.append((g, kt - 4 * g, si))

                def flush_pending():
                    while pending:
                        finish(*pending.pop(0))

                def finish(g, i, si):
                    if i == 0:
                        gstate[g]["ost"] = ost_ring.next()
                    ost_t, ost_r = gstate[g]["ost"]
                    ob_t, ob_r = ob_ring.next()
                    if layer == 0:
                        ri_t, ri_r = rinv_ring.next()
                        P.op("dve", I("reciprocal", out=ri_t[:], in_=acc_ap(si, i, 128, 129)), reads=[("acc", si, i), "pe_tick"], writes=[ri_r])
                        P.op("dve", I("tensor_scalar", out=ob_t[:], in0=acc_ap(si, i, 0, 128), scalar1=ri_t[:], scalar2=None, op0=ALU.mult),
                             reads=[("acc", si, i), ri_r], writes=[ob_r])
                    else:
                        ri1, ri1r = rinv_ring.next()
                        ri2, ri2r = rinv_ring.next()
                        c2_t, c2_r = c2_ring.next()
                        o1_t, o1_r = o1_ring.next()
                        o_t, o_r = o_ring.next()
                        ss_t, ss_r = ss_ring.next()
                        rs_t, rs_r = rstd_ring.next()
                        P.op("dve", I("reciprocal", out=ri1[:], in_=acc_ap(si, i, 128, 129)), reads=[("acc", si, i), "pe_tick"], writes=[ri1r])
                        P.op("dve", I("reciprocal", out=ri2[:], in_=acc_ap(si, 4 + i, 128, 129)), reads=[("acc", si, 4 + i)], writes=[ri2r])
                        P.op("dve", I("tensor_tensor", out=c2_t[:], in0=ri2[:], in1=neglam[:], op=ALU.mult), reads=[ri2r, "neglam"], writes=[c2_r])
                        P.op("dve", I("tensor_scalar", out=o1_t[:], in0=acc_ap(si, i, 0, 128), scalar1=ri1[:], scalar2=None, op0=ALU.mult),
                             reads=[("acc", si, i), ri1r], writes=[o1_r])
                        P.op("dve", I("scalar_tensor_tensor", out=o_t[:], in0=acc_ap(si, 4 + i, 0, 128), scalar=c2_t[:], in1=o1_t[:],
                                      op0=ALU.mult, op1=ALU.add), reads=[("acc", si, 4 + i), c2_r, o1_r], writes=[o_r])
                        P.op("dve", I("tensor_tensor", out=junk2[:], in0=o_t[:], in1=o_t[:], op=ALU.mult), reads=[o_r], writes=["junk2"])
                        P.op("dve", I("tensor_reduce", out=ss_t[:], in_=junk2[:], axis=mybir.AxisListType.X, op=ALU.add), reads=["junk2"], writes=[ss_r])
                        P.op("dve", I("tensor_scalar", out=ss_t[:], in0=ss_t[:], scalar1=1.0 / 128, scalar2=RMS_EPS, op0=ALU.mult, op1=ALU.add),
                             reads=[ss_r], writes=[ss_r])
                        P.op("act", I("activation", out=ss_t[:], in_=ss_t[:], func=AF.Ln), reads=[ss_r], writes=[ss_r])
                        P.op("act", I("activation", out=rs_t[:], in_=ss_t[:], func=AF.Exp, scale=-0.5), reads=[ss_r], writes=[rs_r])
                        P.op("dve", I("scalar_tensor_tensor", out=ob_t[:], in0=o_t[:], scalar=rs_t[:], in1=gvec[:], op0=ALU.mult, op1=ALU.mult),
                             reads=[o_r, rs_r, "gvec"], writes=[ob_r])
                    P.op("pe", I("transpose", out=pt[:, i, :], in_=ob_t[:], identity=ident[:]), reads=[ob_r, "ident"], writes=[("ptA", i)])
                    evac(ost_t[:, i * 128:(i + 1) * 128], pt[:, i, :], [("ptA", i)], [(ost_r, i)], eng="dve")
                    if i == 3:
                        P.op("sp", I("dma_start", out=OTd[h][:, g * GW:(g + 1) * GW], in_=ost_t[:]), reads=[(ost_r, k_) for k_ in range(4)], slot=ost_r)

                N = len(pairs)
                for n in range(min(LOOK, N)):
                    emit_qk(n)
                for n in range(N):
                    if n + LOOK < N:
                        emit_qk(n + LOOK)
                    emit_rest(n)
                P.op("pe", I("transpose", out=pt[:, 7, :], in_=ident[:], identity=ident[:]), reads=["ident"], writes=[("ptA", 7), "pe_tick"])
                flush_pending()
            P.barrier()

    def attention_old(layer):
        st, sb, ps = scope()
        with st:
            C = load_consts(sb, need_ident=True, need_mask=True, need_ones=True)
            ident, mask, ones = C["ident"], C["mask"], C["ones"]
            nmap = 1 if layer == 0 else 2
            if layer == 0:
                qn_ring = sbring(sb, 2, [128, S], BF16, "qn")
                qr_ring = sbring(sb, 2, [64, S], BF16, "qr")
                kn_ring = sbring(sb, 2, [128, S], BF16, "kn")
                kr = sb([64, S], BF16, "kr")
                P.op("sp", I("dma_start", out=kr[:], in_=KR0[:, :]), writes=["kr"], slot="kr")
                Vd, OTd = V0, OT0
            else:
                qn_ring = sbring(sb, 2, [128, S], BF16, "q1")
                kn_ring = sbring(sb, 2, [128, S], BF16, "k1")
                Vd, OTd = V1, OT1
                lt = [sb([128, 64], F32, "lt") for _ in range(4)]
                for k_, src in enumerate((lq1, lk1, lq2, lk2)):
                    P.op("sp", I("dma_start", out=lt[k_][:], in_=src[0:1, :].partition_broadcast(128)),
                         writes=[("lt", k_)], slot=("lt", k_))
                pr = [sb([128, 64], F32, "pr") for _ in range(2)]
                sm = [sb([128, 1], F32, "lsm") for _ in range(2)]
                ex = [sb([128, 1], F32, "lex") for _ in range(2)]
                neglam = sb([128, 1], F32, "neglam")
                gsub = sb([128, 1], F32, "gsub")
                junk = sb([128, 64], F32, "junk")
                for k_ in range(2):
                    P.op("dve", I("tensor_tensor", out=pr[k_][:], in0=lt[2 * k_][:], in1=lt[2 * k_ + 1][:], op=ALU.mult),
                         reads=[("lt", 2 * k_), ("lt", 2 * k_ + 1)], writes=[("pr", k_)])
                    P.op("act", I("activation", out=junk[:], in_=pr[k_][:], func=AF.Copy, accum_out=sm[k_][:]),
                         reads=[("pr", k_)], writes=[("lsm", k_), "junk"])
                    P.op("act", I("activation", out=ex[k_][:], in_=sm[k_][:], func=AF.Exp), reads=[("lsm", k_)], writes=[("lex", k_)])
                P.op("dve", I("tensor_tensor", out=neglam[:], in0=ex[1][:], in1=ex[0][:], op=ALU.subtract),
                     reads=[("lex", 0), ("lex", 1)], writes=["neglam0"])
                P.op("dve", I("tensor_scalar", out=neglam[:], in0=neglam[:], scalar1=-LAMBDA_INIT, scalar2=None, op0=ALU.add),
                     reads=["neglam0"], writes=["neglam"])
                P.op("sp", I("dma_start", out=gsub[:], in_=subln.rearrange("(p o) -> p o", o=1)), writes=["gsub0"], slot="gsub")
                P.op("pool", I("tensor_scalar", out=gsub[:], in0=gsub[:], scalar1=1.0 - LAMBDA_INIT, scalar2=None, op0=ALU.mult),
                     reads=["gsub0"], writes=["gsub"])
            v_ring = sbring(sb, 2, [128, NT, 128], BF16, "v")
            NS = 4 if layer == 0 else 2
            pT_rings = [sbring(sb, 4, [128, GW], BF16, "pT%d" % m) for m in range(nmap)]
            S_rings = [Ring([ps([128, GW], F32, "S") for _ in range(NS)], "S%d_%d" % (layer, m)) for m in range(nmap)]
            NO = 2 if layer == 0 else 1
            O_rings = [Ring([ps([128, GW], F32, "O") for _ in range(NO)], "O%d_%d" % (layer, m)) for m in range(nmap)]
            M_rings = [Ring([ps([128, GW], F32, "M") for _ in range(NO)], "M%d_%d" % (layer, m)) for m in range(nmap)]
            rs_ring = sbring(sb, 2 * nmap, [128, GW], F32, "rs")
            ost_ring = sbring(sb, 2, [128, GW], BF16, "ost")
            if layer == 1:
                tt_ring = sbring(sb, 4, [128, GW], F32, "tt")
                o_ring = sbring(sb, 2, [128, GW], F32, "o")
                sq_ring = sbring(sb, 2, [128, GW], BF16, "sq5")
                ln_ring = sbring(sb, 2, [128, GW], F32, "ln5")
            LOOK = 2 if layer == 0 else 1

            for h in range(H):
                qn_t, qn_r = qn_ring.next()
                kn_t, kn_r = kn_ring.next()
                v_t, v_r = v_ring.next()
                if layer == 0:
                    qr_t, qr_r = qr_ring.next()
                    P.op("sp", I("dma_start", out=qn_t[:], in_=QN0[h]), reads=[("QN0", g) for g in range(NG)], writes=[qn_r], slot=qn_r)
                    P.op("sp", I("dma_start", out=qr_t[:], in_=QR0[h]), reads=[("QR0", g) for g in range(NG)], writes=[qr_r], slot=qr_r)
                    P.op("sp", I("dma_start", out=kn_t[:], in_=KN0[h]), reads=[("KN0", g) for g in range(NG)], writes=[kn_r], slot=kn_r)
                else:
                    P.op("sp", I("dma_start", out=qn_t[:], in_=QT1[h]), writes=[qn_r], slot=qn_r)
                    P.op("sp", I("dma_start", out=kn_t[:], in_=KT1[h]), writes=[kn_r], slot=kn_r)
                P.op("sp", I("dma_start", out=v_t[:], in_=Vd.rearrange("(t p) (h d) -> h p t d", p=128, d=128)[h]),
                     writes=[v_r], slot=v_r)
                pairs = [(g, kt) for g in range(NG) for kt in range(4 * g + 4)]
                state = {}

                def emit_qk(n):
                    g, kt = pairs[n]
                    gs = slice(g * GW, (g + 1) * GW)
                    ks = slice(kt * 128, (kt + 1) * 128)
                    j = kt - 4 * g
                    Sb = []
                    for m in range(nmap):
                        S_t, S_r = S_rings[m].next()
                        if layer == 0:
                            P.op("pe", I("matmul", S_t[:], lhsT=kn_t[:, ks], rhs=qn_t[:, gs], start=True, stop=False),
                                 reads=[kn_r, qn_r], writes=[S_r])
                            P.op("pe", I("matmul", S_t[:], lhsT=kr[:, ks], rhs=qr_t[:, gs], start=False, stop=(j < 0)),
                                 reads=["kr", qr_r], writes=[S_r])
                        else:
                            lo, hi = m * 64, (m + 1) * 64
                            P.op("pe", I("matmul", S_t[:], lhsT=kn_t[lo:hi, ks], rhs=qn_t[lo:hi, gs], start=True, stop=(j < 0)),
                                 reads=[kn_r, qn_r], writes=[S_r])
                        if j >= 0:
                            P.op("pe", I("matmul", S_t[:], lhsT=ident[:], rhs=mask[:, j, :], start=False, stop=True),
                                 reads=["ident", "mask"], writes=[S_r])
                        Sb.append((S_t, S_r))
                    state[n] = Sb

                def emit_rest(n):
                    g, kt = pairs[n]
                    last = 4 * g + 3
                    Sb = state.pop(n)
                    if kt == 0:
                        state["O"] = [O_rings[m].next() for m in range(nmap)]
                        state["M"] = [M_rings[m].next() for m in range(nmap)]
                    for m in range(nmap):
                        S_t, S_r = Sb[m]
                        pT_t, pT_r = pT_rings[m].next()
                        P.op("act", I("activation", out=pT_t[:], in_=S_t[:], func=AF.Exp), reads=[S_r], writes=[pT_r])
                        O_t, O_r = state["O"][m]
                        M_t, M_r = state["M"][m]
                        P.op("pe", I("matmul", O_t[:], lhsT=v_t[:, kt, :], rhs=pT_t[:], start=(kt == 0), stop=(kt == last)),
                             reads=[v_r, pT_r], writes=[O_r])
                        P.op("pe", I("matmul", M_t[:], lhsT=ones[:], rhs=pT_t[:], start=(kt == 0), stop=(kt == last)),
                             reads=["ones", pT_r], writes=[M_r])
                    if kt == last:
                        finish(g)

                def finish(g):
                    gs = slice(g * GW, (g + 1) * GW)
                    ost_t, ost_r = ost_ring.next()
                    if layer == 0:
                        O_t, O_r = state["O"][0]
                        M_t, M_r = state["M"][0]
                        rs_t, rs_r = rs_ring.next()
                        P.op("dve", I("reciprocal", out=rs_t[:], in_=M_t[:]), reads=[M_r], writes=[rs_r])
                        P.op("dve", I("tensor_tensor", out=ost_t[:], in0=O_t[:], in1=rs_t[:], op=ALU.mult), reads=[O_r, rs_r], writes=[ost_r])
                    else:
                        tts = []
                        for m in range(2):
                            O_t, O_r = state["O"][m]
                            M_t, M_r = state["M"][m]
                            rs_t, rs_r = rs_ring.next()
                            tt_t, tt_r = tt_ring.next()
                            P.op("dve", I("reciprocal", out=rs_t[:], in_=M_t[:]), reads=[M_r], writes=[rs_r])
                            P.op("dve", I("tensor_tensor", out=tt_t[:], in0=O_t[:], in1=rs_t[:], op=ALU.mult),
                                 reads=[O_r, rs_r], writes=[tt_r])
                            tts.append((tt_t, tt_r))
                        o_t, o_r = o_ring.next()
                        P.op("dve", I("scalar_tensor_tensor", out=o_t[:], in0=tts[1][0][:], scalar=neglam[:], in1=tts[0][0][:],
                                                                     op0=ALU.mult, op1=ALU.add), reads=[tts[0][1], tts[1][1], "neglam"], writes=[o_r])
                        sq_t, sq_r = sq_ring.next()
                        P.op("pool", I("tensor_tensor", out=sq_t[:], in0=o_t[:], in1=o_t[:], op=ALU.mult), reads=[o_r], writes=[sq_r])
                        M_t, M_r = state["M"][0]
                        P.op("pe", I("matmul", M_t[:], lhsT=ones[:], rhs=sq_t[:], start=True, stop=True), reads=["ones", sq_r], writes=[M_r])
                        ln_t, ln_r = ln_ring.next()
                        rs_t, rs_r = rs_ring.next()
                        P.op("act", I("activation", out=ln_t[:], in_=M_t[:], func=AF.Ln, scale=1.0 / 128, bias=RMS_EPS), reads=[M_r], writes=[ln_r])
                        P.op("act", I("activation", out=rs_t[:], in_=ln_t[:], func=AF.Exp, scale=-0.5), reads=[ln_r], writes=[rs_r])
                        P.op("dve", I("scalar_tensor_tensor", out=ost_t[:], in0=o_t[:], scalar=gsub[:], in1=rs_t[:], op0=ALU.mult, op1=ALU.mult),
                             reads=[o_r, rs_r, "gsub"], writes=[ost_r])
                    P.op("sp", I("dma_start", out=OTd[h][:, gs], in_=ost_t[:]), reads=[ost_r], writes=[("OT", h, g)], slot=ost_r)

                N = len(pairs)
                for n in range(min(LOOK, N)):
                    emit_qk(n)
                for n in range(N):
                    if n + LOOK < N:
                        emit_qk(n + LOOK)
                    emit_rest(n)
            P.barrier()

    def outproj_ln(layer):
        st, sb, ps = scope()
        OTd = OT0 if layer == 0 else OT1
        wo_d = w_o0 if layer == 0 else w_o1
        res_d = x if layer == 0 else H2
        Hd, HTd = (H1, H1T) if layer == 0 else (H3, H3T)
        with st:
            C = load_consts(sb, need_ident=True, need_mh=True)
            ident, mh = C["ident"], C["mh"]
            wo = sb([128, 8, D], BF16, "wo")
            load_w_bf16(wo, wo_d, 8, "wo")
            Gt = sb([128, D], F32, "lnG")
            Bt = sb([128, D], F32, "lnB")
            P.op("sp", I("dma_start", out=Gt[:], in_=ln1_g[layer:layer + 1, :].partition_broadcast(128)), writes=["lnG"], slot="lnG")
            P.op("sp", I("dma_start", out=Bt[:], in_=ln1_b[layer:layer + 1, :].partition_broadcast(128)), writes=["lnB"], slot="lnB")
            ot_ring = sbring(sb, 2, [128, H, GW], BF16, "otg")
            res_ring = sbring(sb, 4, [128, D], F32, "res")
            pt_ring = Ring([ps([128, 8, 128], BF16, "pt") for _ in range(2)], "pt3")
            a_ring = Ring([ps([128, GW], F32, "a") for _ in range(6)], "a3")
            lnp = LNPipe(sb, ps, Gt, Bt, ident, mh, HTd, pt_ring)
            res_q = {}
            ot_q = {}

            def load_res(t):
                if t < NT:
                    res_t, res_r = res_ring.next()
                    P.op("sp", I("dma_start", out=res_t[:], in_=res_d[t * 128:(t + 1) * 128, :]), writes=[(res_r, 0), (res_r, 1)], slot=res_r)
                    res_q[t] = (res_t, res_r)

            def load_ot(g):
                if g < NG:
                    ot_t, ot_r = ot_ring.next()
                    P.op("sp", I("dma_start", out=ot_t[:], in_=OTd.rearrange("h p t -> p h t")[:, :, g * GW:(g + 1) * GW]), writes=[ot_r], slot=ot_r)
                    ot_q[g] = (ot_t, ot_r)

            load_ot(0)
            load_res(0)
            load_res(1)
            for g in range(NG):
                ot_t, ot_r = ot_q.pop(g)
                load_ot(g + 1)
                for i in range(4):
                    t = g * 4 + i
                    load_res(t + 2)
                    res_t, res_r = res_q.pop(t)
                    z_t, z_r = res_t, res_r
                    for hh in range(2):
                        a_t, a_r = a_ring.next()
                        for h in range(H):
                            P.op("pe", I("matmul", a_t[:], lhsT=ot_t[:, h, i * 128:(i + 1) * 128],
                                                                                           rhs=wo[:, h, hh * 512:(hh + 1) * 512], start=(h == 0), stop=(h == H - 1)),
                                 reads=[ot_r, ("wo", 0)], writes=[a_r])
                        P.op("dve", I("scalar_tensor_tensor",
                            out=z_t[:, hh * 512:(hh + 1) * 512], in0=res_t[:, hh * 512:(hh + 1) * 512], scalar=ALPHA, in1=a_t[:], op0=ALU.mult, op1=ALU.add),
                            reads=[(res_r, hh), a_r], writes=[(z_r, hh)])
                    lnp.push(z_t, [(z_r, 0), (z_r, 1)], Hd[t * 128:(t + 1) * 128, :], t)
            lnp.flush()
            P.barrier()

    def ffn_ln(layer):
        st, sb, ps = scope()
        HTin = H1T if layer == 0 else H3T
        Hin = H1 if layer == 0 else H3
        Hd, HTd = (H2, H2T) if layer == 0 else (out, None)
        with st:
            C = load_consts(sb, need_ident=True, need_mh=True)
            ident, mh = C["ident"], C["mh"]
            wgu = sb([128, 8, 2 * DFF], BF16, "wgu")
            wdn = sb([128, NF, D], BF16, "wdn")
            NSPL = 11
            v = w_gu[layer].rearrange("(c p) n -> p c n", p=128)
            for s_ in range(NSPL):
                for half in range(2):
                    lo = half * DFF + s_ * 256
                    P.op("pool", I("dma_start", out=wgu[:, :, lo:lo + 256], in_=v[:, :, lo:lo + 256]),
                         writes=[("wgu", half, s_)], slot=("wgu", half, s_))
            vd = w_dn[layer].rearrange("(f p) n -> p f n", p=128)
            for s_ in range(2):
                P.op("pool", I("dma_start", out=wdn[:, s_ * 11:(s_ + 1) * 11, :], in_=vd[:, s_ * 11:(s_ + 1) * 11, :]),
                     writes=[("wdn", s_)], slot=("wdn", s_))
            Gt = sb([128, D], F32, "lnG")
            Bt = sb([128, D], F32, "lnB")
            P.op("sp", I("dma_start", out=Gt[:], in_=ln2_g[layer:layer + 1, :].partition_broadcast(128)), writes=["lnG"], slot="lnG")
            P.op("sp", I("dma_start", out=Bt[:], in_=ln2_b[layer:layer + 1, :].partition_broadcast(128)), writes=["lnB"], slot="lnB")
            hin_ring = sbring(sb, 1, [128, 8, GW], BF16, "hin")
            actT = sb([128, NF, GW], BF16, "actT")
            sg_ring = sbring(sb, 2, [128, GW], F32, "sg")
            res_ring = sbring(sb, 3, [128, D], F32, "res")
            pt_ring = Ring([ps([128, 8, 128], BF16, "pt")], "pt4")
            lnp = LNPipe(sb, ps, Gt, Bt, ident, mh, HTd, pt_ring)
            g_ring = Ring([ps([128, GW], F32, "gb") for _ in range(2)], "gbk")
            u_ring = Ring([ps([128, GW], F32, "ub") for _ in range(2)], "ubk")
            d_ring = Ring([ps([128, GW], F32, "db") for _ in range(3)], "dbk")
            res_q = {}

            def load_res(t):
                res_t, res_r = res_ring.next()
                P.op("sp", I("dma_start", out=res_t[:], in_=Hin[t * 128:(t + 1) * 128, :]), writes=[(res_r, 0), (res_r, 1)], slot=res_r)
                res_q[t] = (res_t, res_r)

            for g in range(NG):
                gs = slice(g * GW, (g + 1) * GW)
                hin_t, hin_r = hin_ring.next()
                P.op("sp", I("dma_start", out=hin_t[:], in_=HTin.rearrange("c p t -> p c t")[:, :, gs]), writes=[hin_r], slot=hin_r)
                for f in range(NF):
                    gb, gr = g_ring.next()
                    ub, ur = u_ring.next()
                    wres = [("wgu", 0, f // 2), ("wgu", 1, f // 2)]
                    for c in range(8):
                        P.op("pe", I("matmul", gb[:], lhsT=wgu[:, c, f * 128:(f + 1) * 128], rhs=hin_t[:, c, :],
                                                                                start=(c == 0), stop=(c == 7)), reads=[hin_r, wres[0]], writes=[gr])
                    for c in range(8):
                        P.op("pe", I("matmul", ub[:], lhsT=wgu[:, c, DFF + f * 128:DFF + (f + 1) * 128], rhs=hin_t[:, c, :],
                                                                                start=(c == 0), stop=(c == 7)), reads=[hin_r, wres[1]], writes=[ur])
                    sg_t, sg_r = sg_ring.next()
                    P.op("act", I("activation", out=sg_t[:], in_=gb[:], func=AF.Silu), reads=[gr], writes=[sg_r])
                    P.op("dve", I("tensor_tensor", out=actT[:, f, :], in0=ub[:], in1=sg_t[:], op=ALU.mult),
                         reads=[ur, sg_r], writes=[("actT", f)])
                for i in range(4):
                    t = g * 4 + i
                    if i == 0:
                        load_res(t)
                    if i < 3:
                        load_res(t + 1)
                    res_t, res_r = res_q.pop(t)
                    z_t, z_r = res_t, res_r
                    for hh in range(2):
                        db, dr = d_ring.next()
                        for f in range(NF):
                            P.op("pe", I("matmul", db[:], lhsT=actT[:, f, i * 128:(i + 1) * 128],
                                                                              rhs=wdn[:, f, hh * 512:(hh + 1) * 512], start=(f == 0), stop=(f == NF - 1)),
                                 reads=[("actT", f), ("wdn", f // 11)], writes=[dr])
                        P.op("dve", I("scalar_tensor_tensor",
                            out=z_t[:, hh * 512:(hh + 1) * 512], in0=res_t[:, hh * 512:(hh + 1) * 512], scalar=ALPHA, in1=db[:], op0=ALU.mult, op1=ALU.add),
                            reads=[(res_r, hh), dr], writes=[(z_r, hh)])
                    lnp.push(z_t, [(z_r, 0), (z_r, 1)], Hd[t * 128:(t + 1) * 128, :], t)
            lnp.flush()
            P.barrier()

    def phase_proj1():
        st, sb, ps = scope()
        with st:
            wk = sb([128, 8, D], BF16, "wk")
            wkr = sb([128, 8, D], BF16, "wkr")
            wq = sb([128, 8, D], BF16, "wq")
            wqr = sb([128, 8, D], BF16, "wqr")
            wv = sb([128, 8, D], BF16, "wv")
            kvv = kv_w.rearrange("(c p) n -> p c n", p=128)
            P.op("pool", I("dma_start", out=wk[:], in_=kvv[:, :, 0:D]), writes=["wk"], slot="wk")
            P.op("pool", I("dma_start", out=wq[:], in_=w_q1.rearrange("(c p) n -> p c n", p=128)), writes=["wq"], slot="wq")
            P.op("pool", I("dma_start", out=wv[:], in_=kvv[:, :, D:2 * D]), writes=["wv"], slot="wv")
            for (src, dst, nm) in ((wk, wkr, "wk"), (wq, wqr, "wq")):
                s4 = src[:].rearrange("p c (b d) -> p c b d", d=64)
                d4 = dst[:].rearrange("p c (b d) -> p c b d", d=64)
                for c in range(8):
                    P.op("pool", I("tensor_scalar", out=d4[:, c, :, 0:32], in0=s4[:, c, :, 32:64], scalar1=-1.0, scalar2=None, op0=ALU.mult),
                         reads=[nm], writes=[(nm + "r", c, 0)])
                    P.op("pool", I("tensor_copy", out=d4[:, c, :, 32:64], in_=s4[:, c, :, 0:32]),
                         reads=[nm], writes=[(nm + "r", c, 1)])
            rres = {nm: [(nm + "r", c, k) for c in range(8) for k in range(2)] for nm in ("wk", "wq")}
            hin_ring = sbring(sb, 2, [128, 8, GW], BF16, "hin")
            cos_ring = sbring(sb, 2, [128, GW], F32, "cosd")
            sin_ring = sbring(sb, 2, [128, GW], F32, "sind")
            t1_ring = sbring(sb, 3, [128, GW], F32, "t1")
            t2_ring = sbring(sb, 3, [128, GW], F32, "t2")
            kst_ring = sbring(sb, 2, [128, H, GW], BF16, "kst")
            qst_ring = sbring(sb, 2, [128, H, GW], BF16, "qst")
            vst_ring = sbring(sb, 2, [128, 4, D], BF16, "vst")
            banks = Ring([ps([128, GW], F32, "bk") for _ in range(8)], "bank5")
            for g in range(NG):
                gs = slice(g * GW, (g + 1) * GW)
                hin_t, hin_r = hin_ring.next()
                cos_t, cos_r = cos_ring.next()
                sin_t, sin_r = sin_ring.next()
                P.op("sp", I("dma_start", out=hin_t[:], in_=H2T.rearrange("c p t -> p c t")[:, :, gs]), writes=[hin_r], slot=hin_r)
                P.op("sp", I("dma_start", out=cos_t[:], in_=cosd_d[:, gs]), writes=[cos_r], slot=cos_r)
                P.op("sp", I("dma_start", out=sin_t[:], in_=sind_d[:, gs]), writes=[sin_r], slot=sin_r)
                k_t, k_r = kst_ring.next()
                q_t, q_r = qst_ring.next()
                for (w_, wr_, nm, dst_t, dst_r, sc) in ((wk, wkr, "wk", k_t, k_r, 1.0), (wq, wqr, "wq", q_t, q_r, SC1)):
                    for h in range(H):
                        ab, ar = banks.next()
                        bb, br = banks.next()
                        for c in range(8):
                            P.op("pe", I("matmul", ab[:], lhsT=w_[:, c, h * 128:(h + 1) * 128], rhs=hin_t[:, c, :],
                                                                                           start=(c == 0), stop=(c == 7)), reads=[hin_r, nm], writes=[ar])
                        for c in range(8):
                            P.op("pe", I("matmul", bb[:], lhsT=wr_[:, c, h * 128:(h + 1) * 128], rhs=hin_t[:, c, :],
                                                                                             start=(c == 0), stop=(c == 7)), reads=[hin_r] + rres[nm], writes=[br])
                        t1, t1r = t1_ring.next()
                        t2, t2r = t2_ring.next()
                        P.op("dve", I("scalar_tensor_tensor", out=t1[:], in0=ab[:], scalar=sc, in1=cos_t[:],
                                                                                                   op0=ALU.mult, op1=ALU.mult), reads=[ar, cos_r], writes=[t1r])
                        P.op("dve", I("scalar_tensor_tensor", out=t2[:], in0=bb[:], scalar=sc, in1=sin_t[:],
                                                                                                   op0=ALU.mult, op1=ALU.mult), reads=[br, sin_r], writes=[t2r])
                        P.op("pool", I("tensor_tensor", out=dst_t[:, h, :], in0=t1[:], in1=t2[:], op=ALU.add),
                             reads=[t1r, t2r], writes=[(dst_r, h)])
                P.op("sp", I("dma_start", out=KT1.rearrange("h p t -> p h t")[:, :, gs], in_=k_t[:]),
                     reads=[(k_r, h) for h in range(H)], slot=k_r)
                P.op("sp", I("dma_start", out=QT1.rearrange("h p t -> p h t")[:, :, gs], in_=q_t[:]),
                     reads=[(q_r, h) for h in range(H)], slot=q_r)
                v_t, v_r = vst_ring.next()
                for i in range(4):
                    for hh in range(2):
                        bk, bkr = banks.next()
                        for c in range(8):
                            P.op("pe", I("matmul", bk[:], lhsT=hin_t[:, c, i * 128:(i + 1) * 128],
                                                                                           rhs=wv[:, c, hh * 512:(hh + 1) * 512], start=(c == 0), stop=(c == 7)),
                                 reads=[hin_r, "wv"], writes=[bkr])
                        evac(v_t[:, i, hh * 512:(hh + 1) * 512], bk[:], [bkr], [(v_r, i, hh)], eng="act")
                P.op("sp", I("dma_start", out=V1[g * GW:(g + 1) * GW, :].rearrange("(i p) n -> p i n", p=128), in_=v_t[:]),
                     reads=[(v_r, i, hh) for i in range(4) for hh in range(2)], slot=v_r)
            P.barrier()

    if 1 in phases:
        phase1()
    if 2 in phases:
        attention_old(0)
    if 3 in phases:
        outproj_ln(0)
    if 4 in phases:
        ffn_ln(0)
    if 5 in phases:
        phase_proj1()
    if 6 in phases:
        attention(1)
    if 7 in phases:
        outproj_ln(1)
    if 8 in phases:
        ffn_ln(1)
    P.emit()
    return nc, P


def _rope_tables(dim):
    inv = (1.0 / (10000.0 ** (np.arange(0, dim, 2, dtype=np.float32) / np.float32(dim)))).astype(np.float32)
    ang = np.arange(S, dtype=np.float32)[:, None] * inv[None, :]
    ang = np.concatenate([ang, ang], axis=-1).astype(np.float32)
    return np.ascontiguousarray(np.cos(ang).T.astype(np.float32)), np.ascontiguousarray(np.sin(ang).T.astype(np.float32))


def _consts():
    ident = np.eye(128, dtype=np.float32).astype(ml_dtypes.bfloat16)
    ki = np.arange(128)[:, None, None]
    j = np.arange(4)[None, :, None]
    qi = np.arange(GW)[None, None, :]
    mask = np.where(qi >= 128 * j + ki, 0.0, NEG).astype(np.float32).astype(ml_dtypes.bfloat16)
    cm, sm = _rope_tables(64)
    cd = np.ascontiguousarray(np.concatenate([cm, cm], axis=0))
    sd = np.ascontiguousarray(np.concatenate([sm, sm], axis=0))
    return {"c_ident": ident, "c_mask": np.ascontiguousarray(mask), "c_cosm": cm, "c_sinm": sm, "c_cosd": cd, "c_sind": sd}


_SQUEEZE = ("mla_w_dq", "mla_q_norm", "mla_w_uq", "mla_w_dkv", "mla_kv_norm", "mla_w_ukv", "mla_w_o",
            "diff_w_q", "diff_subln", "diff_w_o")


def make_in_maps(inputs, n_cores=8):
    common = dict(_consts())
    for k, v in inputs.items():
        if k == "x":
            continue
        a = np.ascontiguousarray(np.asarray(v, dtype=np.float32))
        if k in _SQUEEZE:
            a = np.ascontiguousarray(a[0])
        common[k] = a
    xs = np.asarray(inputs["x"], dtype=np.float32)
    maps = []
    for c in range(n_cores):
        m = dict(common)
        m["x"] = np.ascontiguousarray(xs[c])
        maps.append(m)
    return maps


_CACHE = {}


def kernel(**inputs):
    if "nc" not in _CACHE:
        _CACHE["nc"] = build_program()[0]
    nc = _CACHE["nc"]
    in_maps = make_in_maps(inputs, 8)
    res = run_bass_kernel_spmd(nc, in_maps, core_ids=list(range(8)))
    return np.stack([np.asarray(r["out"], dtype=np.float32) for r in res.results], axis=0)
```

```python
import math
from contextlib import ExitStack

import numpy as np
import ml_dtypes
import concourse.bass as bass
import concourse.mybir as mybir
from concourse.bass_utils import run_bass_kernel_spmd

F32 = mybir.dt.float32
BF16 = mybir.dt.bfloat16
AF = mybir.ActivationFunctionType
ALU = mybir.AluOpType

S = 4096
D = 1024
NT = 32
NG = 8
GW = 512
H = 8
DFF = 2816
NF = 22
ALPHA = 4.0 ** 0.25
LN_EPS = 1e-5
RMS_EPS = 1e-6
SC0 = 192.0 ** -0.5
SC1 = 64.0 ** -0.5
LAMBDA_INIT = 0.8 - 0.6 * math.exp(-0.3 * 1)
NEG = -30000.0

ENGS = ("pe", "act", "dve", "pool", "sp")


class _Op:
    __slots__ = ("eng", "fn", "deps", "signal", "slot", "sigval", "idx", "phase")


class Prog:
    def __init__(self, nc, same_engine_sync=True):
        self.nc = nc
        self.ops = []
        self.last_w = {}
        self.readers = {}
        self.same_engine_sync = same_engine_sync
        self.fence = set()
        self.last_on_eng = {}
        self.dma_since = []
        self.phase = 0

    def barrier(self):
        self.phase += 1
        self.fence = set(self.last_on_eng.values()) | set(self.dma_since)
        self.dma_since = []
        self.last_w = {}
        self.readers = {}

    def op(self, eng, fn, reads=(), writes=(), slot=None):
        o = _Op()
        o.eng, o.fn, o.slot, o.signal, o.sigval = eng, fn, slot, slot is not None, None
        o.idx = len(self.ops)
        o.phase = self.phase
        deps = set(self.fence)
        for r in reads:
            w = self.last_w.get(r)
            if w is not None:
                deps.add(w)
        for r in writes:
            w = self.last_w.get(r)
            if w is not None:
                deps.add(w)
            for rd in self.readers.get(r, ()):
                deps.add(rd)
        if eng == "pool" and slot is not None:
            if getattr(self, "last_pool_dma", None) is not None:
                deps.add(self.last_pool_dma)
            self.last_pool_dma = o.idx
        o.deps = deps
        for r in reads:
            lst = self.readers.setdefault(r, [])
            if slot is None:
                lst[:] = [i for i in lst if not (self.ops[i].slot is None and self.ops[i].eng == eng)]
            lst.append(o.idx)
        for r in writes:
            self.last_w[r] = o.idx
            self.readers[r] = []
        self.ops.append(o)
        if slot is None:
            self.last_on_eng[eng] = o.idx
        else:
            self.dma_since.append(o.idx)
        return o.idx

    def _skip(self, p, o):
        if p.slot is None and o.slot is None and p.eng == o.eng:
            return p.eng == "pe" or not self.same_engine_sync
        return False

    def emit(self):
        nc = self.nc
        ops = self.ops
        for o in ops:
            for d in o.deps:
                p = ops[d]
                if p.slot is None and not self._skip(p, o):
                    p.signal = True
        cnt = {e: 0 for e in ENGS}
        slotcnt = {}
        physmap = {}
        nphys = {}
        for o in ops:
            if o.slot is not None:
                key = (o.eng, o.phase, o.slot)
                if key not in physmap:
                    physmap[key] = (o.eng, nphys.get((o.eng, o.phase), 0))
                    nphys[(o.eng, o.phase)] = physmap[key][1] + 1
                ph = physmap[key]
                slotcnt[ph] = slotcnt.get(ph, 0) + 16
                o.sigval = (("dma", ph), slotcnt[ph])
            elif o.signal:
                cnt[o.eng] += 1
                o.sigval = (o.eng, cnt[o.eng])
        keys = [e for e in ENGS if cnt[e] > 0] + [("dma", s) for s in slotcnt]
        self.n_sems = len(keys)
        with ExitStack() as st:
            sems = {}
            for i, k in enumerate(keys):
                sems[k] = st.enter_context(nc.semaphore("sm%d" % i))
            block = st.enter_context(nc.Block())
            per_eng = {e: [o for o in ops if o.eng == e] for e in ENGS}

            def run(engname, eng):
                waited = {}
                for o in per_eng[engname]:
                    need = {}
                    for d in o.deps:
                        p = ops[d]
                        if p.sigval is None or self._skip(p, o):
                            continue
                        k, v = p.sigval
                        if need.get(k, 0) < v:
                            need[k] = v
                    for k, v in need.items():
                        if waited.get(k, 0) < v:
                            eng.wait_ge(sems[k], v)
                            waited[k] = v
                    ins = o.fn(eng)
                    if o.sigval is not None:
                        ins.then_inc(sems[o.sigval[0]], 16 if o.slot is not None else 1)
                if engname == "sp":
                    for s_, v in slotcnt.items():
                        if waited.get(("dma", s_), 0) < v:
                            eng.wait_ge(sems[("dma", s_)], v)

            @block.sync
            def _(e):
                run("sp", e)

            @block.tensor
            def _(e):
                run("pe", e)

            @block.scalar
            def _(e):
                run("act", e)

            @block.vector
            def _(e):
                run("dve", e)

            @block.gpsimd
            def _(e):
                run("pool", e)


def I(method, *args, **kwargs):
    return lambda e: getattr(e, method)(*args, **kwargs)


class Ring:
    def __init__(self, tiles, name):
        self.tiles, self.name, self.i = tiles, name, -1

    def next(self):
        self.i += 1
        k = self.i % len(self.tiles)
        return self.tiles[k], (self.name, k)


def build_program(phases=(1, 2, 3, 4, 5, 6, 7, 8), debug=()):
    nc = bass.Bass("TRN2", target_bir_lowering=False)

    def din(name, shape, dt=F32):
        return nc.dram_tensor(name, list(shape), dt, kind="ExternalInput").ap()

    def dscr(name, shape, dt):
        kind = "ExternalOutput" if name in debug else "Internal"
        return nc.dram_tensor(name, list(shape), dt, kind=kind).ap()

    x = din("x", [S, D])
    w_dq = din("mla_w_dq", [D, 384])
    q_norm = din("mla_q_norm", [384])
    w_uq = din("mla_w_uq", [384, 1536])
    w_dkv = din("mla_w_dkv", [D, 320])
    kv_norm = din("mla_kv_norm", [256])
    w_ukv = din("mla_w_ukv", [256, 2048])
    w_o0 = din("mla_w_o", [D, D])
    kv_w = din("kv_w", [D, 2048])
    w_q1 = din("diff_w_q", [D, D])
    lq1 = din("diff_lq1", [1, 64])
    lk1 = din("diff_lk1", [1, 64])
    lq2 = din("diff_lq2", [1, 64])
    lk2 = din("diff_lk2", [1, 64])
    subln = din("diff_subln", [128])
    w_o1 = din("diff_w_o", [D, D])
    ln1_g = din("ln1_g", [2, D])
    ln1_b = din("ln1_b", [2, D])
    ln2_g = din("ln2_g", [2, D])
    ln2_b = din("ln2_b", [2, D])
    w_gu = din("ffn_w_gate_up", [2, D, 2 * DFF])
    w_dn = din("ffn_w_down", [2, DFF, D])
    ident_d = din("c_ident", [128, 128], BF16)
    mask_d = din("c_mask", [128, 4, GW], BF16)
    cosm_d = din("c_cosm", [64, S])
    sinm_d = din("c_sinm", [64, S])
    cosd_d = din("c_cosd", [128, S])
    sind_d = din("c_sind", [128, S])
    out = nc.dram_tensor("out", [S, D], F32, kind="ExternalOutput").ap()

    QN0 = dscr("QN0", [H, 128, S], BF16)
    QR0 = dscr("QR0", [H, 64, S], BF16)
    KN0 = dscr("KN0", [H, 128, S], BF16)
    KR0 = dscr("KR0", [64, S], BF16)
    V0 = dscr("V0", [S, D], BF16)
    OT0 = dscr("OT0", [H, 128, S], BF16)
    H1 = dscr("H1", [S, D], F32)
    H1T = dscr("H1T", [8, 128, S], BF16)
    H2 = dscr("H2", [S, D], F32)
    H2T = dscr("H2T", [8, 128, S], BF16)
    QT1 = dscr("QT1", [H, 128, S], BF16)
    KT1 = dscr("KT1", [H, 128, S], BF16)
    V1 = dscr("V1", [S, D], BF16)
    OT1 = dscr("OT1", [H, 128, S], BF16)
    H3 = dscr("H3", [S, D], F32)
    H3T = dscr("H3T", [8, 128, S], BF16)

    P = Prog(nc)
    uid = [0]

    def scope():
        st = ExitStack()

        def sb(shape, dt, name=None):
            uid[0] += 1
            return st.enter_context(nc.sbuf_tensor("%s_%d" % (name or "t", uid[0]), list(shape), dt))

        def ps(shape, dt, name=None):
            uid[0] += 1
            return st.enter_context(nc.psum_tensor("%s_%d" % (name or "p", uid[0]), list(shape), dt))

        return st, sb, ps

    def sbring(sb, n, shape, dt, name):
        return Ring([sb(shape, dt, name) for _ in range(n)], name + str(uid[0]))

    evac_rr = [0]

    def evac(out_ap, in_ap, reads, writes, scale=None, eng=None):
        if eng is None:
            evac_rr[0] += 1
            eng = "act" if evac_rr[0] % 2 else "dve"
        if eng == "act":
            if scale is None:
                P.op("act", I("activation", out=out_ap, in_=in_ap, func=AF.Copy), reads=reads, writes=writes)
            else:
                P.op("act", I("activation", out=out_ap, in_=in_ap, func=AF.Copy, scale=scale), reads=reads, writes=writes)
        else:
            if scale is None:
                P.op("dve", I("tensor_copy", out=out_ap, in_=in_ap), reads=reads, writes=writes)
            else:
                P.op("dve", I("tensor_scalar", out=out_ap, in0=in_ap, scalar1=scale, scalar2=None, op0=ALU.mult),
                     reads=reads, writes=writes)

    def load_w_bf16(dst_tile, src_ap, nchunk, res_prefix, split=1):
        n = src_ap.shape[-1]
        v = src_ap.rearrange("(c p) n -> p c n", p=128)
        step = n // split
        for s_ in range(split):
            lo, hi = s_ * step, (s_ + 1) * step
            P.op("pool", I("dma_start", out=dst_tile[:, :, lo:hi], in_=v[:, :, lo:hi]),
                 writes=[(res_prefix, s_)], slot=(res_prefix, s_))

    class LNPipe:
        def __init__(self, sb, ps, Gt, Bt, ident, mh, HTd, pt_ring):
            self.stt = sbring(sb, 2, [128, 2, 6], F32, "stt")
            self.mv = sbring(sb, 2, [128, 2], F32, "mv")
            self.ve = sbring(sb, 2, [128, 1], F32, "ve")
            self.rstd = sbring(sb, 3, [128, 1], F32, "rstd")
            self.nmr = sbring(sb, 3, [128, 1], F32, "nmr")
            self.hn = sbring(sb, 2, [128, D], F32, "hn")
            self.Gt, self.Bt, self.ident, self.mh, self.HTd, self.pt_ring = Gt, Bt, ident, mh, HTd, pt_ring
            if HTd is not None:
                self.hb = sbring(sb, 2, [128, D], BF16, "hb")
                self.hTt = sbring(sb, 2, [128, 8, 128], BF16, "hTt")
            self.st = {}
            self.n = 0

        def push(self, z, zres, dst_rows, t):
            k = self.n
            self.n += 1
            zr = list(zres)
            stt_t, stt_r = self.stt.next()
            mv_t, mv_r = self.mv.next()
            ve_t, ve_r = self.ve.next()
            rs_t, rs_r = self.rstd.next()
            nm_t, nm_r = self.nmr.next()
            self.st[k] = dict(z=z, zr=zr, dst=dst_rows, t=t)
            P.op("dve", I("bn_stats", out=stt_t[:, 0, :], in_=z[:, 0:512]), reads=zr, writes=[(stt_r, 0)])
            P.op("dve", I("bn_stats", out=stt_t[:, 1, :], in_=z[:, 512:1024]), reads=zr, writes=[(stt_r, 1)])
            P.op("dve", I("bn_aggr", out=mv_t[:], in_=stt_t[:].rearrange("p a b -> p (a b)")),
                 reads=[(stt_r, 0), (stt_r, 1)], writes=[mv_r])
            P.op("dve", I("tensor_scalar", out=ve_t[:], in0=mv_t[:, 1:2], scalar1=LN_EPS, scalar2=None, op0=ALU.add),
                 reads=[mv_r], writes=[ve_r])
            P.op("pool", I("tensor_tensor", out=rs_t[:], in0=ve_t[:], in1=self.mh[:], op=ALU.pow), reads=[ve_r, "mh"], writes=[rs_r])
            self._stage2(k - 1)
            self._stage3(k - 2)
            P.op("dve", I("tensor_scalar", out=nm_t[:], in0=mv_t[:, 0:1], scalar1=-1.0, scalar2=rs_t[:], op0=ALU.mult, op1=ALU.mult),
                 reads=[mv_r, rs_r], writes=[nm_r])
            P.op("act", I("activation", out=z[:], in_=z[:], func=AF.Identity, scale=rs_t[:], bias=nm_t[:]),
                 reads=zr + [rs_r, nm_r], writes=zr)

        def _stage2(self, k):
            if k < 0 or k not in self.st:
                return
            d = self.st[k]
            z, zr = d["z"], d["zr"]
            hn_t, hn_r = self.hn.next()
            P.op("dve", I("tensor_tensor", out=z[:], in0=z[:], in1=self.Gt[:], op=ALU.mult), reads=zr + ["lnG"], writes=zr)
            P.op("pool", I("tensor_tensor", out=hn_t[:], in0=z[:], in1=self.Bt[:], op=ALU.add), reads=zr + ["lnB"], writes=[hn_r])
            P.op("sp", I("dma_start", out=d["dst"], in_=hn_t[:]), reads=[hn_r], slot=hn_r)
            if self.HTd is not None:
                hb_t, hb_r = self.hb.next()
                P.op("act", I("activation", out=hb_t[:], in_=hn_t[:], func=AF.Copy), reads=[hn_r], writes=[hb_r])
                d["hb"] = (hb_t, hb_r)

        def _stage3(self, k):
            if k < 0 or k not in self.st:
                return
            d = self.st.pop(k)
            if self.HTd is None:
                return
            hb_t, hb_r = d["hb"]
            t = d["t"]
            pt_t, pt_r = self.pt_ring.next()
            for c in range(8):
                P.op("pe", I("transpose", out=pt_t[:, c, :], in_=hb_t[:, c * 128:(c + 1) * 128], identity=self.ident[:]),
                     reads=[hb_r, "ident"], writes=[pt_r])
            hT_t, hT_r = self.hTt.next()
            P.op("dve", I("tensor_copy", out=hT_t[:], in_=pt_t[:]), reads=[pt_r], writes=[hT_r])
            P.op("sp", I("dma_start", out=self.HTd.rearrange("c p t -> p c t")[:, :, t * 128:(t + 1) * 128], in_=hT_t[:]),
                 reads=[hT_r], slot=hT_r)

        def flush(self):
            self._stage2(self.n - 1)
            self._stage3(self.n - 2)
            self._stage3(self.n - 1)

    def load_consts(sb, need_ident=True, need_mask=False, need_ones=False, need_mh=False):
        r = {}
        if need_ident:
            r["ident"] = sb([128, 128], BF16, "ident")
            P.op("sp", I("dma_start", out=r["ident"][:], in_=ident_d[:, :]), writes=["ident"], slot="ident")
        if need_mask:
            r["mask"] = sb([128, 4, GW], BF16, "mask")
            P.op("sp", I("dma_start", out=r["mask"][:], in_=mask_d[:, :, :]), writes=["mask"], slot="mask")
        if need_ones:
            r["ones"] = sb([128, 128], BF16, "ones")
            P.op("pool", I("memset", r["ones"][:], 1.0), writes=["ones"])
        if need_mh:
            r["mh"] = sb([128, 1], F32, "mh")
            P.op("pool", I("memset", r["mh"][:], -0.5), writes=["mh"])
        return r

    def rms_fm(banks, bank_res, nchunk, ndim, gain_t, out_tile, out_res, sq_ring, ss_bank, ss_res, ln_ring, rstd_ring, ones):
        sqs = []
        for m in range(nchunk):
            sq_t, sq_r = sq_ring.next()
            P.op("act", I("activation", out=sq_t[:], in_=banks[m][:], func=AF.Square),
                 reads=[bank_res[m]], writes=[sq_r])
            sqs.append((sq_t, sq_r))
        for m in range(nchunk):
            P.op("pe", I("matmul", ss_bank[:], lhsT=ones[:], rhs=sqs[m][0][:], start=(m == 0), stop=(m == nchunk - 1)),
                 reads=["ones", sqs[m][1]], writes=[ss_res])
        ln_t, ln_r = ln_ring.next()
        rs_t, rs_r = rstd_ring.next()
        P.op("act", I("activation", out=ln_t[:], in_=ss_bank[:], func=AF.Ln, scale=1.0 / ndim, bias=RMS_EPS),
             reads=[ss_res], writes=[ln_r])
        P.op("act", I("activation", out=rs_t[:], in_=ln_t[:], func=AF.Exp, scale=-0.5), reads=[ln_r], writes=[rs_r])
        for m in range(nchunk):
            P.op("dve", I("scalar_tensor_tensor", out=out_tile[:, m, :], in0=banks[m][:], scalar=gain_t[:, m:m + 1],
                                                              in1=rs_t[:], op0=ALU.mult, op1=ALU.mult),
                 reads=[bank_res[m], rs_r, "gains"], writes=[(out_res, m)])

    def phase1():
        st, sb, ps = scope()
        with st:
            C = load_consts(sb, need_ident=True, need_ones=True)
            ident, ones = C["ident"], C["ones"]
            wdq = sb([128, 8, 384], BF16, "wdq")
            wdkv = sb([128, 8, 384], BF16, "wdkv")
            wuq = sb([128, 3, 1536], BF16, "wuq")
            wuqr = sb([128, 3, 8, 64], BF16, "wuqr")
            wukv = sb([128, 2, 2048], BF16, "wukv")
            gq = sb([128, 3], F32, "gq")
            gkv = sb([128, 2], F32, "gkv")
            load_w_bf16(wdq, w_dq, 8, "wdq")
            P.op("pool", I("dma_start", out=wdkv[:, :, 0:320], in_=w_dkv.rearrange("(c p) n -> p c n", p=128)),
                 writes=["wdkv"], slot="wdkv")
            load_w_bf16(wuq, w_uq, 3, "wuq")
            load_w_bf16(wukv, w_ukv, 2, "wukv")
            for m in range(3):
                P.op("sp", I("dma_start", out=gq[:, m:m + 1], in_=q_norm.rearrange("(c p o) -> c p o", p=128, o=1)[m]),
                     writes=["gains"], slot=("gq", m))
            for m in range(2):
                P.op("sp", I("dma_start", out=gkv[:, m:m + 1], in_=kv_norm.rearrange("(c p o) -> c p o", p=128, o=1)[m]),
                     writes=["gains"], slot=("gkv", m))
            P.op("pool", I("tensor_scalar", out=wdkv[:, :, 320:352], in0=wdkv[:, :, 288:320], scalar1=-1.0, scalar2=None, op0=ALU.mult),
                 reads=["wdkv"], writes=["wdkvr"])
            P.op("pool", I("tensor_copy", out=wdkv[:, :, 352:384], in_=wdkv[:, :, 256:288]), reads=["wdkv"], writes=["wdkvr2"])
            wuq4 = wuq[:].rearrange("p c (h d) -> p c h d", d=192)
            for c in range(3):
                P.op("pool", I("tensor_scalar", out=wuqr[:, c, :, 0:32], in0=wuq4[:, c, :, 160:192], scalar1=-1.0, scalar2=None,
                                                             op0=ALU.mult), reads=[("wuq", 0)], writes=[("wuqr", c, 0)])
                P.op("pool", I("tensor_copy", out=wuqr[:, c, :, 32:64], in_=wuq4[:, c, :, 128:160]),
                     reads=[("wuq", 0)], writes=[("wuqr", c, 1)])
            wuqr_res = [("wuqr", c, k) for c in range(3) for k in range(2)]
            wdkv_res = ["wdkv", "wdkvr", "wdkvr2"]

            xs_ring = sbring(sb, 2, [128, D], F32, "xs")
            xb_ring = sbring(sb, 2, [128, D], BF16, "xb")
            xT_ring = sbring(sb, 2, [128, 8, GW], BF16, "xT")
            pt_ring = Ring([ps([128, 8, 128], BF16, "pt")], "pt1")
            banks = [ps([128, GW], F32, "bk") for _ in range(7)]
            bres = [("bank1", i) for i in range(7)]
            sq_ring = sbring(sb, 3, [128, GW], BF16, "sq")
            ln_ring = sbring(sb, 1, [128, GW], F32, "lnss")
            rstd_ring = sbring(sb, 2, [128, GW], F32, "rstdfm")
            cqn_ring = sbring(sb, 2, [128, 3, GW], BF16, "cqn")
            cn_ring = sbring(sb, 2, [128, 2, GW], BF16, "cn")
            cos_ring = sbring(sb, 2, [64, GW], F32, "cosm")
            sin_ring = sbring(sb, 2, [64, GW], F32, "sinm")
            t1_ring = sbring(sb, 2, [64, GW], F32, "t1")
            t2_ring = sbring(sb, 2, [64, GW], F32, "t2")
            qnst_ring = sbring(sb, 2, [128, H, GW], BF16, "qnst")
            qrst_ring = sbring(sb, 2, [64, H, GW], BF16, "qrst")
            knst_ring = sbring(sb, 2, [128, H, GW], BF16, "knst")
            vst_ring = sbring(sb, 2, [128, 4, D], BF16, "vst")
            krst_ring = sbring(sb, 2, [64, GW], BF16, "krst")
            bi = [0]

            def nb():
                bi[0] += 1
                k = bi[0] % 7
                return banks[k], bres[k]

            for g in range(NG):
                gs = slice(g * GW, (g + 1) * GW)
                cos_t, cos_r = cos_ring.next()
                sin_t, sin_r = sin_ring.next()
                P.op("sp", I("dma_start", out=cos_t[:], in_=cosm_d[:, gs]), writes=[cos_r], slot=cos_r)
                P.op("sp", I("dma_start", out=sin_t[:], in_=sinm_d[:, gs]), writes=[sin_r], slot=sin_r)
                xT_t, xT_r = xT_ring.next()
                for i in range(4):
                    t = g * 4 + i
                    xs_t, xs_r = xs_ring.next()
                    xb_t, xb_r = xb_ring.next()
                    P.op("sp", I("dma_start", out=xs_t[:], in_=x[t * 128:(t + 1) * 128, :]), writes=[xs_r], slot=xs_r)
                    P.op("pool", I("tensor_copy", out=xb_t[:], in_=xs_t[:]), reads=[xs_r], writes=[xb_r])
                    pt_t, pt_r = pt_ring.next()
                    for c in range(8):
                        P.op("pe", I("transpose", out=pt_t[:, c, :], in_=xb_t[:, c * 128:(c + 1) * 128],
                                                                                  identity=ident[:]), reads=[xb_r, "ident"], writes=[pt_r])
                    evac(xT_t[:, :, i * 128:(i + 1) * 128], pt_t[:], [pt_r], [(xT_r, i)])
                xT_all = [(xT_r, i) for i in range(4)]
                cqb = [nb() for _ in range(3)]
                for m in range(3):
                    for c in range(8):
                        P.op("pe", I("matmul", cqb[m][0][:], lhsT=wdq[:, c, m * 128:(m + 1) * 128], rhs=xT_t[:, c, :],
                                                                             start=(c == 0), stop=(c == 7)),
                             reads=xT_all + [("wdq", 0)], writes=[cqb[m][1]])
                ssb, ssr = nb()
                cqn_t, cqn_r = cqn_ring.next()
                rms_fm([b[0] for b in cqb], [b[1] for b in cqb], 3, 384, gq, cqn_t, cqn_r, sq_ring, ssb, ssr, ln_ring, rstd_ring, ones)
                cqn_all = [(cqn_r, m) for m in range(3)]
                cb = [nb() for _ in range(2)]
                for m in range(2):
                    for c in range(8):
                        P.op("pe", I("matmul", cb[m][0][:], lhsT=wdkv[:, c, m * 128:(m + 1) * 128], rhs=xT_t[:, c, :],
                                                                            start=(c == 0), stop=(c == 7)),
                             reads=xT_all + wdkv_res, writes=[cb[m][1]])
                ssb, ssr = nb()
                cn_t, cn_r = cn_ring.next()
                rms_fm([b[0] for b in cb], [b[1] for b in cb], 2, 256, gkv, cn_t, cn_r, sq_ring, ssb, ssr, ln_ring, rstd_ring, ones)
                cn_all = [(cn_r, m) for m in range(2)]
                ab, ar = nb()
                bb, br = nb()
                for c in range(8):
                    P.op("pe", I("matmul", ab[0:64, :], lhsT=wdkv[:, c, 256:320], rhs=xT_t[:, c, :], start=(c == 0), stop=(c == 7)),
                         reads=xT_all + wdkv_res, writes=[ar])
                for c in range(8):
                    P.op("pe", I("matmul", bb[0:64, :], lhsT=wdkv[:, c, 320:384], rhs=xT_t[:, c, :], start=(c == 0), stop=(c == 7)),
                         reads=xT_all + wdkv_res, writes=[br])
                t1, t1r = t1_ring.next()
                t2, t2r = t2_ring.next()
                kr_t, kr_r = krst_ring.next()
                P.op("dve", I("tensor_tensor", out=t1[:], in0=ab[0:64, :], in1=cos_t[:], op=ALU.mult),
                     reads=[ar, cos_r], writes=[t1r])
                P.op("dve", I("tensor_tensor", out=t2[:], in0=bb[0:64, :], in1=sin_t[:], op=ALU.mult),
                     reads=[br, sin_r], writes=[t2r])
                P.op("pool", I("tensor_tensor", out=kr_t[:], in0=t1[:], in1=t2[:], op=ALU.add),
                     reads=[t1r, t2r], writes=[kr_r])
                P.op("sp", I("dma_start", out=KR0[:, gs], in_=kr_t[:]), reads=[kr_r], writes=[("KR0", g)], slot=kr_r)
                qn_t, qn_r = qnst_ring.next()
                qr_t, qr_r = qrst_ring.next()
                kn_t, kn_r = knst_ring.next()
                for h in range(H):
                    bk, bkr = nb()
                    for c in range(3):
                        P.op("pe", I("matmul", bk[:], lhsT=wuq[:, c, h * 192:h * 192 + 128], rhs=cqn_t[:, c, :],
                                                                       start=(c == 0), stop=(c == 2)), reads=cqn_all + [("wuq", 0)], writes=[bkr])
                    evac(qn_t[:, h, :], bk[:], [bkr], [(qn_r, h)], scale=SC0)
                    ab, ar = nb()
                    bb, br = nb()
                    for c in range(3):
                        P.op("pe", I("matmul", ab[0:64, :], lhsT=wuq[:, c, h * 192 + 128:h * 192 + 192], rhs=cqn_t[:, c, :],
                                                                       start=(c == 0), stop=(c == 2)), reads=cqn_all + [("wuq", 0)], writes=[ar])
                    for c in range(3):
                        P.op("pe", I("matmul", bb[0:64, :], lhsT=wuqr[:, c, h, :], rhs=cqn_t[:, c, :],
                                                                       start=(c == 0), stop=(c == 2)), reads=cqn_all + wuqr_res, writes=[br])
                    t1, t1r = t1_ring.next()
                    t2, t2r = t2_ring.next()
                    P.op("dve", I("scalar_tensor_tensor", out=t1[:], in0=ab[0:64, :], scalar=SC0, in1=cos_t[:],
                                                                                          op0=ALU.mult, op1=ALU.mult), reads=[ar, cos_r], writes=[t1r])
                    P.op("dve", I("scalar_tensor_tensor", out=t2[:], in0=bb[0:64, :], scalar=SC0, in1=sin_t[:],
                                                                                          op0=ALU.mult, op1=ALU.mult), reads=[br, sin_r], writes=[t2r])
                    P.op("pool", I("tensor_tensor", out=qr_t[:, h, :], in0=t1[:], in1=t2[:], op=ALU.add),
                         reads=[t1r, t2r], writes=[(qr_r, h)])
                    bk, bkr = nb()
                    for c in range(2):
                        P.op("pe", I("matmul", bk[:], lhsT=wukv[:, c, h * 256:h * 256 + 128], rhs=cn_t[:, c, :],
                                                                       start=(c == 0), stop=(c == 1)), reads=cn_all + [("wukv", 0)], writes=[bkr])
                    evac(kn_t[:, h, :], bk[:], [bkr], [(kn_r, h)])
                P.op("sp", I("dma_start", out=QN0.rearrange("h p t -> p h t")[:, :, gs], in_=qn_t[:]),
                     reads=[(qn_r, h) for h in range(H)], writes=[("QN0", g)], slot=qn_r)
                P.op("sp", I("dma_start", out=QR0.rearrange("h p t -> p h t")[:, :, gs], in_=qr_t[:]),
                     reads=[(qr_r, h) for h in range(H)], writes=[("QR0", g)], slot=qr_r)
                P.op("sp", I("dma_start", out=KN0.rearrange("h p t -> p h t")[:, :, gs], in_=kn_t[:]),
                     reads=[(kn_r, h) for h in range(H)], writes=[("KN0", g)], slot=kn_r)
                v_t, v_r = vst_ring.next()
                wv4 = wukv[:].rearrange("p c (h d) -> p c h d", d=256)
                for i in range(4):
                    for hh in range(2):
                        bk, bkr = nb()
                        for c in range(2):
                            P.op("pe", I("matmul", bk[:].rearrange("p (h d) -> p h d", d=128),
                                                                               lhsT=cn_t[:, c, i * 128:(i + 1) * 128],
                                                                               rhs=wv4[:, c, hh * 4:(hh + 1) * 4, 128:256], start=(c == 0), stop=(c == 1)),
                                 reads=cn_all + [("wukv", 0)], writes=[bkr])
                        evac(v_t[:, i, hh * 512:(hh + 1) * 512], bk[:], [bkr], [(v_r, i, hh)])
                P.op("sp", I("dma_start", out=V0[g * GW:(g + 1) * GW, :].rearrange("(i p) n -> p i n", p=128), in_=v_t[:]),
                     reads=[(v_r, i, hh) for i in range(4) for hh in range(2)], writes=[("V0", g)], slot=v_r)
            P.barrier()

    def attention(layer, GQ=GW):
        st, sb, ps = scope()
        with st:
            nmap = 1 if layer == 0 else 2
            NSUB = GQ // 128
            NGQ = S // GQ
            if nmap * GQ <= 512:
                S_ring = Ring([ps([128, nmap, GQ], F32, "S") for _ in range(3)], "S0")
                NSET = 2
            else:
                S_ring = Ring([ps([128, 2, GQ], F32, "S") for _ in range(2)], "S1")
                NSET = 1
            nacc = NSUB * nmap
            nbank = (nacc + 2) // 3
            O_sets = [[ps([128, GW], F32, "O") for _ in range(nbank)] for _ in range(NSET)]
            pt = ps([128, 8, 128], BF16, "ptA")
            AST = 130

            def acc_ap(set_i, a, lo=0, hi=129):
                return O_sets[set_i][a // 3][:, (a % 3) * AST + lo:(a % 3) * AST + hi]

            C = load_consts(sb, need_ident=True, need_mask=True)
            ident, mask = C["ident"], C["mask"]
            if layer == 0:
                qn_ring = sbring(sb, 2, [128, S], BF16, "qn")
                qr_ring = sbring(sb, 2, [64, S], BF16, "qr")
                kn_ring = sbring(sb, 2, [128, S], BF16, "kn")
                kr = sb([64, S], BF16, "kr")
                P.op("sp", I("dma_start", out=kr[:], in_=KR0[:, :]), writes=["kr"], slot="kr")
                Vd, OTd = V0, OT0
            else:
                qn_ring = sbring(sb, 2, [128, S], BF16, "q1")
                kn_ring = sbring(sb, 2, [128, S], BF16, "k1")
                Vd, OTd = V1, OT1
                mh = sb([128, 1], F32, "mh")
                P.op("pool", I("memset", mh[:], -0.5), writes=["mh"])
                lt = [sb([128, 64], F32, "lt") for _ in range(4)]
                for k_, src in enumerate((lq1, lk1, lq2, lk2)):
                    P.op("sp", I("dma_start", out=lt[k_][:], in_=src[0:1, :].partition_broadcast(128)),
                         writes=[("lt", k_)], slot=("lt", k_))
                pr = [sb([128, 64], F32, "pr") for _ in range(2)]
                sm = [sb([128, 1], F32, "lsm") for _ in range(2)]
                ex = [sb([128, 1], F32, "lex") for _ in range(2)]
                neglam = sb([128, 1], F32, "neglam")
                gvec = sb([128, 128], F32, "gvec")
                junk = sb([128, 64], F32, "junk")
                for k_ in range(2):
                    P.op("dve", I("tensor_tensor", out=pr[k_][:], in0=lt[2 * k_][:], in1=lt[2 * k_ + 1][:], op=ALU.mult),
                         reads=[("lt", 2 * k_), ("lt", 2 * k_ + 1)], writes=[("pr", k_)])
                    P.op("act", I("activation", out=junk[:], in_=pr[k_][:], func=AF.Copy, accum_out=sm[k_][:]),
                         reads=[("pr", k_)], writes=[("lsm", k_), "junk"])
                    P.op("act", I("activation", out=ex[k_][:], in_=sm[k_][:], func=AF.Exp), reads=[("lsm", k_)], writes=[("lex", k_)])
                P.op("dve", I("tensor_tensor", out=neglam[:], in0=ex[1][:], in1=ex[0][:], op=ALU.subtract),
                     reads=[("lex", 0), ("lex", 1)], writes=["neglam0"])
                P.op("dve", I("tensor_scalar", out=neglam[:], in0=neglam[:], scalar1=-LAMBDA_INIT, scalar2=None, op0=ALU.add),
                     reads=["neglam0"], writes=["neglam"])
                P.op("sp", I("dma_start", out=gvec[:], in_=subln.rearrange("(o n) -> o n", o=1).partition_broadcast(128)), writes=["gvec0"], slot="gvec")
                P.op("pool", I("tensor_scalar", out=gvec[:], in0=gvec[:], scalar1=1.0 - LAMBDA_INIT, scalar2=None, op0=ALU.mult),
                     reads=["gvec0"], writes=["gvec"])
                o1_ring = sbring(sb, 3, [128, 128], F32, "o1")
                o_ring = sbring(sb, 3, [128, 128], F32, "o")
                junk2 = sb([128, 128], F32, "junk2")
                ss_ring = sbring(sb, 3, [128, 1], F32, "ss")
                rstd_ring = sbring(sb, 3, [128, 1], F32, "rstdA")
                c2_ring = sbring(sb, 3, [128, 1], F32, "c2")
            v_ring = sbring(sb, 2, [128, NT, AST], BF16, "v")
            for k_ in range(2):
                P.op("pool", I("memset", v_ring.tiles[k_][:, :, 128:130], 1.0), writes=[(("vones", k_))])
            pT_ring = sbring(sb, 4, [128, nmap, GQ], BF16, "pT")
            rinv_ring = sbring(sb, 4 * nmap, [128, 1], F32, "rinv")
            ob_ring = sbring(sb, 3, [128, 128], BF16, "ob")
            ost_ring = sbring(sb, 2, [128, GQ], BF16, "ost")
            LOOK = 2 if NSET == 2 else 1
            heads = {}

            def load_head(h):
                if h >= H:
                    return
                d = {}
                d["qn"] = qn_ring.next()
                d["kn"] = kn_ring.next()
                d["v"] = v_ring.next()
                if layer == 0:
                    d["qr"] = qr_ring.next()
                    P.op("sp", I("dma_start", out=d["qn"][0][:], in_=QN0[h]), writes=[d["qn"][1]], slot=d["qn"][1])
                    P.op("sp", I("dma_start", out=d["qr"][0][:], in_=QR0[h]), writes=[d["qr"][1]], slot=d["qr"][1])
                    P.op("sp", I("dma_start", out=d["kn"][0][:], in_=KN0[h]), writes=[d["kn"][1]], slot=d["kn"][1])
                else:
                    P.op("sp", I("dma_start", out=d["qn"][0][:], in_=QT1[h]), writes=[d["qn"][1]], slot=d["qn"][1])
                    P.op("sp", I("dma_start", out=d["kn"][0][:], in_=KT1[h]), writes=[d["kn"][1]], slot=d["kn"][1])
                vk = v_ring.i % 2
                vsrc = Vd.rearrange("(t p) (h d) -> h p t d", p=128, d=128)[h]
                for q_ in range(4):
                    P.op("sp", I("dma_start", out=d["v"][0][:, q_ * 8:(q_ + 1) * 8, 0:128], in_=vsrc[:, q_ * 8:(q_ + 1) * 8, :]),
                         reads=[("vones", vk)], writes=[(d["v"][1], q_)], slot=(d["v"][1], q_))
                heads[h] = d

            set_ctr = [0]
            load_head(0)
            for h in range(H):
                load_head(h + 1)
                hd = heads.pop(h)
                qn_t, qn_r = hd["qn"]
                kn_t, kn_r = hd["kn"]
                v_t, v_r = hd["v"]
                if layer == 0:
                    qr_t, qr_r = hd["qr"]
                pairs = [(g, kt) for g in range(NGQ) for kt in range(NSUB * g + NSUB)]
                state = {}
                gstate = {}
                pending = []

                def emit_qk(n):
                    g, kt = pairs[n]
                    j = max(kt - NSUB * g, 0)
                    q0 = g * GQ + 128 * j
                    q1 = (g + 1) * GQ
                    ks = slice(kt * 128, (kt + 1) * 128)
                    diag = kt - NSUB * g >= 0
                    S_t, S_r = S_ring.next()
                    for m in range(nmap):
                        so = S_t[:, m, 128 * j:GQ]
                        if layer == 0:
                            P.op("pe", I("matmul", so, lhsT=kn_t[:, ks], rhs=qn_t[:, q0:q1], start=True, stop=False),
                                 reads=[kn_r, qn_r], writes=[(S_r, m)])
                            P.op("pe", I("matmul", so, lhsT=kr[:, ks], rhs=qr_t[:, q0:q1], start=False, stop=not diag),
                                 reads=["kr", qr_r], writes=[(S_r, m)])
                        else:
                            lo, hi = m * 64, (m + 1) * 64
                            P.op("pe", I("matmul", so, lhsT=kn_t[lo:hi, ks], rhs=qn_t[lo:hi, q0:q1], start=True, stop=not diag),
                                 reads=[kn_r, qn_r], writes=[(S_r, m)])
                        if diag:
                            P.op("pe", I("matmul", S_t[:, m, 128 * j:128 * j + 128], lhsT=ident[:], rhs=mask[:, 0, 0:128], start=False, stop=True),
                                 reads=["ident", "mask"], writes=[(S_r, m)])
                    state[n] = (S_t, S_r)

                def emit_rest(n):
                    g, kt = pairs[n]
                    j = max(kt - NSUB * g, 0)
                    S_t, S_r = state.pop(n)
                    if kt == 0:
                        set_ctr[0] += 1
                        gstate[g] = dict(set=set_ctr[0] % NSET, started=set())
                    gs_ = gstate[g]
                    si = gs_["set"]
                    pT_t, pT_r = pT_ring.next()
                    for m in range(nmap):
                        P.op("act", I("activation", out=pT_t[:, m, 128 * j:GQ], in_=S_t[:, m, 128 * j:GQ], func=AF.Exp),
                             reads=[(S_r, m)], writes=[(pT_r, m)])
                    if kt == 0 and pending and NSET == 1:
                        P.op("pe", I("transpose", out=pt[:, 7, :], in_=ident[:], identity=ident[:]), reads=["ident"], writes=[("ptA", 7), "pe_tick"])
                        flush_pending()
                    for m in range(nmap):
                        for i in range(j, NSUB):
                            a = m * NSUB + i
                            bank = a // 3
                            first = bank not in gs_["started"]
                            gs_["started"].add(bank)
                            P.op("pe", I("matmul", acc_ap(si, a), lhsT=pT_t[:, m, i * 128:(i + 1) * 128], rhs=v_t[:, kt, 0:129],
                                         start=first, stop=(kt == NSUB * g + i), skip_group_check=True),
                                 reads=[(pT_r, m), (v_r, kt // 8)], writes=[("acc", si, a), "pe_tick"])
                    flush_pending()
                    if kt >= NSUB * g:
                        pending.append((g, kt - NSUB * g, si))

                def flush_pending():
                    while pending:
                        finish(*pending.pop(0))

                def finish(g, i, si):
                    if i == 0:
                        gstate[g]["ost"] = ost_ring.next()
                    ost_t, ost_r = gstate[g]["ost"]
                    ob_t, ob_r = ob_ring.next()
                    if layer == 0:
                        ri_t, ri_r = rinv_ring.next()
                        P.op("dve", I("reciprocal", out=ri_t[:], in_=acc_ap(si, i, 128, 129)), reads=[("acc", si, i), "pe_tick"], writes=[ri_r])
                        P.op("dve", I("tensor_scalar", out=ob_t[:], in0=acc_ap(si, i, 0, 128), scalar1=ri_t[:], scalar2=None, op0=ALU.mult),
                             reads=[("acc", si, i), ri_r], writes=[ob_r])
                    else:
                        ri1, ri1r = rinv_ring.next()
                        ri2, ri2r = rinv_ring.next()
                        c2_t, c2_r = c2_ring.next()
                        o1_t, o1_r = o1_ring.next()
                        o_t, o_r = o_ring.next()
                        ss_t, ss_r = ss_ring.next()
                        rs_t, rs_r = rstd_ring.next()
                        P.op("dve", I("reciprocal", out=ri1[:], in_=acc_ap(si, i, 128, 129)), reads=[("acc", si, i), "pe_tick"], writes=[ri1r])
                        P.op("dve", I("reciprocal", out=ri2[:], in_=acc_ap(si, NSUB + i, 128, 129)), reads=[("acc", si, NSUB + i)], writes=[ri2r])
                        P.op("dve", I("tensor_tensor", out=c2_t[:], in0=ri2[:], in1=neglam[:], op=ALU.mult), reads=[ri2r, "neglam"], writes=[c2_r])
                        P.op("dve", I("tensor_scalar", out=o1_t[:], in0=acc_ap(si, i, 0, 128), scalar1=ri1[:], scalar2=None, op0=ALU.mult),
                             reads=[("acc", si, i), ri1r], writes=[o1_r])
                        P.op("dve", I("scalar_tensor_tensor", out=o_t[:], in0=acc_ap(si, NSUB + i, 0, 128), scalar=c2_t[:], in1=o1_t[:],
                                      op0=ALU.mult, op1=ALU.add), reads=[("acc", si, NSUB + i), c2_r, o1_r], writes=[o_r])
                        P.op("dve", I("tensor_tensor", out=junk2[:], in0=o_t[:], in1=o_t[:], op=ALU.mult), reads=[o_r], writes=["junk2"])
                        P.op("dve", I("tensor_reduce", out=ss_t[:], in_=junk2[:], axis=mybir.AxisListType.X, op=ALU.add), reads=["junk2"], writes=[ss_r])
                        P.op("dve", I("tensor_scalar", out=ss_t[:], in0=ss_t[:], scalar1=1.0 / 128, scalar2=RMS_EPS, op0=ALU.mult, op1=ALU.add),
                             reads=[ss_r], writes=[ss_r])
                        P.op("act", I("activation", out=ss_t[:], in_=ss_t[:], func=AF.Ln), reads=[ss_r], writes=[ss_r])
                        P.op("act", I("activation", out=rs_t[:], in_=ss_t[:], func=AF.Exp, scale=-0.5), reads=[ss_r], writes=[rs_r])
                        P.op("dve", I("scalar_tensor_tensor", out=ob_t[:], in0=o_t[:], scalar=rs_t[:], in1=gvec[:], op0=ALU.mult, op1=ALU.mult),
                             reads=[o_r, rs_r, "gvec"], writes=[ob_r])
                    P.op("pe", I("transpose", out=pt[:, i, :], in_=ob_t[:], identity=ident[:]), reads=[ob_r, "ident"], writes=[("ptA", i)])
                    evac(ost_t[:, i * 128:(i + 1) * 128], pt[:, i, :], [("ptA", i)], [(ost_r, i)], eng="dve")
                    if i == NSUB - 1:
                        P.op("sp", I("dma_start", out=OTd[h][:, g * GQ:(g + 1) * GQ], in_=ost_t[:]), reads=[(ost_r, k_) for k_ in range(NSUB)], slot=ost_r)

                N = len(pairs)
                for n in range(min(LOOK, N)):
                    emit_qk(n)
                for n in range(N):
                    if n + LOOK < N:
                        emit_qk(n + LOOK)
                    emit_rest(n)
                P.op("pe", I("transpose", out=pt[:, 7, :], in_=ident[:], identity=ident[:]), reads=["ident"], writes=[("ptA", 7), "pe_tick"])
                flush_pending()
            P.barrier()

    def attention_old(layer):
        assert layer == 0
        st, sb, ps = scope()
        with st:
            C = load_consts(sb, need_ident=True, need_mask=True)
            ident, mask = C["ident"], C["mask"]
            ones = sb([128, 128], BF16, "ones")
            P.op("pool", I("memset", ones[:], 1.0), writes=["ones"])
            qn_ring = sbring(sb, 2, [128, S], BF16, "qn")
            qr_ring = sbring(sb, 2, [64, S], BF16, "qr")
            kn_ring = sbring(sb, 2, [128, S], BF16, "kn")
            kr = sb([64, S], BF16, "kr")
            P.op("sp", I("dma_start", out=kr[:], in_=KR0[:, :]), writes=["kr"], slot="kr")
            v_ring = sbring(sb, 2, [128, NT, 128], BF16, "v")
            pT_ring = sbring(sb, 5, [128, GW], BF16, "pT")
            S_ring = Ring([ps([128, GW], F32, "S") for _ in range(4)], "S0o")
            O_ring = Ring([ps([128, GW], F32, "O") for _ in range(2)], "O0o")
            M_ring = Ring([ps([128, GW], F32, "M") for _ in range(2)], "M0o")
            ln_ring = sbring(sb, 2, [128, GW], F32, "lnM")
            rs_ring = sbring(sb, 2, [128, GW], F32, "rs")
            ost_ring = sbring(sb, 2, [128, GW], BF16, "ost")
            LOOK = 2
            heads = {}

            def load_head(h):
                if h >= H:
                    return
                d = dict(qn=qn_ring.next(), qr=qr_ring.next(), kn=kn_ring.next(), v=v_ring.next())
                P.op("sp", I("dma_start", out=d["qn"][0][:], in_=QN0[h]), writes=[d["qn"][1]], slot=d["qn"][1])
                P.op("sp", I("dma_start", out=d["qr"][0][:], in_=QR0[h]), writes=[d["qr"][1]], slot=d["qr"][1])
                P.op("sp", I("dma_start", out=d["kn"][0][:], in_=KN0[h]), writes=[d["kn"][1]], slot=d["kn"][1])
                vsrc = V0.rearrange("(t p) (h d) -> h p t d", p=128, d=128)[h]
                for q_ in range(4):
                    P.op("sp", I("dma_start", out=d["v"][0][:, q_ * 8:(q_ + 1) * 8, :], in_=vsrc[:, q_ * 8:(q_ + 1) * 8, :]),
                         writes=[(d["v"][1], q_)], slot=(d["v"][1], q_))
                heads[h] = d

            load_head(0)
            for h in range(H):
                load_head(h + 1)
                hd = heads.pop(h)
                qn_t, qn_r = hd["qn"]
                qr_t, qr_r = hd["qr"]
                kn_t, kn_r = hd["kn"]
                v_t, v_r = hd["v"]
                pairs = [(g, kt) for g in range(NG) for kt in range(4 * g + 4)]
                state = {}

                def emit_qk(n):
                    g, kt = pairs[n]
                    j = max(kt - 4 * g, 0)
                    c0 = 128 * j
                    qs = slice(g * GW + c0, (g + 1) * GW)
                    ks = slice(kt * 128, (kt + 1) * 128)
                    diag = kt >= 4 * g
                    S_t, S_r = S_ring.next()
                    P.op("pe", I("matmul", S_t[:, c0:GW], lhsT=kn_t[:, ks], rhs=qn_t[:, qs], start=True, stop=False),
                         reads=[kn_r, qn_r], writes=[S_r])
                    P.op("pe", I("matmul", S_t[:, c0:GW], lhsT=kr[:, ks], rhs=qr_t[:, qs], start=False, stop=not diag),
                         reads=["kr", qr_r], writes=[S_r])
                    if diag:
                        P.op("pe", I("matmul", S_t[:, c0:c0 + 128], lhsT=ident[:], rhs=mask[:, 0, 0:128], start=False, stop=True),
                             reads=["ident", "mask"], writes=[S_r])
                    state[n] = (S_t, S_r)

                def emit_rest(n):
                    g, kt = pairs[n]
                    last = 4 * g + 3
                    j = max(kt - 4 * g, 0)
                    c0 = 128 * j
                    S_t, S_r = state.pop(n)
                    if kt == 0:
                        state["O"] = O_ring.next()
                        state["M"] = M_ring.next()
                    pT_t, pT_r = pT_ring.next()
                    P.op("act", I("activation", out=pT_t[:, c0:GW], in_=S_t[:, c0:GW], func=AF.Exp), reads=[S_r], writes=[pT_r])
                    O_t, O_r = state["O"]
                    M_t, M_r = state["M"]
                    P.op("pe", I("matmul", O_t[:, c0:GW], lhsT=v_t[:, kt, :], rhs=pT_t[:, c0:GW], start=(kt == 0), stop=(kt == last)),
                         reads=[(v_r, kt // 8), pT_r], writes=[O_r])
                    P.op("pe", I("matmul", M_t[:, c0:GW], lhsT=ones[:], rhs=pT_t[:, c0:GW], start=(kt == 0), stop=(kt == last)),
                         reads=["ones", pT_r], writes=[M_r])
                    if kt == last:
                        finish(g)

                def finish(g):
                    gs = slice(g * GW, (g + 1) * GW)
                    ost_t, ost_r = ost_ring.next()
                    O_t, O_r = state["O"]
                    M_t, M_r = state["M"]
                    ln_t, ln_r = ln_ring.next()
                    rs_t, rs_r = rs_ring.next()
                    P.op("act", I("activation", out=ln_t[:], in_=M_t[:], func=AF.Ln), reads=[M_r], writes=[ln_r])
                    P.op("act", I("activation", out=rs_t[:], in_=ln_t[:], func=AF.Exp, scale=-1.0), reads=[ln_r], writes=[rs_r])
                    P.op("dve", I("tensor_tensor", out=ost_t[:], in0=O_t[:], in1=rs_t[:], op=ALU.mult), reads=[O_r, rs_r], writes=[ost_r])
                    P.op("sp", I("dma_start", out=OT0[h][:, gs], in_=ost_t[:]), reads=[ost_r], slot=ost_r)

                N = len(pairs)
                for n in range(min(LOOK, N)):
                    emit_qk(n)
                for n in range(N):
                    if n + LOOK < N:
                        emit_qk(n + LOOK)
                    emit_rest(n)
            P.barrier()

    def outproj_ln(layer, wgu_pre=None):
        st, sb, ps = scope()
        OTd = OT0 if layer == 0 else OT1
        wo_d = w_o0 if layer == 0 else w_o1
        res_d = x if layer == 0 else H2
        Hd, HTd = (H1, H1T) if layer == 0 else (H3, H3T)
        with st:
            C = load_consts(sb, need_ident=True, need_mh=True)
            ident, mh = C["ident"], C["mh"]
            wo = sb([128, 8, D], BF16, "wo")
            load_w_bf16(wo, wo_d, 8, "wo")
            Gt = sb([128, D], F32, "lnG")
            Bt = sb([128, D], F32, "lnB")
            P.op("sp", I("dma_start", out=Gt[:], in_=ln1_g[layer:layer + 1, :].partition_broadcast(128)), writes=["lnG"], slot="lnG")
            P.op("sp", I("dma_start", out=Bt[:], in_=ln1_b[layer:layer + 1, :].partition_broadcast(128)), writes=["lnB"], slot="lnB")
            ot_ring = sbring(sb, 2, [128, H, GW], BF16, "otg")
            res_ring = sbring(sb, 4, [128, D], F32, "res")
            pt_ring = Ring([ps([128, 8, 128], BF16, "pt") for _ in range(2)], "pt3")
            a_ring = Ring([ps([128, GW], F32, "a") for _ in range(6)], "a3")
            lnp = LNPipe(sb, ps, Gt, Bt, ident, mh, HTd, pt_ring)
            res_q = {}
            ot_q = {}

            def load_res(t):
                if t < NT:
                    res_t, res_r = res_ring.next()
                    P.op("sp", I("dma_start", out=res_t[:], in_=res_d[t * 128:(t + 1) * 128, :]), writes=[(res_r, 0), (res_r, 1)], slot=res_r)
                    res_q[t] = (res_t, res_r)

            def load_ot(g):
                if g < NG:
                    ot_t, ot_r = ot_ring.next()
                    P.op("sp", I("dma_start", out=ot_t[:], in_=OTd.rearrange("h p t -> p h t")[:, :, g * GW:(g + 1) * GW]), writes=[ot_r], slot=ot_r)
                    ot_q[g] = (ot_t, ot_r)

            load_ot(0)
            load_res(0)
            load_res(1)
            for g in range(NG):
                ot_t, ot_r = ot_q.pop(g)
                load_ot(g + 1)
                for i in range(4):
                    t = g * 4 + i
                    load_res(t + 2)
                    if wgu_pre is not None and t < 22:
                        half, s_ = t % 2, t // 2
                        lo = half * DFF + s_ * 256
                        vsrc = w_gu[layer].rearrange("(c p) n -> p c n", p=128)
                        P.op("pool", I("dma_start", out=wgu_pre[:, :, lo:lo + 256], in_=vsrc[:, :, lo:lo + 256]),
                             writes=[("wgu", half, s_)], slot=("wgu", half, s_))
                    res_t, res_r = res_q.pop(t)
                    z_t, z_r = res_t, res_r
                    for hh in range(2):
                        a_t, a_r = a_ring.next()
                        for h in range(H):
                            P.op("pe", I("matmul", a_t[:], lhsT=ot_t[:, h, i * 128:(i + 1) * 128],
                                                                                           rhs=wo[:, h, hh * 512:(hh + 1) * 512], start=(h == 0), stop=(h == H - 1)),
                                 reads=[ot_r, ("wo", 0)], writes=[a_r])
                        P.op("dve", I("scalar_tensor_tensor",
                            out=z_t[:, hh * 512:(hh + 1) * 512], in0=res_t[:, hh * 512:(hh + 1) * 512], scalar=ALPHA, in1=a_t[:], op0=ALU.mult, op1=ALU.add),
                            reads=[(res_r, hh), a_r], writes=[(z_r, hh)])
                    lnp.push(z_t, [(z_r, 0), (z_r, 1)], Hd[t * 128:(t + 1) * 128, :], t)
            lnp.flush()
            P.barrier()

    def ffn_ln(layer, wgu_pre=None):
        st, sb, ps = scope()
        HTin = H1T if layer == 0 else H3T
        Hin = H1 if layer == 0 else H3
        Hd, HTd = (H2, H2T) if layer == 0 else (out, None)
        with st:
            C = load_consts(sb, need_ident=True, need_mh=True)
            ident, mh = C["ident"], C["mh"]
            wdn = sb([128, NF, D], BF16, "wdn")
            NSPL = 11
            v = w_gu[layer].rearrange("(c p) n -> p c n", p=128)
            if wgu_pre is not None:
                wgu = wgu_pre
            else:
                wgu = sb([128, 8, 2 * DFF], BF16, "wgu")
                for s_ in range(NSPL):
                    for half in range(2):
                        lo = half * DFF + s_ * 256
                        P.op("pool", I("dma_start", out=wgu[:, :, lo:lo + 256], in_=v[:, :, lo:lo + 256]),
                             writes=[("wgu", half, s_)], slot=("wgu", half, s_))
            vd = w_dn[layer].rearrange("(f p) n -> p f n", p=128)
            for s_ in range(2):
                P.op("pool", I("dma_start", out=wdn[:, s_ * 11:(s_ + 1) * 11, :], in_=vd[:, s_ * 11:(s_ + 1) * 11, :]),
                     writes=[("wdn", s_)], slot=("wdn", s_))
            Gt = sb([128, D], F32, "lnG")
            Bt = sb([128, D], F32, "lnB")
            P.op("sp", I("dma_start", out=Gt[:], in_=ln2_g[layer:layer + 1, :].partition_broadcast(128)), writes=["lnG"], slot="lnG")
            P.op("sp", I("dma_start", out=Bt[:], in_=ln2_b[layer:layer + 1, :].partition_broadcast(128)), writes=["lnB"], slot="lnB")
            hin_ring = sbring(sb, 1, [128, 8, GW], BF16, "hin")
            actT = sb([128, NF, GW], BF16, "actT")
            sg_ring = sbring(sb, 2, [128, GW], F32, "sg")
            res_ring = sbring(sb, 3, [128, D], F32, "res")
            pt_ring = Ring([ps([128, 8, 128], BF16, "pt")], "pt4")
            lnp = LNPipe(sb, ps, Gt, Bt, ident, mh, HTd, pt_ring)
            g_ring = Ring([ps([128, GW], F32, "gb") for _ in range(2)], "gbk")
            u_ring = Ring([ps([128, GW], F32, "ub") for _ in range(2)], "ubk")
            d_ring = Ring([ps([128, GW], F32, "db") for _ in range(3)], "dbk")
            res_q = {}

            def load_res(t):
                res_t, res_r = res_ring.next()
                P.op("sp", I("dma_start", out=res_t[:], in_=Hin[t * 128:(t + 1) * 128, :]), writes=[(res_r, 0), (res_r, 1)], slot=res_r)
                res_q[t] = (res_t, res_r)

            for g in range(NG):
                gs = slice(g * GW, (g + 1) * GW)
                hin_t, hin_r = hin_ring.next()
                P.op("sp", I("dma_start", out=hin_t[:], in_=HTin.rearrange("c p t -> p c t")[:, :, gs]), writes=[hin_r], slot=hin_r)
                for f in range(NF):
                    gb, gr = g_ring.next()
                    ub, ur = u_ring.next()
                    wres = [("wgu", 0, f // 2), ("wgu", 1, f // 2)]
                    for c in range(8):
                        P.op("pe", I("matmul", gb[:], lhsT=wgu[:, c, f * 128:(f + 1) * 128], rhs=hin_t[:, c, :],
                                                                                start=(c == 0), stop=(c == 7)), reads=[hin_r, wres[0]], writes=[gr])
                    for c in range(8):
                        P.op("pe", I("matmul", ub[:], lhsT=wgu[:, c, DFF + f * 128:DFF + (f + 1) * 128], rhs=hin_t[:, c, :],
                                                                                start=(c == 0), stop=(c == 7)), reads=[hin_r, wres[1]], writes=[ur])
                    sg_t, sg_r = sg_ring.next()
                    P.op("act", I("activation", out=sg_t[:], in_=gb[:], func=AF.Silu), reads=[gr], writes=[sg_r])
                    P.op("dve", I("tensor_tensor", out=actT[:, f, :], in0=ub[:], in1=sg_t[:], op=ALU.mult),
                         reads=[ur, sg_r], writes=[("actT", f)])
                for i in range(4):
                    t = g * 4 + i
                    if i == 0:
                        load_res(t)
                    if i < 3:
                        load_res(t + 1)
                    res_t, res_r = res_q.pop(t)
                    z_t, z_r = res_t, res_r
                    for hh in range(2):
                        db, dr = d_ring.next()
                        for f in range(NF):
                            P.op("pe", I("matmul", db[:], lhsT=actT[:, f, i * 128:(i + 1) * 128],
                                                                              rhs=wdn[:, f, hh * 512:(hh + 1) * 512], start=(f == 0), stop=(f == NF - 1)),
                                 reads=[("actT", f), ("wdn", f // 11)], writes=[dr])
                        P.op("dve", I("scalar_tensor_tensor",
                            out=z_t[:, hh * 512:(hh + 1) * 512], in0=res_t[:, hh * 512:(hh + 1) * 512], scalar=ALPHA, in1=db[:], op0=ALU.mult, op1=ALU.add),
                            reads=[(res_r, hh), dr], writes=[(z_r, hh)])
                    lnp.push(z_t, [(z_r, 0), (z_r, 1)], Hd[t * 128:(t + 1) * 128, :], t)
            lnp.flush()
            P.barrier()

    def phase_proj1():
        st, sb, ps = scope()
        with st:
            wk = sb([128, 8, D], BF16, "wk")
            wkr = sb([128, 8, D], BF16, "wkr")
            wq = sb([128, 8, D], BF16, "wq")
            wqr = sb([128, 8, D], BF16, "wqr")
            wv = sb([128, 8, D], BF16, "wv")
            kvv = kv_w.rearrange("(c p) n -> p c n", p=128)
            P.op("pool", I("dma_start", out=wk[:], in_=kvv[:, :, 0:D]), writes=["wk"], slot="wk")
            P.op("pool", I("dma_start", out=wq[:], in_=w_q1.rearrange("(c p) n -> p c n", p=128)), writes=["wq"], slot="wq")
            P.op("pool", I("dma_start", out=wv[:], in_=kvv[:, :, D:2 * D]), writes=["wv"], slot="wv")
            for (src, dst, nm) in ((wk, wkr, "wk"), (wq, wqr, "wq")):
                s4 = src[:].rearrange("p c (b d) -> p c b d", d=64)
                d4 = dst[:].rearrange("p c (b d) -> p c b d", d=64)
                for c in range(8):
                    P.op("act", I("activation", out=d4[:, c, :, 0:32], in_=s4[:, c, :, 32:64], func=AF.Copy, scale=-1.0),
                         reads=[nm], writes=[(nm + "r", c, 0)])
                    P.op("dve", I("tensor_copy", out=d4[:, c, :, 32:64], in_=s4[:, c, :, 0:32]),
                         reads=[nm], writes=[(nm + "r", c, 1)])
            rres = {nm: [(nm + "r", c, k) for c in range(8) for k in range(2)] for nm in ("wk", "wq")}
            hin_ring = sbring(sb, 2, [128, 8, GW], BF16, "hin")
            cos_ring = sbring(sb, 2, [128, GW], F32, "cosd")
            sin_ring = sbring(sb, 2, [128, GW], F32, "sind")
            t1_ring = sbring(sb, 3, [128, GW], F32, "t1")
            t2_ring = sbring(sb, 3, [128, GW], F32, "t2")
            kst_ring = sbring(sb, 2, [128, H, GW], BF16, "kst")
            qst_ring = sbring(sb, 2, [128, H, GW], BF16, "qst")
            vst_ring = sbring(sb, 2, [128, 4, D], BF16, "vst")
            banks = Ring([ps([128, GW], F32, "bk") for _ in range(8)], "bank5")
            for g in range(NG):
                gs = slice(g * GW, (g + 1) * GW)
                hin_t, hin_r = hin_ring.next()
                cos_t, cos_r = cos_ring.next()
                sin_t, sin_r = sin_ring.next()
                P.op("sp", I("dma_start", out=hin_t[:], in_=H2T.rearrange("c p t -> p c t")[:, :, gs]), writes=[hin_r], slot=hin_r)
                P.op("sp", I("dma_start", out=cos_t[:], in_=cosd_d[:, gs]), writes=[cos_r], slot=cos_r)
                P.op("sp", I("dma_start", out=sin_t[:], in_=sind_d[:, gs]), writes=[sin_r], slot=sin_r)
                k_t, k_r = kst_ring.next()
                q_t, q_r = qst_ring.next()
                for (w_, wr_, nm, dst_t, dst_r, sc) in ((wk, wkr, "wk", k_t, k_r, 1.0), (wq, wqr, "wq", q_t, q_r, SC1)):
                    for h in range(H):
                        ab, ar = banks.next()
                        bb, br = banks.next()
                        for c in range(8):
                            P.op("pe", I("matmul", ab[:], lhsT=w_[:, c, h * 128:(h + 1) * 128], rhs=hin_t[:, c, :],
                                                                                           start=(c == 0), stop=(c == 7)), reads=[hin_r, nm], writes=[ar])
                        for c in range(8):
                            P.op("pe", I("matmul", bb[:], lhsT=wr_[:, c, h * 128:(h + 1) * 128], rhs=hin_t[:, c, :],
                                                                                             start=(c == 0), stop=(c == 7)), reads=[hin_r] + rres[nm], writes=[br])
                        t1, t1r = t1_ring.next()
                        t2, t2r = t2_ring.next()
                        P.op("dve", I("scalar_tensor_tensor", out=t1[:], in0=ab[:], scalar=sc, in1=cos_t[:],
                                                                                                   op0=ALU.mult, op1=ALU.mult), reads=[ar, cos_r], writes=[t1r])
                        P.op("dve", I("scalar_tensor_tensor", out=t2[:], in0=bb[:], scalar=sc, in1=sin_t[:],
                                                                                                   op0=ALU.mult, op1=ALU.mult), reads=[br, sin_r], writes=[t2r])
                        P.op("pool", I("tensor_tensor", out=dst_t[:, h, :], in0=t1[:], in1=t2[:], op=ALU.add),
                             reads=[t1r, t2r], writes=[(dst_r, h)])
                P.op("sp", I("dma_start", out=KT1.rearrange("h p t -> p h t")[:, :, gs], in_=k_t[:]),
                     reads=[(k_r, h) for h in range(H)], slot=k_r)
                P.op("sp", I("dma_start", out=QT1.rearrange("h p t -> p h t")[:, :, gs], in_=q_t[:]),
                     reads=[(q_r, h) for h in range(H)], slot=q_r)
                v_t, v_r = vst_ring.next()
                for i in range(4):
                    for hh in range(2):
                        bk, bkr = banks.next()
                        for c in range(8):
                            P.op("pe", I("matmul", bk[:], lhsT=hin_t[:, c, i * 128:(i + 1) * 128],
                                                                                           rhs=wv[:, c, hh * 512:(hh + 1) * 512], start=(c == 0), stop=(c == 7)),
                                 reads=[hin_r, "wv"], writes=[bkr])
                        evac(v_t[:, i, hh * 512:(hh + 1) * 512], bk[:], [bkr], [(v_r, i, hh)], eng="act")
                P.op("sp", I("dma_start", out=V1[g * GW:(g + 1) * GW, :].rearrange("(i p) n -> p i n", p=128), in_=v_t[:]),
                     reads=[(v_r, i, hh) for i in range(4) for hh in range(2)], slot=v_r)
            P.barrier()

    if 1 in phases:
        phase1()
    if 2 in phases:
        attention_old(0)
    def layer_tail(layer, pa, pb):
        if pa in phases and pb in phases:
            st0, sb0, ps0 = scope()
            with st0:
                wgu_pre = sb0([128, 8, 2 * DFF], BF16, "wgu")
                outproj_ln(layer, wgu_pre)
                ffn_ln(layer, wgu_pre)
        else:
            if pa in phases:
                outproj_ln(layer)
            if pb in phases:
                ffn_ln(layer)

    layer_tail(0, 3, 4)
    if 5 in phases:
        phase_proj1()
    if 6 in phases:
        attention(1)
    layer_tail(1, 7, 8)
    P.emit()
    return nc, P


def _rope_tables(dim):
    inv = (1.0 / (10000.0 ** (np.arange(0, dim, 2, dtype=np.float32) / np.float32(dim)))).astype(np.float32)
    ang = np.arange(S, dtype=np.float32)[:, None] * inv[None, :]
    ang = np.concatenate([ang, ang], axis=-1).astype(np.float32)
    return np.ascontiguousarray(np.cos(ang).T.astype(np.float32)), np.ascontiguousarray(np.sin(ang).T.astype(np.float32))


def _consts():
    ident = np.eye(128, dtype=np.float32).astype(ml_dtypes.bfloat16)
    ki = np.arange(128)[:, None, None]
    j = np.arange(4)[None, :, None]
    qi = np.arange(GW)[None, None, :]
    mask = np.where(qi >= 128 * j + ki, 0.0, NEG).astype(np.float32).astype(ml_dtypes.bfloat16)
    cm, sm = _rope_tables(64)
    cd = np.ascontiguousarray(np.concatenate([cm, cm], axis=0))
    sd = np.ascontiguousarray(np.concatenate([sm, sm], axis=0))
    return {"c_ident": ident, "c_mask": np.ascontiguousarray(mask), "c_cosm": cm, "c_sinm": sm, "c_cosd": cd, "c_sind": sd}


_SQUEEZE = ("mla_w_dq", "mla_q_norm", "mla_w_uq", "mla_w_dkv", "mla_kv_norm", "mla_w_ukv", "mla_w_o",
            "diff_w_q", "diff_subln", "diff_w_o")


def make_in_maps(inputs, n_cores=8):
    common = dict(_consts())
    for k, v in inputs.items():
        if k == "x":
            continue
        a = np.ascontiguousarray(np.asarray(v, dtype=np.float32))
        if k in _SQUEEZE:
            a = np.ascontiguousarray(a[0])
        common[k] = a
    xs = np.asarray(inputs["x"], dtype=np.float32)
    maps = []
    for c in range(n_cores):
        m = dict(common)
        m["x"] = np.ascontiguousarray(xs[c])
        maps.append(m)
    return maps


_CACHE = {}


def kernel(**inputs):
    if "nc" not in _CACHE:
        _CACHE["nc"] = build_program()[0]
    nc = _CACHE["nc"]
    in_maps = make_in_maps(inputs, 8)
    res = run_bass_kernel_spmd(nc, in_maps, core_ids=list(range(8)))
    return np.stack([np.asarray(r["out"], dtype=np.float32) for r in res.results], axis=0)
```

```python
import math
from contextlib import ExitStack

import numpy as np
import ml_dtypes
import concourse.bass as bass
import concourse.mybir as mybir
from concourse.bass_utils import run_bass_kernel_spmd

F32 = mybir.dt.float32
BF16 = mybir.dt.bfloat16
AF = mybir.ActivationFunctionType
ALU = mybir.AluOpType

S = 4096
D = 1024
NT = 32
NG = 8
GW = 512
H = 8
DFF = 2816
NF = 22
ALPHA = 4.0 ** 0.25
LN_EPS = 1e-5
RMS_EPS = 1e-6
SC0 = 192.0 ** -0.5
SC1 = 64.0 ** -0.5
LAMBDA_INIT = 0.8 - 0.6 * math.exp(-0.3 * 1)
NEG = -30000.0

ENGS = ("pe", "act", "dve", "pool", "sp")


class _Op:
    __slots__ = ("eng", "fn", "deps", "signal", "slot", "sigval", "idx", "phase")


class Prog:
    def __init__(self, nc, same_engine_sync=True):
        self.nc = nc
        self.ops = []
        self.last_w = {}
        self.readers = {}
        self.same_engine_sync = same_engine_sync
        self.fence = set()
        self.last_on_eng = {}
        self.dma_since = []
        self.phase = 0

    def barrier(self):
        self.phase += 1
        self.fence = set(self.last_on_eng.values()) | set(self.dma_since)
        self.dma_since = []
        self.last_w = {}
        self.readers = {}

    def op(self, eng, fn, reads=(), writes=(), slot=None):
        o = _Op()
        o.eng, o.fn, o.slot, o.signal, o.sigval = eng, fn, slot, slot is not None, None
        o.idx = len(self.ops)
        o.phase = self.phase
        deps = set(self.fence)
        for r in reads:
            w = self.last_w.get(r)
            if w is not None:
                deps.add(w)
        for r in writes:
            w = self.last_w.get(r)
            if w is not None:
                deps.add(w)
            for rd in self.readers.get(r, ()):
                deps.add(rd)
        if eng == "pool" and slot is not None:
            if getattr(self, "last_pool_dma", None) is not None:
                deps.add(self.last_pool_dma)
            self.last_pool_dma = o.idx
        o.deps = deps
        for r in reads:
            lst = self.readers.setdefault(r, [])
            if slot is None:
                lst[:] = [i for i in lst if not (self.ops[i].slot is None and self.ops[i].eng == eng)]
            lst.append(o.idx)
        for r in writes:
            self.last_w[r] = o.idx
            self.readers[r] = []
        self.ops.append(o)
        if slot is None:
            self.last_on_eng[eng] = o.idx
        else:
            self.dma_since.append(o.idx)
        return o.idx

    def _skip(self, p, o):
        if p.slot is None and o.slot is None and p.eng == o.eng:
            return p.eng == "pe" or not self.same_engine_sync
        return False

    def emit(self):
        nc = self.nc
        ops = self.ops
        for o in ops:
            for d in o.deps:
                p = ops[d]
                if p.slot is None and not self._skip(p, o):
                    p.signal = True
        cnt = {e: 0 for e in ENGS}
        slotcnt = {}
        physmap = {}
        nphys = {}
        for o in ops:
            if o.slot is not None:
                key = (o.eng, o.phase, o.slot)
                if key not in physmap:
                    physmap[key] = (o.eng, nphys.get((o.eng, o.phase), 0))
                    nphys[(o.eng, o.phase)] = physmap[key][1] + 1
                ph = physmap[key]
                slotcnt[ph] = slotcnt.get(ph, 0) + 16
                o.sigval = (("dma", ph), slotcnt[ph])
            elif o.signal:
                cnt[o.eng] += 1
                o.sigval = (o.eng, cnt[o.eng])
        keys = [e for e in ENGS if cnt[e] > 0] + [("dma", s) for s in slotcnt]
        self.n_sems = len(keys)
        with ExitStack() as st:
            sems = {}
            for i, k in enumerate(keys):
                sems[k] = st.enter_context(nc.semaphore("sm%d" % i))
            block = st.enter_context(nc.Block())
            per_eng = {e: [o for o in ops if o.eng == e] for e in ENGS}

            def run(engname, eng):
                waited = {}
                for o in per_eng[engname]:
                    need = {}
                    for d in o.deps:
                        p = ops[d]
                        if p.sigval is None or self._skip(p, o):
                            continue
                        k, v = p.sigval
                        if need.get(k, 0) < v:
                            need[k] = v
                    for k, v in need.items():
                        if waited.get(k, 0) < v:
                            eng.wait_ge(sems[k], v)
                            waited[k] = v
                    ins = o.fn(eng)
                    if o.sigval is not None:
                        ins.then_inc(sems[o.sigval[0]], 16 if o.slot is not None else 1)
                if engname == "sp":
                    for s_, v in slotcnt.items():
                        if waited.get(("dma", s_), 0) < v:
                            eng.wait_ge(sems[("dma", s_)], v)

            @block.sync
            def _(e):
                run("sp", e)

            @block.tensor
            def _(e):
                run("pe", e)

            @block.scalar
            def _(e):
                run("act", e)

            @block.vector
            def _(e):
                run("dve", e)

            @block.gpsimd
            def _(e):
                run("pool", e)


def I(method, *args, **kwargs):
    return lambda e: getattr(e, method)(*args, **kwargs)


class Ring:
    def __init__(self, tiles, name):
        self.tiles, self.name, self.i = tiles, name, -1

    def next(self):
        self.i += 1
        k = self.i % len(self.tiles)
        return self.tiles[k], (self.name, k)


def build_program(phases=(1, 2, 3, 4, 5, 6, 7, 8), debug=()):
    nc = bass.Bass("TRN2", target_bir_lowering=False)

    def din(name, shape, dt=F32):
        return nc.dram_tensor(name, list(shape), dt, kind="ExternalInput").ap()

    def dscr(name, shape, dt):
        kind = "ExternalOutput" if name in debug else "Internal"
        return nc.dram_tensor(name, list(shape), dt, kind=kind).ap()

    x = din("x", [S, D])
    w_dq = din("mla_w_dq", [D, 384])
    q_norm = din("mla_q_norm", [384])
    w_uq = din("mla_w_uq", [384, 1536])
    w_dkv = din("mla_w_dkv", [D, 320])
    kv_norm = din("mla_kv_norm", [256])
    w_ukv = din("mla_w_ukv", [256, 2048])
    w_o0 = din("mla_w_o", [D, D])
    kv_w = din("kv_w", [D, 2048])
    w_q1 = din("diff_w_q", [D, D])
    lq1 = din("diff_lq1", [1, 64])
    lk1 = din("diff_lk1", [1, 64])
    lq2 = din("diff_lq2", [1, 64])
    lk2 = din("diff_lk2", [1, 64])
    subln = din("diff_subln", [128])
    w_o1 = din("diff_w_o", [D, D])
    ln1_g = din("ln1_g", [2, D])
    ln1_b = din("ln1_b", [2, D])
    ln2_g = din("ln2_g", [2, D])
    ln2_b = din("ln2_b", [2, D])
    w_gu = din("ffn_w_gate_up", [2, D, 2 * DFF])
    w_dn = din("ffn_w_down", [2, DFF, D])
    ident_d = din("c_ident", [128, 128], BF16)
    mask_d = din("c_mask", [128, 4, GW], BF16)
    cosm_d = din("c_cosm", [64, S])
    sinm_d = din("c_sinm", [64, S])
    cosd_d = din("c_cosd", [128, S])
    sind_d = din("c_sind", [128, S])
    out = nc.dram_tensor("out", [S, D], F32, kind="ExternalOutput").ap()

    QN0 = dscr("QN0", [H, 128, S], BF16)
    QR0 = dscr("QR0", [H, 64, S], BF16)
    KN0 = dscr("KN0", [H, 128, S], BF16)
    KR0 = dscr("KR0", [64, S], BF16)
    V0 = dscr("V0", [S, D], BF16)
    OT0 = dscr("OT0", [H, 128, S], BF16)
    H1 = dscr("H1", [S, D], F32)
    H1T = dscr("H1T", [8, 128, S], BF16)
    H2 = dscr("H2", [S, D], F32)
    H2T = dscr("H2T", [8, 128, S], BF16)
    QT1 = dscr("QT1", [H, 128, S], BF16)
    KT1 = dscr("KT1", [H, 128, S], BF16)
    V1 = dscr("V1", [S, D], BF16)
    OT1 = dscr("OT1", [H, 128, S], BF16)
    H3 = dscr("H3", [S, D], F32)
    H3T = dscr("H3T", [8, 128, S], BF16)

    P = Prog(nc)
    uid = [0]

    def scope():
        st = ExitStack()

        def sb(shape, dt, name=None):
            uid[0] += 1
            return st.enter_context(nc.sbuf_tensor("%s_%d" % (name or "t", uid[0]), list(shape), dt))

        def ps(shape, dt, name=None):
            uid[0] += 1
            return st.enter_context(nc.psum_tensor("%s_%d" % (name or "p", uid[0]), list(shape), dt))

        return st, sb, ps

    def sbring(sb, n, shape, dt, name):
        return Ring([sb(shape, dt, name) for _ in range(n)], name + str(uid[0]))

    evac_rr = [0]

    def evac(out_ap, in_ap, reads, writes, scale=None, eng=None):
        if eng is None:
            evac_rr[0] += 1
            eng = "act" if evac_rr[0] % 2 else "dve"
        if eng == "act":
            if scale is None:
                P.op("act", I("activation", out=out_ap, in_=in_ap, func=AF.Copy), reads=reads, writes=writes)
            else:
                P.op("act", I("activation", out=out_ap, in_=in_ap, func=AF.Copy, scale=scale), reads=reads, writes=writes)
        else:
            if scale is None:
                P.op("dve", I("tensor_copy", out=out_ap, in_=in_ap), reads=reads, writes=writes)
            else:
                P.op("dve", I("tensor_scalar", out=out_ap, in0=in_ap, scalar1=scale, scalar2=None, op0=ALU.mult),
                     reads=reads, writes=writes)

    def load_w_bf16(dst_tile, src_ap, nchunk, res_prefix, split=1):
        n = src_ap.shape[-1]
        v = src_ap.rearrange("(c p) n -> p c n", p=128)
        step = n // split
        for s_ in range(split):
            lo, hi = s_ * step, (s_ + 1) * step
            P.op("pool", I("dma_start", out=dst_tile[:, :, lo:hi], in_=v[:, :, lo:hi]),
                 writes=[(res_prefix, s_)], slot=(res_prefix, s_))

    class LNPipe:
        def __init__(self, sb, ps, Gt, Bt, ident, mh, HTd, pt_ring):
            self.stt = sbring(sb, 2, [128, 2, 6], F32, "stt")
            self.mv = sbring(sb, 2, [128, 2], F32, "mv")
            self.ve = sbring(sb, 2, [128, 1], F32, "ve")
            self.rstd = sbring(sb, 3, [128, 1], F32, "rstd")
            self.nmr = sbring(sb, 3, [128, 1], F32, "nmr")
            self.hn = sbring(sb, 2, [128, D], F32, "hn")
            self.Gt, self.Bt, self.ident, self.mh, self.HTd, self.pt_ring = Gt, Bt, ident, mh, HTd, pt_ring
            if HTd is not None:
                self.hb = sbring(sb, 2, [128, D], BF16, "hb")
                self.hTt = sbring(sb, 2, [128, 8, 128], BF16, "hTt")
            self.st = {}
            self.n = 0

        def push(self, z, zres, dst_rows, t):
            k = self.n
            self.n += 1
            zr = list(zres)
            stt_t, stt_r = self.stt.next()
            mv_t, mv_r = self.mv.next()
            ve_t, ve_r = self.ve.next()
            rs_t, rs_r = self.rstd.next()
            d = dict(z=z, zr=zr, dst=dst_rows, t=t)
            self.st[k] = d
            P.op("dve", I("bn_stats", out=stt_t[:, 0, :], in_=z[:, 0:512]), reads=zr, writes=[(stt_r, 0)])
            P.op("dve", I("bn_stats", out=stt_t[:, 1, :], in_=z[:, 512:1024]), reads=zr, writes=[(stt_r, 1)])
            P.op("dve", I("bn_aggr", out=mv_t[:], in_=stt_t[:].rearrange("p a b -> p (a b)")),
                 reads=[(stt_r, 0), (stt_r, 1)], writes=[mv_r])
            P.op("dve", I("tensor_scalar", out=ve_t[:], in0=mv_t[:, 1:2], scalar1=LN_EPS, scalar2=None, op0=ALU.add),
                 reads=[mv_r], writes=[ve_r])
            P.op("pool", I("tensor_tensor", out=rs_t[:], in0=ve_t[:], in1=self.mh[:], op=ALU.pow), reads=[ve_r, "mh"], writes=[rs_r])
            P.op("dve", I("scalar_tensor_tensor", out=z[:], in0=z[:], scalar=mv_t[:, 0:1], in1=self.Gt[:], op0=ALU.subtract, op1=ALU.mult),
                 reads=zr + [mv_r, "lnG"], writes=zr)
            self._stage3(k - 1)
            hn_t, hn_r = self.hn.next()
            P.op("dve", I("scalar_tensor_tensor", out=hn_t[:], in0=z[:], scalar=rs_t[:], in1=self.Bt[:], op0=ALU.mult, op1=ALU.add),
                 reads=zr + [rs_r, "lnB"], writes=[hn_r])
            P.op("sp", I("dma_start", out=d["dst"], in_=hn_t[:]), reads=[hn_r], slot=hn_r)
            if self.HTd is not None:
                hb_t, hb_r = self.hb.next()
                P.op("act", I("activation", out=hb_t[:], in_=hn_t[:], func=AF.Copy), reads=[hn_r], writes=[hb_r])
                d["hb"] = (hb_t, hb_r)

        def _stage3(self, k):
            if k < 0 or k not in self.st:
                return
            d = self.st.pop(k)
            if self.HTd is None:
                return
            hb_t, hb_r = d["hb"]
            t = d["t"]
            pt_t, pt_r = self.pt_ring.next()
            for c in range(8):
                P.op("pe", I("transpose", out=pt_t[:, c, :], in_=hb_t[:, c * 128:(c + 1) * 128], identity=self.ident[:]),
                     reads=[hb_r, "ident"], writes=[pt_r])
            hT_t, hT_r = self.hTt.next()
            P.op("dve", I("tensor_copy", out=hT_t[:], in_=pt_t[:]), reads=[pt_r], writes=[hT_r])
            P.op("sp", I("dma_start", out=self.HTd.rearrange("c p t -> p c t")[:, :, t * 128:(t + 1) * 128], in_=hT_t[:]),
                 reads=[hT_r], slot=hT_r)

        def flush(self):
            self._stage3(self.n - 1)

    def load_consts(sb, need_ident=True, need_mask=False, need_ones=False, need_mh=False):
        r = {}
        if need_ident:
            r["ident"] = sb([128, 128], BF16, "ident")
            P.op("sp", I("dma_start", out=r["ident"][:], in_=ident_d[:, :]), writes=["ident"], slot="ident")
        if need_mask:
            r["mask"] = sb([128, 4, GW], BF16, "mask")
            P.op("sp", I("dma_start", out=r["mask"][:], in_=mask_d[:, :, :]), writes=["mask"], slot="mask")
        if need_ones:
            r["ones"] = sb([128, 128], BF16, "ones")
            P.op("pool", I("memset", r["ones"][:], 1.0), writes=["ones"])
        if need_mh:
            r["mh"] = sb([128, 1], F32, "mh")
            P.op("pool", I("memset", r["mh"][:], -0.5), writes=["mh"])
        return r

    def rms_fm(banks, bank_res, nchunk, ndim, gain_t, out_tile, out_res, sq_ring, ss_bank, ss_res, ln_ring, rstd_ring, ones):
        sqs = []
        for m in range(nchunk):
            sq_t, sq_r = sq_ring.next()
            P.op("act", I("activation", out=sq_t[:], in_=banks[m][:], func=AF.Square),
                 reads=[bank_res[m]], writes=[sq_r])
            sqs.append((sq_t, sq_r))
        for m in range(nchunk):
            P.op("pe", I("matmul", ss_bank[:], lhsT=ones[:], rhs=sqs[m][0][:], start=(m == 0), stop=(m == nchunk - 1)),
                 reads=["ones", sqs[m][1]], writes=[ss_res])
        ln_t, ln_r = ln_ring.next()
        rs_t, rs_r = rstd_ring.next()
        P.op("act", I("activation", out=ln_t[:], in_=ss_bank[:], func=AF.Ln, scale=1.0 / ndim, bias=RMS_EPS),
             reads=[ss_res], writes=[ln_r])
        P.op("act", I("activation", out=rs_t[:], in_=ln_t[:], func=AF.Exp, scale=-0.5), reads=[ln_r], writes=[rs_r])
        for m in range(nchunk):
            P.op("dve", I("scalar_tensor_tensor", out=out_tile[:, m, :], in0=banks[m][:], scalar=gain_t[:, m:m + 1],
                                                              in1=rs_t[:], op0=ALU.mult, op1=ALU.mult),
                 reads=[bank_res[m], rs_r, "gains"], writes=[(out_res, m)])

    def phase1():
        st, sb, ps = scope()
        with st:
            C = load_consts(sb, need_ident=True, need_ones=True)
            ident, ones = C["ident"], C["ones"]
            wdq = sb([128, 8, 384], BF16, "wdq")
            wdkv = sb([128, 8, 384], BF16, "wdkv")
            wuq = sb([128, 3, 1536], BF16, "wuq")
            wuqr = sb([128, 3, 8, 64], BF16, "wuqr")
            wukv = sb([128, 2, 2048], BF16, "wukv")
            gq = sb([128, 3], F32, "gq")
            gkv = sb([128, 2], F32, "gkv")
            load_w_bf16(wdq, w_dq, 8, "wdq")
            P.op("pool", I("dma_start", out=wdkv[:, :, 0:320], in_=w_dkv.rearrange("(c p) n -> p c n", p=128)),
                 writes=["wdkv"], slot="wdkv")
            load_w_bf16(wuq, w_uq, 3, "wuq")
            load_w_bf16(wukv, w_ukv, 2, "wukv")
            for m in range(3):
                P.op("sp", I("dma_start", out=gq[:, m:m + 1], in_=q_norm.rearrange("(c p o) -> c p o", p=128, o=1)[m]),
                     writes=["gains"], slot=("gq", m))
            for m in range(2):
                P.op("sp", I("dma_start", out=gkv[:, m:m + 1], in_=kv_norm.rearrange("(c p o) -> c p o", p=128, o=1)[m]),
                     writes=["gains"], slot=("gkv", m))
            P.op("pool", I("tensor_scalar", out=wdkv[:, :, 320:352], in0=wdkv[:, :, 288:320], scalar1=-1.0, scalar2=None, op0=ALU.mult),
                 reads=["wdkv"], writes=["wdkvr"])
            P.op("pool", I("tensor_copy", out=wdkv[:, :, 352:384], in_=wdkv[:, :, 256:288]), reads=["wdkv"], writes=["wdkvr2"])
            wuq4 = wuq[:].rearrange("p c (h d) -> p c h d", d=192)
            for c in range(3):
                P.op("pool", I("tensor_scalar", out=wuqr[:, c, :, 0:32], in0=wuq4[:, c, :, 160:192], scalar1=-1.0, scalar2=None,
                                                             op0=ALU.mult), reads=[("wuq", 0)], writes=[("wuqr", c, 0)])
                P.op("pool", I("tensor_copy", out=wuqr[:, c, :, 32:64], in_=wuq4[:, c, :, 128:160]),
                     reads=[("wuq", 0)], writes=[("wuqr", c, 1)])
            wuqr_res = [("wuqr", c, k) for c in range(3) for k in range(2)]
            wdkv_res = ["wdkv", "wdkvr", "wdkvr2"]

            xs_ring = sbring(sb, 2, [128, D], F32, "xs")
            xb_ring = sbring(sb, 2, [128, D], BF16, "xb")
            xT_ring = sbring(sb, 2, [128, 8, GW], BF16, "xT")
            pt_ring = Ring([ps([128, 8, 128], BF16, "pt")], "pt1")
            banks = [ps([128, GW], F32, "bk") for _ in range(7)]
            bres = [("bank1", i) for i in range(7)]
            sq_ring = sbring(sb, 3, [128, GW], BF16, "sq")
            ln_ring = sbring(sb, 1, [128, GW], F32, "lnss")
            rstd_ring = sbring(sb, 2, [128, GW], F32, "rstdfm")
            cqn_ring = sbring(sb, 2, [128, 3, GW], BF16, "cqn")
            cn_ring = sbring(sb, 2, [128, 2, GW], BF16, "cn")
            cos_ring = sbring(sb, 2, [64, GW], F32, "cosm")
            sin_ring = sbring(sb, 2, [64, GW], F32, "sinm")
            t1_ring = sbring(sb, 2, [64, GW], F32, "t1")
            t2_ring = sbring(sb, 2, [64, GW], F32, "t2")
            qnst_ring = sbring(sb, 2, [128, H, GW], BF16, "qnst")
            qrst_ring = sbring(sb, 2, [64, H, GW], BF16, "qrst")
            knst_ring = sbring(sb, 2, [128, H, GW], BF16, "knst")
            vst_ring = sbring(sb, 2, [128, 4, D], BF16, "vst")
            krst_ring = sbring(sb, 2, [64, GW], BF16, "krst")
            bi = [0]

            def nb():
                bi[0] += 1
                k = bi[0] % 7
                return banks[k], bres[k]

            for g in range(NG):
                gs = slice(g * GW, (g + 1) * GW)
                cos_t, cos_r = cos_ring.next()
                sin_t, sin_r = sin_ring.next()
                P.op("sp", I("dma_start", out=cos_t[:], in_=cosm_d[:, gs]), writes=[cos_r], slot=cos_r)
                P.op("sp", I("dma_start", out=sin_t[:], in_=sinm_d[:, gs]), writes=[sin_r], slot=sin_r)
                xT_t, xT_r = xT_ring.next()
                for i in range(4):
                    t = g * 4 + i
                    xs_t, xs_r = xs_ring.next()
                    xb_t, xb_r = xb_ring.next()
                    P.op("sp", I("dma_start", out=xs_t[:], in_=x[t * 128:(t + 1) * 128, :]), writes=[xs_r], slot=xs_r)
                    P.op("pool", I("tensor_copy", out=xb_t[:], in_=xs_t[:]), reads=[xs_r], writes=[xb_r])
                    pt_t, pt_r = pt_ring.next()
                    for c in range(8):
                        P.op("pe", I("transpose", out=pt_t[:, c, :], in_=xb_t[:, c * 128:(c + 1) * 128],
                                                                                  identity=ident[:]), reads=[xb_r, "ident"], writes=[pt_r])
                    evac(xT_t[:, :, i * 128:(i + 1) * 128], pt_t[:], [pt_r], [(xT_r, i)])
                xT_all = [(xT_r, i) for i in range(4)]
                cqb = [nb() for _ in range(3)]
                for m in range(3):
                    for c in range(8):
                        P.op("pe", I("matmul", cqb[m][0][:], lhsT=wdq[:, c, m * 128:(m + 1) * 128], rhs=xT_t[:, c, :],
                                                                             start=(c == 0), stop=(c == 7)),
                             reads=xT_all + [("wdq", 0)], writes=[cqb[m][1]])
                ssb, ssr = nb()
                cqn_t, cqn_r = cqn_ring.next()
                rms_fm([b[0] for b in cqb], [b[1] for b in cqb], 3, 384, gq, cqn_t, cqn_r, sq_ring, ssb, ssr, ln_ring, rstd_ring, ones)
                cqn_all = [(cqn_r, m) for m in range(3)]
                cb = [nb() for _ in range(2)]
                for m in range(2):
                    for c in range(8):
                        P.op("pe", I("matmul", cb[m][0][:], lhsT=wdkv[:, c, m * 128:(m + 1) * 128], rhs=xT_t[:, c, :],
                                                                            start=(c == 0), stop=(c == 7)),
                             reads=xT_all + wdkv_res, writes=[cb[m][1]])
                ssb, ssr = nb()
                cn_t, cn_r = cn_ring.next()
                rms_fm([b[0] for b in cb], [b[1] for b in cb], 2, 256, gkv, cn_t, cn_r, sq_ring, ssb, ssr, ln_ring, rstd_ring, ones)
                cn_all = [(cn_r, m) for m in range(2)]
                ab, ar = nb()
                bb, br = nb()
                for c in range(8):
                    P.op("pe", I("matmul", ab[0:64, :], lhsT=wdkv[:, c, 256:320], rhs=xT_t[:, c, :], start=(c == 0), stop=(c == 7)),
                         reads=xT_all + wdkv_res, writes=[ar])
                for c in range(8):
                    P.op("pe", I("matmul", bb[0:64, :], lhsT=wdkv[:, c, 320:384], rhs=xT_t[:, c, :], start=(c == 0), stop=(c == 7)),
                         reads=xT_all + wdkv_res, writes=[br])
                t1, t1r = t1_ring.next()
                t2, t2r = t2_ring.next()
                kr_t, kr_r = krst_ring.next()
                P.op("dve", I("tensor_tensor", out=t1[:], in0=ab[0:64, :], in1=cos_t[:], op=ALU.mult),
                     reads=[ar, cos_r], writes=[t1r])
                P.op("dve", I("tensor_tensor", out=t2[:], in0=bb[0:64, :], in1=sin_t[:], op=ALU.mult),
                     reads=[br, sin_r], writes=[t2r])
                P.op("pool", I("tensor_tensor", out=kr_t[:], in0=t1[:], in1=t2[:], op=ALU.add),
                     reads=[t1r, t2r], writes=[kr_r])
                P.op("sp", I("dma_start", out=KR0[:, gs], in_=kr_t[:]), reads=[kr_r], writes=[("KR0", g)], slot=kr_r)
                qn_t, qn_r = qnst_ring.next()
                qr_t, qr_r = qrst_ring.next()
                kn_t, kn_r = knst_ring.next()
                for h in range(H):
                    bk, bkr = nb()
                    for c in range(3):
                        P.op("pe", I("matmul", bk[:], lhsT=wuq[:, c, h * 192:h * 192 + 128], rhs=cqn_t[:, c, :],
                                                                       start=(c == 0), stop=(c == 2)), reads=cqn_all + [("wuq", 0)], writes=[bkr])
                    evac(qn_t[:, h, :], bk[:], [bkr], [(qn_r, h)], scale=SC0)
                    ab, ar = nb()
                    bb, br = nb()
                    for c in range(3):
                        P.op("pe", I("matmul", ab[0:64, :], lhsT=wuq[:, c, h * 192 + 128:h * 192 + 192], rhs=cqn_t[:, c, :],
                                                                       start=(c == 0), stop=(c == 2)), reads=cqn_all + [("wuq", 0)], writes=[ar])
                    for c in range(3):
                        P.op("pe", I("matmul", bb[0:64, :], lhsT=wuqr[:, c, h, :], rhs=cqn_t[:, c, :],
                                                                       start=(c == 0), stop=(c == 2)), reads=cqn_all + wuqr_res, writes=[br])
                    t1, t1r = t1_ring.next()
                    t2, t2r = t2_ring.next()
                    P.op("dve", I("scalar_tensor_tensor", out=t1[:], in0=ab[0:64, :], scalar=SC0, in1=cos_t[:],
                                                                                          op0=ALU.mult, op1=ALU.mult), reads=[ar, cos_r], writes=[t1r])
                    P.op("dve", I("scalar_tensor_tensor", out=t2[:], in0=bb[0:64, :], scalar=SC0, in1=sin_t[:],
                                                                                          op0=ALU.mult, op1=ALU.mult), reads=[br, sin_r], writes=[t2r])
                    P.op("pool", I("tensor_tensor", out=qr_t[:, h, :], in0=t1[:], in1=t2[:], op=ALU.add),
                         reads=[t1r, t2r], writes=[(qr_r, h)])
                    bk, bkr = nb()
                    for c in range(2):
                        P.op("pe", I("matmul", bk[:], lhsT=wukv[:, c, h * 256:h * 256 + 128], rhs=cn_t[:, c, :],
                                                                       start=(c == 0), stop=(c == 1)), reads=cn_all + [("wukv", 0)], writes=[bkr])
                    evac(kn_t[:, h, :], bk[:], [bkr], [(kn_r, h)])
                P.op("sp", I("dma_start", out=QN0.rearrange("h p t -> p h t")[:, :, gs], in_=qn_t[:]),
                     reads=[(qn_r, h) for h in range(H)], writes=[("QN0", g)], slot=qn_r)
                P.op("sp", I("dma_start", out=QR0.rearrange("h p t -> p h t")[:, :, gs], in_=qr_t[:]),
                     reads=[(qr_r, h) for h in range(H)], writes=[("QR0", g)], slot=qr_r)
                P.op("sp", I("dma_start", out=KN0.rearrange("h p t -> p h t")[:, :, gs], in_=kn_t[:]),
                     reads=[(kn_r, h) for h in range(H)], writes=[("KN0", g)], slot=kn_r)
                v_t, v_r = vst_ring.next()
                wv4 = wukv[:].rearrange("p c (h d) -> p c h d", d=256)
                for i in range(4):
                    for hh in range(2):
                        bk, bkr = nb()
                        for c in range(2):
                            P.op("pe", I("matmul", bk[:].rearrange("p (h d) -> p h d", d=128),
                                                                               lhsT=cn_t[:, c, i * 128:(i + 1) * 128],
                                                                               rhs=wv4[:, c, hh * 4:(hh + 1) * 4, 128:256], start=(c == 0), stop=(c == 1)),
                                 reads=cn_all + [("wukv", 0)], writes=[bkr])
                        evac(v_t[:, i, hh * 512:(hh + 1) * 512], bk[:], [bkr], [(v_r, i, hh)])
                P.op("sp", I("dma_start", out=V0[g * GW:(g + 1) * GW, :].rearrange("(i p) n -> p i n", p=128), in_=v_t[:]),
                     reads=[(v_r, i, hh) for i in range(4) for hh in range(2)], writes=[("V0", g)], slot=v_r)
            P.barrier()

    def attention(layer, GQ=GW):
        st, sb, ps = scope()
        with st:
            nmap = 1 if layer == 0 else 2
            NSUB = GQ // 128
            NGQ = S // GQ
            UNIT = False
            if UNIT:
                S_ring = Ring([ps([128, 1, GQ], F32, "S") for _ in range(4)], "S1u")
                NSET = 1
            elif nmap * GQ <= 512:
                S_ring = Ring([ps([128, nmap, GQ], F32, "S") for _ in range(3)], "S0")
                NSET = 2
            else:
                S_ring = Ring([ps([128, 2, GQ], F32, "S") for _ in range(2)], "S1")
                NSET = 1
            nacc = NSUB * nmap
            nbank = (nacc + 2) // 3
            O_sets = [[ps([128, GW], F32, "O") for _ in range(nbank)] for _ in range(NSET)]
            pt = ps([128, 8, 128], BF16, "ptA")
            AST = 130
            VST = 130
            NV = 129

            def acc_ap(set_i, a, lo=0, hi=NV):
                return O_sets[set_i][a // 3][:, (a % 3) * AST + lo:(a % 3) * AST + hi]

            C = load_consts(sb, need_ident=True, need_mask=True)
            ident, mask = C["ident"], C["mask"]
            if layer == 0:
                qn_ring = sbring(sb, 2, [128, S], BF16, "qn")
                qr_ring = sbring(sb, 2, [64, S], BF16, "qr")
                kn_ring = sbring(sb, 2, [128, S], BF16, "kn")
                kr = sb([64, S], BF16, "kr")
                P.op("sp", I("dma_start", out=kr[:], in_=KR0[:, :]), writes=["kr"], slot="kr")
                Vd, OTd = V0, OT0
            else:
                qn_ring = sbring(sb, 2, [128, S], BF16, "q1")
                kn_ring = sbring(sb, 2, [128, S], BF16, "k1")
                Vd, OTd = V1, OT1
                mh = sb([128, 1], F32, "mh")
                P.op("pool", I("memset", mh[:], -0.5), writes=["mh"])
                lt = [sb([128, 64], F32, "lt") for _ in range(4)]
                for k_, src in enumerate((lq1, lk1, lq2, lk2)):
                    P.op("sp", I("dma_start", out=lt[k_][:], in_=src[0:1, :].partition_broadcast(128)),
                         writes=[("lt", k_)], slot=("lt", k_))
                pr = [sb([128, 64], F32, "pr") for _ in range(2)]
                sm = [sb([128, 1], F32, "lsm") for _ in range(2)]
                ex = [sb([128, 1], F32, "lex") for _ in range(2)]
                neglam = sb([128, 1], F32, "neglam")
                gvec = sb([128, 128], F32, "gvec")
                junk = sb([128, 64], F32, "junk")
                for k_ in range(2):
                    P.op("dve", I("tensor_tensor", out=pr[k_][:], in0=lt[2 * k_][:], in1=lt[2 * k_ + 1][:], op=ALU.mult),
                         reads=[("lt", 2 * k_), ("lt", 2 * k_ + 1)], writes=[("pr", k_)])
                    P.op("act", I("activation", out=junk[:], in_=pr[k_][:], func=AF.Copy, accum_out=sm[k_][:]),
                         reads=[("pr", k_)], writes=[("lsm", k_), "junk"])
                    P.op("act", I("activation", out=ex[k_][:], in_=sm[k_][:], func=AF.Exp), reads=[("lsm", k_)], writes=[("lex", k_)])
                P.op("dve", I("tensor_tensor", out=neglam[:], in0=ex[1][:], in1=ex[0][:], op=ALU.subtract),
                     reads=[("lex", 0), ("lex", 1)], writes=["neglam0"])
                P.op("dve", I("tensor_scalar", out=neglam[:], in0=neglam[:], scalar1=-LAMBDA_INIT, scalar2=None, op0=ALU.add),
                     reads=["neglam0"], writes=["neglam"])
                P.op("sp", I("dma_start", out=gvec[:], in_=subln.rearrange("(o n) -> o n", o=1).partition_broadcast(128)), writes=["gvec0"], slot="gvec")
                P.op("pool", I("tensor_scalar", out=gvec[:], in0=gvec[:], scalar1=1.0 - LAMBDA_INIT, scalar2=None, op0=ALU.mult),
                     reads=["gvec0"], writes=["gvec"])
                o1_ring = sbring(sb, 3, [128, 128], F32, "o1")
                o_ring = sbring(sb, 3, [128, 128], F32, "o")
                junk2 = sb([128, 128], F32, "junk2")
                ss_ring = sbring(sb, 3, [128, 1], F32, "ss")
                rstd_ring = sbring(sb, 3, [128, 1], F32, "rstdA")
                c2_ring = sbring(sb, 3, [128, 1], F32, "c2")
            v_ring = sbring(sb, 2, [128, NT, VST], BF16, "v")
            for k_ in range(2):
                P.op("pool", I("memset", v_ring.tiles[k_][:, :, 128:130], 1.0), writes=[(("vones", k_))])
            pT_ring = sbring(sb, 4, [128, nmap, GQ], BF16, "pT")
            rinv_ring = sbring(sb, 4 * nmap, [128, 1], F32, "rinv")
            ob_ring = sbring(sb, 3, [128, 128], BF16, "ob")
            ost_ring = sbring(sb, 2, [128, GQ], BF16, "ost")
            LOOK = 2 if NSET == 2 else 1
            heads = {}

            def load_head(h):
                if h >= H:
                    return
                d = {}
                d["qn"] = qn_ring.next()
                d["kn"] = kn_ring.next()
                d["v"] = v_ring.next()
                if layer == 0:
                    d["qr"] = qr_ring.next()
                    P.op("sp", I("dma_start", out=d["qn"][0][:], in_=QN0[h]), writes=[d["qn"][1]], slot=d["qn"][1])
                    P.op("sp", I("dma_start", out=d["qr"][0][:], in_=QR0[h]), writes=[d["qr"][1]], slot=d["qr"][1])
                    P.op("sp", I("dma_start", out=d["kn"][0][:], in_=KN0[h]), writes=[d["kn"][1]], slot=d["kn"][1])
                else:
                    P.op("sp", I("dma_start", out=d["qn"][0][:], in_=QT1[h]), writes=[d["qn"][1]], slot=d["qn"][1])
                    P.op("sp", I("dma_start", out=d["kn"][0][:], in_=KT1[h]), writes=[d["kn"][1]], slot=d["kn"][1])
                vk = v_ring.i % 2
                vsrc = Vd.rearrange("(t p) (h d) -> h p t d", p=128, d=128)[h]
                for q_ in range(4):
                    P.op("sp", I("dma_start", out=d["v"][0][:, q_ * 8:(q_ + 1) * 8, 0:128], in_=vsrc[:, q_ * 8:(q_ + 1) * 8, :]),
                         reads=[("vones", vk)], writes=[(d["v"][1], q_)], slot=(d["v"][1], q_))
                heads[h] = d

            set_ctr = [0]
            load_head(0)
            for h in range(H):
                load_head(h + 1)
                hd = heads.pop(h)
                qn_t, qn_r = hd["qn"]
                kn_t, kn_r = hd["kn"]
                v_t, v_r = hd["v"]
                if layer == 0:
                    qr_t, qr_r = hd["qr"]
                pairs = [(g, kt) for g in range(NGQ) for kt in range(NSUB * g + NSUB)]
                state = {}
                gstate = {}
                pending = []

                def emit_qk(n):
                    g, kt = pairs[n]
                    j = max(kt - NSUB * g, 0)
                    q0 = g * GQ + 128 * j
                    q1 = (g + 1) * GQ
                    ks = slice(kt * 128, (kt + 1) * 128)
                    diag = kt - NSUB * g >= 0
                    S_t, S_r = S_ring.next()
                    for m in range(nmap):
                        so = S_t[:, m, 128 * j:GQ]
                        if layer == 0:
                            P.op("pe", I("matmul", so, lhsT=kn_t[:, ks], rhs=qn_t[:, q0:q1], start=True, stop=False),
                                 reads=[kn_r, qn_r], writes=[(S_r, m)])
                            P.op("pe", I("matmul", so, lhsT=kr[:, ks], rhs=qr_t[:, q0:q1], start=False, stop=not diag),
                                 reads=["kr", qr_r], writes=[(S_r, m)])
                        else:
                            lo, hi = m * 64, (m + 1) * 64
                            P.op("pe", I("matmul", so, lhsT=kn_t[lo:hi, ks], rhs=qn_t[lo:hi, q0:q1], start=True, stop=not diag),
                                 reads=[kn_r, qn_r], writes=[(S_r, m)])
                        if diag:
                            P.op("pe", I("matmul", S_t[:, m, 128 * j:128 * j + 128], lhsT=ident[:], rhs=mask[:, 0, 0:128], start=False, stop=True),
                                 reads=["ident", "mask"], writes=[(S_r, m)])
                    state[n] = (S_t, S_r)

                def emit_rest(n):
                    g, kt = pairs[n]
                    j = max(kt - NSUB * g, 0)
                    S_t, S_r = state.pop(n)
                    if kt == 0:
                        set_ctr[0] += 1
                        gstate[g] = dict(set=set_ctr[0] % NSET, started=set())
                    gs_ = gstate[g]
                    si = gs_["set"]
                    pT_t, pT_r = pT_ring.next()
                    for m in range(nmap):
                        P.op("act", I("activation", out=pT_t[:, m, 128 * j:GQ], in_=S_t[:, m, 128 * j:GQ], func=AF.Exp),
                             reads=[(S_r, m)], writes=[(pT_r, m)])
                    if kt == 0 and pending and NSET == 1:
                        P.op("pe", I("transpose", out=pt[:, 7, :], in_=ident[:], identity=ident[:]), reads=["ident"], writes=[("ptA", 7), "pe_tick"])
                        flush_pending()
                    for m in range(nmap):
                        for i in range(j, NSUB):
                            a = m * NSUB + i
                            bank = a // 3
                            first = bank not in gs_["started"]
                            gs_["started"].add(bank)
                            P.op("pe", I("matmul", acc_ap(si, a), lhsT=pT_t[:, m, i * 128:(i + 1) * 128], rhs=v_t[:, kt, 0:NV],
                                         start=first, stop=(kt == NSUB * g + i), skip_group_check=True),
                                 reads=[(pT_r, m), (v_r, kt // 8)], writes=[("acc", si, a), "pe_tick"])
                    flush_pending()
                    if kt >= NSUB * g:
                        pending.append((g, kt - NSUB * g, si))

                def flush_pending():
                    while pending:
                        finish(*pending.pop(0))

                def finish(g, i, si):
                    if i == 0:
                        gstate[g]["ost"] = ost_ring.next()
                    ost_t, ost_r = gstate[g]["ost"]
                    ob_t, ob_r = ob_ring.next()
                    if layer == 0:
                        ri_t, ri_r = rinv_ring.next()
                        P.op("dve", I("reciprocal", out=ri_t[:], in_=acc_ap(si, i, 128, 129)), reads=[("acc", si, i), "pe_tick"], writes=[ri_r])
                        P.op("dve", I("tensor_scalar", out=ob_t[:], in0=acc_ap(si, i, 0, 128), scalar1=ri_t[:], scalar2=None, op0=ALU.mult),
                             reads=[("acc", si, i), ri_r], writes=[ob_r])
                    else:
                        ri1, ri1r = rinv_ring.next()
                        ri2, ri2r = rinv_ring.next()
                        c2_t, c2_r = c2_ring.next()
                        o1_t, o1_r = o1_ring.next()
                        o_t, o_r = o_ring.next()
                        ss_t, ss_r = ss_ring.next()
                        rs_t, rs_r = rstd_ring.next()
                        P.op("dve", I("reciprocal", out=ri1[:], in_=acc_ap(si, i, 128, 129)), reads=[("acc", si, i), "pe_tick"], writes=[ri1r])
                        P.op("dve", I("reciprocal", out=ri2[:], in_=acc_ap(si, NSUB + i, 128, 129)), reads=[("acc", si, NSUB + i)], writes=[ri2r])
                        P.op("dve", I("tensor_tensor", out=c2_t[:], in0=ri2[:], in1=neglam[:], op=ALU.mult), reads=[ri2r, "neglam"], writes=[c2_r])
                        P.op("dve", I("tensor_scalar", out=o1_t[:], in0=acc_ap(si, i, 0, 128), scalar1=ri1[:], scalar2=None, op0=ALU.mult),
                             reads=[("acc", si, i), ri1r], writes=[o1_r])
                        P.op("dve", I("scalar_tensor_tensor", out=o_t[:], in0=acc_ap(si, NSUB + i, 0, 128), scalar=c2_t[:], in1=o1_t[:],
                                      op0=ALU.mult, op1=ALU.add), reads=[("acc", si, NSUB + i), c2_r, o1_r], writes=[o_r])
                        P.op("dve", I("tensor_tensor", out=junk2[:], in0=o_t[:], in1=o_t[:], op=ALU.mult), reads=[o_r], writes=["junk2"])
                        P.op("dve", I("tensor_reduce", out=ss_t[:], in_=junk2[:], axis=mybir.AxisListType.X, op=ALU.add), reads=["junk2"], writes=[ss_r])
                        P.op("dve", I("tensor_scalar", out=ss_t[:], in0=ss_t[:], scalar1=1.0 / 128, scalar2=RMS_EPS, op0=ALU.mult, op1=ALU.add),
                             reads=[ss_r], writes=[ss_r])
                        P.op("act", I("activation", out=ss_t[:], in_=ss_t[:], func=AF.Ln), reads=[ss_r], writes=[ss_r])
                        P.op("act", I("activation", out=rs_t[:], in_=ss_t[:], func=AF.Exp, scale=-0.5), reads=[ss_r], writes=[rs_r])
                        P.op("dve", I("scalar_tensor_tensor", out=ob_t[:], in0=o_t[:], scalar=rs_t[:], in1=gvec[:], op0=ALU.mult, op1=ALU.mult),
                             reads=[o_r, rs_r, "gvec"], writes=[ob_r])
                    P.op("pe", I("transpose", out=pt[:, i, :], in_=ob_t[:], identity=ident[:]), reads=[ob_r, "ident"], writes=[("ptA", i)])
                    evac(ost_t[:, i * 128:(i + 1) * 128], pt[:, i, :], [("ptA", i)], [(ost_r, i)], eng="dve")
                    if i == NSUB - 1:
                        P.op("sp", I("dma_start", out=OTd[h][:, g * GQ:(g + 1) * GQ], in_=ost_t[:]), reads=[(ost_r, k_) for k_ in range(NSUB)], slot=ost_r)

                units = [(g, kt, m) for (g, kt) in pairs for m in range(nmap)]

                def emit_qk_u(u):
                    g, kt, m = units[u]
                    j = max(kt - NSUB * g, 0)
                    q0 = g * GQ + 128 * j
                    q1 = (g + 1) * GQ
                    ks = slice(kt * 128, (kt + 1) * 128)
                    diag = kt - NSUB * g >= 0
                    S_t, S_r = S_ring.next()
                    lo, hi = m * 64, (m + 1) * 64
                    P.op("pe", I("matmul", S_t[:, 0, 128 * j:GQ], lhsT=kn_t[lo:hi, ks], rhs=qn_t[lo:hi, q0:q1], start=True, stop=not diag),
                         reads=[kn_r, qn_r], writes=[S_r])
                    if diag:
                        P.op("pe", I("matmul", S_t[:, 0, 128 * j:128 * j + 128], lhsT=ident[:], rhs=mask[:, 0, 0:128], start=False, stop=True),
                             reads=["ident", "mask"], writes=[S_r])
                    state[("u", u)] = (S_t, S_r)

                def emit_rest_u(u):
                    g, kt, m = units[u]
                    j = max(kt - NSUB * g, 0)
                    S_t, S_r = state.pop(("u", u))
                    if m == 0:
                        if kt == 0:
                            set_ctr[0] += 1
                            gstate[g] = dict(set=set_ctr[0] % NSET, started=set())
                        state[("pT", g, kt)] = pT_ring.next()
                    gs_ = gstate[g]
                    si = gs_["set"]
                    pT_t, pT_r = state[("pT", g, kt)]
                    P.op("act", I("activation", out=pT_t[:, m, 128 * j:GQ], in_=S_t[:, 0, 128 * j:GQ], func=AF.Exp),
                         reads=[S_r], writes=[(pT_r, m)])
                    if m == 0 and kt == 0 and pending:
                        P.op("pe", I("transpose", out=pt[:, 7, :], in_=ident[:], identity=ident[:]), reads=["ident"], writes=[("ptA", 7), "pe_tick"])
                        flush_pending()
                    for i in range(j, NSUB):
                        a_ = m * NSUB + i
                        bank = a_ // 3
                        first = bank not in gs_["started"]
                        gs_["started"].add(bank)
                        P.op("pe", I("matmul", acc_ap(si, a_), lhsT=pT_t[:, m, i * 128:(i + 1) * 128], rhs=v_t[:, kt, 0:NV],
                                     start=first, stop=(kt == NSUB * g + i), skip_group_check=True),
                             reads=[(pT_r, m), (v_r, kt // 8)], writes=[("acc", si, a_), "pe_tick"])
                    if m == nmap - 1:
                        state.pop(("pT", g, kt))
                        flush_pending()
                        if kt >= NSUB * g:
                            pending.append((g, kt - NSUB * g, si))

                if UNIT:
                    LOOKU = 3
                    NU = len(units)
                    for u in range(min(LOOKU, NU)):
                        emit_qk_u(u)
                    for u in range(NU):
                        if u + LOOKU < NU:
                            emit_qk_u(u + LOOKU)
                        emit_rest_u(u)
                else:
                    N = len(pairs)
                    for n in range(min(LOOK, N)):
                        emit_qk(n)
                    for n in range(N):
                        if n + LOOK < N:
                            emit_qk(n + LOOK)
                        emit_rest(n)
                P.op("pe", I("transpose", out=pt[:, 7, :], in_=ident[:], identity=ident[:]), reads=["ident"], writes=[("ptA", 7), "pe_tick"])
                flush_pending()
            P.barrier()

    def attention_old(layer):
        assert layer == 0
        st, sb, ps = scope()
        with st:
            C = load_consts(sb, need_ident=True, need_mask=True)
            ident, mask = C["ident"], C["mask"]
            ones = sb([128, 128], BF16, "ones")
            P.op("pool", I("memset", ones[:], 1.0), writes=["ones"])
            qn_ring = sbring(sb, 2, [128, S], BF16, "qn")
            qr_ring = sbring(sb, 2, [64, S], BF16, "qr")
            kn_ring = sbring(sb, 2, [128, S], BF16, "kn")
            kr = sb([64, S], BF16, "kr")
            P.op("sp", I("dma_start", out=kr[:], in_=KR0[:, :]), writes=["kr"], slot="kr")
            v_ring = sbring(sb, 2, [128, NT, 128], BF16, "v")
            pT_ring = sbring(sb, 5, [128, GW], BF16, "pT")
            S_ring = Ring([ps([128, GW], F32, "S") for _ in range(4)], "S0o")
            O_ring = Ring([ps([128, GW], F32, "O") for _ in range(2)], "O0o")
            M_ring = Ring([ps([128, GW], F32, "M") for _ in range(2)], "M0o")
            ln_ring = sbring(sb, 2, [128, GW], F32, "lnM")
            rs_ring = sbring(sb, 2, [128, GW], F32, "rs")
            ost_ring = sbring(sb, 2, [128, GW], BF16, "ost")
            LOOK = 2
            heads = {}

            def load_head(h):
                if h >= H:
                    return
                d = dict(qn=qn_ring.next(), qr=qr_ring.next(), kn=kn_ring.next(), v=v_ring.next())
                P.op("sp", I("dma_start", out=d["qn"][0][:], in_=QN0[h]), writes=[d["qn"][1]], slot=d["qn"][1])
                P.op("sp", I("dma_start", out=d["qr"][0][:], in_=QR0[h]), writes=[d["qr"][1]], slot=d["qr"][1])
                P.op("sp", I("dma_start", out=d["kn"][0][:], in_=KN0[h]), writes=[d["kn"][1]], slot=d["kn"][1])
                vsrc = V0.rearrange("(t p) (h d) -> h p t d", p=128, d=128)[h]
                for q_ in range(4):
                    P.op("sp", I("dma_start", out=d["v"][0][:, q_ * 8:(q_ + 1) * 8, :], in_=vsrc[:, q_ * 8:(q_ + 1) * 8, :]),
                         writes=[(d["v"][1], q_)], slot=(d["v"][1], q_))
                heads[h] = d

            load_head(0)
            for h in range(H):
                load_head(h + 1)
                hd = heads.pop(h)
                qn_t, qn_r = hd["qn"]
                qr_t, qr_r = hd["qr"]
                kn_t, kn_r = hd["kn"]
                v_t, v_r = hd["v"]
                pairs = [(g, kt) for g in range(NG) for kt in range(4 * g + 4)]
                state = {}

                def emit_qk(n):
                    g, kt = pairs[n]
                    j = max(kt - 4 * g, 0)
                    c0 = 128 * j
                    qs = slice(g * GW + c0, (g + 1) * GW)
                    ks = slice(kt * 128, (kt + 1) * 128)
                    diag = kt >= 4 * g
                    S_t, S_r = S_ring.next()
                    P.op("pe", I("matmul", S_t[:, c0:GW], lhsT=kn_t[:, ks], rhs=qn_t[:, qs], start=True, stop=False),
                         reads=[kn_r, qn_r], writes=[S_r])
                    P.op("pe", I("matmul", S_t[:, c0:GW], lhsT=kr[:, ks], rhs=qr_t[:, qs], start=False, stop=not diag),
                         reads=["kr", qr_r], writes=[S_r])
                    if diag:
                        P.op("pe", I("matmul", S_t[:, c0:c0 + 128], lhsT=ident[:], rhs=mask[:, 0, 0:128], start=False, stop=True),
                             reads=["ident", "mask"], writes=[S_r])
                    state[n] = (S_t, S_r)

                def emit_rest(n):
                    g, kt = pairs[n]
                    last = 4 * g + 3
                    j = max(kt - 4 * g, 0)
                    c0 = 128 * j
                    S_t, S_r = state.pop(n)
                    if kt == 0:
                        state["O"] = O_ring.next()
                        state["M"] = M_ring.next()
                    pT_t, pT_r = pT_ring.next()
                    P.op("act", I("activation", out=pT_t[:, c0:GW], in_=S_t[:, c0:GW], func=AF.Exp), reads=[S_r], writes=[pT_r])
                    O_t, O_r = state["O"]
                    M_t, M_r = state["M"]
                    P.op("pe", I("matmul", O_t[:, c0:GW], lhsT=v_t[:, kt, :], rhs=pT_t[:, c0:GW], start=(kt == 0), stop=(kt == last)),
                         reads=[(v_r, kt // 8), pT_r], writes=[O_r])
                    P.op("pe", I("matmul", M_t[:, c0:GW], lhsT=ones[:], rhs=pT_t[:, c0:GW], start=(kt == 0), stop=(kt == last)),
                         reads=["ones", pT_r], writes=[M_r])
                    if kt == last:
                        finish(g)

                def finish(g):
                    gs = slice(g * GW, (g + 1) * GW)
                    ost_t, ost_r = ost_ring.next()
                    O_t, O_r = state["O"]
                    M_t, M_r = state["M"]
                    ln_t, ln_r = ln_ring.next()
                    rs_t, rs_r = rs_ring.next()
                    P.op("act", I("activation", out=ln_t[:], in_=M_t[:], func=AF.Ln), reads=[M_r], writes=[ln_r])
                    P.op("act", I("activation", out=rs_t[:], in_=ln_t[:], func=AF.Exp, scale=-1.0), reads=[ln_r], writes=[rs_r])
                    P.op("dve", I("tensor_tensor", out=ost_t[:], in0=O_t[:], in1=rs_t[:], op=ALU.mult), reads=[O_r, rs_r], writes=[ost_r])
                    P.op("sp", I("dma_start", out=OT0[h][:, gs], in_=ost_t[:]), reads=[ost_r], slot=ost_r)

                N = len(pairs)
                for n in range(min(LOOK, N)):
                    emit_qk(n)
                for n in range(N):
                    if n + LOOK < N:
                        emit_qk(n + LOOK)
                    emit_rest(n)
            P.barrier()

    def outproj_ln(layer, wgu_pre=None):
        st, sb, ps = scope()
        OTd = OT0 if layer == 0 else OT1
        wo_d = w_o0 if layer == 0 else w_o1
        res_d = x if layer == 0 else H2
        Hd, HTd = (H1, H1T) if layer == 0 else (H3, H3T)
        with st:
            C = load_consts(sb, need_ident=True, need_mh=True)
            ident, mh = C["ident"], C["mh"]
            wo = sb([128, 8, D], BF16, "wo")
            load_w_bf16(wo, wo_d, 8, "wo")
            Gt = sb([128, D], F32, "lnG")
            Bt = sb([128, D], F32, "lnB")
            P.op("sp", I("dma_start", out=Gt[:], in_=ln1_g[layer:layer + 1, :].partition_broadcast(128)), writes=["lnG"], slot="lnG")
            P.op("sp", I("dma_start", out=Bt[:], in_=ln1_b[layer:layer + 1, :].partition_broadcast(128)), writes=["lnB"], slot="lnB")
            ot_ring = sbring(sb, 2, [128, H, GW], BF16, "otg")
            res_ring = sbring(sb, 4, [128, D], F32, "res")
            pt_ring = Ring([ps([128, 8, 128], BF16, "pt") for _ in range(2)], "pt3")
            a_ring = Ring([ps([128, GW], F32, "a") for _ in range(6)], "a3")
            lnp = LNPipe(sb, ps, Gt, Bt, ident, mh, HTd, pt_ring)
            res_q = {}
            ot_q = {}

            def load_res(t):
                if t < NT:
                    res_t, res_r = res_ring.next()
                    P.op("sp", I("dma_start", out=res_t[:], in_=res_d[t * 128:(t + 1) * 128, :]), writes=[(res_r, 0), (res_r, 1)], slot=res_r)
                    res_q[t] = (res_t, res_r)

            def load_ot(g):
                if g < NG:
                    ot_t, ot_r = ot_ring.next()
                    P.op("sp", I("dma_start", out=ot_t[:], in_=OTd.rearrange("h p t -> p h t")[:, :, g * GW:(g + 1) * GW]), writes=[ot_r], slot=ot_r)
                    ot_q[g] = (ot_t, ot_r)

            load_ot(0)
            load_res(0)
            load_res(1)
            for g in range(NG):
                ot_t, ot_r = ot_q.pop(g)
                load_ot(g + 1)
                for i in range(4):
                    t = g * 4 + i
                    load_res(t + 2)
                    if wgu_pre is not None and t < 22:
                        half, s_ = t % 2, t // 2
                        lo = half * DFF + s_ * 256
                        vsrc = w_gu[layer].rearrange("(c p) n -> p c n", p=128)
                        P.op("pool", I("dma_start", out=wgu_pre[:, :, lo:lo + 256], in_=vsrc[:, :, lo:lo + 256]),
                             writes=[("wgu", half, s_)], slot=("wgu", half, s_))
                    res_t, res_r = res_q.pop(t)
                    z_t, z_r = res_t, res_r
                    for hh in range(2):
                        a_t, a_r = a_ring.next()
                        for h in range(H):
                            P.op("pe", I("matmul", a_t[:], lhsT=ot_t[:, h, i * 128:(i + 1) * 128],
                                                                                           rhs=wo[:, h, hh * 512:(hh + 1) * 512], start=(h == 0), stop=(h == H - 1)),
                                 reads=[ot_r, ("wo", 0)], writes=[a_r])
                        P.op("dve", I("scalar_tensor_tensor",
                            out=z_t[:, hh * 512:(hh + 1) * 512], in0=res_t[:, hh * 512:(hh + 1) * 512], scalar=ALPHA, in1=a_t[:], op0=ALU.mult, op1=ALU.add),
                            reads=[(res_r, hh), a_r], writes=[(z_r, hh)])
                    lnp.push(z_t, [(z_r, 0), (z_r, 1)], Hd[t * 128:(t + 1) * 128, :], t)
            lnp.flush()
            P.barrier()

    def ffn_ln(layer, wgu_pre=None):
        st, sb, ps = scope()
        HTin = H1T if layer == 0 else H3T
        Hin = H1 if layer == 0 else H3
        Hd, HTd = (H2, H2T) if layer == 0 else (out, None)
        with st:
            C = load_consts(sb, need_ident=True, need_mh=True)
            ident, mh = C["ident"], C["mh"]
            wdn = sb([128, NF, D], BF16, "wdn")
            NSPL = 11
            v = w_gu[layer].rearrange("(c p) n -> p c n", p=128)
            if wgu_pre is not None:
                wgu = wgu_pre
            else:
                wgu = sb([128, 8, 2 * DFF], BF16, "wgu")
                for s_ in range(NSPL):
                    for half in range(2):
                        lo = half * DFF + s_ * 256
                        P.op("pool", I("dma_start", out=wgu[:, :, lo:lo + 256], in_=v[:, :, lo:lo + 256]),
                             writes=[("wgu", half, s_)], slot=("wgu", half, s_))
            vd = w_dn[layer].rearrange("(f p) n -> p f n", p=128)
            for s_ in range(2):
                P.op("pool", I("dma_start", out=wdn[:, s_ * 11:(s_ + 1) * 11, :], in_=vd[:, s_ * 11:(s_ + 1) * 11, :]),
                     writes=[("wdn", s_)], slot=("wdn", s_))
            Gt = sb([128, D], F32, "lnG")
            Bt = sb([128, D], F32, "lnB")
            P.op("sp", I("dma_start", out=Gt[:], in_=ln2_g[layer:layer + 1, :].partition_broadcast(128)), writes=["lnG"], slot="lnG")
            P.op("sp", I("dma_start", out=Bt[:], in_=ln2_b[layer:layer + 1, :].partition_broadcast(128)), writes=["lnB"], slot="lnB")
            hin_ring = sbring(sb, 1, [128, 8, GW], BF16, "hin")
            actT = sb([128, NF, GW], BF16, "actT")
            sg_ring = sbring(sb, 2, [128, GW], F32, "sg")
            res_ring = sbring(sb, 3, [128, D], F32, "res")
            pt_ring = Ring([ps([128, 8, 128], BF16, "pt")], "pt4")
            lnp = LNPipe(sb, ps, Gt, Bt, ident, mh, HTd, pt_ring)
            g_ring = Ring([ps([128, GW], F32, "gb") for _ in range(2)], "gbk")
            u_ring = Ring([ps([128, GW], F32, "ub") for _ in range(2)], "ubk")
            d_ring = Ring([ps([128, GW], F32, "db") for _ in range(3)], "dbk")
            res_q = {}

            def load_res(t):
                res_t, res_r = res_ring.next()
                P.op("sp", I("dma_start", out=res_t[:], in_=Hin[t * 128:(t + 1) * 128, :]), writes=[(res_r, 0), (res_r, 1)], slot=res_r)
                res_q[t] = (res_t, res_r)

            for g in range(NG):
                gs = slice(g * GW, (g + 1) * GW)
                hin_t, hin_r = hin_ring.next()
                P.op("sp", I("dma_start", out=hin_t[:], in_=HTin.rearrange("c p t -> p c t")[:, :, gs]), writes=[hin_r], slot=hin_r)
                for f in range(NF):
                    gb, gr = g_ring.next()
                    ub, ur = u_ring.next()
                    wres = [("wgu", 0, f // 2), ("wgu", 1, f // 2)]
                    for c in range(8):
                        P.op("pe", I("matmul", gb[:], lhsT=wgu[:, c, f * 128:(f + 1) * 128], rhs=hin_t[:, c, :],
                                                                                start=(c == 0), stop=(c == 7)), reads=[hin_r, wres[0]], writes=[gr])
                    for c in range(8):
                        P.op("pe", I("matmul", ub[:], lhsT=wgu[:, c, DFF + f * 128:DFF + (f + 1) * 128], rhs=hin_t[:, c, :],
                                                                                start=(c == 0), stop=(c == 7)), reads=[hin_r, wres[1]], writes=[ur])
                    sg_t, sg_r = sg_ring.next()
                    P.op("act", I("activation", out=sg_t[:], in_=gb[:], func=AF.Silu), reads=[gr], writes=[sg_r])
                    P.op("dve", I("tensor_tensor", out=actT[:, f, :], in0=ub[:], in1=sg_t[:], op=ALU.mult),
                         reads=[ur, sg_r], writes=[("actT", f)])
                for i in range(4):
                    t = g * 4 + i
                    if i == 0:
                        load_res(t)
                    if i < 3:
                        load_res(t + 1)
                    res_t, res_r = res_q.pop(t)
                    z_t, z_r = res_t, res_r
                    for hh in range(2):
                        db, dr = d_ring.next()
                        for f in range(NF):
                            P.op("pe", I("matmul", db[:], lhsT=actT[:, f, i * 128:(i + 1) * 128],
                                                                              rhs=wdn[:, f, hh * 512:(hh + 1) * 512], start=(f == 0), stop=(f == NF - 1)),
                                 reads=[("actT", f), ("wdn", f // 11)], writes=[dr])
                        P.op("dve", I("scalar_tensor_tensor",
                            out=z_t[:, hh * 512:(hh + 1) * 512], in0=res_t[:, hh * 512:(hh + 1) * 512], scalar=ALPHA, in1=db[:], op0=ALU.mult, op1=ALU.add),
                            reads=[(res_r, hh), dr], writes=[(z_r, hh)])
                    lnp.push(z_t, [(z_r, 0), (z_r, 1)], Hd[t * 128:(t + 1) * 128, :], t)
            lnp.flush()
            P.barrier()

    def phase_proj1():
        st, sb, ps = scope()
        with st:
            wk = sb([128, 8, D], BF16, "wk")
            wkr = sb([128, 8, D], BF16, "wkr")
            wq = sb([128, 8, D], BF16, "wq")
            wqr = sb([128, 8, D], BF16, "wqr")
            wv = sb([128, 8, D], BF16, "wv")
            kvv = kv_w.rearrange("(c p) n -> p c n", p=128)
            P.op("pool", I("dma_start", out=wk[:], in_=kvv[:, :, 0:D]), writes=["wk"], slot="wk")
            P.op("pool", I("dma_start", out=wq[:], in_=w_q1.rearrange("(c p) n -> p c n", p=128)), writes=["wq"], slot="wq")
            P.op("pool", I("dma_start", out=wv[:], in_=kvv[:, :, D:2 * D]), writes=["wv"], slot="wv")
            for (src, dst, nm) in ((wk, wkr, "wk"), (wq, wqr, "wq")):
                s4 = src[:].rearrange("p c (b d) -> p c b d", d=64)
                d4 = dst[:].rearrange("p c (b d) -> p c b d", d=64)
                for c in range(8):
                    P.op("act", I("activation", out=d4[:, c, :, 0:32], in_=s4[:, c, :, 32:64], func=AF.Copy, scale=-1.0),
                         reads=[nm], writes=[(nm + "r", c, 0)])
                    P.op("dve", I("tensor_copy", out=d4[:, c, :, 32:64], in_=s4[:, c, :, 0:32]),
                         reads=[nm], writes=[(nm + "r", c, 1)])
            rres = {nm: [(nm + "r", c, k) for c in range(8) for k in range(2)] for nm in ("wk", "wq")}
            hin_ring = sbring(sb, 2, [128, 8, GW], BF16, "hin")
            cos_ring = sbring(sb, 2, [128, GW], F32, "cosd")
            sin_ring = sbring(sb, 2, [128, GW], F32, "sind")
            t1_ring = sbring(sb, 3, [128, GW], F32, "t1")
            t2_ring = sbring(sb, 3, [128, GW], F32, "t2")
            kst_ring = sbring(sb, 2, [128, H, GW], BF16, "kst")
            qst_ring = sbring(sb, 2, [128, H, GW], BF16, "qst")
            vst_ring = sbring(sb, 2, [128, 4, D], BF16, "vst")
            banks = Ring([ps([128, GW], F32, "bk") for _ in range(8)], "bank5")
            for g in range(NG):
                gs = slice(g * GW, (g + 1) * GW)
                hin_t, hin_r = hin_ring.next()
                cos_t, cos_r = cos_ring.next()
                sin_t, sin_r = sin_ring.next()
                P.op("sp", I("dma_start", out=hin_t[:], in_=H2T.rearrange("c p t -> p c t")[:, :, gs]), writes=[hin_r], slot=hin_r)
                P.op("sp", I("dma_start", out=cos_t[:], in_=cosd_d[:, gs]), writes=[cos_r], slot=cos_r)
                P.op("sp", I("dma_start", out=sin_t[:], in_=sind_d[:, gs]), writes=[sin_r], slot=sin_r)
                k_t, k_r = kst_ring.next()
                q_t, q_r = qst_ring.next()
                for (w_, wr_, nm, dst_t, dst_r, sc) in ((wk, wkr, "wk", k_t, k_r, 1.0), (wq, wqr, "wq", q_t, q_r, SC1)):
                    for h in range(H):
                        ab, ar = banks.next()
                        bb, br = banks.next()
                        for c in range(8):
                            P.op("pe", I("matmul", ab[:], lhsT=w_[:, c, h * 128:(h + 1) * 128], rhs=hin_t[:, c, :],
                                                                                           start=(c == 0), stop=(c == 7)), reads=[hin_r, nm], writes=[ar])
                        for c in range(8):
                            P.op("pe", I("matmul", bb[:], lhsT=wr_[:, c, h * 128:(h + 1) * 128], rhs=hin_t[:, c, :],
                                                                                             start=(c == 0), stop=(c == 7)), reads=[hin_r] + rres[nm], writes=[br])
                        t1, t1r = t1_ring.next()
                        t2, t2r = t2_ring.next()
                        P.op("dve", I("scalar_tensor_tensor", out=t1[:], in0=ab[:], scalar=sc, in1=cos_t[:],
                                                                                                   op0=ALU.mult, op1=ALU.mult), reads=[ar, cos_r], writes=[t1r])
                        P.op("dve", I("scalar_tensor_tensor", out=t2[:], in0=bb[:], scalar=sc, in1=sin_t[:],
                                                                                                   op0=ALU.mult, op1=ALU.mult), reads=[br, sin_r], writes=[t2r])
                        P.op("pool", I("tensor_tensor", out=dst_t[:, h, :], in0=t1[:], in1=t2[:], op=ALU.add),
                             reads=[t1r, t2r], writes=[(dst_r, h)])
                P.op("sp", I("dma_start", out=KT1.rearrange("h p t -> p h t")[:, :, gs], in_=k_t[:]),
                     reads=[(k_r, h) for h in range(H)], slot=k_r)
                P.op("sp", I("dma_start", out=QT1.rearrange("h p t -> p h t")[:, :, gs], in_=q_t[:]),
                     reads=[(q_r, h) for h in range(H)], slot=q_r)
                v_t, v_r = vst_ring.next()
                for i in range(4):
                    for hh in range(2):
                        bk, bkr = banks.next()
                        for c in range(8):
                            P.op("pe", I("matmul", bk[:], lhsT=hin_t[:, c, i * 128:(i + 1) * 128],
                                                                                           rhs=wv[:, c, hh * 512:(hh + 1) * 512], start=(c == 0), stop=(c == 7)),
                                 reads=[hin_r, "wv"], writes=[bkr])
                        evac(v_t[:, i, hh * 512:(hh + 1) * 512], bk[:], [bkr], [(v_r, i, hh)], eng="act")
                P.op("sp", I("dma_start", out=V1[g * GW:(g + 1) * GW, :].rearrange("(i p) n -> p i n", p=128), in_=v_t[:]),
                     reads=[(v_r, i, hh) for i in range(4) for hh in range(2)], slot=v_r)
            P.barrier()

    if 1 in phases:
        phase1()
    if 2 in phases:
        attention_old(0)
    def layer_tail(layer, pa, pb):
        if pa in phases and pb in phases:
            st0, sb0, ps0 = scope()
            with st0:
                wgu_pre = sb0([128, 8, 2 * DFF], BF16, "wgu")
                outproj_ln(layer, wgu_pre)
                ffn_ln(layer, wgu_pre)
        else:
            if pa in phases:
                outproj_ln(layer)
            if pb in phases:
                ffn_ln(layer)

    layer_tail(0, 3, 4)
    if 5 in phases:
        phase_proj1()
    if 6 in phases:
        attention(1)
    layer_tail(1, 7, 8)
    P.emit()
    return nc, P


def _rope_tables(dim):
    inv = (1.0 / (10000.0 ** (np.arange(0, dim, 2, dtype=np.float32) / np.float32(dim)))).astype(np.float32)
    ang = np.arange(S, dtype=np.float32)[:, None] * inv[None, :]
    ang = np.concatenate([ang, ang], axis=-1).astype(np.float32)
    return np.ascontiguousarray(np.cos(ang).T.astype(np.float32)), np.ascontiguousarray(np.sin(ang).T.astype(np.float32))


def _consts():
    ident = np.eye(128, dtype=np.float32).astype(ml_dtypes.bfloat16)
    ki = np.arange(128)[:, None, None]
    j = np.arange(4)[None, :, None]
    qi = np.arange(GW)[None, None, :]
    mask = np.where(qi >= 128 * j + ki, 0.0, NEG).astype(np.float32).astype(ml_dtypes.bfloat16)
    cm, sm = _rope_tables(64)
    cd = np.ascontiguousarray(np.concatenate([cm, cm], axis=0))
    sd = np.ascontiguousarray(np.concatenate([sm, sm], axis=0))
    return {"c_ident": ident, "c_mask": np.ascontiguousarray(mask), "c_cosm": cm, "c_sinm": sm, "c_cosd": cd, "c_sind": sd}


_SQUEEZE = ("mla_w_dq", "mla_q_norm", "mla_w_uq", "mla_w_dkv", "mla_kv_norm", "mla_w_ukv", "mla_w_o",
            "diff_w_q", "diff_subln", "diff_w_o")


def make_in_maps(inputs, n_cores=8):
    common = dict(_consts())
    for k, v in inputs.items():
        if k == "x":
            continue
        a = np.ascontiguousarray(np.asarray(v, dtype=np.float32))
        if k in _SQUEEZE:
            a = np.ascontiguousarray(a[0])
        common[k] = a
    xs = np.asarray(inputs["x"], dtype=np.float32)
    maps = []
    for c in range(n_cores):
        m = dict(common)
        m["x"] = np.ascontiguousarray(xs[c])
        maps.append(m)
    return maps


_CACHE = {}


def kernel(**inputs):
    if "nc" not in _CACHE:
        _CACHE["nc"] = build_program()[0]
    nc = _CACHE["nc"]
    in_maps = make_in_maps(inputs, 8)
    res = run_bass_kernel_spmd(nc, in_maps, core_ids=list(range(8)))
    return np.stack([np.asarray(r["out"], dtype=np.float32) for r in res.results], axis=0)
```

```python
import math
from contextlib import ExitStack

import numpy as np
import ml_dtypes
import concourse.bass as bass
import concourse.mybir as mybir
from concourse.bass_utils import run_bass_kernel_spmd

F32 = mybir.dt.float32
BF16 = mybir.dt.bfloat16
AF = mybir.ActivationFunctionType
ALU = mybir.AluOpType

S = 4096
D = 1024
NT = 32
NG = 8
GW = 512
H = 8
DFF = 2816
NF = 22
ALPHA = 4.0 ** 0.25
LN_EPS = 1e-5
RMS_EPS = 1e-6
SC0 = 192.0 ** -0.5
SC1 = 64.0 ** -0.5
LAMBDA_INIT = 0.8 - 0.6 * math.exp(-0.3 * 1)
NEG = -30000.0

ENGS = ("pe", "act", "dve", "pool", "sp")


class _Op:
    __slots__ = ("eng", "fn", "deps", "signal", "slot", "sigval", "idx", "phase")


class Prog:
    def __init__(self, nc, same_engine_sync=True):
        self.nc = nc
        self.ops = []
        self.last_w = {}
        self.readers = {}
        self.same_engine_sync = same_engine_sync
        self.fence = set()
        self.last_on_eng = {}
        self.dma_since = []
        self.phase = 0

    def barrier(self):
        self.phase += 1
        self.fence = set(self.last_on_eng.values()) | set(self.dma_since)
        self.dma_since = []
        self.last_w = {}
        self.readers = {}

    def op(self, eng, fn, reads=(), writes=(), slot=None):
        o = _Op()
        o.eng, o.fn, o.slot, o.signal, o.sigval = eng, fn, slot, slot is not None, None
        o.idx = len(self.ops)
        o.phase = self.phase
        deps = set(self.fence)
        for r in reads:
            w = self.last_w.get(r)
            if w is not None:
                deps.add(w)
        for r in writes:
            w = self.last_w.get(r)
            if w is not None:
                deps.add(w)
            for rd in self.readers.get(r, ()):
                deps.add(rd)
        if eng == "pool" and slot is not None:
            if getattr(self, "last_pool_dma", None) is not None:
                deps.add(self.last_pool_dma)
            self.last_pool_dma = o.idx
        o.deps = deps
        for r in reads:
            lst = self.readers.setdefault(r, [])
            if slot is None:
                lst[:] = [i for i in lst if not (self.ops[i].slot is None and self.ops[i].eng == eng)]
            lst.append(o.idx)
        for r in writes:
            self.last_w[r] = o.idx
            self.readers[r] = []
        self.ops.append(o)
        if slot is None:
            self.last_on_eng[eng] = o.idx
        else:
            self.dma_since.append(o.idx)
        return o.idx

    def _skip(self, p, o):
        if p.slot is None and o.slot is None and p.eng == o.eng:
            return p.eng == "pe" or not self.same_engine_sync
        return False

    def emit(self):
        nc = self.nc
        ops = self.ops
        for o in ops:
            for d in o.deps:
                p = ops[d]
                if p.slot is None and not self._skip(p, o):
                    p.signal = True
        cnt = {e: 0 for e in ENGS}
        slotcnt = {}
        physmap = {}
        nphys = {}
        for o in ops:
            if o.slot is not None:
                key = (o.eng, o.phase, o.slot)
                if key not in physmap:
                    physmap[key] = (o.eng, nphys.get((o.eng, o.phase), 0))
                    nphys[(o.eng, o.phase)] = physmap[key][1] + 1
                ph = physmap[key]
                slotcnt[ph] = slotcnt.get(ph, 0) + 16
                o.sigval = (("dma", ph), slotcnt[ph])
            elif o.signal:
                cnt[o.eng] += 1
                o.sigval = (o.eng, cnt[o.eng])
        keys = [e for e in ENGS if cnt[e] > 0] + [("dma", s) for s in slotcnt]
        self.n_sems = len(keys)
        with ExitStack() as st:
            sems = {}
            for i, k in enumerate(keys):
                sems[k] = st.enter_context(nc.semaphore("sm%d" % i))
            block = st.enter_context(nc.Block())
            per_eng = {e: [o for o in ops if o.eng == e] for e in ENGS}

            def run(engname, eng):
                waited = {}
                for o in per_eng[engname]:
                    need = {}
                    for d in o.deps:
                        p = ops[d]
                        if p.sigval is None or self._skip(p, o):
                            continue
                        k, v = p.sigval
                        if need.get(k, 0) < v:
                            need[k] = v
                    for k, v in need.items():
                        if waited.get(k, 0) < v:
                            eng.wait_ge(sems[k], v)
                            waited[k] = v
                    ins = o.fn(eng)
                    if o.sigval is not None:
                        ins.then_inc(sems[o.sigval[0]], 16 if o.slot is not None else 1)
                if engname == "sp":
                    for s_, v in slotcnt.items():
                        if waited.get(("dma", s_), 0) < v:
                            eng.wait_ge(sems[("dma", s_)], v)

            @block.sync
            def _(e):
                run("sp", e)

            @block.tensor
            def _(e):
                run("pe", e)

            @block.scalar
            def _(e):
                run("act", e)

            @block.vector
            def _(e):
                run("dve", e)

            @block.gpsimd
            def _(e):
                run("pool", e)


def I(method, *args, **kwargs):
    return lambda e: getattr(e, method)(*args, **kwargs)


class Ring:
    def __init__(self, tiles, name):
        self.tiles, self.name, self.i = tiles, name, -1

    def next(self):
        self.i += 1
        k = self.i % len(self.tiles)
        return self.tiles[k], (self.name, k)


def build_program(phases=(1, 2, 3, 4, 5, 6, 7, 8), debug=()):
    nc = bass.Bass("TRN2", target_bir_lowering=False)

    def din(name, shape, dt=F32):
        return nc.dram_tensor(name, list(shape), dt, kind="ExternalInput").ap()

    def dscr(name, shape, dt):
        kind = "ExternalOutput" if name in debug else "Internal"
        return nc.dram_tensor(name, list(shape), dt, kind=kind).ap()

    x = din("x", [S, D])
    w_dq = din("mla_w_dq", [D, 384])
    q_norm = din("mla_q_norm", [384])
    w_uq = din("mla_w_uq", [384, 1536])
    w_dkv = din("mla_w_dkv", [D, 320])
    kv_norm = din("mla_kv_norm", [256])
    w_ukv = din("mla_w_ukv", [256, 2048])
    w_o0 = din("mla_w_o", [D, D])
    kv_w = din("kv_w", [D, 2048])
    w_q1 = din("diff_w_q", [D, D])
    lq1 = din("diff_lq1", [1, 64])
    lk1 = din("diff_lk1", [1, 64])
    lq2 = din("diff_lq2", [1, 64])
    lk2 = din("diff_lk2", [1, 64])
    subln = din("diff_subln", [128])
    w_o1 = din("diff_w_o", [D, D])
    ln1_g = din("ln1_g", [2, D])
    ln1_b = din("ln1_b", [2, D])
    ln2_g = din("ln2_g", [2, D])
    ln2_b = din("ln2_b", [2, D])
    w_gu = din("ffn_w_gate_up", [2, D, 2 * DFF])
    w_dn = din("ffn_w_down", [2, DFF, D])
    ident_d = din("c_ident", [128, 128], BF16)
    mask_d = din("c_mask", [128, 4, GW], BF16)
    cosm_d = din("c_cosm", [64, S])
    sinm_d = din("c_sinm", [64, S])
    cosd_d = din("c_cosd", [128, S])
    sind_d = din("c_sind", [128, S])
    out = nc.dram_tensor("out", [S, D], F32, kind="ExternalOutput").ap()

    QN0 = dscr("QN0", [H, 128, S], BF16)
    QR0 = dscr("QR0", [H, 64, S], BF16)
    KN0 = dscr("KN0", [H, 128, S], BF16)
    KR0 = dscr("KR0", [64, S], BF16)
    V0 = dscr("V0", [S, D], BF16)
    OT0 = dscr("OT0", [H, 128, S], BF16)
    H1 = dscr("H1", [S, D], F32)
    H1T = dscr("H1T", [8, 128, S], BF16)
    H2 = dscr("H2", [S, D], F32)
    H2T = dscr("H2T", [8, 128, S], BF16)
    QT1 = dscr("QT1", [H, 128, S], BF16)
    KT1 = dscr("KT1", [H, 128, S], BF16)
    V1 = dscr("V1", [S, D], BF16)
    OT1 = dscr("OT1", [H, 128, S], BF16)
    H3 = dscr("H3", [S, D], F32)
    H3T = dscr("H3T", [8, 128, S], BF16)

    P = Prog(nc)
    uid = [0]

    def scope():
        st = ExitStack()

        def sb(shape, dt, name=None):
            uid[0] += 1
            return st.enter_context(nc.sbuf_tensor("%s_%d" % (name or "t", uid[0]), list(shape), dt))

        def ps(shape, dt, name=None):
            uid[0] += 1
            return st.enter_context(nc.psum_tensor("%s_%d" % (name or "p", uid[0]), list(shape), dt))

        return st, sb, ps

    def sbring(sb, n, shape, dt, name):
        return Ring([sb(shape, dt, name) for _ in range(n)], name + str(uid[0]))

    evac_rr = [0]

    def evac(out_ap, in_ap, reads, writes, scale=None, eng=None):
        if eng is None:
            evac_rr[0] += 1
            eng = "act" if evac_rr[0] % 2 else "dve"
        if eng == "act":
            if scale is None:
                P.op("act", I("activation", out=out_ap, in_=in_ap, func=AF.Copy), reads=reads, writes=writes)
            else:
                P.op("act", I("activation", out=out_ap, in_=in_ap, func=AF.Copy, scale=scale), reads=reads, writes=writes)
        else:
            if scale is None:
                P.op("dve", I("tensor_copy", out=out_ap, in_=in_ap), reads=reads, writes=writes)
            else:
                P.op("dve", I("tensor_scalar", out=out_ap, in0=in_ap, scalar1=scale, scalar2=None, op0=ALU.mult),
                     reads=reads, writes=writes)

    def load_w_bf16(dst_tile, src_ap, nchunk, res_prefix, split=1):
        n = src_ap.shape[-1]
        v = src_ap.rearrange("(c p) n -> p c n", p=128)
        step = n // split
        for s_ in range(split):
            lo, hi = s_ * step, (s_ + 1) * step
            P.op("pool", I("dma_start", out=dst_tile[:, :, lo:hi], in_=v[:, :, lo:hi]),
                 writes=[(res_prefix, s_)], slot=(res_prefix, s_))

    class LNPipe:
        def __init__(self, sb, ps, Gt, Bt, ident, mh, HTd, pt_ring):
            self.stt = sbring(sb, 2, [128, 2, 6], F32, "stt")
            self.mv = sbring(sb, 2, [128, 2], F32, "mv")
            self.ve = sbring(sb, 2, [128, 1], F32, "ve")
            self.rstd = sbring(sb, 3, [128, 1], F32, "rstd")
            self.nmr = sbring(sb, 3, [128, 1], F32, "nmr")
            self.hn = sbring(sb, 2, [128, D], F32, "hn")
            self.Gt, self.Bt, self.ident, self.mh, self.HTd, self.pt_ring = Gt, Bt, ident, mh, HTd, pt_ring
            if HTd is not None:
                self.hb = sbring(sb, 2, [128, D], BF16, "hb")
                self.hTt = sbring(sb, 2, [128, 8, 128], BF16, "hTt")
            self.st = {}
            self.n = 0

        def push(self, z, zres, dst_rows, t):
            k = self.n
            self.n += 1
            zr = list(zres)
            stt_t, stt_r = self.stt.next()
            mv_t, mv_r = self.mv.next()
            ve_t, ve_r = self.ve.next()
            rs_t, rs_r = self.rstd.next()
            nm_t, nm_r = self.nmr.next()
            self.st[k] = dict(z=z, zr=zr, dst=dst_rows, t=t)
            P.op("dve", I("bn_stats", out=stt_t[:, 0, :], in_=z[:, 0:512]), reads=zr, writes=[(stt_r, 0)])
            P.op("dve", I("bn_stats", out=stt_t[:, 1, :], in_=z[:, 512:1024]), reads=zr, writes=[(stt_r, 1)])
            P.op("dve", I("bn_aggr", out=mv_t[:], in_=stt_t[:].rearrange("p a b -> p (a b)")),
                 reads=[(stt_r, 0), (stt_r, 1)], writes=[mv_r])
            P.op("dve", I("tensor_scalar", out=ve_t[:], in0=mv_t[:, 1:2], scalar1=LN_EPS, scalar2=None, op0=ALU.add),
                 reads=[mv_r], writes=[ve_r])
            P.op("pool", I("tensor_tensor", out=rs_t[:], in0=ve_t[:], in1=self.mh[:], op=ALU.pow), reads=[ve_r, "mh"], writes=[rs_r])
            self._stage2(k - 1)
            self._stage3(k - 2)
            P.op("dve", I("tensor_scalar", out=nm_t[:], in0=mv_t[:, 0:1], scalar1=-1.0, scalar2=rs_t[:], op0=ALU.mult, op1=ALU.mult),
                 reads=[mv_r, rs_r], writes=[nm_r])
            P.op("act", I("activation", out=z[:], in_=z[:], func=AF.Identity, scale=rs_t[:], bias=nm_t[:]),
                 reads=zr + [rs_r, nm_r], writes=zr)

        def _stage2(self, k):
            if k < 0 or k not in self.st:
                return
            d = self.st[k]
            z, zr = d["z"], d["zr"]
            hn_t, hn_r = self.hn.next()
            P.op("dve", I("tensor_tensor", out=z[:], in0=z[:], in1=self.Gt[:], op=ALU.mult), reads=zr + ["lnG"], writes=zr)
            P.op("pool", I("tensor_tensor", out=hn_t[:], in0=z[:], in1=self.Bt[:], op=ALU.add), reads=zr + ["lnB"], writes=[hn_r])
            P.op("sp", I("dma_start", out=d["dst"], in_=hn_t[:]), reads=[hn_r], slot=hn_r)
            if self.HTd is not None:
                hb_t, hb_r = self.hb.next()
                P.op("act", I("activation", out=hb_t[:], in_=hn_t[:], func=AF.Copy), reads=[hn_r], writes=[hb_r])
                d["hb"] = (hb_t, hb_r)

        def _stage3(self, k):
            if k < 0 or k not in self.st:
                return
            d = self.st.pop(k)
            if self.HTd is None:
                return
            hb_t, hb_r = d["hb"]
            t = d["t"]
            pt_t, pt_r = self.pt_ring.next()
            for c in range(8):
                P.op("pe", I("transpose", out=pt_t[:, c, :], in_=hb_t[:, c * 128:(c + 1) * 128], identity=self.ident[:]),
                     reads=[hb_r, "ident"], writes=[pt_r])
            hT_t, hT_r = self.hTt.next()
            P.op("dve", I("tensor_copy", out=hT_t[:], in_=pt_t[:]), reads=[pt_r], writes=[hT_r])
            P.op("sp", I("dma_start", out=self.HTd.rearrange("c p t -> p c t")[:, :, t * 128:(t + 1) * 128], in_=hT_t[:]),
                 reads=[hT_r], slot=hT_r)

        def flush(self):
            self._stage2(self.n - 1)
            self._stage3(self.n - 2)
            self._stage3(self.n - 1)

    def load_consts(sb, need_ident=True, need_mask=False, need_ones=False, need_mh=False):
        r = {}
        if need_ident:
            r["ident"] = sb([128, 128], BF16, "ident")
            P.op("sp", I("dma_start", out=r["ident"][:], in_=ident_d[:, :]), writes=["ident"], slot="ident")
        if need_mask:
            r["mask"] = sb([128, 4, GW], BF16, "mask")
            P.op("sp", I("dma_start", out=r["mask"][:], in_=mask_d[:, :, :]), writes=["mask"], slot="mask")
        if need_ones:
            r["ones"] = sb([128, 128], BF16, "ones")
            P.op("pool", I("memset", r["ones"][:], 1.0), writes=["ones"])
        if need_mh:
            r["mh"] = sb([128, 1], F32, "mh")
            P.op("pool", I("memset", r["mh"][:], -0.5), writes=["mh"])
        return r

    def rms_fm(banks, bank_res, nchunk, ndim, gain_t, out_tile, out_res, sq_ring, ss_bank, ss_res, ln_ring, rstd_ring, ones):
        sqs = []
        for m in range(nchunk):
            sq_t, sq_r = sq_ring.next()
            P.op("act", I("activation", out=sq_t[:], in_=banks[m][:], func=AF.Square),
                 reads=[bank_res[m]], writes=[sq_r])
            sqs.append((sq_t, sq_r))
        for m in range(nchunk):
            P.op("pe", I("matmul", ss_bank[:], lhsT=ones[:], rhs=sqs[m][0][:], start=(m == 0), stop=(m == nchunk - 1)),
                 reads=["ones", sqs[m][1]], writes=[ss_res])
        ln_t, ln_r = ln_ring.next()
        rs_t, rs_r = rstd_ring.next()
        P.op("act", I("activation", out=ln_t[:], in_=ss_bank[:], func=AF.Ln, scale=1.0 / ndim, bias=RMS_EPS),
             reads=[ss_res], writes=[ln_r])
        P.op("act", I("activation", out=rs_t[:], in_=ln_t[:], func=AF.Exp, scale=-0.5), reads=[ln_r], writes=[rs_r])
        for m in range(nchunk):
            P.op("dve", I("scalar_tensor_tensor", out=out_tile[:, m, :], in0=banks[m][:], scalar=gain_t[:, m:m + 1],
                                                              in1=rs_t[:], op0=ALU.mult, op1=ALU.mult),
                 reads=[bank_res[m], rs_r, "gains"], writes=[(out_res, m)])

    def phase1():
        st, sb, ps = scope()
        with st:
            C = load_consts(sb, need_ident=True, need_ones=True)
            ident, ones = C["ident"], C["ones"]
            wdq = sb([128, 8, 384], BF16, "wdq")
            wdkv = sb([128, 8, 384], BF16, "wdkv")
            wuq = sb([128, 3, 1536], BF16, "wuq")
            wuqr = sb([128, 3, 8, 64], BF16, "wuqr")
            wukv = sb([128, 2, 2048], BF16, "wukv")
            gq = sb([128, 3], F32, "gq")
            gkv = sb([128, 2], F32, "gkv")
            load_w_bf16(wdq, w_dq, 8, "wdq")
            P.op("pool", I("dma_start", out=wdkv[:, :, 0:320], in_=w_dkv.rearrange("(c p) n -> p c n", p=128)),
                 writes=["wdkv"], slot="wdkv")
            load_w_bf16(wuq, w_uq, 3, "wuq")
            load_w_bf16(wukv, w_ukv, 2, "wukv")
            for m in range(3):
                P.op("sp", I("dma_start", out=gq[:, m:m + 1], in_=q_norm.rearrange("(c p o) -> c p o", p=128, o=1)[m]),
                     writes=["gains"], slot=("gq", m))
            for m in range(2):
                P.op("sp", I("dma_start", out=gkv[:, m:m + 1], in_=kv_norm.rearrange("(c p o) -> c p o", p=128, o=1)[m]),
                     writes=["gains"], slot=("gkv", m))
            P.op("pool", I("tensor_scalar", out=wdkv[:, :, 320:352], in0=wdkv[:, :, 288:320], scalar1=-1.0, scalar2=None, op0=ALU.mult),
                 reads=["wdkv"], writes=["wdkvr"])
            P.op("pool", I("tensor_copy", out=wdkv[:, :, 352:384], in_=wdkv[:, :, 256:288]), reads=["wdkv"], writes=["wdkvr2"])
            wuq4 = wuq[:].rearrange("p c (h d) -> p c h d", d=192)
            for c in range(3):
                P.op("pool", I("tensor_scalar", out=wuqr[:, c, :, 0:32], in0=wuq4[:, c, :, 160:192], scalar1=-1.0, scalar2=None,
                                                             op0=ALU.mult), reads=[("wuq", 0)], writes=[("wuqr", c, 0)])
                P.op("pool", I("tensor_copy", out=wuqr[:, c, :, 32:64], in_=wuq4[:, c, :, 128:160]),
                     reads=[("wuq", 0)], writes=[("wuqr", c, 1)])
            wuqr_res = [("wuqr", c, k) for c in range(3) for k in range(2)]
            wdkv_res = ["wdkv", "wdkvr", "wdkvr2"]

            xs_ring = sbring(sb, 2, [128, D], F32, "xs")
            xb_ring = sbring(sb, 2, [128, D], BF16, "xb")
            xT_ring = sbring(sb, 2, [128, 8, GW], BF16, "xT")
            pt_ring = Ring([ps([128, 8, 128], BF16, "pt")], "pt1")
            banks = [ps([128, GW], F32, "bk") for _ in range(7)]
            bres = [("bank1", i) for i in range(7)]
            sq_ring = sbring(sb, 3, [128, GW], BF16, "sq")
            ln_ring = sbring(sb, 1, [128, GW], F32, "lnss")
            rstd_ring = sbring(sb, 2, [128, GW], F32, "rstdfm")
            cqn_ring = sbring(sb, 2, [128, 3, GW], BF16, "cqn")
            cn_ring = sbring(sb, 2, [128, 2, GW], BF16, "cn")
            cos_ring = sbring(sb, 2, [64, GW], F32, "cosm")
            sin_ring = sbring(sb, 2, [64, GW], F32, "sinm")
            t1_ring = sbring(sb, 2, [64, GW], F32, "t1")
            t2_ring = sbring(sb, 2, [64, GW], F32, "t2")
            qnst_ring = sbring(sb, 2, [128, H, GW], BF16, "qnst")
            qrst_ring = sbring(sb, 2, [64, H, GW], BF16, "qrst")
            knst_ring = sbring(sb, 2, [128, H, GW], BF16, "knst")
            vst_ring = sbring(sb, 2, [128, 4, D], BF16, "vst")
            krst_ring = sbring(sb, 2, [64, GW], BF16, "krst")
            bi = [0]

            def nb():
                bi[0] += 1
                k = bi[0] % 7
                return banks[k], bres[k]

            for g in range(NG):
                gs = slice(g * GW, (g + 1) * GW)
                cos_t, cos_r = cos_ring.next()
                sin_t, sin_r = sin_ring.next()
                P.op("sp", I("dma_start", out=cos_t[:], in_=cosm_d[:, gs]), writes=[cos_r], slot=cos_r)
                P.op("sp", I("dma_start", out=sin_t[:], in_=sinm_d[:, gs]), writes=[sin_r], slot=sin_r)
                xT_t, xT_r = xT_ring.next()
                for i in range(4):
                    t = g * 4 + i
                    xs_t, xs_r = xs_ring.next()
                    xb_t, xb_r = xb_ring.next()
                    P.op("sp", I("dma_start", out=xs_t[:], in_=x[t * 128:(t + 1) * 128, :]), writes=[xs_r], slot=xs_r)
                    P.op("pool", I("tensor_copy", out=xb_t[:], in_=xs_t[:]), reads=[xs_r], writes=[xb_r])
                    pt_t, pt_r = pt_ring.next()
                    for c in range(8):
                        P.op("pe", I("transpose", out=pt_t[:, c, :], in_=xb_t[:, c * 128:(c + 1) * 128],
                                                                                  identity=ident[:]), reads=[xb_r, "ident"], writes=[pt_r])
                    evac(xT_t[:, :, i * 128:(i + 1) * 128], pt_t[:], [pt_r], [(xT_r, i)])
                xT_all = [(xT_r, i) for i in range(4)]
                cqb = [nb() for _ in range(3)]
                for m in range(3):
                    for c in range(8):
                        P.op("pe", I("matmul", cqb[m][0][:], lhsT=wdq[:, c, m * 128:(m + 1) * 128], rhs=xT_t[:, c, :],
                                                                             start=(c == 0), stop=(c == 7)),
                             reads=xT_all + [("wdq", 0)], writes=[cqb[m][1]])
                ssb, ssr = nb()
                cqn_t, cqn_r = cqn_ring.next()
                rms_fm([b[0] for b in cqb], [b[1] for b in cqb], 3, 384, gq, cqn_t, cqn_r, sq_ring, ssb, ssr, ln_ring, rstd_ring, ones)
                cqn_all = [(cqn_r, m) for m in range(3)]
                cb = [nb() for _ in range(2)]
                for m in range(2):
                    for c in range(8):
                        P.op("pe", I("matmul", cb[m][0][:], lhsT=wdkv[:, c, m * 128:(m + 1) * 128], rhs=xT_t[:, c, :],
                                                                            start=(c == 0), stop=(c == 7)),
                             reads=xT_all + wdkv_res, writes=[cb[m][1]])
                ssb, ssr = nb()
                cn_t, cn_r = cn_ring.next()
                rms_fm([b[0] for b in cb], [b[1] for b in cb], 2, 256, gkv, cn_t, cn_r, sq_ring, ssb, ssr, ln_ring, rstd_ring, ones)
                cn_all = [(cn_r, m) for m in range(2)]
                ab, ar = nb()
                bb, br = nb()
                for c in range(8):
                    P.op("pe", I("matmul", ab[0:64, :], lhsT=wdkv[:, c, 256:320], rhs=xT_t[:, c, :], start=(c == 0), stop=(c == 7)),
                         reads=xT_all + wdkv_res, writes=[ar])
                for c in range(8):
                    P.op("pe", I("matmul", bb[0:64, :], lhsT=wdkv[:, c, 320:384], rhs=xT_t[:, c, :], start=(c == 0), stop=(c == 7)),
                         reads=xT_all + wdkv_res, writes=[br])
                t1, t1r = t1_ring.next()
                t2, t2r = t2_ring.next()
                kr_t, kr_r = krst_ring.next()
                P.op("dve", I("tensor_tensor", out=t1[:], in0=ab[0:64, :], in1=cos_t[:], op=ALU.mult),
                     reads=[ar, cos_r], writes=[t1r])
                P.op("dve", I("tensor_tensor", out=t2[:], in0=bb[0:64, :], in1=sin_t[:], op=ALU.mult),
                     reads=[br, sin_r], writes=[t2r])
                P.op("pool", I("tensor_tensor", out=kr_t[:], in0=t1[:], in1=t2[:], op=ALU.add),
                     reads=[t1r, t2r], writes=[kr_r])
                P.op("sp", I("dma_start", out=KR0[:, gs], in_=kr_t[:]), reads=[kr_r], writes=[("KR0", g)], slot=kr_r)
                qn_t, qn_r = qnst_ring.next()
                qr_t, qr_r = qrst_ring.next()
                kn_t, kn_r = knst_ring.next()
                for h in range(H):
                    bk, bkr = nb()
                    for c in range(3):
                        P.op("pe", I("matmul", bk[:], lhsT=wuq[:, c, h * 192:h * 192 + 128], rhs=cqn_t[:, c, :],
                                                                       start=(c == 0), stop=(c == 2)), reads=cqn_all + [("wuq", 0)], writes=[bkr])
                    evac(qn_t[:, h, :], bk[:], [bkr], [(qn_r, h)], scale=SC0)
                    ab, ar = nb()
                    bb, br = nb()
                    for c in range(3):
                        P.op("pe", I("matmul", ab[0:64, :], lhsT=wuq[:, c, h * 192 + 128:h * 192 + 192], rhs=cqn_t[:, c, :],
                                                                       start=(c == 0), stop=(c == 2)), reads=cqn_all + [("wuq", 0)], writes=[ar])
                    for c in range(3):
                        P.op("pe", I("matmul", bb[0:64, :], lhsT=wuqr[:, c, h, :], rhs=cqn_t[:, c, :],
                                                                       start=(c == 0), stop=(c == 2)), reads=cqn_all + wuqr_res, writes=[br])
                    t1, t1r = t1_ring.next()
                    t2, t2r = t2_ring.next()
                    P.op("dve", I("scalar_tensor_tensor", out=t1[:], in0=ab[0:64, :], scalar=SC0, in1=cos_t[:],
                                                                                          op0=ALU.mult, op1=ALU.mult), reads=[ar, cos_r], writes=[t1r])
                    P.op("dve", I("scalar_tensor_tensor", out=t2[:], in0=bb[0:64, :], scalar=SC0, in1=sin_t[:],
                                                                                          op0=ALU.mult, op1=ALU.mult), reads=[br, sin_r], writes=[t2r])
                    P.op("pool", I("tensor_tensor", out=qr_t[:, h, :], in0=t1[:], in1=t2[:], op=ALU.add),
                         reads=[t1r, t2r], writes=[(qr_r, h)])
                    bk, bkr = nb()
                    for c in range(2):
                        P.op("pe", I("matmul", bk[:], lhsT=wukv[:, c, h * 256:h * 256 + 128], rhs=cn_t[:, c, :],
                                                                       start=(c == 0), stop=(c == 1)), reads=cn_all + [("wukv", 0)], writes=[bkr])
                    evac(kn_t[:, h, :], bk[:], [bkr], [(kn_r, h)])
                P.op("sp", I("dma_start", out=QN0.rearrange("h p t -> p h t")[:, :, gs], in_=qn_t[:]),
                     reads=[(qn_r, h) for h in range(H)], writes=[("QN0", g)], slot=qn_r)
                P.op("sp", I("dma_start", out=QR0.rearrange("h p t -> p h t")[:, :, gs], in_=qr_t[:]),
                     reads=[(qr_r, h) for h in range(H)], writes=[("QR0", g)], slot=qr_r)
                P.op("sp", I("dma_start", out=KN0.rearrange("h p t -> p h t")[:, :, gs], in_=kn_t[:]),
                     reads=[(kn_r, h) for h in range(H)], writes=[("KN0", g)], slot=kn_r)
                v_t, v_r = vst_ring.next()
                wv4 = wukv[:].rearrange("p c (h d) -> p c h d", d=256)
                for i in range(4):
                    for hh in range(2):
                        bk, bkr = nb()
                        for c in range(2):
                            P.op("pe", I("matmul", bk[:].rearrange("p (h d) -> p h d", d=128),
                                                                               lhsT=cn_t[:, c, i * 128:(i + 1) * 128],
                                                                               rhs=wv4[:, c, hh * 4:(hh + 1) * 4, 128:256], start=(c == 0), stop=(c == 1)),
                                 reads=cn_all + [("wukv", 0)], writes=[bkr])
                        evac(v_t[:, i, hh * 512:(hh + 1) * 512], bk[:], [bkr], [(v_r, i, hh)])
                P.op("sp", I("dma_start", out=V0[g * GW:(g + 1) * GW, :].rearrange("(i p) n -> p i n", p=128), in_=v_t[:]),
                     reads=[(v_r, i, hh) for i in range(4) for hh in range(2)], writes=[("V0", g)], slot=v_r)
            P.barrier()

    def attention(layer, GQ=GW):
        st, sb, ps = scope()
        with st:
            nmap = 1 if layer == 0 else 2
            NSUB = GQ // 128
            NGQ = S // GQ
            UNIT = False
            if UNIT:
                S_ring = Ring([ps([128, 1, GQ], F32, "S") for _ in range(4)], "S1u")
                NSET = 1
            elif nmap * GQ <= 512:
                S_ring = Ring([ps([128, nmap, GQ], F32, "S") for _ in range(3)], "S0")
                NSET = 2
            else:
                S_ring = Ring([ps([128, 2, GQ], F32, "S") for _ in range(2)], "S1")
                NSET = 1
            nacc = NSUB * nmap
            nbank = (nacc + 2) // 3
            O_sets = [[ps([128, GW], F32, "O") for _ in range(nbank)] for _ in range(NSET)]
            pt = ps([128, 8, 128], BF16, "ptA")
            AST = 130
            VST = 130
            NV = 129

            def acc_ap(set_i, a, lo=0, hi=NV):
                return O_sets[set_i][a // 3][:, (a % 3) * AST + lo:(a % 3) * AST + hi]

            C = load_consts(sb, need_ident=True, need_mask=True)
            ident, mask = C["ident"], C["mask"]
            if layer == 0:
                qn_ring = sbring(sb, 2, [128, S], BF16, "qn")
                qr_ring = sbring(sb, 2, [64, S], BF16, "qr")
                kn_ring = sbring(sb, 2, [128, S], BF16, "kn")
                kr = sb([64, S], BF16, "kr")
                P.op("sp", I("dma_start", out=kr[:], in_=KR0[:, :]), writes=["kr"], slot="kr")
                Vd, OTd = V0, OT0
            else:
                qn_ring = sbring(sb, 2, [128, S], BF16, "q1")
                kn_ring = sbring(sb, 2, [128, S], BF16, "k1")
                Vd, OTd = V1, OT1
                mh = sb([128, 1], F32, "mh")
                P.op("pool", I("memset", mh[:], -0.5), writes=["mh"])
                lt = [sb([128, 64], F32, "lt") for _ in range(4)]
                for k_, src in enumerate((lq1, lk1, lq2, lk2)):
                    P.op("sp", I("dma_start", out=lt[k_][:], in_=src[0:1, :].partition_broadcast(128)),
                         writes=[("lt", k_)], slot=("lt", k_))
                pr = [sb([128, 64], F32, "pr") for _ in range(2)]
                sm = [sb([128, 1], F32, "lsm") for _ in range(2)]
                ex = [sb([128, 1], F32, "lex") for _ in range(2)]
                neglam = sb([128, 1], F32, "neglam")
                gvec = sb([128, 128], F32, "gvec")
                junk = sb([128, 64], F32, "junk")
                for k_ in range(2):
                    P.op("dve", I("tensor_tensor", out=pr[k_][:], in0=lt[2 * k_][:], in1=lt[2 * k_ + 1][:], op=ALU.mult),
                         reads=[("lt", 2 * k_), ("lt", 2 * k_ + 1)], writes=[("pr", k_)])
                    P.op("act", I("activation", out=junk[:], in_=pr[k_][:], func=AF.Copy, accum_out=sm[k_][:]),
                         reads=[("pr", k_)], writes=[("lsm", k_), "junk"])
                    P.op("act", I("activation", out=ex[k_][:], in_=sm[k_][:], func=AF.Exp), reads=[("lsm", k_)], writes=[("lex", k_)])
                P.op("dve", I("tensor_tensor", out=neglam[:], in0=ex[1][:], in1=ex[0][:], op=ALU.subtract),
                     reads=[("lex", 0), ("lex", 1)], writes=["neglam0"])
                P.op("dve", I("tensor_scalar", out=neglam[:], in0=neglam[:], scalar1=-LAMBDA_INIT, scalar2=None, op0=ALU.add),
                     reads=["neglam0"], writes=["neglam"])
                P.op("sp", I("dma_start", out=gvec[:], in_=subln.rearrange("(o n) -> o n", o=1).partition_broadcast(128)), writes=["gvec0"], slot="gvec")
                P.op("pool", I("tensor_scalar", out=gvec[:], in0=gvec[:], scalar1=1.0 - LAMBDA_INIT, scalar2=None, op0=ALU.mult),
                     reads=["gvec0"], writes=["gvec"])
                o1_ring = sbring(sb, 3, [128, 128], F32, "o1")
                o_ring = sbring(sb, 3, [128, 128], F32, "o")
                junk2 = sb([128, 128], F32, "junk2")
                ss_ring = sbring(sb, 3, [128, 1], F32, "ss")
                rstd_ring = sbring(sb, 3, [128, 1], F32, "rstdA")
                c2_ring = sbring(sb, 3, [128, 1], F32, "c2")
            v_ring = sbring(sb, 2, [128, NT, VST], BF16, "v")
            for k_ in range(2):
                P.op("pool", I("memset", v_ring.tiles[k_][:, :, 128:130], 1.0), writes=[(("vones", k_))])
            pT_ring = sbring(sb, 4, [128, nmap, GQ], BF16, "pT")
            rinv_ring = sbring(sb, 4 * nmap, [128, 1], F32, "rinv")
            ob_ring = sbring(sb, 3, [128, 128], BF16, "ob")
            ost_ring = sbring(sb, 2, [128, GQ], BF16, "ost")
            LOOK = 2 if NSET == 2 else 1
            heads = {}

            def load_head(h):
                if h >= H:
                    return
                d = {}
                d["qn"] = qn_ring.next()
                d["kn"] = kn_ring.next()
                d["v"] = v_ring.next()
                if layer == 0:
                    d["qr"] = qr_ring.next()
                    P.op("sp", I("dma_start", out=d["qn"][0][:], in_=QN0[h]), writes=[d["qn"][1]], slot=d["qn"][1])
                    P.op("sp", I("dma_start", out=d["qr"][0][:], in_=QR0[h]), writes=[d["qr"][1]], slot=d["qr"][1])
                    P.op("sp", I("dma_start", out=d["kn"][0][:], in_=KN0[h]), writes=[d["kn"][1]], slot=d["kn"][1])
                else:
                    P.op("sp", I("dma_start", out=d["qn"][0][:], in_=QT1[h]), writes=[d["qn"][1]], slot=d["qn"][1])
                    P.op("sp", I("dma_start", out=d["kn"][0][:], in_=KT1[h]), writes=[d["kn"][1]], slot=d["kn"][1])
                vk = v_ring.i % 2
                vsrc = Vd.rearrange("(t p) (h d) -> h p t d", p=128, d=128)[h]
                for q_ in range(4):
                    P.op("sp", I("dma_start", out=d["v"][0][:, q_ * 8:(q_ + 1) * 8, 0:128], in_=vsrc[:, q_ * 8:(q_ + 1) * 8, :]),
                         reads=[("vones", vk)], writes=[(d["v"][1], q_)], slot=(d["v"][1], q_))
                heads[h] = d

            set_ctr = [0]
            load_head(0)
            for h in range(H):
                load_head(h + 1)
                hd = heads.pop(h)
                qn_t, qn_r = hd["qn"]
                kn_t, kn_r = hd["kn"]
                v_t, v_r = hd["v"]
                if layer == 0:
                    qr_t, qr_r = hd["qr"]
                pairs = [(g, kt) for g in range(NGQ) for kt in range(NSUB * g + NSUB)]
                state = {}
                gstate = {}
                pending = []

                def emit_qk(n):
                    g, kt = pairs[n]
                    j = max(kt - NSUB * g, 0)
                    q0 = g * GQ + 128 * j
                    q1 = (g + 1) * GQ
                    ks = slice(kt * 128, (kt + 1) * 128)
                    diag = kt - NSUB * g >= 0
                    S_t, S_r = S_ring.next()
                    for m in range(nmap):
                        so = S_t[:, m, 128 * j:GQ]
                        if layer == 0:
                            P.op("pe", I("matmul", so, lhsT=kn_t[:, ks], rhs=qn_t[:, q0:q1], start=True, stop=False),
                                 reads=[kn_r, qn_r], writes=[(S_r, m)])
                            P.op("pe", I("matmul", so, lhsT=kr[:, ks], rhs=qr_t[:, q0:q1], start=False, stop=not diag),
                                 reads=["kr", qr_r], writes=[(S_r, m)])
                        else:
                            lo, hi = m * 64, (m + 1) * 64
                            P.op("pe", I("matmul", so, lhsT=kn_t[lo:hi, ks], rhs=qn_t[lo:hi, q0:q1], start=True, stop=not diag),
                                 reads=[kn_r, qn_r], writes=[(S_r, m)])
                        if diag:
                            P.op("pe", I("matmul", S_t[:, m, 128 * j:128 * j + 128], lhsT=ident[:], rhs=mask[:, 0, 0:128], start=False, stop=True),
                                 reads=["ident", "mask"], writes=[(S_r, m)])
                    state[n] = (S_t, S_r)

                def emit_rest(n):
                    g, kt = pairs[n]
                    j = max(kt - NSUB * g, 0)
                    S_t, S_r = state.pop(n)
                    if kt == 0:
                        set_ctr[0] += 1
                        gstate[g] = dict(set=set_ctr[0] % NSET, started=set())
                    gs_ = gstate[g]
                    si = gs_["set"]
                    pT_t, pT_r = pT_ring.next()
                    for m in range(nmap):
                        P.op("act", I("activation", out=pT_t[:, m, 128 * j:GQ], in_=S_t[:, m, 128 * j:GQ], func=AF.Exp),
                             reads=[(S_r, m)], writes=[(pT_r, m)])
                    if kt == 0 and pending and NSET == 1:
                        P.op("pe", I("transpose", out=pt[:, 7, :], in_=ident[:], identity=ident[:]), reads=["ident"], writes=[("ptA", 7), "pe_tick"])
                        flush_pending()
                    for m in range(nmap):
                        for i in range(j, NSUB):
                            a = m * NSUB + i
                            bank = a // 3
                            first = bank not in gs_["started"]
                            gs_["started"].add(bank)
                            P.op("pe", I("matmul", acc_ap(si, a), lhsT=pT_t[:, m, i * 128:(i + 1) * 128], rhs=v_t[:, kt, 0:NV],
                                         start=first, stop=(kt == NSUB * g + i), skip_group_check=True),
                                 reads=[(pT_r, m), (v_r, kt // 8)], writes=[("acc", si, a), "pe_tick"])
                    flush_pending()
                    if kt >= NSUB * g:
                        pending.append((g, kt - NSUB * g, si))

                def flush_pending():
                    while pending:
                        finish(*pending.pop(0))

                def finish(g, i, si):
                    if i == 0:
                        gstate[g]["ost"] = ost_ring.next()
                    ost_t, ost_r = gstate[g]["ost"]
                    ob_t, ob_r = ob_ring.next()
                    if layer == 0:
                        ri_t, ri_r = rinv_ring.next()
                        P.op("dve", I("reciprocal", out=ri_t[:], in_=acc_ap(si, i, 128, 129)), reads=[("acc", si, i), "pe_tick"], writes=[ri_r])
                        P.op("dve", I("tensor_scalar", out=ob_t[:], in0=acc_ap(si, i, 0, 128), scalar1=ri_t[:], scalar2=None, op0=ALU.mult),
                             reads=[("acc", si, i), ri_r], writes=[ob_r])
                    else:
                        ri1, ri1r = rinv_ring.next()
                        ri2, ri2r = rinv_ring.next()
                        c2_t, c2_r = c2_ring.next()
                        o1_t, o1_r = o1_ring.next()
                        o_t, o_r = o_ring.next()
                        ss_t, ss_r = ss_ring.next()
                        rs_t, rs_r = rstd_ring.next()
                        P.op("dve", I("reciprocal", out=ri1[:], in_=acc_ap(si, i, 128, 129)), reads=[("acc", si, i), "pe_tick"], writes=[ri1r])
                        P.op("dve", I("reciprocal", out=ri2[:], in_=acc_ap(si, NSUB + i, 128, 129)), reads=[("acc", si, NSUB + i)], writes=[ri2r])
                        P.op("dve", I("tensor_tensor", out=c2_t[:], in0=ri2[:], in1=neglam[:], op=ALU.mult), reads=[ri2r, "neglam"], writes=[c2_r])
                        P.op("dve", I("tensor_scalar", out=o1_t[:], in0=acc_ap(si, i, 0, 128), scalar1=ri1[:], scalar2=None, op0=ALU.mult),
                             reads=[("acc", si, i), ri1r], writes=[o1_r])
                        P.op("dve", I("scalar_tensor_tensor", out=o_t[:], in0=acc_ap(si, NSUB + i, 0, 128), scalar=c2_t[:], in1=o1_t[:],
                                      op0=ALU.mult, op1=ALU.add), reads=[("acc", si, NSUB + i), c2_r, o1_r], writes=[o_r])
                        P.op("dve", I("tensor_tensor", out=junk2[:], in0=o_t[:], in1=o_t[:], op=ALU.mult), reads=[o_r], writes=["junk2"])
                        P.op("dve", I("tensor_reduce", out=ss_t[:], in_=junk2[:], axis=mybir.AxisListType.X, op=ALU.add), reads=["junk2"], writes=[ss_r])
                        P.op("dve", I("tensor_scalar", out=ss_t[:], in0=ss_t[:], scalar1=1.0 / 128, scalar2=RMS_EPS, op0=ALU.mult, op1=ALU.add),
                             reads=[ss_r], writes=[ss_r])
                        P.op("act", I("activation", out=ss_t[:], in_=ss_t[:], func=AF.Ln), reads=[ss_r], writes=[ss_r])
                        P.op("act", I("activation", out=rs_t[:], in_=ss_t[:], func=AF.Exp, scale=-0.5), reads=[ss_r], writes=[rs_r])
                        P.op("dve", I("scalar_tensor_tensor", out=ob_t[:], in0=o_t[:], scalar=rs_t[:], in1=gvec[:], op0=ALU.mult, op1=ALU.mult),
                             reads=[o_r, rs_r, "gvec"], writes=[ob_r])
                    P.op("pe", I("transpose", out=pt[:, i, :], in_=ob_t[:], identity=ident[:]), reads=[ob_r, "ident"], writes=[("ptA", i)])
                    evac(ost_t[:, i * 128:(i + 1) * 128], pt[:, i, :], [("ptA", i)], [(ost_r, i)], eng="dve")
                    if i == NSUB - 1:
                        P.op("sp", I("dma_start", out=OTd[h][:, g * GQ:(g + 1) * GQ], in_=ost_t[:]), reads=[(ost_r, k_) for k_ in range(NSUB)], slot=ost_r)

                units = [(g, kt, m) for (g, kt) in pairs for m in range(nmap)]

                def emit_qk_u(u):
                    g, kt, m = units[u]
                    j = max(kt - NSUB * g, 0)
                    q0 = g * GQ + 128 * j
                    q1 = (g + 1) * GQ
                    ks = slice(kt * 128, (kt + 1) * 128)
                    diag = kt - NSUB * g >= 0
                    S_t, S_r = S_ring.next()
                    lo, hi = m * 64, (m + 1) * 64
                    P.op("pe", I("matmul", S_t[:, 0, 128 * j:GQ], lhsT=kn_t[lo:hi, ks], rhs=qn_t[lo:hi, q0:q1], start=True, stop=not diag),
                         reads=[kn_r, qn_r], writes=[S_r])
                    if diag:
                        P.op("pe", I("matmul", S_t[:, 0, 128 * j:128 * j + 128], lhsT=ident[:], rhs=mask[:, 0, 0:128], start=False, stop=True),
                             reads=["ident", "mask"], writes=[S_r])
                    state[("u", u)] = (S_t, S_r)

                def emit_rest_u(u):
                    g, kt, m = units[u]
                    j = max(kt - NSUB * g, 0)
                    S_t, S_r = state.pop(("u", u))
                    if m == 0:
                        if kt == 0:
                            set_ctr[0] += 1
                            gstate[g] = dict(set=set_ctr[0] % NSET, started=set())
                        state[("pT", g, kt)] = pT_ring.next()
                    gs_ = gstate[g]
                    si = gs_["set"]
                    pT_t, pT_r = state[("pT", g, kt)]
                    P.op("act", I("activation", out=pT_t[:, m, 128 * j:GQ], in_=S_t[:, 0, 128 * j:GQ], func=AF.Exp),
                         reads=[S_r], writes=[(pT_r, m)])
                    if m == 0 and kt == 0 and pending:
                        P.op("pe", I("transpose", out=pt[:, 7, :], in_=ident[:], identity=ident[:]), reads=["ident"], writes=[("ptA", 7), "pe_tick"])
                        flush_pending()
                    for i in range(j, NSUB):
                        a_ = m * NSUB + i
                        bank = a_ // 3
                        first = bank not in gs_["started"]
                        gs_["started"].add(bank)
                        P.op("pe", I("matmul", acc_ap(si, a_), lhsT=pT_t[:, m, i * 128:(i + 1) * 128], rhs=v_t[:, kt, 0:NV],
                                     start=first, stop=(kt == NSUB * g + i), skip_group_check=True),
                             reads=[(pT_r, m), (v_r, kt // 8)], writes=[("acc", si, a_), "pe_tick"])
                    if m == nmap - 1:
                        state.pop(("pT", g, kt))
                        flush_pending()
                        if kt >= NSUB * g:
                            pending.append((g, kt - NSUB * g, si))

                if UNIT:
                    LOOKU = 3
                    NU = len(units)
                    for u in range(min(LOOKU, NU)):
                        emit_qk_u(u)
                    for u in range(NU):
                        if u + LOOKU < NU:
                            emit_qk_u(u + LOOKU)
                        emit_rest_u(u)
                else:
                    N = len(pairs)
                    for n in range(min(LOOK, N)):
                        emit_qk(n)
                    for n in range(N):
                        if n + LOOK < N:
                            emit_qk(n + LOOK)
                        emit_rest(n)
                P.op("pe", I("transpose", out=pt[:, 7, :], in_=ident[:], identity=ident[:]), reads=["ident"], writes=[("ptA", 7), "pe_tick"])
                flush_pending()
            P.barrier()

    def attention_old(layer):
        assert layer == 0
        st, sb, ps = scope()
        with st:
            C = load_consts(sb, need_ident=True, need_mask=True)
            ident, mask = C["ident"], C["mask"]
            ones = sb([128, 128], BF16, "ones")
            P.op("pool", I("memset", ones[:], 1.0), writes=["ones"])
            qn_ring = sbring(sb, 2, [128, S], BF16, "qn")
            qr_ring = sbring(sb, 2, [128, S], BF16, "qr")
            kn_ring = sbring(sb, 2, [128, S], BF16, "kn")
            kr = sb([128, S], BF16, "kr")
            P.op("pool", I("memset", kr[64:128, :], 0.0), writes=["krz"])
            for k_ in range(2):
                P.op("pool", I("memset", qr_ring.tiles[k_][64:128, :], 0.0), writes=[("qrz", k_)])
            P.op("sp", I("dma_start", out=kr[0:64, :], in_=KR0[:, :]), writes=["kr"], slot="kr")
            v_ring = sbring(sb, 2, [128, NT, 128], BF16, "v")
            pT_ring = sbring(sb, 5, [128, GW], BF16, "pT")
            S_ring = Ring([ps([128, GW], F32, "S") for _ in range(4)], "S0o")
            O_ring = Ring([ps([128, GW], F32, "O") for _ in range(2)], "O0o")
            M_ring = Ring([ps([128, GW], F32, "M") for _ in range(2)], "M0o")
            ln_ring = sbring(sb, 2, [128, GW], F32, "lnM")
            rs_ring = sbring(sb, 2, [128, GW], F32, "rs")
            ost_ring = sbring(sb, 2, [128, GW], BF16, "ost")
            LOOK = 2
            heads = {}

            def load_head(h):
                if h >= H:
                    return
                d = dict(qn=qn_ring.next(), qr=qr_ring.next(), kn=kn_ring.next(), v=v_ring.next())
                P.op("sp", I("dma_start", out=d["qn"][0][:], in_=QN0[h]), writes=[d["qn"][1]], slot=d["qn"][1])
                P.op("sp", I("dma_start", out=d["qr"][0][0:64, :], in_=QR0[h]), writes=[d["qr"][1]], slot=d["qr"][1])
                P.op("sp", I("dma_start", out=d["kn"][0][:], in_=KN0[h]), writes=[d["kn"][1]], slot=d["kn"][1])
                vsrc = V0.rearrange("(t p) (h d) -> h p t d", p=128, d=128)[h]
                for q_ in range(4):
                    P.op("sp", I("dma_start", out=d["v"][0][:, q_ * 8:(q_ + 1) * 8, :], in_=vsrc[:, q_ * 8:(q_ + 1) * 8, :]),
                         writes=[(d["v"][1], q_)], slot=(d["v"][1], q_))
                heads[h] = d

            load_head(0)
            for h in range(H):
                load_head(h + 1)
                hd = heads.pop(h)
                qn_t, qn_r = hd["qn"]
                qr_t, qr_r = hd["qr"]
                kn_t, kn_r = hd["kn"]
                v_t, v_r = hd["v"]
                pairs = [(g, kt) for g in range(NG) for kt in range(4 * g + 4)]
                state = {}

                def emit_qk(n):
                    g, kt = pairs[n]
                    j = max(kt - 4 * g, 0)
                    c0 = 128 * j
                    qs = slice(g * GW + c0, (g + 1) * GW)
                    ks = slice(kt * 128, (kt + 1) * 128)
                    diag = kt >= 4 * g
                    S_t, S_r = S_ring.next()
                    P.op("pe", I("matmul", S_t[:, c0:GW], lhsT=kn_t[:, ks], rhs=qn_t[:, qs], start=True, stop=False),
                         reads=[kn_r, qn_r], writes=[S_r])
                    P.op("pe", I("matmul", S_t[:, c0:GW], lhsT=kr[:, ks], rhs=qr_t[:, qs], start=False, stop=not diag),
                         reads=["kr", "krz", qr_r, ("qrz", 0), ("qrz", 1)], writes=[S_r])
                    if diag:
                        P.op("pe", I("matmul", S_t[:, c0:c0 + 128], lhsT=ident[:], rhs=mask[:, 0, 0:128], start=False, stop=True),
                             reads=["ident", "mask"], writes=[S_r])
                    state[n] = (S_t, S_r)

                def emit_rest(n):
                    g, kt = pairs[n]
                    last = 4 * g + 3
                    j = max(kt - 4 * g, 0)
                    c0 = 128 * j
                    S_t, S_r = state.pop(n)
                    if kt == 0:
                        state["O"] = O_ring.next()
                        state["M"] = M_ring.next()
                    pT_t, pT_r = pT_ring.next()
                    P.op("act", I("activation", out=pT_t[:, c0:GW], in_=S_t[:, c0:GW], func=AF.Exp), reads=[S_r], writes=[pT_r])
                    O_t, O_r = state["O"]
                    M_t, M_r = state["M"]
                    P.op("pe", I("matmul", O_t[:, c0:GW], lhsT=v_t[:, kt, :], rhs=pT_t[:, c0:GW], start=(kt == 0), stop=(kt == last)),
                         reads=[(v_r, kt // 8), pT_r], writes=[O_r])
                    P.op("pe", I("matmul", M_t[:, c0:GW], lhsT=ones[:], rhs=pT_t[:, c0:GW], start=(kt == 0), stop=(kt == last)),
                         reads=["ones", pT_r], writes=[M_r])
                    if kt == last:
                        finish(g)

                def finish(g):
                    gs = slice(g * GW, (g + 1) * GW)
                    ost_t, ost_r = ost_ring.next()
                    O_t, O_r = state["O"]
                    M_t, M_r = state["M"]
                    ln_t, ln_r = ln_ring.next()
                    rs_t, rs_r = rs_ring.next()
                    P.op("act", I("activation", out=ln_t[:], in_=M_t[:], func=AF.Ln), reads=[M_r], writes=[ln_r])
                    P.op("act", I("activation", out=rs_t[:], in_=ln_t[:], func=AF.Exp, scale=-1.0), reads=[ln_r], writes=[rs_r])
                    P.op("dve", I("tensor_tensor", out=ost_t[:], in0=O_t[:], in1=rs_t[:], op=ALU.mult), reads=[O_r, rs_r], writes=[ost_r])
                    P.op("sp", I("dma_start", out=OT0[h][:, gs], in_=ost_t[:]), reads=[ost_r], slot=ost_r)

                N = len(pairs)
                for n in range(min(LOOK, N)):
                    emit_qk(n)
                for n in range(N):
                    if n + LOOK < N:
                        emit_qk(n + LOOK)
                    emit_rest(n)
            P.barrier()

    def outproj_ln(layer, wgu_pre=None):
        st, sb, ps = scope()
        OTd = OT0 if layer == 0 else OT1
        wo_d = w_o0 if layer == 0 else w_o1
        res_d = x if layer == 0 else H2
        Hd, HTd = (H1, H1T) if layer == 0 else (H3, H3T)
        with st:
            C = load_consts(sb, need_ident=True, need_mh=True)
            ident, mh = C["ident"], C["mh"]
            wo = sb([128, 8, D], BF16, "wo")
            load_w_bf16(wo, wo_d, 8, "wo")
            Gt = sb([128, D], F32, "lnG")
            Bt = sb([128, D], F32, "lnB")
            P.op("sp", I("dma_start", out=Gt[:], in_=ln1_g[layer:layer + 1, :].partition_broadcast(128)), writes=["lnG"], slot="lnG")
            P.op("sp", I("dma_start", out=Bt[:], in_=ln1_b[layer:layer + 1, :].partition_broadcast(128)), writes=["lnB"], slot="lnB")
            ot_ring = sbring(sb, 2, [128, H, GW], BF16, "otg")
            res_ring = sbring(sb, 4, [128, D], F32, "res")
            pt_ring = Ring([ps([128, 8, 128], BF16, "pt") for _ in range(2)], "pt3")
            a_ring = Ring([ps([128, GW], F32, "a") for _ in range(6)], "a3")
            lnp = LNPipe(sb, ps, Gt, Bt, ident, mh, HTd, pt_ring)
            res_q = {}
            ot_q = {}

            def load_res(t):
                if t < NT:
                    res_t, res_r = res_ring.next()
                    P.op("sp", I("dma_start", out=res_t[:], in_=res_d[t * 128:(t + 1) * 128, :]), writes=[(res_r, 0), (res_r, 1)], slot=res_r)
                    res_q[t] = (res_t, res_r)

            def load_ot(g):
                if g < NG:
                    ot_t, ot_r = ot_ring.next()
                    P.op("sp", I("dma_start", out=ot_t[:], in_=OTd.rearrange("h p t -> p h t")[:, :, g * GW:(g + 1) * GW]), writes=[ot_r], slot=ot_r)
                    ot_q[g] = (ot_t, ot_r)

            load_ot(0)
            load_res(0)
            load_res(1)
            for g in range(NG):
                ot_t, ot_r = ot_q.pop(g)
                load_ot(g + 1)
                for i in range(4):
                    t = g * 4 + i
                    load_res(t + 2)
                    if wgu_pre is not None and t < 22:
                        half, s_ = t % 2, t // 2
                        lo = half * DFF + s_ * 256
                        vsrc = w_gu[layer].rearrange("(c p) n -> p c n", p=128)
                        P.op("pool", I("dma_start", out=wgu_pre[:, :, lo:lo + 256], in_=vsrc[:, :, lo:lo + 256]),
                             writes=[("wgu", half, s_)], slot=("wgu", half, s_))
                    res_t, res_r = res_q.pop(t)
                    z_t, z_r = res_t, res_r
                    for hh in range(2):
                        a_t, a_r = a_ring.next()
                        for h in range(H):
                            P.op("pe", I("matmul", a_t[:], lhsT=ot_t[:, h, i * 128:(i + 1) * 128],
                                                                                           rhs=wo[:, h, hh * 512:(hh + 1) * 512], start=(h == 0), stop=(h == H - 1)),
                                 reads=[ot_r, ("wo", 0)], writes=[a_r])
                        P.op("dve", I("scalar_tensor_tensor",
                            out=z_t[:, hh * 512:(hh + 1) * 512], in0=res_t[:, hh * 512:(hh + 1) * 512], scalar=ALPHA, in1=a_t[:], op0=ALU.mult, op1=ALU.add),
                            reads=[(res_r, hh), a_r], writes=[(z_r, hh)])
                    lnp.push(z_t, [(z_r, 0), (z_r, 1)], Hd[t * 128:(t + 1) * 128, :], t)
            lnp.flush()
            P.barrier()

    def ffn_ln(layer, wgu_pre=None):
        st, sb, ps = scope()
        HTin = H1T if layer == 0 else H3T
        Hin = H1 if layer == 0 else H3
        Hd, HTd = (H2, H2T) if layer == 0 else (out, None)
        with st:
            C = load_consts(sb, need_ident=True, need_mh=True)
            ident, mh = C["ident"], C["mh"]
            wdn = sb([128, NF, D], BF16, "wdn")
            NSPL = 11
            v = w_gu[layer].rearrange("(c p) n -> p c n", p=128)
            if wgu_pre is not None:
                wgu = wgu_pre
            else:
                wgu = sb([128, 8, 2 * DFF], BF16, "wgu")
                for s_ in range(NSPL):
                    for half in range(2):
                        lo = half * DFF + s_ * 256
                        P.op("pool", I("dma_start", out=wgu[:, :, lo:lo + 256], in_=v[:, :, lo:lo + 256]),
                             writes=[("wgu", half, s_)], slot=("wgu", half, s_))
            vd = w_dn[layer].rearrange("(f p) n -> p f n", p=128)
            for s_ in range(2):
                P.op("pool", I("dma_start", out=wdn[:, s_ * 11:(s_ + 1) * 11, :], in_=vd[:, s_ * 11:(s_ + 1) * 11, :]),
                     writes=[("wdn", s_)], slot=("wdn", s_))
            Gt = sb([128, D], F32, "lnG")
            Bt = sb([128, D], F32, "lnB")
            P.op("sp", I("dma_start", out=Gt[:], in_=ln2_g[layer:layer + 1, :].partition_broadcast(128)), writes=["lnG"], slot="lnG")
            P.op("sp", I("dma_start", out=Bt[:], in_=ln2_b[layer:layer + 1, :].partition_broadcast(128)), writes=["lnB"], slot="lnB")
            hin_ring = sbring(sb, 1, [128, 8, GW], BF16, "hin")
            actT = sb([128, NF, GW], BF16, "actT")
            sg_ring = sbring(sb, 2, [128, GW], F32, "sg")
            res_ring = sbring(sb, 3, [128, D], F32, "res")
            pt_ring = Ring([ps([128, 8, 128], BF16, "pt")], "pt4")
            lnp = LNPipe(sb, ps, Gt, Bt, ident, mh, HTd, pt_ring)
            g_ring = Ring([ps([128, GW], F32, "gb") for _ in range(2)], "gbk")
            u_ring = Ring([ps([128, GW], F32, "ub") for _ in range(2)], "ubk")
            d_ring = Ring([ps([128, GW], F32, "db") for _ in range(3)], "dbk")
            res_q = {}

            def load_res(t):
                res_t, res_r = res_ring.next()
                P.op("sp", I("dma_start", out=res_t[:], in_=Hin[t * 128:(t + 1) * 128, :]), writes=[(res_r, 0), (res_r, 1)], slot=res_r)
                res_q[t] = (res_t, res_r)

            for g in range(NG):
                gs = slice(g * GW, (g + 1) * GW)
                hin_t, hin_r = hin_ring.next()
                P.op("sp", I("dma_start", out=hin_t[:], in_=HTin.rearrange("c p t -> p c t")[:, :, gs]), writes=[hin_r], slot=hin_r)
                for f in range(NF):
                    gb, gr = g_ring.next()
                    ub, ur = u_ring.next()
                    wres = [("wgu", 0, f // 2), ("wgu", 1, f // 2)]
                    for c in range(8):
                        P.op("pe", I("matmul", gb[:], lhsT=wgu[:, c, f * 128:(f + 1) * 128], rhs=hin_t[:, c, :],
                                                                                start=(c == 0), stop=(c == 7)), reads=[hin_r, wres[0]], writes=[gr])
                    for c in range(8):
                        P.op("pe", I("matmul", ub[:], lhsT=wgu[:, c, DFF + f * 128:DFF + (f + 1) * 128], rhs=hin_t[:, c, :],
                                                                                start=(c == 0), stop=(c == 7)), reads=[hin_r, wres[1]], writes=[ur])
                    sg_t, sg_r = sg_ring.next()
                    P.op("act", I("activation", out=sg_t[:], in_=gb[:], func=AF.Silu), reads=[gr], writes=[sg_r])
                    P.op("dve", I("tensor_tensor", out=actT[:, f, :], in0=ub[:], in1=sg_t[:], op=ALU.mult),
                         reads=[ur, sg_r], writes=[("actT", f)])
                for i in range(4):
                    t = g * 4 + i
                    if i == 0:
                        load_res(t)
                    if i < 3:
                        load_res(t + 1)
                    res_t, res_r = res_q.pop(t)
                    z_t, z_r = res_t, res_r
                    for hh in range(2):
                        db, dr = d_ring.next()
                        for f in range(NF):
                            P.op("pe", I("matmul", db[:], lhsT=actT[:, f, i * 128:(i + 1) * 128],
                                                                              rhs=wdn[:, f, hh * 512:(hh + 1) * 512], start=(f == 0), stop=(f == NF - 1)),
                                 reads=[("actT", f), ("wdn", f // 11)], writes=[dr])
                        P.op("dve", I("scalar_tensor_tensor",
                            out=z_t[:, hh * 512:(hh + 1) * 512], in0=res_t[:, hh * 512:(hh + 1) * 512], scalar=ALPHA, in1=db[:], op0=ALU.mult, op1=ALU.add),
                            reads=[(res_r, hh), dr], writes=[(z_r, hh)])
                    lnp.push(z_t, [(z_r, 0), (z_r, 1)], Hd[t * 128:(t + 1) * 128, :], t)
            lnp.flush()
            P.barrier()

    def phase_proj1():
        st, sb, ps = scope()
        with st:
            wk = sb([128, 8, D], BF16, "wk")
            wkr = sb([128, 8, D], BF16, "wkr")
            wq = sb([128, 8, D], BF16, "wq")
            wqr = sb([128, 8, D], BF16, "wqr")
            wv = sb([128, 8, D], BF16, "wv")
            kvv = kv_w.rearrange("(c p) n -> p c n", p=128)
            P.op("pool", I("dma_start", out=wk[:], in_=kvv[:, :, 0:D]), writes=["wk"], slot="wk")
            P.op("pool", I("dma_start", out=wq[:], in_=w_q1.rearrange("(c p) n -> p c n", p=128)), writes=["wq"], slot="wq")
            P.op("pool", I("dma_start", out=wv[:], in_=kvv[:, :, D:2 * D]), writes=["wv"], slot="wv")
            for (src, dst, nm) in ((wk, wkr, "wk"), (wq, wqr, "wq")):
                s4 = src[:].rearrange("p c (b d) -> p c b d", d=64)
                d4 = dst[:].rearrange("p c (b d) -> p c b d", d=64)
                for c in range(8):
                    P.op("act", I("activation", out=d4[:, c, :, 0:32], in_=s4[:, c, :, 32:64], func=AF.Copy, scale=-1.0),
                         reads=[nm], writes=[(nm + "r", c, 0)])
                    P.op("dve", I("tensor_copy", out=d4[:, c, :, 32:64], in_=s4[:, c, :, 0:32]),
                         reads=[nm], writes=[(nm + "r", c, 1)])
            rres = {nm: [(nm + "r", c, k) for c in range(8) for k in range(2)] for nm in ("wk", "wq")}
            hin_ring = sbring(sb, 2, [128, 8, GW], BF16, "hin")
            cos_ring = sbring(sb, 2, [128, GW], F32, "cosd")
            sin_ring = sbring(sb, 2, [128, GW], F32, "sind")
            t1_ring = sbring(sb, 3, [128, GW], F32, "t1")
            t2_ring = sbring(sb, 3, [128, GW], F32, "t2")
            kst_ring = sbring(sb, 2, [128, H, GW], BF16, "kst")
            qst_ring = sbring(sb, 2, [128, H, GW], BF16, "qst")
            vst_ring = sbring(sb, 2, [128, 4, D], BF16, "vst")
            banks = Ring([ps([128, GW], F32, "bk") for _ in range(8)], "bank5")
            for g in range(NG):
                gs = slice(g * GW, (g + 1) * GW)
                hin_t, hin_r = hin_ring.next()
                cos_t, cos_r = cos_ring.next()
                sin_t, sin_r = sin_ring.next()
                P.op("sp", I("dma_start", out=hin_t[:], in_=H2T.rearrange("c p t -> p c t")[:, :, gs]), writes=[hin_r], slot=hin_r)
                P.op("sp", I("dma_start", out=cos_t[:], in_=cosd_d[:, gs]), writes=[cos_r], slot=cos_r)
                P.op("sp", I("dma_start", out=sin_t[:], in_=sind_d[:, gs]), writes=[sin_r], slot=sin_r)
                k_t, k_r = kst_ring.next()
                q_t, q_r = qst_ring.next()
                for (w_, wr_, nm, dst_t, dst_r, sc) in ((wk, wkr, "wk", k_t, k_r, 1.0), (wq, wqr, "wq", q_t, q_r, SC1)):
                    for h in range(H):
                        ab, ar = banks.next()
                        bb, br = banks.next()
                        for c in range(8):
                            P.op("pe", I("matmul", ab[:], lhsT=w_[:, c, h * 128:(h + 1) * 128], rhs=hin_t[:, c, :],
                                                                                           start=(c == 0), stop=(c == 7)), reads=[hin_r, nm], writes=[ar])
                        for c in range(8):
                            P.op("pe", I("matmul", bb[:], lhsT=wr_[:, c, h * 128:(h + 1) * 128], rhs=hin_t[:, c, :],
                                                                                             start=(c == 0), stop=(c == 7)), reads=[hin_r] + rres[nm], writes=[br])
                        t1, t1r = t1_ring.next()
                        t2, t2r = t2_ring.next()
                        P.op("dve", I("scalar_tensor_tensor", out=t1[:], in0=ab[:], scalar=sc, in1=cos_t[:],
                                                                                                   op0=ALU.mult, op1=ALU.mult), reads=[ar, cos_r], writes=[t1r])
                        P.op("dve", I("scalar_tensor_tensor", out=t2[:], in0=bb[:], scalar=sc, in1=sin_t[:],
                                                                                                   op0=ALU.mult, op1=ALU.mult), reads=[br, sin_r], writes=[t2r])
                        P.op("pool", I("tensor_tensor", out=dst_t[:, h, :], in0=t1[:], in1=t2[:], op=ALU.add),
                             reads=[t1r, t2r], writes=[(dst_r, h)])
                P.op("sp", I("dma_start", out=KT1.rearrange("h p t -> p h t")[:, :, gs], in_=k_t[:]),
                     reads=[(k_r, h) for h in range(H)], slot=k_r)
                P.op("sp", I("dma_start", out=QT1.rearrange("h p t -> p h t")[:, :, gs], in_=q_t[:]),
                     reads=[(q_r, h) for h in range(H)], slot=q_r)
                v_t, v_r = vst_ring.next()
                for i in range(4):
                    for hh in range(2):
                        bk, bkr = banks.next()
                        for c in range(8):
                            P.op("pe", I("matmul", bk[:], lhsT=hin_t[:, c, i * 128:(i + 1) * 128],
                                                                                           rhs=wv[:, c, hh * 512:(hh + 1) * 512], start=(c == 0), stop=(c == 7)),
                                 reads=[hin_r, "wv"], writes=[bkr])
                        evac(v_t[:, i, hh * 512:(hh + 1) * 512], bk[:], [bkr], [(v_r, i, hh)], eng="act")
                P.op("sp", I("dma_start", out=V1[g * GW:(g + 1) * GW, :].rearrange("(i p) n -> p i n", p=128), in_=v_t[:]),
                     reads=[(v_r, i, hh) for i in range(4) for hh in range(2)], slot=v_r)
            P.barrier()

    if 1 in phases:
        phase1()
    if 2 in phases:
        attention_old(0)
    def layer_tail(layer, pa, pb):
        if pa in phases and pb in phases:
            st0, sb0, ps0 = scope()
            with st0:
                wgu_pre = sb0([128, 8, 2 * DFF], BF16, "wgu")
                outproj_ln(layer, wgu_pre)
                ffn_ln(layer, wgu_pre)
        else:
            if pa in phases:
                outproj_ln(layer)
            if pb in phases:
                ffn_ln(layer)

    layer_tail(0, 3, 4)
    if 5 in phases:
        phase_proj1()
    if 6 in phases:
        attention(1)
    layer_tail(1, 7, 8)
    P.emit()
    return nc, P


def _rope_tables(dim):
    inv = (1.0 / (10000.0 ** (np.arange(0, dim, 2, dtype=np.float32) / np.float32(dim)))).astype(np.float32)
    ang = np.arange(S, dtype=np.float32)[:, None] * inv[None, :]
    ang = np.concatenate([ang, ang], axis=-1).astype(np.float32)
    return np.ascontiguousarray(np.cos(ang).T.astype(np.float32)), np.ascontiguousarray(np.sin(ang).T.astype(np.float32))


def _consts():
    ident = np.eye(128, dtype=np.float32).astype(ml_dtypes.bfloat16)
    ki = np.arange(128)[:, None, None]
    j = np.arange(4)[None, :, None]
    qi = np.arange(GW)[None, None, :]
    mask = np.where(qi >= 128 * j + ki, 0.0, NEG).astype(np.float32).astype(ml_dtypes.bfloat16)
    cm, sm = _rope_tables(64)
    cd = np.ascontiguousarray(np.concatenate([cm, cm], axis=0))
    sd = np.ascontiguousarray(np.concatenate([sm, sm], axis=0))
    return {"c_ident": ident, "c_mask": np.ascontiguousarray(mask), "c_cosm": cm, "c_sinm": sm, "c_cosd": cd, "c_sind": sd}


_SQUEEZE = ("mla_w_dq", "mla_q_norm", "mla_w_uq", "mla_w_dkv", "mla_kv_norm", "mla_w_ukv", "mla_w_o",
            "diff_w_q", "diff_subln", "diff_w_o")


def make_in_maps(inputs, n_cores=8):
    common = dict(_consts())
    for k, v in inputs.items():
        if k == "x":
            continue
        a = np.ascontiguousarray(np.asarray(v, dtype=np.float32))
        if k in _SQUEEZE:
            a = np.ascontiguousarray(a[0])
        common[k] = a
    xs = np.asarray(inputs["x"], dtype=np.float32)
    maps = []
    for c in range(n_cores):
        m = dict(common)
        m["x"] = np.ascontiguousarray(xs[c])
        maps.append(m)
    return maps


_CACHE = {}


def kernel(**inputs):
    if "nc" not in _CACHE:
        _CACHE["nc"] = build_program()[0]
    nc = _CACHE["nc"]
    in_maps = make_in_maps(inputs, 8)
    res = run_bass_kernel_spmd(nc, in_maps, core_ids=list(range(8)))
    return np.stack([np.asarray(r["out"], dtype=np.float32) for r in res.results], axis=0)
```

```python
import math
from contextlib import ExitStack

import numpy as np
import ml_dtypes
import concourse.bass as bass
import concourse.mybir as mybir
from concourse.bass_utils import run_bass_kernel_spmd

F32 = mybir.dt.float32
BF16 = mybir.dt.bfloat16
AF = mybir.ActivationFunctionType
ALU = mybir.AluOpType

S = 4096
D = 1024
NT = 32
NG = 8
GW = 512
H = 8
DFF = 2816
NF = 22
ALPHA = 4.0 ** 0.25
LN_EPS = 1e-5
RMS_EPS = 1e-6
SC0 = 192.0 ** -0.5
SC1 = 64.0 ** -0.5
LAMBDA_INIT = 0.8 - 0.6 * math.exp(-0.3 * 1)
NEG = -30000.0

ENGS = ("pe", "act", "dve", "pool", "sp")


class _Op:
    __slots__ = ("eng", "fn", "deps", "signal", "slot", "sigval", "idx", "phase")


class Prog:
    def __init__(self, nc, same_engine_sync=True):
        self.nc = nc
        self.ops = []
        self.last_w = {}
        self.readers = {}
        self.same_engine_sync = same_engine_sync
        self.fence = set()
        self.last_on_eng = {}
        self.dma_since = []
        self.phase = 0

    def barrier(self):
        self.phase += 1
        self.fence = set(self.last_on_eng.values()) | set(self.dma_since)
        self.dma_since = []
        self.last_w = {}
        self.readers = {}

    def op(self, eng, fn, reads=(), writes=(), slot=None):
        o = _Op()
        o.eng, o.fn, o.slot, o.signal, o.sigval = eng, fn, slot, slot is not None, None
        o.idx = len(self.ops)
        o.phase = self.phase
        deps = set(self.fence)
        for r in reads:
            w = self.last_w.get(r)
            if w is not None:
                deps.add(w)
        for r in writes:
            w = self.last_w.get(r)
            if w is not None:
                deps.add(w)
            for rd in self.readers.get(r, ()):
                deps.add(rd)
        if eng == "pool" and slot is not None:
            if getattr(self, "last_pool_dma", None) is not None:
                deps.add(self.last_pool_dma)
            self.last_pool_dma = o.idx
        o.deps = deps
        for r in reads:
            lst = self.readers.setdefault(r, [])
            if slot is None:
                lst[:] = [i for i in lst if not (self.ops[i].slot is None and self.ops[i].eng == eng)]
            lst.append(o.idx)
        for r in writes:
            self.last_w[r] = o.idx
            self.readers[r] = []
        self.ops.append(o)
        if slot is None:
            self.last_on_eng[eng] = o.idx
        else:
            self.dma_since.append(o.idx)
        return o.idx

    def _skip(self, p, o):
        if p.slot is None and o.slot is None and p.eng == o.eng:
            return p.eng == "pe" or not self.same_engine_sync
        return False

    def emit(self):
        nc = self.nc
        ops = self.ops
        for o in ops:
            for d in o.deps:
                p = ops[d]
                if p.slot is None and not self._skip(p, o):
                    p.signal = True
        cnt = {e: 0 for e in ENGS}
        slotcnt = {}
        physmap = {}
        nphys = {}
        for o in ops:
            if o.slot is not None:
                key = (o.eng, o.phase, o.slot)
                if key not in physmap:
                    physmap[key] = (o.eng, nphys.get((o.eng, o.phase), 0))
                    nphys[(o.eng, o.phase)] = physmap[key][1] + 1
                ph = physmap[key]
                slotcnt[ph] = slotcnt.get(ph, 0) + 16
                o.sigval = (("dma", ph), slotcnt[ph])
            elif o.signal:
                cnt[o.eng] += 1
                o.sigval = (o.eng, cnt[o.eng])
        keys = [e for e in ENGS if cnt[e] > 0] + [("dma", s) for s in slotcnt]
        self.n_sems = len(keys)
        with ExitStack() as st:
            sems = {}
            for i, k in enumerate(keys):
                sems[k] = st.enter_context(nc.semaphore("sm%d" % i))
            block = st.enter_context(nc.Block())
            per_eng = {e: [o for o in ops if o.eng == e] for e in ENGS}

            def run(engname, eng):
                waited = {}
                for o in per_eng[engname]:
                    need = {}
                    for d in o.deps:
                        p = ops[d]
                        if p.sigval is None or self._skip(p, o):
                            continue
                        k, v = p.sigval
                        if need.get(k, 0) < v:
                            need[k] = v
                    for k, v in need.items():
                        if waited.get(k, 0) < v:
                            eng.wait_ge(sems[k], v)
                            waited[k] = v
                    ins = o.fn(eng)
                    if o.sigval is not None:
                        ins.then_inc(sems[o.sigval[0]], 16 if o.slot is not None else 1)
                if engname == "sp":
                    for s_, v in slotcnt.items():
                        if waited.get(("dma", s_), 0) < v:
                            eng.wait_ge(sems[("dma", s_)], v)

            @block.sync
            def _(e):
                run("sp", e)

            @block.tensor
            def _(e):
                run("pe", e)

            @block.scalar
            def _(e):
                run("act", e)

            @block.vector
            def _(e):
                run("dve", e)

            @block.gpsimd
            def _(e):
                run("pool", e)


def I(method, *args, **kwargs):
    return lambda e: getattr(e, method)(*args, **kwargs)


class Ring:
    def __init__(self, tiles, name):
        self.tiles, self.name, self.i = tiles, name, -1

    def next(self):
        self.i += 1
        k = self.i % len(self.tiles)
        return self.tiles[k], (self.name, k)


def build_program(phases=(1, 2, 3, 4, 5, 6, 7, 8), debug=()):
    nc = bass.Bass("TRN2", target_bir_lowering=False)

    def din(name, shape, dt=F32):
        return nc.dram_tensor(name, list(shape), dt, kind="ExternalInput").ap()

    def dscr(name, shape, dt):
        kind = "ExternalOutput" if name in debug else "Internal"
        return nc.dram_tensor(name, list(shape), dt, kind=kind).ap()

    x = din("x", [S, D])
    w_dq = din("mla_w_dq", [D, 384])
    q_norm = din("mla_q_norm", [384])
    w_uq = din("mla_w_uq", [384, 1536])
    w_dkv = din("mla_w_dkv", [D, 320])
    kv_norm = din("mla_kv_norm", [256])
    w_ukv = din("mla_w_ukv", [256, 2048])
    w_o0 = din("mla_w_o", [D, D])
    kv_w = din("kv_w", [D, 2048])
    w_q1 = din("diff_w_q", [D, D])
    lq1 = din("diff_lq1", [1, 64])
    lk1 = din("diff_lk1", [1, 64])
    lq2 = din("diff_lq2", [1, 64])
    lk2 = din("diff_lk2", [1, 64])
    subln = din("diff_subln", [128])
    w_o1 = din("diff_w_o", [D, D])
    ln1_g = din("ln1_g", [2, D])
    ln1_b = din("ln1_b", [2, D])
    ln2_g = din("ln2_g", [2, D])
    ln2_b = din("ln2_b", [2, D])
    w_gu = din("ffn_w_gate_up", [2, D, 2 * DFF])
    w_dn = din("ffn_w_down", [2, DFF, D])
    ident_d = din("c_ident", [128, 128], BF16)
    mask_d = din("c_mask", [128, 4, GW], BF16)
    cosm_d = din("c_cosm", [64, S])
    sinm_d = din("c_sinm", [64, S])
    cosd_d = din("c_cosd", [128, S])
    sind_d = din("c_sind", [128, S])
    out = nc.dram_tensor("out", [S, D], F32, kind="ExternalOutput").ap()

    QN0 = dscr("QN0", [H, 128, S], BF16)
    QR0 = dscr("QR0", [H, 64, S], BF16)
    KN0 = dscr("KN0", [H, 128, S], BF16)
    KR0 = dscr("KR0", [64, S], BF16)
    V0 = dscr("V0", [S, D], BF16)
    OT0 = dscr("OT0", [H, 128, S], BF16)
    H1 = dscr("H1", [S, D], F32)
    H1T = dscr("H1T", [8, 128, S], BF16)
    H2 = dscr("H2", [S, D], F32)
    H2T = dscr("H2T", [8, 128, S], BF16)
    QT1 = dscr("QT1", [H, 128, S], BF16)
    KT1 = dscr("KT1", [H, 128, S], BF16)
    V1 = dscr("V1", [S, D], BF16)
    OT1 = dscr("OT1", [H, 128, S], BF16)
    H3 = dscr("H3", [S, D], F32)
    H3T = dscr("H3T", [8, 128, S], BF16)

    P = Prog(nc)
    uid = [0]

    def scope():
        st = ExitStack()

        def sb(shape, dt, name=None):
            uid[0] += 1
            return st.enter_context(nc.sbuf_tensor("%s_%d" % (name or "t", uid[0]), list(shape), dt))

        def ps(shape, dt, name=None):
            uid[0] += 1
            return st.enter_context(nc.psum_tensor("%s_%d" % (name or "p", uid[0]), list(shape), dt))

        return st, sb, ps

    def sbring(sb, n, shape, dt, name):
        return Ring([sb(shape, dt, name) for _ in range(n)], name + str(uid[0]))

    evac_rr = [0]

    def evac(out_ap, in_ap, reads, writes, scale=None, eng=None):
        if eng is None:
            evac_rr[0] += 1
            eng = "act" if evac_rr[0] % 2 else "dve"
        if eng == "act":
            if scale is None:
                P.op("act", I("activation", out=out_ap, in_=in_ap, func=AF.Copy), reads=reads, writes=writes)
            else:
                P.op("act", I("activation", out=out_ap, in_=in_ap, func=AF.Copy, scale=scale), reads=reads, writes=writes)
        else:
            if scale is None:
                P.op("dve", I("tensor_copy", out=out_ap, in_=in_ap), reads=reads, writes=writes)
            else:
                P.op("dve", I("tensor_scalar", out=out_ap, in0=in_ap, scalar1=scale, scalar2=None, op0=ALU.mult),
                     reads=reads, writes=writes)

    def load_w_bf16(dst_tile, src_ap, nchunk, res_prefix, split=1):
        n = src_ap.shape[-1]
        v = src_ap.rearrange("(c p) n -> p c n", p=128)
        step = n // split
        for s_ in range(split):
            lo, hi = s_ * step, (s_ + 1) * step
            P.op("pool", I("dma_start", out=dst_tile[:, :, lo:hi], in_=v[:, :, lo:hi]),
                 writes=[(res_prefix, s_)], slot=(res_prefix, s_))

    class LNPipe:
        def __init__(self, sb, ps, Gt, Bt, ident, mh, HTd, pt_ring):
            self.stt = sbring(sb, 2, [128, 2, 6], F32, "stt")
            self.mv = sbring(sb, 2, [128, 2], F32, "mv")
            self.ve = sbring(sb, 2, [128, 1], F32, "ve")
            self.rstd = sbring(sb, 3, [128, 1], F32, "rstd")
            self.nmr = sbring(sb, 3, [128, 1], F32, "nmr")
            self.hn = sbring(sb, 2, [128, D], F32, "hn")
            self.Gt, self.Bt, self.ident, self.mh, self.HTd, self.pt_ring = Gt, Bt, ident, mh, HTd, pt_ring
            if HTd is not None:
                self.hb = sbring(sb, 2, [128, D], BF16, "hb")
                self.hTt = sbring(sb, 2, [128, 8, 128], BF16, "hTt")
            self.st = {}
            self.n = 0

        def push(self, z, zres, dst_rows, t):
            k = self.n
            self.n += 1
            zr = list(zres)
            stt_t, stt_r = self.stt.next()
            mv_t, mv_r = self.mv.next()
            ve_t, ve_r = self.ve.next()
            rs_t, rs_r = self.rstd.next()
            nm_t, nm_r = self.nmr.next()
            self.st[k] = dict(z=z, zr=zr, dst=dst_rows, t=t)
            P.op("dve", I("bn_stats", out=stt_t[:, 0, :], in_=z[:, 0:512]), reads=zr, writes=[(stt_r, 0)])
            P.op("dve", I("bn_stats", out=stt_t[:, 1, :], in_=z[:, 512:1024]), reads=zr, writes=[(stt_r, 1)])
            P.op("dve", I("bn_aggr", out=mv_t[:], in_=stt_t[:].rearrange("p a b -> p (a b)")),
                 reads=[(stt_r, 0), (stt_r, 1)], writes=[mv_r])
            P.op("dve", I("tensor_scalar", out=ve_t[:], in0=mv_t[:, 1:2], scalar1=LN_EPS, scalar2=None, op0=ALU.add),
                 reads=[mv_r], writes=[ve_r])
            P.op("pool", I("tensor_tensor", out=rs_t[:], in0=ve_t[:], in1=self.mh[:], op=ALU.pow), reads=[ve_r, "mh"], writes=[rs_r])
            self._stage2(k - 1)
            self._stage3(k - 2)
            P.op("dve", I("tensor_scalar", out=nm_t[:], in0=mv_t[:, 0:1], scalar1=-1.0, scalar2=rs_t[:], op0=ALU.mult, op1=ALU.mult),
                 reads=[mv_r, rs_r], writes=[nm_r])
            P.op("act", I("activation", out=z[:], in_=z[:], func=AF.Identity, scale=rs_t[:], bias=nm_t[:]),
                 reads=zr + [rs_r, nm_r], writes=zr)

        def _stage2(self, k):
            if k < 0 or k not in self.st:
                return
            d = self.st[k]
            z, zr = d["z"], d["zr"]
            hn_t, hn_r = self.hn.next()
            P.op("dve", I("tensor_tensor", out=z[:], in0=z[:], in1=self.Gt[:], op=ALU.mult), reads=zr + ["lnG"], writes=zr)
            P.op("pool", I("tensor_tensor", out=hn_t[:], in0=z[:], in1=self.Bt[:], op=ALU.add), reads=zr + ["lnB"], writes=[hn_r])
            P.op("sp", I("dma_start", out=d["dst"], in_=hn_t[:]), reads=[hn_r], slot=hn_r)
            if self.HTd is not None:
                hb_t, hb_r = self.hb.next()
                P.op("act", I("activation", out=hb_t[:], in_=hn_t[:], func=AF.Copy), reads=[hn_r], writes=[hb_r])
                d["hb"] = (hb_t, hb_r)

        def _stage3(self, k):
            if k < 0 or k not in self.st:
                return
            d = self.st.pop(k)
            if self.HTd is None:
                return
            hb_t, hb_r = d["hb"]
            t = d["t"]
            pt_t, pt_r = self.pt_ring.next()
            for c in range(8):
                P.op("pe", I("transpose", out=pt_t[:, c, :], in_=hb_t[:, c * 128:(c + 1) * 128], identity=self.ident[:]),
                     reads=[hb_r, "ident"], writes=[pt_r])
            hT_t, hT_r = self.hTt.next()
            P.op("dve", I("tensor_copy", out=hT_t[:], in_=pt_t[:]), reads=[pt_r], writes=[hT_r])
            P.op("sp", I("dma_start", out=self.HTd.rearrange("c p t -> p c t")[:, :, t * 128:(t + 1) * 128], in_=hT_t[:]),
                 reads=[hT_r], slot=hT_r)

        def flush(self):
            self._stage2(self.n - 1)
            self._stage3(self.n - 2)
            self._stage3(self.n - 1)

    def load_consts(sb, need_ident=True, need_mask=False, need_ones=False, need_mh=False):
        r = {}
        if need_ident:
            r["ident"] = sb([128, 128], BF16, "ident")
            P.op("sp", I("dma_start", out=r["ident"][:], in_=ident_d[:, :]), writes=["ident"], slot="ident")
        if need_mask:
            r["mask"] = sb([128, 4, GW], BF16, "mask")
            P.op("sp", I("dma_start", out=r["mask"][:], in_=mask_d[:, :, :]), writes=["mask"], slot="mask")
        if need_ones:
            r["ones"] = sb([128, 128], BF16, "ones")
            P.op("pool", I("memset", r["ones"][:], 1.0), writes=["ones"])
        if need_mh:
            r["mh"] = sb([128, 1], F32, "mh")
            P.op("pool", I("memset", r["mh"][:], -0.5), writes=["mh"])
        return r

    def rms_fm(banks, bank_res, nchunk, ndim, gain_t, out_tile, out_res, sq_ring, ss_bank, ss_res, ln_ring, rstd_ring, ones):
        sqs = []
        for m in range(nchunk):
            sq_t, sq_r = sq_ring.next()
            P.op("act", I("activation", out=sq_t[:], in_=banks[m][:], func=AF.Square),
                 reads=[bank_res[m]], writes=[sq_r])
            sqs.append((sq_t, sq_r))
        for m in range(nchunk):
            P.op("pe", I("matmul", ss_bank[:], lhsT=ones[:], rhs=sqs[m][0][:], start=(m == 0), stop=(m == nchunk - 1)),
                 reads=["ones", sqs[m][1]], writes=[ss_res])
        ln_t, ln_r = ln_ring.next()
        rs_t, rs_r = rstd_ring.next()
        P.op("act", I("activation", out=ln_t[:], in_=ss_bank[:], func=AF.Ln, scale=1.0 / ndim, bias=RMS_EPS),
             reads=[ss_res], writes=[ln_r])
        P.op("act", I("activation", out=rs_t[:], in_=ln_t[:], func=AF.Exp, scale=-0.5), reads=[ln_r], writes=[rs_r])
        for m in range(nchunk):
            P.op("dve", I("scalar_tensor_tensor", out=out_tile[:, m, :], in0=banks[m][:], scalar=gain_t[:, m:m + 1],
                                                              in1=rs_t[:], op0=ALU.mult, op1=ALU.mult),
                 reads=[bank_res[m], rs_r, "gains"], writes=[(out_res, m)])

    def phase1():
        st, sb, ps = scope()
        with st:
            C = load_consts(sb, need_ident=True, need_ones=True)
            ident, ones = C["ident"], C["ones"]
            wdq = sb([128, 8, 384], BF16, "wdq")
            wdkv = sb([128, 8, 384], BF16, "wdkv")
            wuq = sb([128, 3, 1536], BF16, "wuq")
            wuqr = sb([128, 3, 8, 64], BF16, "wuqr")
            wukv = sb([128, 2, 2048], BF16, "wukv")
            gq = sb([128, 3], F32, "gq")
            gkv = sb([128, 2], F32, "gkv")
            load_w_bf16(wdq, w_dq, 8, "wdq")
            P.op("pool", I("dma_start", out=wdkv[:, :, 0:320], in_=w_dkv.rearrange("(c p) n -> p c n", p=128)),
                 writes=["wdkv"], slot="wdkv")
            load_w_bf16(wuq, w_uq, 3, "wuq")
            load_w_bf16(wukv, w_ukv, 2, "wukv")
            for m in range(3):
                P.op("sp", I("dma_start", out=gq[:, m:m + 1], in_=q_norm.rearrange("(c p o) -> c p o", p=128, o=1)[m]),
                     writes=["gains"], slot=("gq", m))
            for m in range(2):
                P.op("sp", I("dma_start", out=gkv[:, m:m + 1], in_=kv_norm.rearrange("(c p o) -> c p o", p=128, o=1)[m]),
                     writes=["gains"], slot=("gkv", m))
            P.op("pool", I("tensor_scalar", out=wdkv[:, :, 320:352], in0=wdkv[:, :, 288:320], scalar1=-1.0, scalar2=None, op0=ALU.mult),
                 reads=["wdkv"], writes=["wdkvr"])
            P.op("pool", I("tensor_copy", out=wdkv[:, :, 352:384], in_=wdkv[:, :, 256:288]), reads=["wdkv"], writes=["wdkvr2"])
            wuq4 = wuq[:].rearrange("p c (h d) -> p c h d", d=192)
            for c in range(3):
                P.op("pool", I("tensor_scalar", out=wuqr[:, c, :, 0:32], in0=wuq4[:, c, :, 160:192], scalar1=-1.0, scalar2=None,
                                                             op0=ALU.mult), reads=[("wuq", 0)], writes=[("wuqr", c, 0)])
                P.op("pool", I("tensor_copy", out=wuqr[:, c, :, 32:64], in_=wuq4[:, c, :, 128:160]),
                     reads=[("wuq", 0)], writes=[("wuqr", c, 1)])
            wuqr_res = [("wuqr", c, k) for c in range(3) for k in range(2)]
            wdkv_res = ["wdkv", "wdkvr", "wdkvr2"]

            xs_ring = sbring(sb, 2, [128, D], F32, "xs")
            xb_ring = sbring(sb, 2, [128, D], BF16, "xb")
            xT_ring = sbring(sb, 2, [128, 8, GW], BF16, "xT")
            pt_ring = Ring([ps([128, 8, 128], BF16, "pt")], "pt1")
            banks = [ps([128, GW], F32, "bk") for _ in range(7)]
            bres = [("bank1", i) for i in range(7)]
            sq_ring = sbring(sb, 3, [128, GW], BF16, "sq")
            ln_ring = sbring(sb, 1, [128, GW], F32, "lnss")
            rstd_ring = sbring(sb, 2, [128, GW], F32, "rstdfm")
            cqn_ring = sbring(sb, 2, [128, 3, GW], BF16, "cqn")
            cn_ring = sbring(sb, 2, [128, 2, GW], BF16, "cn")
            cos_ring = sbring(sb, 2, [64, GW], F32, "cosm")
            sin_ring = sbring(sb, 2, [64, GW], F32, "sinm")
            t1_ring = sbring(sb, 2, [64, GW], F32, "t1")
            t2_ring = sbring(sb, 2, [64, GW], F32, "t2")
            qnst_ring = sbring(sb, 2, [128, H, GW], BF16, "qnst")
            qrst_ring = sbring(sb, 2, [64, H, GW], BF16, "qrst")
            knst_ring = sbring(sb, 2, [128, H, GW], BF16, "knst")
            vst_ring = sbring(sb, 2, [128, 4, D], BF16, "vst")
            krst_ring = sbring(sb, 2, [64, GW], BF16, "krst")
            bi = [0]

            def nb():
                bi[0] += 1
                k = bi[0] % 7
                return banks[k], bres[k]

            for g in range(NG):
                gs = slice(g * GW, (g + 1) * GW)
                cos_t, cos_r = cos_ring.next()
                sin_t, sin_r = sin_ring.next()
                P.op("sp", I("dma_start", out=cos_t[:], in_=cosm_d[:, gs]), writes=[cos_r], slot=cos_r)
                P.op("sp", I("dma_start", out=sin_t[:], in_=sinm_d[:, gs]), writes=[sin_r], slot=sin_r)
                xT_t, xT_r = xT_ring.next()
                for i in range(4):
                    t = g * 4 + i
                    xs_t, xs_r = xs_ring.next()
                    xb_t, xb_r = xb_ring.next()
                    P.op("sp", I("dma_start", out=xs_t[:], in_=x[t * 128:(t + 1) * 128, :]), writes=[xs_r], slot=xs_r)
                    P.op("pool", I("tensor_copy", out=xb_t[:], in_=xs_t[:]), reads=[xs_r], writes=[xb_r])
                    pt_t, pt_r = pt_ring.next()
                    for c in range(8):
                        P.op("pe", I("transpose", out=pt_t[:, c, :], in_=xb_t[:, c * 128:(c + 1) * 128],
                                                                                  identity=ident[:]), reads=[xb_r, "ident"], writes=[pt_r])
                    evac(xT_t[:, :, i * 128:(i + 1) * 128], pt_t[:], [pt_r], [(xT_r, i)])
                xT_all = [(xT_r, i) for i in range(4)]
                cqb = [nb() for _ in range(3)]
                for m in range(3):
                    for c in range(8):
                        P.op("pe", I("matmul", cqb[m][0][:], lhsT=wdq[:, c, m * 128:(m + 1) * 128], rhs=xT_t[:, c, :],
                                                                             start=(c == 0), stop=(c == 7)),
                             reads=xT_all + [("wdq", 0)], writes=[cqb[m][1]])
                ssb, ssr = nb()
                cqn_t, cqn_r = cqn_ring.next()
                rms_fm([b[0] for b in cqb], [b[1] for b in cqb], 3, 384, gq, cqn_t, cqn_r, sq_ring, ssb, ssr, ln_ring, rstd_ring, ones)
                cqn_all = [(cqn_r, m) for m in range(3)]
                cb = [nb() for _ in range(2)]
                for m in range(2):
                    for c in range(8):
                        P.op("pe", I("matmul", cb[m][0][:], lhsT=wdkv[:, c, m * 128:(m + 1) * 128], rhs=xT_t[:, c, :],
                                                                            start=(c == 0), stop=(c == 7)),
                             reads=xT_all + wdkv_res, writes=[cb[m][1]])
                ssb, ssr = nb()
                cn_t, cn_r = cn_ring.next()
                rms_fm([b[0] for b in cb], [b[1] for b in cb], 2, 256, gkv, cn_t, cn_r, sq_ring, ssb, ssr, ln_ring, rstd_ring, ones)
                cn_all = [(cn_r, m) for m in range(2)]
                ab, ar = nb()
                bb, br = nb()
                for c in range(8):
                    P.op("pe", I("matmul", ab[0:64, :], lhsT=wdkv[:, c, 256:320], rhs=xT_t[:, c, :], start=(c == 0), stop=(c == 7)),
                         reads=xT_all + wdkv_res, writes=[ar])
                for c in range(8):
                    P.op("pe", I("matmul", bb[0:64, :], lhsT=wdkv[:, c, 320:384], rhs=xT_t[:, c, :], start=(c == 0), stop=(c == 7)),
                         reads=xT_all + wdkv_res, writes=[br])
                t1, t1r = t1_ring.next()
                t2, t2r = t2_ring.next()
                kr_t, kr_r = krst_ring.next()
                P.op("dve", I("tensor_tensor", out=t1[:], in0=ab[0:64, :], in1=cos_t[:], op=ALU.mult),
                     reads=[ar, cos_r], writes=[t1r])
                P.op("dve", I("tensor_tensor", out=t2[:], in0=bb[0:64, :], in1=sin_t[:], op=ALU.mult),
                     reads=[br, sin_r], writes=[t2r])
                P.op("pool", I("tensor_tensor", out=kr_t[:], in0=t1[:], in1=t2[:], op=ALU.add),
                     reads=[t1r, t2r], writes=[kr_r])
                P.op("sp", I("dma_start", out=KR0[:, gs], in_=kr_t[:]), reads=[kr_r], writes=[("KR0", g)], slot=kr_r)
                qn_t, qn_r = qnst_ring.next()
                qr_t, qr_r = qrst_ring.next()
                kn_t, kn_r = knst_ring.next()
                for h in range(H):
                    bk, bkr = nb()
                    for c in range(3):
                        P.op("pe", I("matmul", bk[:], lhsT=wuq[:, c, h * 192:h * 192 + 128], rhs=cqn_t[:, c, :],
                                                                       start=(c == 0), stop=(c == 2)), reads=cqn_all + [("wuq", 0)], writes=[bkr])
                    evac(qn_t[:, h, :], bk[:], [bkr], [(qn_r, h)], scale=SC0)
                    ab, ar = nb()
                    bb, br = nb()
                    for c in range(3):
                        P.op("pe", I("matmul", ab[0:64, :], lhsT=wuq[:, c, h * 192 + 128:h * 192 + 192], rhs=cqn_t[:, c, :],
                                                                       start=(c == 0), stop=(c == 2)), reads=cqn_all + [("wuq", 0)], writes=[ar])
                    for c in range(3):
                        P.op("pe", I("matmul", bb[0:64, :], lhsT=wuqr[:, c, h, :], rhs=cqn_t[:, c, :],
                                                                       start=(c == 0), stop=(c == 2)), reads=cqn_all + wuqr_res, writes=[br])
                    t1, t1r = t1_ring.next()
                    t2, t2r = t2_ring.next()
                    P.op("dve", I("scalar_tensor_tensor", out=t1[:], in0=ab[0:64, :], scalar=SC0, in1=cos_t[:],
                                                                                          op0=ALU.mult, op1=ALU.mult), reads=[ar, cos_r], writes=[t1r])
                    P.op("dve", I("scalar_tensor_tensor", out=t2[:], in0=bb[0:64, :], scalar=SC0, in1=sin_t[:],
                                                                                          op0=ALU.mult, op1=ALU.mult), reads=[br, sin_r], writes=[t2r])
                    P.op("pool", I("tensor_tensor", out=qr_t[:, h, :], in0=t1[:], in1=t2[:], op=ALU.add),
                         reads=[t1r, t2r], writes=[(qr_r, h)])
                    bk, bkr = nb()
                    for c in range(2):
                        P.op("pe", I("matmul", bk[:], lhsT=wukv[:, c, h * 256:h * 256 + 128], rhs=cn_t[:, c, :],
                                                                       start=(c == 0), stop=(c == 1)), reads=cn_all + [("wukv", 0)], writes=[bkr])
                    evac(kn_t[:, h, :], bk[:], [bkr], [(kn_r, h)])
                P.op("sp", I("dma_start", out=QN0.rearrange("h p t -> p h t")[:, :, gs], in_=qn_t[:]),
                     reads=[(qn_r, h) for h in range(H)], writes=[("QN0", g)], slot=qn_r)
                P.op("sp", I("dma_start", out=QR0.rearrange("h p t -> p h t")[:, :, gs], in_=qr_t[:]),
                     reads=[(qr_r, h) for h in range(H)], writes=[("QR0", g)], slot=qr_r)
                P.op("sp", I("dma_start", out=KN0.rearrange("h p t -> p h t")[:, :, gs], in_=kn_t[:]),
                     reads=[(kn_r, h) for h in range(H)], writes=[("KN0", g)], slot=kn_r)
                v_t, v_r = vst_ring.next()
                wv4 = wukv[:].rearrange("p c (h d) -> p c h d", d=256)
                for i in range(4):
                    for hh in range(2):
                        bk, bkr = nb()
                        for c in range(2):
                            P.op("pe", I("matmul", bk[:].rearrange("p (h d) -> p h d", d=128),
                                                                               lhsT=cn_t[:, c, i * 128:(i + 1) * 128],
                                                                               rhs=wv4[:, c, hh * 4:(hh + 1) * 4, 128:256], start=(c == 0), stop=(c == 1)),
                                 reads=cn_all + [("wukv", 0)], writes=[bkr])
                        evac(v_t[:, i, hh * 512:(hh + 1) * 512], bk[:], [bkr], [(v_r, i, hh)])
                P.op("sp", I("dma_start", out=V0[g * GW:(g + 1) * GW, :].rearrange("(i p) n -> p i n", p=128), in_=v_t[:]),
                     reads=[(v_r, i, hh) for i in range(4) for hh in range(2)], writes=[("V0", g)], slot=v_r)
            P.barrier()

    def attention(layer, GQ=GW):
        st, sb, ps = scope()
        with st:
            nmap = 1 if layer == 0 else 2
            NSUB = GQ // 128
            NGQ = S // GQ
            UNIT = False
            if UNIT:
                S_ring = Ring([ps([128, 1, GQ], F32, "S") for _ in range(4)], "S1u")
                NSET = 1
            elif nmap * GQ <= 512:
                S_ring = Ring([ps([128, nmap, GQ], F32, "S") for _ in range(3)], "S0")
                NSET = 2
            else:
                S_ring = Ring([ps([128, 2, GQ], F32, "S") for _ in range(2)], "S1")
                NSET = 1
            nacc = NSUB * nmap
            nbank = (nacc + 2) // 3
            O_sets = [[ps([128, GW], F32, "O") for _ in range(nbank)] for _ in range(NSET)]
            pt = ps([128, 8, 128], BF16, "ptA")
            AST = 130
            VST = 130
            NV = 129

            def acc_ap(set_i, a, lo=0, hi=NV):
                return O_sets[set_i][a // 3][:, (a % 3) * AST + lo:(a % 3) * AST + hi]

            C = load_consts(sb, need_ident=True, need_mask=True)
            ident, mask = C["ident"], C["mask"]
            if layer == 0:
                qn_ring = sbring(sb, 2, [128, S], BF16, "qn")
                qr_ring = sbring(sb, 2, [64, S], BF16, "qr")
                kn_ring = sbring(sb, 2, [128, S], BF16, "kn")
                kr = sb([64, S], BF16, "kr")
                P.op("sp", I("dma_start", out=kr[:], in_=KR0[:, :]), writes=["kr"], slot="kr")
                Vd, OTd = V0, OT0
            else:
                qn_ring = sbring(sb, 2, [128, 2, S], BF16, "q1")
                kn_ring = sbring(sb, 2, [128, S], BF16, "k1")
                for k_ in range(2):
                    P.op("pool", I("memset", qn_ring.tiles[k_][64:128, 0, :], 0.0), writes=[("qz", k_, 0)])
                    P.op("pool", I("memset", qn_ring.tiles[k_][0:64, 1, :], 0.0), writes=[("qz", k_, 1)])
                Vd, OTd = V1, OT1
                mh = sb([128, 1], F32, "mh")
                P.op("pool", I("memset", mh[:], -0.5), writes=["mh"])
                lt = [sb([128, 64], F32, "lt") for _ in range(4)]
                for k_, src in enumerate((lq1, lk1, lq2, lk2)):
                    P.op("sp", I("dma_start", out=lt[k_][:], in_=src[0:1, :].partition_broadcast(128)),
                         writes=[("lt", k_)], slot=("lt", k_))
                pr = [sb([128, 64], F32, "pr") for _ in range(2)]
                sm = [sb([128, 1], F32, "lsm") for _ in range(2)]
                ex = [sb([128, 1], F32, "lex") for _ in range(2)]
                neglam = sb([128, 1], F32, "neglam")
                gvec = sb([128, 128], F32, "gvec")
                junk = sb([128, 64], F32, "junk")
                for k_ in range(2):
                    P.op("dve", I("tensor_tensor", out=pr[k_][:], in0=lt[2 * k_][:], in1=lt[2 * k_ + 1][:], op=ALU.mult),
                         reads=[("lt", 2 * k_), ("lt", 2 * k_ + 1)], writes=[("pr", k_)])
                    P.op("act", I("activation", out=junk[:], in_=pr[k_][:], func=AF.Copy, accum_out=sm[k_][:]),
                         reads=[("pr", k_)], writes=[("lsm", k_), "junk"])
                    P.op("act", I("activation", out=ex[k_][:], in_=sm[k_][:], func=AF.Exp), reads=[("lsm", k_)], writes=[("lex", k_)])
                P.op("dve", I("tensor_tensor", out=neglam[:], in0=ex[1][:], in1=ex[0][:], op=ALU.subtract),
                     reads=[("lex", 0), ("lex", 1)], writes=["neglam0"])
                P.op("dve", I("tensor_scalar", out=neglam[:], in0=neglam[:], scalar1=-LAMBDA_INIT, scalar2=None, op0=ALU.add),
                     reads=["neglam0"], writes=["neglam"])
                P.op("sp", I("dma_start", out=gvec[:], in_=subln.rearrange("(o n) -> o n", o=1).partition_broadcast(128)), writes=["gvec0"], slot="gvec")
                P.op("pool", I("tensor_scalar", out=gvec[:], in0=gvec[:], scalar1=1.0 - LAMBDA_INIT, scalar2=None, op0=ALU.mult),
                     reads=["gvec0"], writes=["gvec"])
                o1_ring = sbring(sb, 3, [128, 128], F32, "o1")
                o_ring = sbring(sb, 3, [128, 128], F32, "o")
                junk2 = sb([128, 128], F32, "junk2")
                ss_ring = sbring(sb, 3, [128, 1], F32, "ss")
                rstd_ring = sbring(sb, 3, [128, 1], F32, "rstdA")
                c2_ring = sbring(sb, 3, [128, 1], F32, "c2")
            v_ring = sbring(sb, 2, [128, NT, VST], BF16, "v")
            for k_ in range(2):
                P.op("pool", I("memset", v_ring.tiles[k_][:, :, 128:130], 1.0), writes=[(("vones", k_))])
            pT_ring = sbring(sb, 4, [128, nmap, GQ], BF16, "pT")
            rinv_ring = sbring(sb, 4 * nmap, [128, 1], F32, "rinv")
            ob_ring = sbring(sb, 3, [128, 128], BF16, "ob")
            ost_ring = sbring(sb, 2, [128, GQ], BF16, "ost")
            LOOK = 2 if NSET == 2 else 1
            heads = {}

            def load_head(h):
                if h >= H:
                    return
                d = {}
                d["qn"] = qn_ring.next()
                d["kn"] = kn_ring.next()
                d["v"] = v_ring.next()
                if layer == 0:
                    d["qr"] = qr_ring.next()
                    P.op("sp", I("dma_start", out=d["qn"][0][:], in_=QN0[h]), writes=[d["qn"][1]], slot=d["qn"][1])
                    P.op("sp", I("dma_start", out=d["qr"][0][:], in_=QR0[h]), writes=[d["qr"][1]], slot=d["qr"][1])
                    P.op("sp", I("dma_start", out=d["kn"][0][:], in_=KN0[h]), writes=[d["kn"][1]], slot=d["kn"][1])
                else:
                    qk_ = qn_ring.i % 2
                    P.op("sp", I("dma_start", out=d["qn"][0][0:64, 0, :], in_=QT1[h][0:64, :]), reads=[("qz", qk_, 0), ("qz", qk_, 1)],
                         writes=[(d["qn"][1], 0)], slot=(d["qn"][1], 0))
                    P.op("sp", I("dma_start", out=d["qn"][0][64:128, 1, :], in_=QT1[h][64:128, :]), reads=[("qz", qk_, 0), ("qz", qk_, 1)],
                         writes=[(d["qn"][1], 1)], slot=(d["qn"][1], 1))
                    P.op("sp", I("dma_start", out=d["kn"][0][:], in_=KT1[h]), writes=[d["kn"][1]], slot=d["kn"][1])
                vk = v_ring.i % 2
                vsrc = Vd.rearrange("(t p) (h d) -> h p t d", p=128, d=128)[h]
                for q_ in range(4):
                    P.op("sp", I("dma_start", out=d["v"][0][:, q_ * 8:(q_ + 1) * 8, 0:128], in_=vsrc[:, q_ * 8:(q_ + 1) * 8, :]),
                         reads=[("vones", vk)], writes=[(d["v"][1], q_)], slot=(d["v"][1], q_))
                heads[h] = d

            set_ctr = [0]
            load_head(0)
            for h in range(H):
                load_head(h + 1)
                hd = heads.pop(h)
                qn_t, qn_r = hd["qn"]
                kn_t, kn_r = hd["kn"]
                v_t, v_r = hd["v"]
                if layer == 0:
                    qr_t, qr_r = hd["qr"]
                pairs = [(g, kt) for g in range(NGQ) for kt in range(NSUB * g + NSUB)]
                state = {}
                gstate = {}
                pending = []

                def emit_qk(n):
                    g, kt = pairs[n]
                    j = max(kt - NSUB * g, 0)
                    q0 = g * GQ + 128 * j
                    q1 = (g + 1) * GQ
                    ks = slice(kt * 128, (kt + 1) * 128)
                    diag = kt - NSUB * g >= 0
                    S_t, S_r = S_ring.next()
                    for m in range(nmap):
                        so = S_t[:, m, 128 * j:GQ]
                        if layer == 0:
                            P.op("pe", I("matmul", so, lhsT=kn_t[:, ks], rhs=qn_t[:, q0:q1], start=True, stop=False),
                                 reads=[kn_r, qn_r], writes=[(S_r, m)])
                            P.op("pe", I("matmul", so, lhsT=kr[:, ks], rhs=qr_t[:, q0:q1], start=False, stop=not diag),
                                 reads=["kr", qr_r], writes=[(S_r, m)])
                        else:
                            P.op("pe", I("matmul", so, lhsT=kn_t[:, ks], rhs=qn_t[:, m, q0:q1], start=True, stop=not diag),
                                 reads=[kn_r, (qn_r, m)], writes=[(S_r, m)])
                        if diag:
                            P.op("pe", I("matmul", S_t[:, m, 128 * j:128 * j + 128], lhsT=ident[:], rhs=mask[:, 0, 0:128], start=False, stop=True),
                                 reads=["ident", "mask"], writes=[(S_r, m)])
                    state[n] = (S_t, S_r)

                def emit_rest(n):
                    g, kt = pairs[n]
                    j = max(kt - NSUB * g, 0)
                    S_t, S_r = state.pop(n)
                    if kt == 0:
                        set_ctr[0] += 1
                        gstate[g] = dict(set=set_ctr[0] % NSET, started=set())
                    gs_ = gstate[g]
                    si = gs_["set"]
                    pT_t, pT_r = pT_ring.next()
                    for m in range(nmap):
                        P.op("act", I("activation", out=pT_t[:, m, 128 * j:GQ], in_=S_t[:, m, 128 * j:GQ], func=AF.Exp),
                             reads=[(S_r, m)], writes=[(pT_r, m)])
                    if kt == 0 and pending and NSET == 1:
                        P.op("pe", I("transpose", out=pt[:, 7, :], in_=ident[:], identity=ident[:]), reads=["ident"], writes=[("ptA", 7), "pe_tick"])
                        flush_pending()
                    for m in range(nmap):
                        for i in range(j, NSUB):
                            a = m * NSUB + i
                            bank = a // 3
                            first = bank not in gs_["started"]
                            gs_["started"].add(bank)
                            P.op("pe", I("matmul", acc_ap(si, a), lhsT=pT_t[:, m, i * 128:(i + 1) * 128], rhs=v_t[:, kt, 0:NV],
                                         start=first, stop=(kt == NSUB * g + i), skip_group_check=True),
                                 reads=[(pT_r, m), (v_r, kt // 8)], writes=[("acc", si, a), "pe_tick"])
                    flush_pending()
                    if kt >= NSUB * g:
                        pending.append((g, kt - NSUB * g, si))

                def flush_pending():
                    while pending:
                        finish(*pending.pop(0))

                def finish(g, i, si):
                    if i == 0:
                        gstate[g]["ost"] = ost_ring.next()
                    ost_t, ost_r = gstate[g]["ost"]
                    ob_t, ob_r = ob_ring.next()
                    if layer == 0:
                        ri_t, ri_r = rinv_ring.next()
                        P.op("dve", I("reciprocal", out=ri_t[:], in_=acc_ap(si, i, 128, 129)), reads=[("acc", si, i), "pe_tick"], writes=[ri_r])
                        P.op("dve", I("tensor_scalar", out=ob_t[:], in0=acc_ap(si, i, 0, 128), scalar1=ri_t[:], scalar2=None, op0=ALU.mult),
                             reads=[("acc", si, i), ri_r], writes=[ob_r])
                    else:
                        ri1, ri1r = rinv_ring.next()
                        ri2, ri2r = rinv_ring.next()
                        c2_t, c2_r = c2_ring.next()
                        o1_t, o1_r = o1_ring.next()
                        o_t, o_r = o_ring.next()
                        ss_t, ss_r = ss_ring.next()
                        rs_t, rs_r = rstd_ring.next()
                        P.op("dve", I("reciprocal", out=ri1[:], in_=acc_ap(si, i, 128, 129)), reads=[("acc", si, i), "pe_tick"], writes=[ri1r])
                        P.op("dve", I("reciprocal", out=ri2[:], in_=acc_ap(si, NSUB + i, 128, 129)), reads=[("acc", si, NSUB + i)], writes=[ri2r])
                        P.op("dve", I("tensor_tensor", out=c2_t[:], in0=ri2[:], in1=neglam[:], op=ALU.mult), reads=[ri2r, "neglam"], writes=[c2_r])
                        P.op("dve", I("tensor_scalar", out=o1_t[:], in0=acc_ap(si, i, 0, 128), scalar1=ri1[:], scalar2=None, op0=ALU.mult),
                             reads=[("acc", si, i), ri1r], writes=[o1_r])
                        P.op("dve", I("scalar_tensor_tensor", out=o_t[:], in0=acc_ap(si, NSUB + i, 0, 128), scalar=c2_t[:], in1=o1_t[:],
                                      op0=ALU.mult, op1=ALU.add), reads=[("acc", si, NSUB + i), c2_r, o1_r], writes=[o_r])
                        P.op("dve", I("tensor_tensor", out=junk2[:], in0=o_t[:], in1=o_t[:], op=ALU.mult), reads=[o_r], writes=["junk2"])
                        P.op("dve", I("tensor_reduce", out=ss_t[:], in_=junk2[:], axis=mybir.AxisListType.X, op=ALU.add), reads=["junk2"], writes=[ss_r])
                        P.op("dve", I("tensor_scalar", out=ss_t[:], in0=ss_t[:], scalar1=1.0 / 128, scalar2=RMS_EPS, op0=ALU.mult, op1=ALU.add),
                             reads=[ss_r], writes=[ss_r])
                        P.op("act", I("activation", out=ss_t[:], in_=ss_t[:], func=AF.Ln), reads=[ss_r], writes=[ss_r])
                        P.op("act", I("activation", out=rs_t[:], in_=ss_t[:], func=AF.Exp, scale=-0.5), reads=[ss_r], writes=[rs_r])
                        P.op("dve", I("scalar_tensor_tensor", out=ob_t[:], in0=o_t[:], scalar=rs_t[:], in1=gvec[:], op0=ALU.mult, op1=ALU.mult),
                             reads=[o_r, rs_r, "gvec"], writes=[ob_r])
                    P.op("pe", I("transpose", out=pt[:, i, :], in_=ob_t[:], identity=ident[:]), reads=[ob_r, "ident"], writes=[("ptA", i)])
                    evac(ost_t[:, i * 128:(i + 1) * 128], pt[:, i, :], [("ptA", i)], [(ost_r, i)], eng="dve")
                    if i == NSUB - 1:
                        P.op("sp", I("dma_start", out=OTd[h][:, g * GQ:(g + 1) * GQ], in_=ost_t[:]), reads=[(ost_r, k_) for k_ in range(NSUB)], slot=ost_r)

                units = [(g, kt, m) for (g, kt) in pairs for m in range(nmap)]

                def emit_qk_u(u):
                    g, kt, m = units[u]
                    j = max(kt - NSUB * g, 0)
                    q0 = g * GQ + 128 * j
                    q1 = (g + 1) * GQ
                    ks = slice(kt * 128, (kt + 1) * 128)
                    diag = kt - NSUB * g >= 0
                    S_t, S_r = S_ring.next()
                    P.op("pe", I("matmul", S_t[:, 0, 128 * j:GQ], lhsT=kn_t[:, ks], rhs=qn_t[:, m, q0:q1], start=True, stop=not diag),
                         reads=[kn_r, (qn_r, m)], writes=[S_r])
                    if diag:
                        P.op("pe", I("matmul", S_t[:, 0, 128 * j:128 * j + 128], lhsT=ident[:], rhs=mask[:, 0, 0:128], start=False, stop=True),
                             reads=["ident", "mask"], writes=[S_r])
                    state[("u", u)] = (S_t, S_r)

                def emit_rest_u(u):
                    g, kt, m = units[u]
                    j = max(kt - NSUB * g, 0)
                    S_t, S_r = state.pop(("u", u))
                    if m == 0:
                        if kt == 0:
                            set_ctr[0] += 1
                            gstate[g] = dict(set=set_ctr[0] % NSET, started=set())
                        state[("pT", g, kt)] = pT_ring.next()
                    gs_ = gstate[g]
                    si = gs_["set"]
                    pT_t, pT_r = state[("pT", g, kt)]
                    P.op("act", I("activation", out=pT_t[:, m, 128 * j:GQ], in_=S_t[:, 0, 128 * j:GQ], func=AF.Exp),
                         reads=[S_r], writes=[(pT_r, m)])
                    if m == 0 and kt == 0 and pending:
                        P.op("pe", I("transpose", out=pt[:, 7, :], in_=ident[:], identity=ident[:]), reads=["ident"], writes=[("ptA", 7), "pe_tick"])
                        flush_pending()
                    for i in range(j, NSUB):
                        a_ = m * NSUB + i
                        bank = a_ // 3
                        first = bank not in gs_["started"]
                        gs_["started"].add(bank)
                        P.op("pe", I("matmul", acc_ap(si, a_), lhsT=pT_t[:, m, i * 128:(i + 1) * 128], rhs=v_t[:, kt, 0:NV],
                                     start=first, stop=(kt == NSUB * g + i), skip_group_check=True),
                             reads=[(pT_r, m), (v_r, kt // 8)], writes=[("acc", si, a_), "pe_tick"])
                    if m == nmap - 1:
                        state.pop(("pT", g, kt))
                        flush_pending()
                        if kt >= NSUB * g:
                            pending.append((g, kt - NSUB * g, si))

                if UNIT:
                    LOOKU = 3
                    NU = len(units)
                    for u in range(min(LOOKU, NU)):
                        emit_qk_u(u)
                    for u in range(NU):
                        if u + LOOKU < NU:
                            emit_qk_u(u + LOOKU)
                        emit_rest_u(u)
                else:
                    N = len(pairs)
                    for n in range(min(LOOK, N)):
                        emit_qk(n)
                    for n in range(N):
                        if n + LOOK < N:
                            emit_qk(n + LOOK)
                        emit_rest(n)
                P.op("pe", I("transpose", out=pt[:, 7, :], in_=ident[:], identity=ident[:]), reads=["ident"], writes=[("ptA", 7), "pe_tick"])
                flush_pending()
            P.barrier()

    def attention_old(layer):
        assert layer == 0
        st, sb, ps = scope()
        with st:
            C = load_consts(sb, need_ident=True, need_mask=True)
            ident, mask = C["ident"], C["mask"]
            ones = sb([128, 128], BF16, "ones")
            P.op("pool", I("memset", ones[:], 1.0), writes=["ones"])
            qn_ring = sbring(sb, 2, [128, S], BF16, "qn")
            qr_ring = sbring(sb, 2, [128, S], BF16, "qr")
            kn_ring = sbring(sb, 2, [128, S], BF16, "kn")
            kr = sb([128, S], BF16, "kr")
            P.op("pool", I("memset", kr[64:128, :], 0.0), writes=["krz"])
            for k_ in range(2):
                P.op("pool", I("memset", qr_ring.tiles[k_][64:128, :], 0.0), writes=[("qrz", k_)])
            P.op("sp", I("dma_start", out=kr[0:64, :], in_=KR0[:, :]), writes=["kr"], slot="kr")
            v_ring = sbring(sb, 2, [128, NT, 128], BF16, "v")
            pT_ring = sbring(sb, 5, [128, GW], BF16, "pT")
            S_ring = Ring([ps([128, GW], F32, "S") for _ in range(4)], "S0o")
            O_ring = Ring([ps([128, GW], F32, "O") for _ in range(2)], "O0o")
            M_ring = Ring([ps([128, GW], F32, "M") for _ in range(2)], "M0o")
            ln_ring = sbring(sb, 2, [128, GW], F32, "lnM")
            rs_ring = sbring(sb, 2, [128, GW], F32, "rs")
            ost_ring = sbring(sb, 2, [128, GW], BF16, "ost")
            LOOK = 2
            heads = {}

            def load_head(h):
                if h >= H:
                    return
                d = dict(qn=qn_ring.next(), qr=qr_ring.next(), kn=kn_ring.next(), v=v_ring.next())
                P.op("sp", I("dma_start", out=d["qn"][0][:], in_=QN0[h]), writes=[d["qn"][1]], slot=d["qn"][1])
                P.op("sp", I("dma_start", out=d["qr"][0][0:64, :], in_=QR0[h]), writes=[d["qr"][1]], slot=d["qr"][1])
                P.op("sp", I("dma_start", out=d["kn"][0][:], in_=KN0[h]), writes=[d["kn"][1]], slot=d["kn"][1])
                vsrc = V0.rearrange("(t p) (h d) -> h p t d", p=128, d=128)[h]
                for q_ in range(4):
                    P.op("sp", I("dma_start", out=d["v"][0][:, q_ * 8:(q_ + 1) * 8, :], in_=vsrc[:, q_ * 8:(q_ + 1) * 8, :]),
                         writes=[(d["v"][1], q_)], slot=(d["v"][1], q_))
                heads[h] = d

            load_head(0)
            for h in range(H):
                load_head(h + 1)
                hd = heads.pop(h)
                qn_t, qn_r = hd["qn"]
                qr_t, qr_r = hd["qr"]
                kn_t, kn_r = hd["kn"]
                v_t, v_r = hd["v"]
                pairs = [(g, kt) for g in range(NG) for kt in range(4 * g + 4)]
                state = {}

                def emit_qk(n):
                    g, kt = pairs[n]
                    j = max(kt - 4 * g, 0)
                    c0 = 128 * j
                    qs = slice(g * GW + c0, (g + 1) * GW)
                    ks = slice(kt * 128, (kt + 1) * 128)
                    diag = kt >= 4 * g
                    S_t, S_r = S_ring.next()
                    P.op("pe", I("matmul", S_t[:, c0:GW], lhsT=kn_t[:, ks], rhs=qn_t[:, qs], start=True, stop=False),
                         reads=[kn_r, qn_r], writes=[S_r])
                    P.op("pe", I("matmul", S_t[:, c0:GW], lhsT=kr[:, ks], rhs=qr_t[:, qs], start=False, stop=not diag),
                         reads=["kr", "krz", qr_r, ("qrz", 0), ("qrz", 1)], writes=[S_r])
                    if diag:
                        P.op("pe", I("matmul", S_t[:, c0:c0 + 128], lhsT=ident[:], rhs=mask[:, 0, 0:128], start=False, stop=True),
                             reads=["ident", "mask"], writes=[S_r])
                    state[n] = (S_t, S_r)

                def emit_rest(n):
                    g, kt = pairs[n]
                    last = 4 * g + 3
                    j = max(kt - 4 * g, 0)
                    c0 = 128 * j
                    S_t, S_r = state.pop(n)
                    if kt == 0:
                        state["O"] = O_ring.next()
                        state["M"] = M_ring.next()
                    pT_t, pT_r = pT_ring.next()
                    P.op("act", I("activation", out=pT_t[:, c0:GW], in_=S_t[:, c0:GW], func=AF.Exp), reads=[S_r], writes=[pT_r])
                    O_t, O_r = state["O"]
                    M_t, M_r = state["M"]
                    P.op("pe", I("matmul", O_t[:, c0:GW], lhsT=v_t[:, kt, :], rhs=pT_t[:, c0:GW], start=(kt == 0), stop=(kt == last)),
                         reads=[(v_r, kt // 8), pT_r], writes=[O_r])
                    P.op("pe", I("matmul", M_t[:, c0:GW], lhsT=ones[:], rhs=pT_t[:, c0:GW], start=(kt == 0), stop=(kt == last)),
                         reads=["ones", pT_r], writes=[M_r])
                    if kt == last:
                        finish(g)

                def finish(g):
                    gs = slice(g * GW, (g + 1) * GW)
                    ost_t, ost_r = ost_ring.next()
                    O_t, O_r = state["O"]
                    M_t, M_r = state["M"]
                    ln_t, ln_r = ln_ring.next()
                    rs_t, rs_r = rs_ring.next()
                    P.op("act", I("activation", out=ln_t[:], in_=M_t[:], func=AF.Ln), reads=[M_r], writes=[ln_r])
                    P.op("act", I("activation", out=rs_t[:], in_=ln_t[:], func=AF.Exp, scale=-1.0), reads=[ln_r], writes=[rs_r])
                    P.op("dve", I("tensor_tensor", out=ost_t[:], in0=O_t[:], in1=rs_t[:], op=ALU.mult), reads=[O_r, rs_r], writes=[ost_r])
                    P.op("sp", I("dma_start", out=OT0[h][:, gs], in_=ost_t[:]), reads=[ost_r], slot=ost_r)

                N = len(pairs)
                for n in range(min(LOOK, N)):
                    emit_qk(n)
                for n in range(N):
                    if n + LOOK < N:
                        emit_qk(n + LOOK)
                    emit_rest(n)
            P.barrier()

    def outproj_ln(layer, wgu_pre=None):
        st, sb, ps = scope()
        OTd = OT0 if layer == 0 else OT1
        wo_d = w_o0 if layer == 0 else w_o1
        res_d = x if layer == 0 else H2
        Hd, HTd = (H1, H1T) if layer == 0 else (H3, H3T)
        with st:
            C = load_consts(sb, need_ident=True, need_mh=True)
            ident, mh = C["ident"], C["mh"]
            wo = sb([128, 8, D], BF16, "wo")
            load_w_bf16(wo, wo_d, 8, "wo")
            Gt = sb([128, D], F32, "lnG")
            Bt = sb([128, D], F32, "lnB")
            P.op("sp", I("dma_start", out=Gt[:], in_=ln1_g[layer:layer + 1, :].partition_broadcast(128)), writes=["lnG"], slot="lnG")
            P.op("sp", I("dma_start", out=Bt[:], in_=ln1_b[layer:layer + 1, :].partition_broadcast(128)), writes=["lnB"], slot="lnB")
            ot_ring = sbring(sb, 2, [128, H, GW], BF16, "otg")
            res_ring = sbring(sb, 4, [128, D], F32, "res")
            pt_ring = Ring([ps([128, 8, 128], BF16, "pt") for _ in range(2)], "pt3")
            a_ring = Ring([ps([128, GW], F32, "a") for _ in range(6)], "a3")
            lnp = LNPipe(sb, ps, Gt, Bt, ident, mh, HTd, pt_ring)
            res_q = {}
            ot_q = {}

            def load_res(t):
                if t < NT:
                    res_t, res_r = res_ring.next()
                    P.op("sp", I("dma_start", out=res_t[:], in_=res_d[t * 128:(t + 1) * 128, :]), writes=[(res_r, 0), (res_r, 1)], slot=res_r)
                    res_q[t] = (res_t, res_r)

            def load_ot(g):
                if g < NG:
                    ot_t, ot_r = ot_ring.next()
                    P.op("sp", I("dma_start", out=ot_t[:], in_=OTd.rearrange("h p t -> p h t")[:, :, g * GW:(g + 1) * GW]), writes=[ot_r], slot=ot_r)
                    ot_q[g] = (ot_t, ot_r)

            load_ot(0)
            load_res(0)
            load_res(1)
            for g in range(NG):
                ot_t, ot_r = ot_q.pop(g)
                load_ot(g + 1)
                for i in range(4):
                    t = g * 4 + i
                    load_res(t + 2)
                    if wgu_pre is not None and t < 22:
                        half, s_ = t % 2, t // 2
                        lo = half * DFF + s_ * 256
                        vsrc = w_gu[layer].rearrange("(c p) n -> p c n", p=128)
                        P.op("pool", I("dma_start", out=wgu_pre[:, :, lo:lo + 256], in_=vsrc[:, :, lo:lo + 256]),
                             writes=[("wgu", half, s_)], slot=("wgu", half, s_))
                    res_t, res_r = res_q.pop(t)
                    z_t, z_r = res_t, res_r
                    for hh in range(2):
                        a_t, a_r = a_ring.next()
                        for h in range(H):
                            P.op("pe", I("matmul", a_t[:], lhsT=ot_t[:, h, i * 128:(i + 1) * 128],
                                                                                           rhs=wo[:, h, hh * 512:(hh + 1) * 512], start=(h == 0), stop=(h == H - 1)),
                                 reads=[ot_r, ("wo", 0)], writes=[a_r])
                        P.op("dve", I("scalar_tensor_tensor",
                            out=z_t[:, hh * 512:(hh + 1) * 512], in0=res_t[:, hh * 512:(hh + 1) * 512], scalar=ALPHA, in1=a_t[:], op0=ALU.mult, op1=ALU.add),
                            reads=[(res_r, hh), a_r], writes=[(z_r, hh)])
                    lnp.push(z_t, [(z_r, 0), (z_r, 1)], Hd[t * 128:(t + 1) * 128, :], t)
            lnp.flush()
            P.barrier()

    def ffn_ln(layer, wgu_pre=None):
        st, sb, ps = scope()
        HTin = H1T if layer == 0 else H3T
        Hin = H1 if layer == 0 else H3
        Hd, HTd = (H2, H2T) if layer == 0 else (out, None)
        with st:
            C = load_consts(sb, need_ident=True, need_mh=True)
            ident, mh = C["ident"], C["mh"]
            wdn = sb([128, NF, D], BF16, "wdn")
            NSPL = 11
            v = w_gu[layer].rearrange("(c p) n -> p c n", p=128)
            if wgu_pre is not None:
                wgu = wgu_pre
            else:
                wgu = sb([128, 8, 2 * DFF], BF16, "wgu")
                for s_ in range(NSPL):
                    for half in range(2):
                        lo = half * DFF + s_ * 256
                        P.op("pool", I("dma_start", out=wgu[:, :, lo:lo + 256], in_=v[:, :, lo:lo + 256]),
                             writes=[("wgu", half, s_)], slot=("wgu", half, s_))
            vd = w_dn[layer].rearrange("(f p) n -> p f n", p=128)
            for s_ in range(2):
                P.op("pool", I("dma_start", out=wdn[:, s_ * 11:(s_ + 1) * 11, :], in_=vd[:, s_ * 11:(s_ + 1) * 11, :]),
                     writes=[("wdn", s_)], slot=("wdn", s_))
            Gt = sb([128, D], F32, "lnG")
            Bt = sb([128, D], F32, "lnB")
            P.op("sp", I("dma_start", out=Gt[:], in_=ln2_g[layer:layer + 1, :].partition_broadcast(128)), writes=["lnG"], slot="lnG")
            P.op("sp", I("dma_start", out=Bt[:], in_=ln2_b[layer:layer + 1, :].partition_broadcast(128)), writes=["lnB"], slot="lnB")
            hin_ring = sbring(sb, 1, [128, 8, GW], BF16, "hin")
            actT = sb([128, NF, GW], BF16, "actT")
            sg_ring = sbring(sb, 2, [128, GW], F32, "sg")
            res_ring = sbring(sb, 3, [128, D], F32, "res")
            pt_ring = Ring([ps([128, 8, 128], BF16, "pt")], "pt4")
            lnp = LNPipe(sb, ps, Gt, Bt, ident, mh, HTd, pt_ring)
            g_ring = Ring([ps([128, GW], F32, "gb") for _ in range(2)], "gbk")
            u_ring = Ring([ps([128, GW], F32, "ub") for _ in range(2)], "ubk")
            d_ring = Ring([ps([128, GW], F32, "db") for _ in range(3)], "dbk")
            res_q = {}

            def load_res(t):
                res_t, res_r = res_ring.next()
                P.op("sp", I("dma_start", out=res_t[:], in_=Hin[t * 128:(t + 1) * 128, :]), writes=[(res_r, 0), (res_r, 1)], slot=res_r)
                res_q[t] = (res_t, res_r)

            for g in range(NG):
                gs = slice(g * GW, (g + 1) * GW)
                hin_t, hin_r = hin_ring.next()
                P.op("sp", I("dma_start", out=hin_t[:], in_=HTin.rearrange("c p t -> p c t")[:, :, gs]), writes=[hin_r], slot=hin_r)
                for f in range(NF):
                    gb, gr = g_ring.next()
                    ub, ur = u_ring.next()
                    wres = [("wgu", 0, f // 2), ("wgu", 1, f // 2)]
                    for c in range(8):
                        P.op("pe", I("matmul", gb[:], lhsT=wgu[:, c, f * 128:(f + 1) * 128], rhs=hin_t[:, c, :],
                                                                                start=(c == 0), stop=(c == 7)), reads=[hin_r, wres[0]], writes=[gr])
                    for c in range(8):
                        P.op("pe", I("matmul", ub[:], lhsT=wgu[:, c, DFF + f * 128:DFF + (f + 1) * 128], rhs=hin_t[:, c, :],
                                                                                start=(c == 0), stop=(c == 7)), reads=[hin_r, wres[1]], writes=[ur])
                    sg_t, sg_r = sg_ring.next()
                    P.op("act", I("activation", out=sg_t[:], in_=gb[:], func=AF.Silu), reads=[gr], writes=[sg_r])
                    P.op("dve", I("tensor_tensor", out=actT[:, f, :], in0=ub[:], in1=sg_t[:], op=ALU.mult),
                         reads=[ur, sg_r], writes=[("actT", f)])
                for i in range(4):
                    t = g * 4 + i
                    if i == 0:
                        load_res(t)
                    if i < 3:
                        load_res(t + 1)
                    res_t, res_r = res_q.pop(t)
                    z_t, z_r = res_t, res_r
                    for hh in range(2):
                        db, dr = d_ring.next()
                        for f in range(NF):
                            P.op("pe", I("matmul", db[:], lhsT=actT[:, f, i * 128:(i + 1) * 128],
                                                                              rhs=wdn[:, f, hh * 512:(hh + 1) * 512], start=(f == 0), stop=(f == NF - 1)),
                                 reads=[("actT", f), ("wdn", f // 11)], writes=[dr])
                        P.op("dve", I("scalar_tensor_tensor",
                            out=z_t[:, hh * 512:(hh + 1) * 512], in0=res_t[:, hh * 512:(hh + 1) * 512], scalar=ALPHA, in1=db[:], op0=ALU.mult, op1=ALU.add),
                            reads=[(res_r, hh), dr], writes=[(z_r, hh)])
                    lnp.push(z_t, [(z_r, 0), (z_r, 1)], Hd[t * 128:(t + 1) * 128, :], t)
            lnp.flush()
            P.barrier()

    def phase_proj1():
        st, sb, ps = scope()
        with st:
            wk = sb([128, 8, D], BF16, "wk")
            wkr = sb([128, 8, D], BF16, "wkr")
            wq = sb([128, 8, D], BF16, "wq")
            wqr = sb([128, 8, D], BF16, "wqr")
            wv = sb([128, 8, D], BF16, "wv")
            kvv = kv_w.rearrange("(c p) n -> p c n", p=128)
            P.op("pool", I("dma_start", out=wk[:], in_=kvv[:, :, 0:D]), writes=["wk"], slot="wk")
            P.op("pool", I("dma_start", out=wq[:], in_=w_q1.rearrange("(c p) n -> p c n", p=128)), writes=["wq"], slot="wq")
            P.op("pool", I("dma_start", out=wv[:], in_=kvv[:, :, D:2 * D]), writes=["wv"], slot="wv")
            for (src, dst, nm) in ((wk, wkr, "wk"), (wq, wqr, "wq")):
                s4 = src[:].rearrange("p c (b d) -> p c b d", d=64)
                d4 = dst[:].rearrange("p c (b d) -> p c b d", d=64)
                for c in range(8):
                    P.op("act", I("activation", out=d4[:, c, :, 0:32], in_=s4[:, c, :, 32:64], func=AF.Copy, scale=-1.0),
                         reads=[nm], writes=[(nm + "r", c, 0)])
                    P.op("dve", I("tensor_copy", out=d4[:, c, :, 32:64], in_=s4[:, c, :, 0:32]),
                         reads=[nm], writes=[(nm + "r", c, 1)])
            rres = {nm: [(nm + "r", c, k) for c in range(8) for k in range(2)] for nm in ("wk", "wq")}
            hin_ring = sbring(sb, 2, [128, 8, GW], BF16, "hin")
            cos_ring = sbring(sb, 2, [128, GW], F32, "cosd")
            sin_ring = sbring(sb, 2, [128, GW], F32, "sind")
            t1_ring = sbring(sb, 3, [128, GW], F32, "t1")
            t2_ring = sbring(sb, 3, [128, GW], F32, "t2")
            kst_ring = sbring(sb, 2, [128, H, GW], BF16, "kst")
            qst_ring = sbring(sb, 2, [128, H, GW], BF16, "qst")
            vst_ring = sbring(sb, 2, [128, 4, D], BF16, "vst")
            banks = Ring([ps([128, GW], F32, "bk") for _ in range(8)], "bank5")
            for g in range(NG):
                gs = slice(g * GW, (g + 1) * GW)
                hin_t, hin_r = hin_ring.next()
                cos_t, cos_r = cos_ring.next()
                sin_t, sin_r = sin_ring.next()
                P.op("sp", I("dma_start", out=hin_t[:], in_=H2T.rearrange("c p t -> p c t")[:, :, gs]), writes=[hin_r], slot=hin_r)
                P.op("sp", I("dma_start", out=cos_t[:], in_=cosd_d[:, gs]), writes=[cos_r], slot=cos_r)
                P.op("sp", I("dma_start", out=sin_t[:], in_=sind_d[:, gs]), writes=[sin_r], slot=sin_r)
                k_t, k_r = kst_ring.next()
                q_t, q_r = qst_ring.next()
                for (w_, wr_, nm, dst_t, dst_r, sc) in ((wk, wkr, "wk", k_t, k_r, 1.0), (wq, wqr, "wq", q_t, q_r, SC1)):
                    for h in range(H):
                        ab, ar = banks.next()
                        bb, br = banks.next()
                        for c in range(8):
                            P.op("pe", I("matmul", ab[:], lhsT=w_[:, c, h * 128:(h + 1) * 128], rhs=hin_t[:, c, :],
                                                                                           start=(c == 0), stop=(c == 7)), reads=[hin_r, nm], writes=[ar])
                        for c in range(8):
                            P.op("pe", I("matmul", bb[:], lhsT=wr_[:, c, h * 128:(h + 1) * 128], rhs=hin_t[:, c, :],
                                                                                             start=(c == 0), stop=(c == 7)), reads=[hin_r] + rres[nm], writes=[br])
                        t1, t1r = t1_ring.next()
                        t2, t2r = t2_ring.next()
                        P.op("dve", I("scalar_tensor_tensor", out=t1[:], in0=ab[:], scalar=sc, in1=cos_t[:],
                                                                                                   op0=ALU.mult, op1=ALU.mult), reads=[ar, cos_r], writes=[t1r])
                        P.op("dve", I("scalar_tensor_tensor", out=t2[:], in0=bb[:], scalar=sc, in1=sin_t[:],
                                                                                                   op0=ALU.mult, op1=ALU.mult), reads=[br, sin_r], writes=[t2r])
                        P.op("pool", I("tensor_tensor", out=dst_t[:, h, :], in0=t1[:], in1=t2[:], op=ALU.add),
                             reads=[t1r, t2r], writes=[(dst_r, h)])
                P.op("sp", I("dma_start", out=KT1.rearrange("h p t -> p h t")[:, :, gs], in_=k_t[:]),
                     reads=[(k_r, h) for h in range(H)], slot=k_r)
                P.op("sp", I("dma_start", out=QT1.rearrange("h p t -> p h t")[:, :, gs], in_=q_t[:]),
                     reads=[(q_r, h) for h in range(H)], slot=q_r)
                v_t, v_r = vst_ring.next()
                for i in range(4):
                    for hh in range(2):
                        bk, bkr = banks.next()
                        for c in range(8):
                            P.op("pe", I("matmul", bk[:], lhsT=hin_t[:, c, i * 128:(i + 1) * 128],
                                                                                           rhs=wv[:, c, hh * 512:(hh + 1) * 512], start=(c == 0), stop=(c == 7)),
                                 reads=[hin_r, "wv"], writes=[bkr])
                        evac(v_t[:, i, hh * 512:(hh + 1) * 512], bk[:], [bkr], [(v_r, i, hh)], eng="act")
                P.op("sp", I("dma_start", out=V1[g * GW:(g + 1) * GW, :].rearrange("(i p) n -> p i n", p=128), in_=v_t[:]),
                     reads=[(v_r, i, hh) for i in range(4) for hh in range(2)], slot=v_r)
            P.barrier()

    if 1 in phases:
        phase1()
    if 2 in phases:
        attention_old(0)
    def layer_tail(layer, pa, pb):
        if pa in phases and pb in phases:
            st0, sb0, ps0 = scope()
            with st0:
                wgu_pre = sb0([128, 8, 2 * DFF], BF16, "wgu")
                outproj_ln(layer, wgu_pre)
                ffn_ln(layer, wgu_pre)
        else:
            if pa in phases:
                outproj_ln(layer)
            if pb in phases:
                ffn_ln(layer)

    layer_tail(0, 3, 4)
    if 5 in phases:
        phase_proj1()
    if 6 in phases:
        attention(1)
    layer_tail(1, 7, 8)
    P.emit()
    return nc, P


def _rope_tables(dim):
    inv = (1.0 / (10000.0 ** (np.arange(0, dim, 2, dtype=np.float32) / np.float32(dim)))).astype(np.float32)
    ang = np.arange(S, dtype=np.float32)[:, None] * inv[None, :]
    ang = np.concatenate([ang, ang], axis=-1).astype(np.float32)
    return np.ascontiguousarray(np.cos(ang).T.astype(np.float32)), np.ascontiguousarray(np.sin(ang).T.astype(np.float32))


def _consts():
    ident = np.eye(128, dtype=np.float32).astype(ml_dtypes.bfloat16)
    ki = np.arange(128)[:, None, None]
    j = np.arange(4)[None, :, None]
    qi = np.arange(GW)[None, None, :]
    mask = np.where(qi >= 128 * j + ki, 0.0, NEG).astype(np.float32).astype(ml_dtypes.bfloat16)
    cm, sm = _rope_tables(64)
    cd = np.ascontiguousarray(np.concatenate([cm, cm], axis=0))
    sd = np.ascontiguousarray(np.concatenate([sm, sm], axis=0))
    return {"c_ident": ident, "c_mask": np.ascontiguousarray(mask), "c_cosm": cm, "c_sinm": sm, "c_cosd": cd, "c_sind": sd}


_SQUEEZE = ("mla_w_dq", "mla_q_norm", "mla_w_uq", "mla_w_dkv", "mla_kv_norm", "mla_w_ukv", "mla_w_o",
            "diff_w_q", "diff_subln", "diff_w_o")


def make_in_maps(inputs, n_cores=8):
    common = dict(_consts())
    for k, v in inputs.items():
        if k == "x":
            continue
        a = np.ascontiguousarray(np.asarray(v, dtype=np.float32))
        if k in _SQUEEZE:
            a = np.ascontiguousarray(a[0])
        common[k] = a
    xs = np.asarray(inputs["x"], dtype=np.float32)
    maps = []
    for c in range(n_cores):
        m = dict(common)
        m["x"] = np.ascontiguousarray(xs[c])
        maps.append(m)
    return maps


_CACHE = {}


def kernel(**inputs):
    if "nc" not in _CACHE:
        _CACHE["nc"] = build_program()[0]
    nc = _CACHE["nc"]
    in_maps = make_in_maps(inputs, 8)
    res = run_bass_kernel_spmd(nc, in_maps, core_ids=list(range(8)))
    return np.stack([np.asarray(r["out"], dtype=np.float32) for r in res.results], axis=0)
```
